# Optimizing a Trainium2 kernel written in Bass

```python
import jax, jax.numpy as jnp
from jax import lax
import numpy as np

D_MODEL = 1024
BATCH = 8
SEQ = 4096
DEPTH = 1

HEAD_DIM = 64
NA_HEADS = 8
GQA_HEADS = 8
GQA_KV_HEADS = 2
GQA_GROUP = GQA_HEADS // GQA_KV_HEADS
NA_WIDTH = NA_HEADS * HEAD_DIM
GQA_WIDTH = GQA_HEADS * HEAD_DIM
KV_WIDTH = GQA_KV_HEADS * HEAD_DIM
MIX_WIDTH = NA_WIDTH + GQA_WIDTH
IN_WIDTH = 3 * NA_WIDTH + GQA_WIDTH + 2 * KV_WIDTH
IN_SPLITS = (NA_WIDTH, 2 * NA_WIDTH, 3 * NA_WIDTH,
             3 * NA_WIDTH + GQA_WIDTH, 3 * NA_WIDTH + GQA_WIDTH + KV_WIDTH)
GRID_W = 64
NA_WIN_H = 8
NA_WIN_W = 16
Q_BLOCK = 128
ROPE_THETA = 10000.0
ROPE_ROW_DIMS = HEAD_DIM // 2
ROPE_COL_DIMS = HEAD_DIM - ROPE_ROW_DIMS
D_FF = -(-8 * D_MODEL // (3 * 256)) * 256
N_MOD = 6
EPS = 1e-6

kernel_name = "hybrid_natten_gqa_axialrope_adaln_block"


def rms_norm(x, gain):
    xf = x.astype(jnp.float32)
    y = xf * lax.rsqrt(jnp.mean(xf * xf, axis=-1, keepdims=True) + EPS)
    return (y * gain.astype(jnp.float32)).astype(x.dtype)


def axial_rope_tables(n_tokens):
    t = jnp.arange(n_tokens)
    row = (t // GRID_W).astype(jnp.float32)
    col = (t % GRID_W).astype(jnp.float32)

    def angles(pos, dims):
        inv = ROPE_THETA ** (-jnp.arange(0, dims, 2, dtype=jnp.float32) / dims)
        return pos[:, None] * inv[None, :]

    ang = jnp.concatenate([angles(row, ROPE_ROW_DIMS), angles(col, ROPE_COL_DIMS)], axis=-1)
    return jnp.cos(ang), jnp.sin(ang)


def apply_axial_rope(x, cos, sin):
    xf = x.astype(jnp.float32)
    x1, x2 = jnp.split(xf, 2, axis=-1)
    out = jnp.concatenate([x1 * cos - x2 * sin, x1 * sin + x2 * cos], axis=-1)
    return out.astype(x.dtype)


def neighborhood_attention(q, k, v, rpb):
    B, H, T, D = q.shape
    rows = T // GRID_W
    kh = min(NA_WIN_H, rows)
    kw = NA_WIN_W
    q = q.reshape(B, H, rows, GRID_W, D)
    k = k.reshape(B, H, rows, GRID_W, D)
    v = v.reshape(B, H, rows, GRID_W, D)
    cols = jnp.arange(GRID_W)
    col_start = jnp.clip(cols - kw // 2, 0, GRID_W - kw)
    col_idx = col_start[:, None] + jnp.arange(kw)[None, :]
    col_off = col_idx - cols[:, None] + (NA_WIN_W - 1)
    scale = D ** -0.5

    def row_block(r):
        rs = jnp.clip(r - kh // 2, 0, rows - kh)
        q_r = lax.dynamic_index_in_dim(q, r, axis=2, keepdims=False)
        k_band = lax.dynamic_slice_in_dim(k, rs, kh, axis=2)
        v_band = lax.dynamic_slice_in_dim(v, rs, kh, axis=2)
        k_nb = k_band[:, :, :, col_idx, :]
        v_nb = v_band[:, :, :, col_idx, :]
        row_off = rs + jnp.arange(kh) - r + (NA_WIN_H - 1)
        bias = rpb[:, row_off[None, :, None], col_off[:, None, :]]
        s = jnp.einsum('bhwd,bhiwjd->bhwij', q_r, k_nb,
                       preferred_element_type=jnp.float32) * scale
        s = s + bias[None].astype(jnp.float32)
        p = jax.nn.softmax(s.reshape(B, H, GRID_W, kh * kw), axis=-1)
        p = p.reshape(B, H, GRID_W, kh, kw).astype(v.dtype)
        return jnp.einsum('bhwij,bhiwjd->bhwd', p, v_nb)

    o = lax.map(row_block, jnp.arange(rows))
    return o.transpose(1, 2, 0, 3, 4).reshape(B, H, T, D)


def gqa_attention(q, k, v):
    B, Hkv, G, T, D = q.shape
    nb = T // Q_BLOCK
    qb = q.reshape(B, Hkv, G, nb, Q_BLOCK, D).transpose(3, 0, 1, 2, 4, 5)
    scale = D ** -0.5

    def block(q_i):
        s = jnp.einsum('bkgqd,bksd->bkgqs', q_i, k,
                       preferred_element_type=jnp.float32) * scale
        p = jax.nn.softmax(s, axis=-1).astype(v.dtype)
        return jnp.einsum('bkgqs,bksd->bkgqd', p, v)

    o = lax.map(block, qb)
    return o.transpose(1, 2, 3, 0, 4, 5).reshape(B, Hkv * G, T, D)


def setup_inputs(seed: int = 0) -> dict:
    key = jax.random.key(seed)
    ks = jax.random.split(key, 16)
    f32 = jnp.float32
    nrm = lambda k, shape, s: jax.random.normal(k, shape, f32) * s
    return {
        "x": nrm(ks[0], (BATCH, SEQ, D_MODEL), 1.0),
        "c": nrm(ks[1], (BATCH, D_MODEL), 1.0),
        "w_ada": nrm(ks[2], (DEPTH, D_MODEL, N_MOD * D_MODEL), 0.5 * D_MODEL ** -0.5),
        "b_ada": nrm(ks[3], (DEPTH, N_MOD * D_MODEL), 0.02),
        "g_attn": 1.0 + nrm(ks[4], (DEPTH, D_MODEL), 0.02),
        "w_in": nrm(ks[5], (DEPTH, D_MODEL, IN_WIDTH), D_MODEL ** -0.5),
        "g_q": 1.0 + nrm(ks[6], (DEPTH, HEAD_DIM), 0.02),
        "g_k": 1.0 + nrm(ks[7], (DEPTH, HEAD_DIM), 0.02),
        "rpb": nrm(ks[8], (DEPTH, NA_HEADS, 2 * NA_WIN_H - 1, 2 * NA_WIN_W - 1), 0.1),
        "w_o": nrm(ks[9], (DEPTH, MIX_WIDTH, D_MODEL), MIX_WIDTH ** -0.5),
        "g_ffn": 1.0 + nrm(ks[10], (DEPTH, D_MODEL), 0.02),
        "w_gate": nrm(ks[11], (DEPTH, D_MODEL, D_FF), D_MODEL ** -0.5),
        "w_up": nrm(ks[12], (DEPTH, D_MODEL, D_FF), D_MODEL ** -0.5),
        "w_down": nrm(ks[13], (DEPTH, D_FF, D_MODEL), D_FF ** -0.5),
        "g_final": 1.0 + nrm(ks[14], (D_MODEL,), 0.02),
    }


def reference(x, c, w_ada, b_ada, g_attn, w_in, g_q, g_k, rpb, w_o,
              g_ffn, w_gate, w_up, w_down, g_final):
    B, T, _ = x.shape
    cos, sin = axial_rope_tables(T)
    c_act = jax.nn.silu(c)

    def heads(t, n):
        return t.reshape(B, T, n, HEAD_DIM).transpose(0, 2, 1, 3)

    for l in range(DEPTH):
        mod = (jnp.dot(c_act, w_ada[l]) + b_ada[l])[:, None, :]
        shift_a, scale_a, gate_a, shift_f, scale_f, gate_f = jnp.split(mod, N_MOD, axis=-1)

        h = rms_norm(x, g_attn[l]) * (1.0 + scale_a) + shift_a
        proj = jnp.dot(h, w_in[l])
        q_na, k_na, v_na, q_g, k_g, v_g = jnp.split(proj, IN_SPLITS, axis=-1)

        o_na = neighborhood_attention(heads(q_na, NA_HEADS), heads(k_na, NA_HEADS),
                                      heads(v_na, NA_HEADS), rpb[l])

        q_g = apply_axial_rope(rms_norm(heads(q_g, GQA_HEADS), g_q[l]), cos, sin)
        k_g = apply_axial_rope(rms_norm(heads(k_g, GQA_KV_HEADS), g_k[l]), cos, sin)
        v_g = heads(v_g, GQA_KV_HEADS)
        o_g = gqa_attention(q_g.reshape(B, GQA_KV_HEADS, GQA_GROUP, T, HEAD_DIM),
                            k_g, v_g)

        o = jnp.concatenate([o_na, o_g], axis=1).transpose(0, 2, 1, 3).reshape(B, T, MIX_WIDTH)
        x = x + gate_a * jnp.dot(o, w_o[l])

        h = rms_norm(x, g_ffn[l]) * (1.0 + scale_f) + shift_f
        ff = jax.nn.silu(jnp.dot(h, w_gate[l])) * jnp.dot(h, w_up[l])
        x = x + gate_f * jnp.dot(ff, w_down[l])

    return rms_norm(x, g_final)
```

```python
import numpy as np
import ml_dtypes
from contextlib import ExitStack
import concourse.bass as bass
import concourse.mybir as mybir
from concourse.bass_utils import run_bass_kernel_spmd

F32 = mybir.dt.float32
BF16 = mybir.dt.bfloat16
AF = mybir.ActivationFunctionType
ALU = mybir.AluOpType
AX = mybir.AxisListType

T = 4096
D = 1024
DFF = 2816
NF = DFF // 128
NB = 8
EPS = 1e-6
NEG = -30000.0
N_CORES = 8


class Sched:
    N_DMA_SEMS = 12

    def __init__(self, nc, stack):
        self.nc = nc
        self.eng = {"pe": nc.tensor, "act": nc.scalar, "dve": nc.vector,
                    "pool": nc.gpsimd, "sp": nc.sync}
        self.sem = {e: stack.enter_context(nc.semaphore("s_" + e)) for e in self.eng}
        self.cnt = {e: 0 for e in self.eng}
        self.dsem = {q: [stack.enter_context(nc.semaphore(f"d_{q}{i}")) for i in range(self.N_DMA_SEMS)]
                     for q in ("sp", "pool", "act")}
        self.dcnt = {q: [0] * self.N_DMA_SEMS for q in self.dsem}
        self.dnext = {q: 0 for q in self.dsem}
        self.dlast = {q: [None] * self.N_DMA_SEMS for q in self.dsem}
        self.seen = {e: {} for e in self.eng}
        self.res = {}
        self.nwaits = 0
        self.nops = {e: 0 for e in self.eng}

    def _wait(self, e, tok):
        sem, val, peng = tok
        key = sem.name
        if self.seen[e].get(key, 0) >= val:
            return
        self.eng[e].wait_ge(sem, val)
        self.seen[e][key] = val
        self.nwaits += 1

    def _deps(self, e, reads, writes):
        toks = []
        for r in reads:
            st = self.res.get(r)
            if st and st[0] is not None:
                toks.append((st[0], "raw"))
        for w in writes:
            st = self.res.get(w)
            if st:
                if st[0] is not None:
                    toks.append((st[0], "waw"))
                for t in st[1]:
                    toks.append((t, "war"))
        for tok, kind in toks:
            if tok[2] == e and e == "pe":
                continue
            self._wait(e, tok)

    def _commit(self, tok, reads, writes):
        for r in reads:
            st = self.res.setdefault(r, [None, []])
            st[1].append(tok)
            if len(st[1]) > 64:
                best = {}
                for t in st[1]:
                    k = t[0].name
                    if k not in best or best[k][1] < t[1]:
                        best[k] = t
                st[1] = list(best.values())
        for w in writes:
            self.res[w] = [tok, []]

    def op(self, e, fn, reads=(), writes=(), sig=True):
        self._deps(e, reads, writes)
        ins = fn(self.eng[e])
        self.nops[e] += 1
        if sig:
            self.cnt[e] += 1
            ins.then_inc(self.sem[e], 1)
            tok = (self.sem[e], self.cnt[e], e)
        else:
            tok = (self.sem[e], self.cnt[e] + 1, e)
        self._commit(tok, reads, writes)
        return ins

    def dma(self, q, out, in_, reads=(), writes=(), **kw):
        self._deps(q, reads, writes)
        i = self.dnext[q]
        self.dnext[q] = (i + 1) % self.N_DMA_SEMS
        prev = self.dlast[q][i]
        if prev is not None:
            self._wait(q, prev)
        self.dcnt[q][i] += 16
        ins = self.eng[q].dma_start(out=out, in_=in_, **kw)
        ins.then_inc(self.dsem[q][i], 16)
        tok = (self.dsem[q][i], self.dcnt[q][i], None)
        self.dlast[q][i] = tok
        self._commit(tok, reads, writes)
        return ins

    def all_tokens(self):
        toks = []
        for e in self.eng:
            if self.cnt[e] > 0:
                toks.append((self.sem[e], self.cnt[e], e))
        for q in self.dsem:
            for t in self.dlast[q]:
                if t is not None:
                    toks.append(t)
        return toks

    def barrier(self):
        toks = self.all_tokens()
        for e in self.eng:
            for t in toks:
                if t[2] == e:
                    continue
                self._wait(e, t)
        self.res = {}

    def finish(self, e="sp"):
        for t in self.all_tokens():
            if t[2] != e:
                self._wait(e, t)


def _rs(r):
    return min(max(r - 4, 0), 56)


def _cs(w):
    return min(max(w - 8, 0), 48)


def _chunk_rows(j):
    rows = [r for r in range(64) if any(_rs(r) <= ka < _rs(r) + 8 for ka in (2 * j, 2 * j + 1))]
    return rows[0], rows[-1]


SPECIAL = {2: 0, 3: 1, 28: 2, 29: 3}


def _tile_row0(j):
    return _chunk_rows(j)[0] if j in SPECIAL else 2 * j - 4


def _bias_block(rpb_h, ka, r):
    blk = np.full((64, 64), NEG, np.float32)
    if not (_rs(r) <= ka < _rs(r) + 8):
        return blk
    a = ka - r + 7
    for w in range(64):
        c0 = _cs(w)
        kc = np.arange(c0, c0 + 16)
        blk[kc, w] = rpb_h[a, kc - w + 15]
    return blk


def build_bias_tiles(rpb):
    rpb = np.asarray(rpb, np.float32)
    b_int = np.full((8, 128, 640), NEG, np.float32)
    b_sp = np.full((8, 4, 128, 768), NEG, np.float32)
    j0 = 10
    for h in range(8):
        for kdr in range(2):
            for qr in range(10):
                b_int[h, kdr * 64:(kdr + 1) * 64, qr * 64:(qr + 1) * 64] = \
                    _bias_block(rpb[h], 2 * j0 + kdr, 2 * j0 - 4 + qr)
        for j, si in SPECIAL.items():
            r0, r1 = _chunk_rows(j)
            for kdr in range(2):
                for r in range(r0, r1 + 1):
                    b_sp[h, si, kdr * 64:(kdr + 1) * 64, (r - r0) * 64:(r - r0 + 1) * 64] = \
                        _bias_block(rpb[h], 2 * j + kdr, r)
    return b_int, b_sp


def rope_table():
    t = np.arange(T)
    row = (t // 64).astype(np.float32)
    col = (t % 64).astype(np.float32)
    inv = (10000.0 ** (-np.arange(0, 32, 2, dtype=np.float32) / 32)).astype(np.float32)
    ang = np.concatenate([row[:, None] * inv[None, :], col[:, None] * inv[None, :]], axis=-1)
    return np.concatenate([np.cos(ang), np.sin(ang)], axis=-1).astype(np.float32)


def build_program(stop_after=None, debug=False):
    nc = bass.Bass("TRN2", target_bir_lowering=False)
    dt_in = lambda name, shape, dt=F32: nc.dram_tensor(name, shape, dt, kind="ExternalInput")
    x_d = dt_in("x", [T, D])
    ccol_d = dt_in("ccol", [128, 8])
    wada_d = dt_in("w_ada", [D, 6 * D])
    bada_d = dt_in("bada", [128, 48])
    gattn_d = dt_in("gattn", [128, 8])
    gffn_d = dt_in("gffn", [128, 8])
    gfin_d = dt_in("gfin", [128, D])
    win_d = dt_in("w_in", [D, 2304])
    gq_d = dt_in("gq", [128, 64])
    gk_d = dt_in("gk", [128, 64])
    bint_d = dt_in("bint", [8, 128, 640])
    bsp_d = dt_in("bsp", [8, 4, 128, 768])
    wo_d = dt_in("w_o", [D, D])
    wg_d = dt_in("w_gate", [D, DFF])
    wu_d = dt_in("w_up", [D, DFF])
    wd_d = dt_in("w_down", [DFF, D])
    ident_d = dt_in("ident", [128, 128], BF16)
    identf_d = dt_in("identf", [128, 128])
    sel_d = dt_in("sel64", [128, 128])
    cs_d = dt_in("cs", [T, 64])
    y_d = nc.dram_tensor("y", [T, D], F32, kind="ExternalOutput")
    wg_s = nc.dram_tensor("wg_s", [NF, 128, D], BF16)
    wu_s = nc.dram_tensor("wu_s", [NF, 128, D], BF16)
    wd_s = nc.dram_tensor("wd_s", [2, NF, 128, 512], BF16)
    ot_s = nc.dram_tensor("ot_s", [8, 128, T], BF16)
    dbg = {}
    if debug:
        dbg["d_mod"] = nc.dram_tensor("d_mod", [128, 48], F32, kind="ExternalOutput")
        dbg["d_ot"] = nc.dram_tensor("d_ot", [8, 128, T], BF16, kind="ExternalOutput")

    with ExitStack() as st:
        S = Sched(nc, st)
        ps = st.enter_context(nc.psum_tensor("ps", [128, 8 * 512], F32))

        def bank(b, n=512, off=0):
            return ps[:, b * 512 + off: b * 512 + off + n]

        def bankbf(b):
            return ps[:, b * 512:(b + 1) * 512].bitcast(BF16)

        PB = lambda b: ("pb", b)

        def sbt(stack, name, shape, dt):
            return stack.enter_context(nc.sbuf_tensor("sb_" + name, shape, dt))

        idt = sbt(st, "idt", [128, 128], BF16)
        identf = sbt(st, "identf", [128, 128], F32)
        sel64 = sbt(st, "sel64", [128, 128], F32)
        onesf = sbt(st, "onesf", [128, 128], F32)
        mh = sbt(st, "mh", [128, 8], F32)
        modsb = sbt(st, "modsb", [128, 48], F32)
        gsa = sbt(st, "gsa", [128, 8], F32)
        gsf = sbt(st, "gsf", [128, 8], F32)
        rec = sbt(st, "rec", [128, 512], F32)
        bcsb = sbt(st, "bcsb", [64, 512], F32)

        S.dma("sp", idt[:, :], ident_d.ap(), writes=["idt"])
        S.dma("sp", identf[:, :], identf_d.ap(), writes=["identf"])
        S.dma("sp", sel64[:, :], sel_d.ap(), writes=["sel64"])
        S.op("pool", lambda e: e.memset(onesf[:, :], 1.0), writes=["onesf"])
        S.op("pool", lambda e: e.memset(mh[:, :], -0.5), writes=["mh"])
        S.op("pool", lambda e: e.memset(rec[:, :], 0.0), writes=["rec"])

        prep = []
        for f in range(NF):
            prep.append(lambda f=f: S.dma("pool", wg_s.ap()[f].rearrange("p (c j) -> p c j", c=8),
                                          wg_d.ap()[:, f * 128:(f + 1) * 128].rearrange("(c p) j -> p c j", p=128),
                                          writes=[("wg_s", f)]))
            prep.append(lambda f=f: S.dma("pool", wu_s.ap()[f].rearrange("p (c j) -> p c j", c=8),
                                          wu_d.ap()[:, f * 128:(f + 1) * 128].rearrange("(c p) j -> p c j", p=128),
                                          writes=[("wu_s", f)]))
        for hf in range(2):
            prep.append(lambda hf=hf: S.dma("pool", wd_s.ap()[hf].rearrange("f p n -> p f n"),
                                            wd_d.ap()[:, hf * 512:(hf + 1) * 512].rearrange("(f p) n -> p f n", p=128),
                                            writes=[("wd_s", hf)]))

        with ExitStack() as s0:
            ccol = sbt(s0, "ccol", [128, 8], F32)
            ctmp = sbt(s0, "ctmp", [128, 8], F32)
            cact = sbt(s0, "cact", [128, 8], BF16)
            bada = sbt(s0, "bada", [128, 48], F32)
            gat = sbt(s0, "gat", [128, 8], F32)
            gff = sbt(s0, "gff", [128, 8], F32)
            wa = [sbt(s0, f"wa{i}", [128, 8, 512], BF16) for i in range(2)]
            S.dma("sp", ccol[:, :], ccol_d.ap(), writes=["ccol"])
            S.dma("sp", bada[:, :], bada_d.ap(), writes=["bada"])
            S.dma("sp", gat[:, :], gattn_d.ap(), writes=["gat"])
            S.dma("sp", gff[:, :], gffn_d.ap(), writes=["gff"])
            S.op("act", lambda e: e.activation(out=ctmp[:, :], in_=ccol[:, :], func=AF.Exp, scale=-1.0),
                 reads=["ccol"], writes=["ctmp"])
            S.op("dve", lambda e: e.tensor_scalar(out=ctmp[:, :], in0=ctmp[:, :], scalar1=1.0, scalar2=None, op0=ALU.add),
                 reads=["ctmp"], writes=["ctmp"])
            S.op("dve", lambda e: e.reciprocal(out=ctmp[:, :], in_=ctmp[:, :]), reads=["ctmp"], writes=["ctmp"])
            S.op("dve", lambda e: e.tensor_tensor(out=cact[:, :], in0=ctmp[:, :], in1=ccol[:, :], op=ALU.mult),
                 reads=["ctmp", "ccol"], writes=["cact"])
            wada_v = wada_d.ap().rearrange("(k p) n -> p k n", p=128)
            for nb in range(12):
                wb = wa[nb % 2]
                S.dma("pool", wb[:, :, :], wada_v[:, :, nb * 512:(nb + 1) * 512], writes=[("wa", nb % 2)])
                for jj in range(4):
                    j = nb * 4 + jj
                    for k in range(8):
                        S.op("pe", lambda e, wb=wb, j=j, jj=jj, k=k: e.matmul(
                            bank(0, 1, j), lhsT=wb[:, k, jj * 128:(jj + 1) * 128], rhs=cact[:, k:k + 1],
                            start=(k == 0), stop=(k == 7)),
                            reads=[("wa", nb % 2), "cact"], writes=[PB(0)], sig=(k == 7))
            S.op("dve", lambda e: e.tensor_tensor(out=modsb[:, :], in0=bank(0, 48), in1=bada[:, :], op=ALU.add),
                 reads=[PB(0), "bada"], writes=["modsb"])
            S.op("dve", lambda e: e.scalar_tensor_tensor(out=gsa[:, :], in0=modsb[:, 8:16], scalar=1.0, in1=gat[:, :],
                                                         op0=ALU.add, op1=ALU.mult),
                 reads=["modsb", "gat"], writes=["gsa"])
            S.op("dve", lambda e: e.scalar_tensor_tensor(out=gsf[:, :], in0=modsb[:, 32:40], scalar=1.0, in1=gff[:, :],
                                                         op0=ALU.add, op1=ALU.mult),
                 reads=["modsb", "gff"], writes=["gsf"])
            if debug:
                S.dma("sp", dbg["d_mod"].ap(), modsb[:, :], reads=["modsb"], writes=["d_mod"])
            S.barrier()
        sha = modsb[:, 0:8]
        shf = modsb[:, 24:32]
        if stop_after == "p0":
            S.finish("sp")
            return nc

        with ExitStack() as s1:
            win = sbt(s1, "win", [128, 8, 2304], BF16)
            cs = sbt(s1, "cs", [128, 32, 64], F32)
            gqb = sbt(s1, "gqb", [128, 64], F32)
            gkb = sbt(s1, "gkb", [128, 64], F32)
            KTg = sbt(s1, "KTg", [128, T], BF16)
            Vg = sbt(s1, "Vg", [128, 32, 2, 65], BF16)
            bint = sbt(s1, "bint", [128, 8, 640], F32)
            bsp = [sbt(s1, f"bsp{i}", [128, 768], F32) for i in range(3)]
            kTn = sbt(s1, "kTn", [128, 3, 4, 512], BF16)
            Vn = sbt(s1, "Vn", [128, 12, 8, 65], BF16)
            qTlo = sbt(s1, "qTlo", [128, 4, 512], BF16)
            qThi = sbt(s1, "qThi", [128, 4, 512], BF16)
            QTlo = sbt(s1, "QTlo", [128, 4, 512], BF16)
            QThi = sbt(s1, "QThi", [128, 4, 512], BF16)
            xt = [sbt(s1, f"xt{i}", [128, D], F32) for i in range(4)]
            xn = [sbt(s1, f"xn{i}", [128, D], BF16) for i in range(4)]
            ss4 = sbt(s1, "ss4", [128, 4], F32)
            rs4 = sbt(s1, "rs4", [128, 4], F32)
            hT = sbt(s1, "hT", [128, 8, 512], BF16)
            qsb = [sbt(s1, f"qsb{i}", [128, 512], F32) for i in range(2)]
            qtmp = sbt(s1, "qtmp", [128, 512], F32)
            ssq = sbt(s1, "ssq", [128, 8], F32)
            rq = sbt(s1, "rq", [128, 8], F32)
            rt1 = sbt(s1, "rt1", [128, 8, 32], F32)
            rt2 = sbt(s1, "rt2", [128, 8, 32], F32)
            qhat = sbt(s1, "qhat", [128, 4, 512], BF16)
            khat = sbt(s1, "khat", [128, 128], BF16)
            Ssb = [sbt(s1, f"Ssb{i}", [128, 512], F32) for i in range(2)]
            Pna = [sbt(s1, f"Pna{i}", [128, 512], BF16) for i in range(2)]
            Pg = [sbt(s1, f"Pg{i}", [128, 1024], BF16) for i in range(2)]
            OT = sbt(s1, "OT", [128, 8, 512], BF16)

            S.dma("pool", win[:, :, :], win_d.ap().rearrange("(c p) n -> p c n", p=128), writes=["win"])
            S.dma("sp", cs[:, :, :], cs_d.ap().rearrange("(tt p) k -> p tt k", p=128), writes=["cs"])
            S.dma("sp", gqb[:, :], gq_d.ap(), writes=["gqb"])
            S.dma("sp", gkb[:, :], gk_d.ap(), writes=["gkb"])
            S.dma("sp", bint[:, :, :], bint_d.ap().rearrange("h p n -> p h n"), writes=["bint"])
            S.op("pool", lambda e: e.memset(Vg[:, :, :, 64:65], 1.0), writes=["Vg1"])
            S.op("pool", lambda e: e.memset(Vn[:, :, :, 64:65], 1.0), writes=["Vn1"])
            for tns, nm in ((qTlo, "qTlo0"), (qThi, "qThi0"), (QTlo, "QTlo0"), (QThi, "QThi0")):
                S.op("pool", lambda e, tns=tns: e.memset(tns[:, :, :], 0.0), writes=[nm])
            zero_deps = {"qTlo": "qTlo0", "qThi": "qThi0", "QTlo": "QTlo0", "QThi": "QThi0"}

            x_v = x_d.ap().rearrange("(tt p) d -> tt p d", p=128)

            def norm_block(tb, gs, sh, src_tiles=None):
                for i in range(4):
                    if src_tiles is None:
                        S.dma("sp", xt[i][:, :], x_v[tb * 4 + i], writes=[("xt", i)])
                    S.op("act", lambda e, i=i: e.activation(out=xn[i][:, :], in_=xt[i][:, :], func=AF.Square,
                                                            accum_out=ss4[:, i:i + 1]),
                         reads=[("xt", i)], writes=[("xn", i), ("ss4", i)])
                S.op("dve", lambda e: e.tensor_scalar(out=rs4[:, :], in0=ss4[:, :], scalar1=1.0 / D, scalar2=EPS,
                                                      op0=ALU.mult, op1=ALU.add),
                     reads=[("ss4", i) for i in range(4)], writes=["rs4"])
                S.op("pool", lambda e: e.tensor_tensor(out=rs4[:, :], in0=rs4[:, :], in1=mh[:, 0:4], op=ALU.pow),
                     reads=["rs4", "mh"], writes=["rs4"])
                for i in range(4):
                    S.op("dve", lambda e, i=i: e.tensor_scalar(out=xn[i][:, :], in0=xt[i][:, :], scalar1=rs4[:, i:i + 1],
                                                               scalar2=None, op0=ALU.mult),
                         reads=[("xt", i), "rs4"], writes=[("xn", i)])
                for c in range(8):
                    bk = c // 2
                    for i in range(4):
                        S.op("pe", lambda e, c=c, i=i, bk=bk: e.transpose(
                            out=bankbf(bk)[:, (c % 2) * 512 + i * 128:(c % 2) * 512 + (i + 1) * 128],
                            in_=xn[i][:, c * 128:(c + 1) * 128], identity=idt[:, :]),
                            reads=[("xn", i), "idt"], writes=[PB(bk)], sig=(i == 3))
                for c in range(8):
                    bk = c // 2
                    S.op("act", lambda e, c=c, bk=bk: e.activation(
                        out=hT[:, c, :], in_=bankbf(bk)[:, (c % 2) * 512:(c % 2 + 1) * 512], func=AF.Identity,
                        scale=gs[:, c:c + 1], bias=sh[:, c:c + 1]),
                        reads=[PB(bk), "gsa", "modsb"], writes=[("hT", c)])

            def qk_post(src_ap, nheads, gb, gname, tt, dst_fn, dst_keys, sb_i):
                H = nheads
                W = H * 64
                v3 = lambda ap: ap.rearrange("p (h d) -> p h d", d=64)
                src3 = v3(src_ap)
                tmp3 = v3(qtmp[:, 0:W])
                S.op("dve", lambda e: e.tensor_tensor(out=qtmp[:, 0:W], in0=src_ap, in1=src_ap, op=ALU.mult),
                     reads=[("qsb", sb_i)], writes=["qtmp"])
                S.op("dve", lambda e: e.tensor_reduce(out=ssq[:, 0:H], in_=tmp3, axis=AX.X, op=ALU.add),
                     reads=["qtmp"], writes=["ssq"])
                S.op("dve", lambda e: e.tensor_scalar(out=rq[:, 0:H], in0=ssq[:, 0:H], scalar1=1.0 / 64, scalar2=EPS,
                                                      op0=ALU.mult, op1=ALU.add), reads=["ssq"], writes=["rq"])
                S.op("pool", lambda e: e.tensor_tensor(out=rq[:, 0:H], in0=rq[:, 0:H], in1=mh[:, 0:H], op=ALU.pow),
                     reads=["rq", "mh"], writes=["rq"])
                S.op("dve", lambda e: e.tensor_tensor(out=tmp3, in0=src3, in1=rq[:, 0:H].unsqueeze(2).to_broadcast([128, H, 64]),
                                                      op=ALU.mult), reads=[("qsb", sb_i), "rq"], writes=["qtmp"])
                S.op("dve", lambda e: e.tensor_tensor(out=tmp3, in0=tmp3, in1=gb[:, :].unsqueeze(1).to_broadcast([128, H, 64]),
                                                      op=ALU.mult), reads=["qtmp", gname], writes=["qtmp"])
                x1 = tmp3[:, :, 0:32]
                x2 = tmp3[:, :, 32:64]
                cosb = cs[:, tt, 0:32].unsqueeze(1).to_broadcast([128, H, 32])
                sinb = cs[:, tt, 32:64].unsqueeze(1).to_broadcast([128, H, 32])
                t1 = rt1[:, 0:H, :]
                t2 = rt2[:, 0:H, :]
                S.op("pool", lambda e: e.tensor_tensor(out=t1, in0=x1, in1=cosb, op=ALU.mult), reads=["qtmp", "cs"], writes=["rt1"])
                S.op("pool", lambda e: e.tensor_tensor(out=t2, in0=x2, in1=sinb, op=ALU.mult), reads=["qtmp", "cs"], writes=["rt2"])
                S.op("pool", lambda e: e.tensor_tensor(out=dst_fn(0), in0=t1, in1=t2, op=ALU.subtract),
                     reads=["rt1", "rt2"], writes=dst_keys)
                S.op("pool", lambda e: e.tensor_tensor(out=t1, in0=x1, in1=sinb, op=ALU.mult), reads=["qtmp", "cs"], writes=["rt1"])
                S.op("pool", lambda e: e.tensor_tensor(out=t2, in0=x2, in1=cosb, op=ALU.mult), reads=["qtmp", "cs"], writes=["rt2"])
                S.op("pool", lambda e: e.tensor_tensor(out=dst_fn(1), in0=t1, in1=t2, op=ALU.add),
                     reads=["rt1", "rt2"], writes=dst_keys)

            for tb in range(NB):
                norm_block(tb, gsa, sha)
                for i in range(4):
                    tt = tb * 4 + i
                    bk = 4 + (i % 2)
                    for c in range(8):
                        S.op("pe", lambda e, c=c, i=i, bk=bk: e.matmul(bank(bk, 256), lhsT=hT[:, c, i * 128:(i + 1) * 128],
                                                                     rhs=win[:, c, 2048:2304], start=(c == 0), stop=(c == 7)),
                             reads=[("hT", c), "win"], writes=[PB(bk)], sig=(c == 7))
                    sbi = i % 2
                    S.op("act", lambda e, bk=bk, sbi=sbi: e.copy(out=qsb[sbi][:, 0:256], in_=bank(bk, 256)),
                         reads=[PB(bk)], writes=[("qsb", sbi)])
                    S.op("dve", lambda e, sbi=sbi, tt=tt: e.tensor_copy(
                        out=Vg[:, tt, :, 0:64], in_=qsb[sbi][:, 128:256].rearrange("p (h d) -> p h d", d=64)),
                        reads=[("qsb", sbi)], writes=[("Vg", tt)])
                    kh3 = khat[:, :].rearrange("p (h d) -> p h d", d=64)
                    qk_post(qsb[sbi][:, 0:128], 2, gkb, "gkb", tt,
                            lambda half, kh3=kh3: kh3[:, :, half * 32:(half + 1) * 32], ["khat"], sbi)
                    S.op("pe", lambda e: e.transpose(out=bankbf(6)[:, 0:128], in_=khat[:, :], identity=idt[:, :]),
                         reads=["khat", "idt"], writes=[PB(6)])
                    S.op("act", lambda e, tt=tt: e.copy(out=KTg[:, tt * 128:(tt + 1) * 128], in_=bankbf(6)[:, 0:128]),
                         reads=[PB(6)], writes=[("KTg", tt)])

            if stop_after == "p1a":
                S.barrier()
                S.finish("sp")
                return nc
            def stage_a(tb, part="all"):
                slot = tb % 3
                if part in ("all", "kv"):
                    norm_block(tb, gsa, sha)
                n_mm = 0
                for m in range(4):
                    for which in range(2):
                        if which == 0 and part == "kv":
                            continue
                        if which == 1 and part == "q":
                            continue
                        bk = 4 + (n_mm % 2)
                        n_mm += 1
                        col0 = which * 512 + m * 128
                        for c in range(8):
                            S.op("pe", lambda e, c=c, bk=bk, col0=col0: e.matmul(
                                bank(bk), lhsT=win[:, c, col0:col0 + 128], rhs=hT[:, c, :], start=(c == 0), stop=(c == 7)),
                                reads=[("hT", c), "win"], writes=[PB(bk)], sig=(c == 7))
                        if which == 0:
                            S.op("act", lambda e, bk=bk, m=m: e.copy(out=qTlo[0:64, m, :], in_=bank(bk)[0:64, :]),
                                 reads=[PB(bk), "qTlo0"], writes=[("qTlo", m)])
                            S.op("dve", lambda e, bk=bk, m=m: e.tensor_copy(out=qThi[64:128, m, :], in_=bank(bk)[64:128, :]),
                                 reads=[PB(bk), "qThi0"], writes=[("qThi", m)])
                        else:
                            S.op("act", lambda e, bk=bk, m=m: e.copy(out=kTn[:, slot, m, :], in_=bank(bk)),
                                 reads=[PB(bk)], writes=[("kTn", slot, m)])
                for i in range(4 if part in ("all", "kv") else 0):
                    tt = tb * 4 + i
                    bk = 4 + (i % 2)
                    for c in range(8):
                        S.op("pe", lambda e, c=c, i=i, bk=bk: e.matmul(bank(bk), lhsT=hT[:, c, i * 128:(i + 1) * 128],
                                                                     rhs=win[:, c, 1024:1536], start=(c == 0), stop=(c == 7)),
                             reads=[("hT", c), "win"], writes=[PB(bk)], sig=(c == 7))
                    S.op("act", lambda e, bk=bk, i=i: e.copy(
                        out=Vn[:, slot * 4 + i, :, 0:64], in_=bank(bk).rearrange("p (h d) -> p h d", d=64)),
                        reads=[PB(bk), "Vn1"], writes=[("Vn", slot * 4 + i)])
                if part == "kv":
                    return
                for i in range(4):
                    tt = tb * 4 + i
                    bk = 6 + (i % 2)
                    for c in range(8):
                        S.op("pe", lambda e, c=c, i=i, bk=bk: e.matmul(bank(bk), lhsT=hT[:, c, i * 128:(i + 1) * 128],
                                                                     rhs=win[:, c, 1536:2048], start=(c == 0), stop=(c == 7)),
                             reads=[("hT", c), "win"], writes=[PB(bk)], sig=(c == 7))
                    sbi = i % 2
                    S.op("act", lambda e, bk=bk, sbi=sbi: e.copy(
                        out=qsb[sbi][:, :].rearrange("p (g kv d) -> p g kv d", g=4, kv=2),
                        in_=bank(bk).rearrange("p (kv g d) -> p g kv d", g=4, kv=2)),
                        reads=[PB(bk)], writes=[("qsb", sbi)])
                    qh3 = qhat[:, i, :].rearrange("p (h d) -> p h d", d=64)
                    qk_post(qsb[sbi][:, :], 8, gqb, "gqb", tt,
                            lambda half, qh3=qh3: qh3[:, :, half * 32:(half + 1) * 32], [("qhat", i)], sbi)
                for g in range(4):
                    bk = 4 + g // 2
                    for i in range(4):
                        S.op("pe", lambda e, g=g, i=i, bk=bk: e.transpose(
                            out=bankbf(bk)[:, (g % 2) * 512 + i * 128:(g % 2) * 512 + (i + 1) * 128],
                            in_=qhat[:, i, g * 128:(g + 1) * 128], identity=idt[:, :]),
                            reads=[("qhat", i), "idt"], writes=[PB(bk)], sig=(i == 3))
                for g in range(4):
                    bk = 4 + g // 2
                    src = bankbf(bk)[:, (g % 2) * 512:(g % 2 + 1) * 512]
                    S.op("act", lambda e, g=g, src=src: e.copy(out=QTlo[0:64, g, :], in_=src[0:64, :]),
                         reads=[PB(bk), "QTlo0"], writes=[("QTlo", g)])
                    S.op("dve", lambda e, g=g, src=src: e.tensor_copy(out=QThi[64:128, g, :], in_=src[64:128, :]),
                         reads=[PB(bk), "QThi0"], writes=[("QThi", g)])

            def normalize_head(acc_bk, h_chunk, half, bc_bk):
                S.op("dve", lambda e: e.reciprocal(out=rec[64:65, :], in_=bank(acc_bk)[64:65, :]),
                     reads=[PB(acc_bk)], writes=["rec"])
                S.op("pe", lambda e: e.matmul(bank(bc_bk)[0:128, :], lhsT=sel64[:, :], rhs=rec[:, :], start=True, stop=True),
                     reads=["sel64", "rec"], writes=[PB(bc_bk)])
                S.op("act", lambda e: e.copy(out=bcsb[:, :], in_=bank(bc_bk)[0:64, :]), reads=[PB(bc_bk)], writes=["bcsb"])
                S.op("dve", lambda e: e.tensor_tensor(out=OT[half * 64:(half + 1) * 64, h_chunk, :], in0=bank(acc_bk)[0:64, :],
                                                      in1=bcsb[:, :], op=ALU.mult),
                     reads=[PB(acc_bk), "bcsb"], writes=[("OT", h_chunk, half)])

            sp_loaded = {}
            sp_next = [0]

            def na_block(tb):
                n_s = 0
                for h in range(8):
                    m, half = h // 2, h % 2
                    qT = qTlo if half == 0 else qThi
                    qkey = ("qTlo", m) if half == 0 else ("qThi", m)
                    acc_bk = 6 + (h % 2)
                    first = True
                    chunks = [j for j in range(max(0, 4 * tb - 2), min(31, 4 * tb + 5) + 1)]
                    items = []
                    for j in chunks:
                        r0, r1 = _chunk_rows(j)
                        lo, hi = max(8 * tb, r0), min(8 * tb + 7, r1)
                        if lo > hi:
                            continue
                        items.append((j, lo, hi))
                    for idx, (j, lo, hi) in enumerate(items):
                        nq = (hi - lo + 1) * 64
                        qc0 = (lo - 8 * tb) * 64
                        kslot = (j // 4) % 3
                        kcol = (j % 4) * 128
                        vt = kslot * 4 + (j % 4)
                        boff = (lo - _tile_row0(j)) * 64
                        if j in SPECIAL:
                            key = (h, j)
                            if key not in sp_loaded:
                                si = sp_next[0] % 3
                                sp_next[0] += 1
                                S.dma("sp", bsp[si][:, :], bsp_d.ap()[h, SPECIAL[j]], writes=[("bsp", si)])
                                for k2 in [k for k, v in sp_loaded.items() if v == si]:
                                    del sp_loaded[k2]
                                sp_loaded[key] = si
                            si = sp_loaded[key]
                            b_ap = bsp[si][:, boff:boff + nq]
                            bkey = ("bsp", si)
                        else:
                            b_ap = bint[:, h, boff:boff + nq]
                            bkey = "bint"
                        sbk = 4 + (n_s % 2)
                        sb_i = n_s % 2
                        n_s += 1
                        S.op("pe", lambda e, sbk=sbk, nq=nq, kslot=kslot, m=m, kcol=kcol, qT=qT, qc0=qc0: e.matmul(
                            bank(sbk, nq), lhsT=kTn[:, kslot, m, kcol:kcol + 128], rhs=qT[:, m, qc0:qc0 + nq],
                            start=True, stop=True),
                            reads=[("kTn", kslot, m), qkey], writes=[PB(sbk)])
                        S.op("dve", lambda e, sbk=sbk, nq=nq, sb_i=sb_i, b_ap=b_ap: e.scalar_tensor_tensor(
                            out=Ssb[sb_i][:, 0:nq], in0=bank(sbk, nq), scalar=0.125, in1=b_ap, op0=ALU.mult, op1=ALU.add),
                            reads=[PB(sbk), bkey], writes=[("Ssb", sb_i)])
                        S.op("act", lambda e, nq=nq, sb_i=sb_i: e.activation(out=Pna[sb_i][:, 0:nq], in_=Ssb[sb_i][:, 0:nq],
                                                                              func=AF.Exp),
                             reads=[("Ssb", sb_i)], writes=[("Pna", sb_i)])
                        last = (idx == len(items) - 1)
                        S.op("pe", lambda e, acc_bk=acc_bk, qc0=qc0, nq=nq, vt=vt, h=h, sb_i=sb_i, first=first, last=last: e.matmul(
                            bank(acc_bk)[0:65, qc0:qc0 + nq], lhsT=Vn[:, vt, h, 0:65], rhs=Pna[sb_i][:, 0:nq],
                            start=first, stop=last, skip_group_check=True),
                            reads=[("Vn", vt), "Vn1", ("Pna", sb_i)], writes=[PB(acc_bk)])
                        first = False
                    normalize_head(acc_bk, m, half, 3)

            def gqa_block(tb):
                for kv in range(2):
                    QT = QTlo if kv == 0 else QThi
                    qname = "QTlo" if kv == 0 else "QThi"
                    steps = [(kc, gp) for kc in range(32) for gp in range(2)]

                    def qk(si):
                        kc, gp = steps[si]
                        b0 = 4 + 2 * (si % 2)
                        for u in range(2):
                            g = gp * 2 + u
                            S.op("pe", lambda e, b0=b0, u=u, kc=kc, g=g: e.matmul(
                                bank(b0 + u), lhsT=KTg[:, kc * 128:(kc + 1) * 128], rhs=QT[:, g, :], start=True, stop=True),
                                reads=[("KTg", kc), (qname, g)], writes=[PB(b0), PB(b0 + 1)], sig=(u == 1))

                    def ex(si):
                        b0 = 4 + 2 * (si % 2)
                        S.op("act", lambda e, b0=b0, si=si: e.activation(
                            out=Pg[si % 2][:, :], in_=ps[:, b0 * 512:(b0 + 2) * 512], func=AF.Exp, scale=0.125),
                            reads=[PB(b0), PB(b0 + 1)], writes=[("Pg", si % 2)])

                    def pv(si):
                        kc, gp = steps[si]
                        for u in range(2):
                            g = gp * 2 + u
                            S.op("pe", lambda e, u=u, kc=kc, g=g, si=si: e.matmul(
                                bank(g)[0:65, :], lhsT=Vg[:, kc, kv, 0:65], rhs=Pg[si % 2][:, u * 512:(u + 1) * 512],
                                start=(kc == 0), stop=(kc == 31)),
                                reads=[("Vg", kc), "Vg1", ("Pg", si % 2)], writes=[PB(g)])

                    qk(0)
                    for si in range(len(steps)):
                        if si + 1 < len(steps):
                            qk(si + 1)
                        ex(si)
                        pv(si)
                    for g in range(4):
                        h = kv * 4 + g
                        normalize_head(g, 4 + h // 2, h % 2, 4)

            def store_ot(tb):
                for c in range(8):
                    S.dma("sp", ot_s.ap()[c][:, tb * 512:(tb + 1) * 512], OT[:, c, :],
                          reads=[("OT", c, 0), ("OT", c, 1)], writes=[("ot_s", tb)])

            stage_a(0)
            for tb in range(NB):
                for _ in range(6):
                    if prep:
                        prep.pop(0)()
                if tb + 1 < NB:
                    stage_a(tb + 1, "kv")
                na_block(tb)
                gqa_block(tb)
                store_ot(tb)
                if tb + 1 < NB:
                    stage_a(tb + 1, "q")
            if debug:
                S.barrier()
                S.dma("sp", dbg["d_ot"].ap(), ot_s.ap(), writes=["d_ot"])
            S.barrier()

        if stop_after == "p1":
            S.finish("sp")
            return nc

        with ExitStack() as s2:
            wo = sbt(s2, "wo", [128, 8, D], BF16)
            OTb = [sbt(s2, f"OTb{i}", [128, 8, 512], BF16) for i in range(2)]
            x1 = [sbt(s2, f"x1_{i}", [128, D], F32) for i in range(4)]
            ytmp = [sbt(s2, f"ytmp{i}", [128, 512], F32) for i in range(2)]
            xn2 = [sbt(s2, f"xn2_{i}", [128, D], BF16) for i in range(4)]
            junk2 = sbt(s2, "junk2", [128, D], BF16)
            ss4b = sbt(s2, "ss4b", [128, 4], F32)
            rs4b = sbt(s2, "rs4b", [128, 4], F32)
            hT2 = sbt(s2, "hT2", [128, 8, 512], BF16)
            wgb = [sbt(s2, f"wgb{i}", [128, D], BF16) for i in range(3)]
            wub = [sbt(s2, f"wub{i}", [128, D], BF16) for i in range(3)]
            wdb = [sbt(s2, f"wdb{i}", [128, 512], BF16) for i in range(4)]
            sg = [sbt(s2, f"sg{i}", [128, 512], F32) for i in range(2)]
            hid = sbt(s2, "hid", [128, NF, 512], BF16)
            ot = [sbt(s2, f"ot{i}", [128, D], F32) for i in range(2)]

            S.dma("pool", wo[:, :, :], wo_d.ap().rearrange("(c p) n -> p c n", p=128), writes=["wo"])
            gate_a = sbt(s2, "gate_a", [128, D], F32)
            gate_f = sbt(s2, "gate_f", [128, D], F32)
            gfin = sbt(s2, "gfin", [128, D], F32)
            dg = [sbt(s2, f"dg{i}", [128, 128], F32) for i in range(2)]
            S.dma("sp", gfin[:, :], gfin_d.ap(), writes=["gfin"])
            for gi, (off, gt, gname) in enumerate(((16, gate_a, "gate_a"), (40, gate_f, "gate_f"))):
                for j in range(8):
                    d = dg[j % 2]
                    S.op("dve", lambda e, d=d, off=off, j=j: e.tensor_scalar(
                        out=d[:, :], in0=identf[:, :], scalar1=modsb[:, off + j:off + j + 1], scalar2=None, op0=ALU.mult),
                        reads=["identf", "modsb"], writes=[("dg", j % 2)])
                    bk = 1 + (j // 4)
                    S.op("pe", lambda e, d=d, bk=bk, j=j: e.matmul(bank(bk, 128, (j % 4) * 128), lhsT=onesf[:, :], rhs=d[:, :],
                                                                 start=True, stop=True),
                         reads=["onesf", ("dg", j % 2)], writes=[PB(bk)])
                for hb in range(2):
                    S.op("act", lambda e, gt=gt, hb=hb: e.copy(out=gt[:, hb * 512:(hb + 1) * 512], in_=bank(1 + hb)),
                         reads=[PB(1 + hb)], writes=[(gname, hb)])
            x_v = x_d.ap().rearrange("(tt p) d -> tt p d", p=128)
            y_v = y_d.ap().rearrange("(tt p) d -> tt p d", p=128)
            n_ld = {"g": 0, "d": 0}
            n_out = [0]
            for tb in range(NB):
                ob = OTb[tb % 2]
                for c in range(8):
                    S.dma("sp", ob[:, c, :], ot_s.ap()[c][:, tb * 512:(tb + 1) * 512], writes=[("OTb", tb % 2, c)])
                for i in range(4):
                    tt = tb * 4 + i
                    S.dma("sp", x1[i][:, :], x_v[tt], writes=[("x1", i, 0), ("x1", i, 1)])
                    for hf in range(2):
                        bk = (i * 2 + hf) % 4
                        for c in range(8):
                            S.op("pe", lambda e, c=c, i=i, hf=hf, bk=bk, ob=ob: e.matmul(
                                bank(bk), lhsT=ob[:, c, i * 128:(i + 1) * 128], rhs=wo[:, c, hf * 512:(hf + 1) * 512],
                                start=(c == 0), stop=(c == 7)),
                                reads=[("OTb", tb % 2, c), "wo"], writes=[PB(bk)], sig=(c == 7))
                        yt = ytmp[hf]
                        S.op("dve", lambda e, bk=bk, hf=hf, yt=yt: e.tensor_tensor(
                            out=yt[:, :], in0=bank(bk), in1=gate_a[:, hf * 512:(hf + 1) * 512], op=ALU.mult),
                            reads=[PB(bk), ("gate_a", hf)], writes=[("ytmp", hf)])
                        S.op("pool", lambda e, i=i, hf=hf, yt=yt: e.tensor_tensor(
                            out=x1[i][:, hf * 512:(hf + 1) * 512], in0=x1[i][:, hf * 512:(hf + 1) * 512], in1=yt[:, :], op=ALU.add),
                            reads=[("x1", i, hf), ("ytmp", hf)], writes=[("x1", i, hf)])
                for i in range(4):
                    S.op("act", lambda e, i=i: e.activation(out=junk2[:, :], in_=x1[i][:, :], func=AF.Square,
                                                            accum_out=ss4b[:, i:i + 1]),
                         reads=[("x1", i, 0), ("x1", i, 1)], writes=["junk2", ("ss4b", i)])
                S.op("dve", lambda e: e.tensor_scalar(out=rs4b[:, :], in0=ss4b[:, :], scalar1=1.0 / D, scalar2=EPS,
                                                      op0=ALU.mult, op1=ALU.add),
                     reads=[("ss4b", i) for i in range(4)], writes=["rs4b"])
                S.op("pool", lambda e: e.tensor_tensor(out=rs4b[:, :], in0=rs4b[:, :], in1=mh[:, 0:4], op=ALU.pow),
                     reads=["rs4b", "mh"], writes=["rs4b"])
                for i in range(4):
                    S.op("dve", lambda e, i=i: e.tensor_scalar(out=xn2[i][:, :], in0=x1[i][:, :], scalar1=rs4b[:, i:i + 1],
                                                               scalar2=None, op0=ALU.mult),
                         reads=[("x1", i, 0), ("x1", i, 1), "rs4b"], writes=[("xn2", i)])
                for c in range(8):
                    bk = 4 + c // 2
                    for i in range(4):
                        S.op("pe", lambda e, c=c, i=i, bk=bk: e.transpose(
                            out=bankbf(bk)[:, (c % 2) * 512 + i * 128:(c % 2) * 512 + (i + 1) * 128],
                            in_=xn2[i][:, c * 128:(c + 1) * 128], identity=idt[:, :]),
                            reads=[("xn2", i), "idt"], writes=[PB(bk)], sig=(i == 3))
                for c in range(8):
                    bk = 4 + c // 2
                    S.op("act", lambda e, c=c, bk=bk: e.activation(
                        out=hT2[:, c, :], in_=bankbf(bk)[:, (c % 2) * 512:(c % 2 + 1) * 512], func=AF.Identity,
                        scale=gsf[:, c:c + 1], bias=shf[:, c:c + 1]),
                        reads=[PB(bk), "gsf", "modsb"], writes=[("hT2", c)])
                for f in range(NF):
                    wi = n_ld["g"] % 3
                    n_ld["g"] += 1
                    S.dma("sp", wgb[wi][:, :], wg_s.ap()[f], reads=[("wg_s", f)], writes=[("wgb", wi)])
                    S.dma("sp", wub[wi][:, :], wu_s.ap()[f], reads=[("wu_s", f)], writes=[("wub", wi)])
                    bg = 2 * (f % 2)
                    bu = bg + 1
                    for c in range(8):
                        S.op("pe", lambda e, c=c, wi=wi, bg=bg: e.matmul(bank(bg), lhsT=wgb[wi][:, c * 128:(c + 1) * 128],
                                                                       rhs=hT2[:, c, :], start=(c == 0), stop=(c == 7)),
                             reads=[("wgb", wi), ("hT2", c)], writes=[PB(bg)], sig=(c == 7))
                    for c in range(8):
                        S.op("pe", lambda e, c=c, wi=wi, bu=bu: e.matmul(bank(bu), lhsT=wub[wi][:, c * 128:(c + 1) * 128],
                                                                       rhs=hT2[:, c, :], start=(c == 0), stop=(c == 7)),
                             reads=[("wub", wi), ("hT2", c)], writes=[PB(bu)], sig=(c == 7))
                    S.op("act", lambda e, f=f, bg=bg: e.activation(out=sg[f % 2][:, :], in_=bank(bg), func=AF.Silu),
                         reads=[PB(bg)], writes=[("sg", f % 2)])
                    S.op("dve", lambda e, f=f, bu=bu: e.tensor_tensor(out=hid[:, f, :], in0=bank(bu), in1=sg[f % 2][:, :], op=ALU.mult),
                         reads=[PB(bu), ("sg", f % 2)], writes=[("hid", f)])
                for hf in range(2):
                    for f in range(NF):
                        wi = n_ld["d"] % 4
                        n_ld["d"] += 1
                        S.dma("sp", wdb[wi][:, :], wd_s.ap()[hf, f], reads=[("wd_s", hf)], writes=[("wdb", wi)])
                        for i in range(4):
                            S.op("pe", lambda e, f=f, i=i, wi=wi: e.matmul(
                                bank(4 + i), lhsT=hid[:, f, i * 128:(i + 1) * 128], rhs=wdb[wi][:, :],
                                start=(f == 0), stop=(f == NF - 1)),
                                reads=[("hid", f), ("wdb", wi)], writes=[PB(4 + i)], sig=(f == NF - 1 or i == 3))
                    for i in range(4):
                        yt = ytmp[i % 2]
                        S.op("dve", lambda e, i=i, hf=hf, yt=yt: e.tensor_tensor(
                            out=yt[:, :], in0=bank(4 + i), in1=gate_f[:, hf * 512:(hf + 1) * 512], op=ALU.mult),
                            reads=[PB(4 + i), ("gate_f", hf)], writes=[("ytmp", i % 2)])
                        S.op("pool", lambda e, i=i, hf=hf, yt=yt: e.tensor_tensor(
                            out=x1[i][:, hf * 512:(hf + 1) * 512], in0=x1[i][:, hf * 512:(hf + 1) * 512], in1=yt[:, :], op=ALU.add),
                            reads=[("x1", i, hf), ("ytmp", i % 2)], writes=[("x1", i, hf)])
                for i in range(4):
                    S.op("act", lambda e, i=i: e.activation(out=junk2[:, :], in_=x1[i][:, :], func=AF.Square,
                                                            accum_out=ss4b[:, i:i + 1]),
                         reads=[("x1", i, 0), ("x1", i, 1)], writes=["junk2", ("ss4b", i)])
                S.op("dve", lambda e: e.tensor_scalar(out=rs4b[:, :], in0=ss4b[:, :], scalar1=1.0 / D, scalar2=EPS,
                                                      op0=ALU.mult, op1=ALU.add),
                     reads=[("ss4b", i) for i in range(4)], writes=["rs4b"])
                S.op("pool", lambda e: e.tensor_tensor(out=rs4b[:, :], in0=rs4b[:, :], in1=mh[:, 0:4], op=ALU.pow),
                     reads=["rs4b", "mh"], writes=["rs4b"])
                for i in range(4):
                    oi = n_out[0] % 2
                    n_out[0] += 1
                    S.op("dve", lambda e, i=i, oi=oi: e.scalar_tensor_tensor(
                        out=ot[oi][:, :], in0=x1[i][:, :], scalar=rs4b[:, i:i + 1], in1=gfin[:, :], op0=ALU.mult, op1=ALU.mult),
                        reads=[("x1", i, 0), ("x1", i, 1), "rs4b", "gfin"], writes=[("ot", oi)])
                    S.dma("sp", y_v[tb * 4 + i], ot[oi][:, :], reads=[("ot", oi)], writes=[("y", tb * 4 + i)])
            S.barrier()
        S.finish("sp")
    return nc


_CACHE = {}


def _get_program():
    if "nc" not in _CACHE:
        _CACHE["nc"] = build_program()
    return _CACHE["nc"]


def make_in_maps(x, c, w_ada, b_ada, g_attn, w_in, g_q, g_k, rpb, w_o, g_ffn, w_gate, w_up, w_down, g_final):
    f32 = lambda a: np.ascontiguousarray(np.asarray(a, dtype=np.float32))
    colmajor = lambda v, n: f32(np.asarray(v, np.float32).reshape(n, 128).T)
    b_int, b_sp = build_bias_tiles(np.asarray(rpb)[0])
    sel = np.zeros((128, 128), np.float32)
    sel[64, :] = 1.0
    shared = {
        "w_ada": f32(np.asarray(w_ada)[0]),
        "bada": colmajor(np.asarray(b_ada)[0], 48),
        "gattn": colmajor(np.asarray(g_attn)[0], 8),
        "gffn": colmajor(np.asarray(g_ffn)[0], 8),
        "gfin": f32(np.broadcast_to(np.asarray(g_final, np.float32)[None, :], (128, D))),
        "w_in": f32(np.asarray(w_in)[0]),
        "gq": f32(np.broadcast_to(np.asarray(g_q, np.float32)[0][None, :], (128, 64))),
        "gk": f32(np.broadcast_to(np.asarray(g_k, np.float32)[0][None, :], (128, 64))),
        "bint": b_int,
        "bsp": b_sp,
        "w_o": f32(np.asarray(w_o)[0]),
        "w_gate": f32(np.asarray(w_gate)[0]),
        "w_up": f32(np.asarray(w_up)[0]),
        "w_down": f32(np.asarray(w_down)[0]),
        "ident": np.eye(128, dtype=np.float32).astype(ml_dtypes.bfloat16),
        "identf": np.eye(128, dtype=np.float32),
        "sel64": sel,
        "cs": rope_table(),
    }
    x = np.asarray(x, np.float32)
    c = np.asarray(c, np.float32)
    maps = []
    for b in range(x.shape[0]):
        m = dict(shared)
        m["x"] = np.ascontiguousarray(x[b])
        m["ccol"] = colmajor(c[b], 8)
        maps.append(m)
    return maps


def kernel(x, c, w_ada, b_ada, g_attn, w_in, g_q, g_k, rpb, w_o, g_ffn, w_gate, w_up, w_down, g_final):
    nc = _get_program()
    in_maps = make_in_maps(x, c, w_ada, b_ada, g_attn, w_in, g_q, g_k, rpb, w_o, g_ffn, w_gate, w_up, w_down, g_final)
    res = run_bass_kernel_spmd(nc, in_maps, core_ids=list(range(N_CORES)))
    out = np.stack([np.asarray(r["y"], dtype=np.float32) for r in res.results], axis=0)
    return out
```

```python
import numpy as np
import ml_dtypes
from contextlib import ExitStack
import concourse.bass as bass
import concourse.mybir as mybir
from concourse.bass_utils import run_bass_kernel_spmd

F32 = mybir.dt.float32
BF16 = mybir.dt.bfloat16
AF = mybir.ActivationFunctionType
ALU = mybir.AluOpType
AX = mybir.AxisListType

T = 4096
D = 1024
DFF = 2816
NF = DFF // 128
NB = 8
EPS = 1e-6
NEG = -30000.0
N_CORES = 8


class Sched:
    N_DMA_SEMS = 12

    def __init__(self, nc, stack):
        self.nc = nc
        self.eng = {"pe": nc.tensor, "act": nc.scalar, "dve": nc.vector,
                    "pool": nc.gpsimd, "sp": nc.sync}
        self.sem = {e: stack.enter_context(nc.semaphore("s_" + e)) for e in self.eng}
        self.cnt = {e: 0 for e in self.eng}
        self.dsem = {q: [stack.enter_context(nc.semaphore(f"d_{q}{i}")) for i in range(self.N_DMA_SEMS)]
                     for q in ("sp", "pool", "act")}
        self.dcnt = {q: [0] * self.N_DMA_SEMS for q in self.dsem}
        self.dnext = {q: 0 for q in self.dsem}
        self.dlast = {q: [None] * self.N_DMA_SEMS for q in self.dsem}
        self.seen = {e: {} for e in self.eng}
        self.res = {}
        self.nwaits = 0
        self.nops = {e: 0 for e in self.eng}

    def _wait(self, e, tok):
        sem, val, peng = tok
        key = sem.name
        if self.seen[e].get(key, 0) >= val:
            return
        self.eng[e].wait_ge(sem, val)
        self.seen[e][key] = val
        self.nwaits += 1

    def _deps(self, e, reads, writes):
        toks = []
        for r in reads:
            st = self.res.get(r)
            if st and st[0] is not None:
                toks.append((st[0], "raw"))
        for w in writes:
            st = self.res.get(w)
            if st:
                if st[0] is not None:
                    toks.append((st[0], "waw"))
                for t in st[1]:
                    toks.append((t, "war"))
        for tok, kind in toks:
            if tok[2] == e and e == "pe":
                continue
            self._wait(e, tok)

    def _commit(self, tok, reads, writes):
        for r in reads:
            st = self.res.setdefault(r, [None, []])
            st[1].append(tok)
            if len(st[1]) > 64:
                best = {}
                for t in st[1]:
                    k = t[0].name
                    if k not in best or best[k][1] < t[1]:
                        best[k] = t
                st[1] = list(best.values())
        for w in writes:
            self.res[w] = [tok, []]

    def op(self, e, fn, reads=(), writes=(), sig=True):
        pbr = [r for r in reads if isinstance(r, tuple) and r[0] == "pb"]
        if pbr:
            writes = list(writes) + [r for r in pbr if r not in writes]
            reads = [r for r in reads if not (isinstance(r, tuple) and r[0] == "pb")]
        self._deps(e, reads, writes)
        ins = fn(self.eng[e])
        self.nops[e] += 1
        if sig:
            self.cnt[e] += 1
            ins.then_inc(self.sem[e], 1)
            tok = (self.sem[e], self.cnt[e], e)
        else:
            tok = (self.sem[e], self.cnt[e] + 1, e)
        self._commit(tok, reads, writes)
        return ins

    def dma(self, q, out, in_, reads=(), writes=(), **kw):
        self._deps(q, reads, writes)
        i = self.dnext[q]
        self.dnext[q] = (i + 1) % self.N_DMA_SEMS
        prev = self.dlast[q][i]
        if prev is not None:
            self._wait(q, prev)
        self.dcnt[q][i] += 16
        ins = self.eng[q].dma_start(out=out, in_=in_, **kw)
        ins.then_inc(self.dsem[q][i], 16)
        tok = (self.dsem[q][i], self.dcnt[q][i], None)
        self.dlast[q][i] = tok
        self._commit(tok, reads, writes)
        return ins

    def all_tokens(self):
        toks = []
        for e in self.eng:
            if self.cnt[e] > 0:
                toks.append((self.sem[e], self.cnt[e], e))
        for q in self.dsem:
            for t in self.dlast[q]:
                if t is not None:
                    toks.append(t)
        return toks

    def barrier(self):
        toks = self.all_tokens()
        for e in self.eng:
            for t in toks:
                if t[2] == e:
                    continue
                self._wait(e, t)
        self.res = {}

    def finish(self, e="sp"):
        for t in self.all_tokens():
            if t[2] != e:
                self._wait(e, t)


def _rs(r):
    return min(max(r - 4, 0), 56)


def _cs(w):
    return min(max(w - 8, 0), 48)


def _chunk_rows(j):
    rows = [r for r in range(64) if any(_rs(r) <= ka < _rs(r) + 8 for ka in (2 * j, 2 * j + 1))]
    return rows[0], rows[-1]


SPECIAL = {2: 0, 3: 1, 28: 2, 29: 3}


def _tile_row0(j):
    return _chunk_rows(j)[0] if j in SPECIAL else 2 * j - 4


def _bias_block(rpb_h, ka, r):
    blk = np.full((64, 64), NEG, np.float32)
    if not (_rs(r) <= ka < _rs(r) + 8):
        return blk
    a = ka - r + 7
    for w in range(64):
        c0 = _cs(w)
        kc = np.arange(c0, c0 + 16)
        blk[kc, w] = rpb_h[a, kc - w + 15]
    return blk


def build_bias_tiles(rpb):
    rpb = np.asarray(rpb, np.float32)
    b_int = np.full((8, 128, 640), NEG, np.float32)
    b_sp = np.full((8, 4, 128, 768), NEG, np.float32)
    j0 = 10
    for h in range(8):
        for kdr in range(2):
            for qr in range(10):
                b_int[h, kdr * 64:(kdr + 1) * 64, qr * 64:(qr + 1) * 64] = \
                    _bias_block(rpb[h], 2 * j0 + kdr, 2 * j0 - 4 + qr)
        for j, si in SPECIAL.items():
            r0, r1 = _chunk_rows(j)
            for kdr in range(2):
                for r in range(r0, r1 + 1):
                    b_sp[h, si, kdr * 64:(kdr + 1) * 64, (r - r0) * 64:(r - r0 + 1) * 64] = \
                        _bias_block(rpb[h], 2 * j + kdr, r)
    return b_int, b_sp


def rope_table():
    t = np.arange(T)
    row = (t // 64).astype(np.float64)
    col = (t % 64).astype(np.float64)
    inv = 10000.0 ** (-np.arange(0, 32, 2, dtype=np.float64) / 32)
    ang = np.concatenate([row[:, None] * inv[None, :], col[:, None] * inv[None, :]], axis=-1)
    return np.concatenate([np.cos(ang), np.sin(ang)], axis=-1).astype(np.float32)


def build_program(stop_after=None, debug=False):
    nc = bass.Bass("TRN2", target_bir_lowering=False)
    dt_in = lambda name, shape, dt=F32: nc.dram_tensor(name, shape, dt, kind="ExternalInput")
    x_d = dt_in("x", [T, D])
    ccol_d = dt_in("ccol", [128, 8])
    wada_d = dt_in("w_ada", [D, 6 * D])
    bada_d = dt_in("bada", [128, 48])
    gattn_d = dt_in("gattn", [128, 8])
    gffn_d = dt_in("gffn", [128, 8])
    gfin_d = dt_in("gfin", [128, D])
    win_d = dt_in("w_in", [D, 2304])
    gq_d = dt_in("gq", [128, 64])
    gk_d = dt_in("gk", [128, 64])
    bint_d = dt_in("bint", [8, 128, 640])
    bsp_d = dt_in("bsp", [8, 4, 128, 768])
    wo_d = dt_in("w_o", [D, D])
    wg_d = dt_in("w_gate", [D, DFF])
    wu_d = dt_in("w_up", [D, DFF])
    wd_d = dt_in("w_down", [DFF, D])
    ident_d = dt_in("ident", [128, 128], BF16)
    identf_d = dt_in("identf", [128, 128])
    sel_d = dt_in("sel64", [128, 128])
    cs_d = dt_in("cs", [T, 64])
    y_d = nc.dram_tensor("y", [T, D], F32, kind="ExternalOutput")
    wg_s = nc.dram_tensor("wg_s", [NF, 128, D], BF16)
    wu_s = nc.dram_tensor("wu_s", [NF, 128, D], BF16)
    wd_s = nc.dram_tensor("wd_s", [2, NF, 128, 512], BF16)
    ot_s = nc.dram_tensor("ot_s", [8, 128, T], BF16)
    dbg = {}
    if debug:
        dbg["d_mod"] = nc.dram_tensor("d_mod", [128, 48], F32, kind="ExternalOutput")
        dbg["d_ot"] = nc.dram_tensor("d_ot", [8, 128, T], BF16, kind="ExternalOutput")

    with ExitStack() as st:
        S = Sched(nc, st)
        ps = st.enter_context(nc.psum_tensor("ps", [128, 8 * 512], F32))

        def bank(b, n=512, off=0):
            return ps[:, b * 512 + off: b * 512 + off + n]

        def bankbf(b):
            return ps[:, b * 512:(b + 1) * 512].bitcast(BF16)

        PB = lambda b: ("pb", b)

        def sbt(stack, name, shape, dt):
            return stack.enter_context(nc.sbuf_tensor("sb_" + name, shape, dt))

        idt = sbt(st, "idt", [128, 128], BF16)
        identf = sbt(st, "identf", [128, 128], F32)
        sel64 = sbt(st, "sel64", [128, 128], F32)
        onesf = sbt(st, "onesf", [128, 128], F32)
        mh = sbt(st, "mh", [128, 8], F32)
        modsb = sbt(st, "modsb", [128, 48], F32)
        gsa = sbt(st, "gsa", [128, 8], F32)
        gsf = sbt(st, "gsf", [128, 8], F32)
        rec = sbt(st, "rec", [128, 512], F32)
        bcsb = sbt(st, "bcsb", [64, 512], F32)

        S.dma("sp", idt[:, :], ident_d.ap(), writes=["idt"])
        S.dma("sp", identf[:, :], identf_d.ap(), writes=["identf"])
        S.dma("sp", sel64[:, :], sel_d.ap(), writes=["sel64"])
        S.op("dve", lambda e: e.memset(onesf[:, :], 1.0), writes=["onesf"])
        S.op("dve", lambda e: e.memset(mh[:, :], -0.5), writes=["mh"])
        S.op("dve", lambda e: e.memset(rec[:, :], 0.0), writes=["rec"])

        prep = []
        for f in range(NF):
            prep.append(lambda f=f: S.dma("pool", wg_s.ap()[f].rearrange("p (c j) -> p c j", c=8),
                                          wg_d.ap()[:, f * 128:(f + 1) * 128].rearrange("(c p) j -> p c j", p=128),
                                          writes=[("wg_s", f)]))
            prep.append(lambda f=f: S.dma("pool", wu_s.ap()[f].rearrange("p (c j) -> p c j", c=8),
                                          wu_d.ap()[:, f * 128:(f + 1) * 128].rearrange("(c p) j -> p c j", p=128),
                                          writes=[("wu_s", f)]))
        for hf in range(2):
            prep.append(lambda hf=hf: S.dma("pool", wd_s.ap()[hf].rearrange("f p n -> p f n"),
                                            wd_d.ap()[:, hf * 512:(hf + 1) * 512].rearrange("(f p) n -> p f n", p=128),
                                            writes=[("wd_s", hf)]))

        with ExitStack() as s0:
            ccol = sbt(s0, "ccol", [128, 8], F32)
            ctmp = sbt(s0, "ctmp", [128, 8], F32)
            cact = sbt(s0, "cact", [128, 8], BF16)
            bada = sbt(s0, "bada", [128, 48], F32)
            gat = sbt(s0, "gat", [128, 8], F32)
            gff = sbt(s0, "gff", [128, 8], F32)
            wa = [sbt(s0, f"wa{i}", [128, 8, 512], BF16) for i in range(2)]
            S.dma("sp", ccol[:, :], ccol_d.ap(), writes=["ccol"])
            S.dma("sp", bada[:, :], bada_d.ap(), writes=["bada"])
            S.dma("sp", gat[:, :], gattn_d.ap(), writes=["gat"])
            S.dma("sp", gff[:, :], gffn_d.ap(), writes=["gff"])
            S.op("act", lambda e: e.activation(out=ctmp[:, :], in_=ccol[:, :], func=AF.Exp, scale=-1.0),
                 reads=["ccol"], writes=["ctmp"])
            S.op("dve", lambda e: e.tensor_scalar(out=ctmp[:, :], in0=ctmp[:, :], scalar1=1.0, scalar2=None, op0=ALU.add),
                 reads=["ctmp"], writes=["ctmp"])
            S.op("dve", lambda e: e.reciprocal(out=ctmp[:, :], in_=ctmp[:, :]), reads=["ctmp"], writes=["ctmp"])
            S.op("dve", lambda e: e.tensor_tensor(out=cact[:, :], in0=ctmp[:, :], in1=ccol[:, :], op=ALU.mult),
                 reads=["ctmp", "ccol"], writes=["cact"])
            wada_v = wada_d.ap().rearrange("(k p) n -> p k n", p=128)
            for nb in range(12):
                wb = wa[nb % 2]
                S.dma("pool", wb[:, :, :], wada_v[:, :, nb * 512:(nb + 1) * 512], writes=[("wa", nb % 2)])
                for jj in range(4):
                    j = nb * 4 + jj
                    for k in range(8):
                        S.op("pe", lambda e, wb=wb, j=j, jj=jj, k=k: e.matmul(
                            bank(0, 1, j), lhsT=wb[:, k, jj * 128:(jj + 1) * 128], rhs=cact[:, k:k + 1],
                            start=(k == 0), stop=(k == 7)),
                            reads=[("wa", nb % 2), "cact"], writes=[PB(0)], sig=(k == 7))
            S.op("dve", lambda e: e.tensor_tensor(out=modsb[:, :], in0=bank(0, 48), in1=bada[:, :], op=ALU.add),
                 reads=[PB(0), "bada"], writes=["modsb"])
            S.op("dve", lambda e: e.scalar_tensor_tensor(out=gsa[:, :], in0=modsb[:, 8:16], scalar=1.0, in1=gat[:, :],
                                                         op0=ALU.add, op1=ALU.mult),
                 reads=["modsb", "gat"], writes=["gsa"])
            S.op("dve", lambda e: e.scalar_tensor_tensor(out=gsf[:, :], in0=modsb[:, 32:40], scalar=1.0, in1=gff[:, :],
                                                         op0=ALU.add, op1=ALU.mult),
                 reads=["modsb", "gff"], writes=["gsf"])
            if debug:
                S.dma("sp", dbg["d_mod"].ap(), modsb[:, :], reads=["modsb"], writes=["d_mod"])
            S.barrier()
        sha = modsb[:, 0:8]
        shf = modsb[:, 24:32]
        if stop_after == "p0":
            S.finish("sp")
            return nc

        with ExitStack() as s1:
            win = sbt(s1, "win", [128, 8, 2304], BF16)
            cs = sbt(s1, "cs", [128, 32, 64], F32)
            gqb = sbt(s1, "gqb", [128, 64], F32)
            gkb = sbt(s1, "gkb", [128, 64], F32)
            KTg = sbt(s1, "KTg", [128, T], BF16)
            Vg = sbt(s1, "Vg", [128, 32, 2, 128], BF16)
            bint = sbt(s1, "bint", [128, 8, 640], F32)
            bsp = [sbt(s1, f"bsp{i}", [128, 768], F32) for i in range(3)]
            kTn = sbt(s1, "kTn", [128, 3, 4, 512], BF16)
            Vn = sbt(s1, "Vn", [128, 12, 8, 65], BF16)
            qTlo = sbt(s1, "qTlo", [128, 4, 512], BF16)
            qThi = sbt(s1, "qThi", [128, 4, 512], BF16)
            QTlo = sbt(s1, "QTlo", [128, 4, 512], BF16)
            QThi = sbt(s1, "QThi", [128, 4, 512], BF16)
            xt = [sbt(s1, f"xt{i}", [128, D], F32) for i in range(2)]
            xn = [sbt(s1, f"xn{i}", [128, D], BF16) for i in range(4)]
            ss4 = sbt(s1, "ss4", [128, 4], F32)
            rs4 = sbt(s1, "rs4", [128, 4], F32)
            hT = sbt(s1, "hT", [128, 8, 512], BF16)
            qsb = [sbt(s1, f"qsb{i}", [128, 512], F32) for i in range(2)]
            qtmp = sbt(s1, "qtmp", [128, 512], F32)
            ssq = sbt(s1, "ssq", [128, 8], F32)
            rq = sbt(s1, "rq", [128, 8], F32)
            rt1 = sbt(s1, "rt1", [128, 8, 32], F32)
            rt2 = sbt(s1, "rt2", [128, 8, 32], F32)
            qhat = sbt(s1, "qhat", [128, 4, 512], BF16)
            khat = sbt(s1, "khat", [128, 4, 128], BF16)
            Ssb = [sbt(s1, f"Ssb{i}", [128, 512], F32) for i in range(3)]
            Pna = [sbt(s1, f"Pna{i}", [128, 512], BF16) for i in range(3)]
            Pg = [sbt(s1, f"Pg{i}", [128, 1024], BF16) for i in range(2)]
            OT = sbt(s1, "OT", [128, 8, 512], BF16)
            rg = [sbt(s1, f"rg{i}", [64, 512], F32) for i in range(2)]

            S.dma("pool", win[:, :, :], win_d.ap().rearrange("(c p) n -> p c n", p=128), writes=["win"])
            S.dma("sp", cs[:, :, :], cs_d.ap().rearrange("(tt p) k -> p tt k", p=128), writes=["cs"])
            S.dma("sp", gqb[:, :], gq_d.ap(), writes=["gqb"])
            S.dma("sp", gkb[:, :], gk_d.ap(), writes=["gkb"])
            S.dma("sp", bint[:, :, :], bint_d.ap().rearrange("h p n -> p h n"), writes=["bint"])
            S.op("dve", lambda e: e.memset(Vg[:, :, :, 64:128], 1.0), writes=["Vg1"])
            S.op("dve", lambda e: e.memset(Vn[:, :, :, 64:65], 1.0), writes=["Vn1"])
            for tns, nm in ((qTlo, "qTlo0"), (qThi, "qThi0"), (QTlo, "QTlo0"), (QThi, "QThi0")):
                S.op("dve", lambda e, tns=tns: e.memset(tns[:, :, :], 0.0), writes=[nm])
            zero_deps = {"qTlo": "qTlo0", "qThi": "qThi0", "QTlo": "QTlo0", "QThi": "QThi0"}

            x_v = x_d.ap().rearrange("(tt p) d -> tt p d", p=128)

            def norm_block(tb, gs, sh, dst=None, key="hT"):
                dst = hT if dst is None else dst
                for i in range(4):
                    xb = xt[i % 2]
                    S.dma("sp", xb[:, :], x_v[tb * 4 + i], writes=[("xt", i % 2)])
                    S.op("act", lambda e, i=i, xb=xb: e.activation(out=xn[i][:, :], in_=xb[:, :], func=AF.Square,
                                                                   accum_out=ss4[:, i:i + 1]),
                         reads=[("xt", i % 2)], writes=[("xn", i), ("ss4", i)])
                    S.op("dve", lambda e, i=i: e.tensor_scalar(out=rs4[:, i:i + 1], in0=ss4[:, i:i + 1], scalar1=1.0 / D,
                                                               scalar2=EPS, op0=ALU.mult, op1=ALU.add),
                         reads=[("ss4", i)], writes=[("rs4", i)])
                    S.op("act", lambda e, i=i: e.activation(out=rs4[:, i:i + 1], in_=rs4[:, i:i + 1], func=AF.Ln),
                         reads=[("rs4", i), "mh"], writes=[("rs4", i)])
                    S.op("act", lambda e, i=i: e.activation(out=rs4[:, i:i + 1], in_=rs4[:, i:i + 1], func=AF.Exp, scale=-0.5),
                         reads=[("rs4", i), "mh"], writes=[("rs4", i)])
                    S.op("dve", lambda e, i=i, xb=xb: e.tensor_scalar(out=xn[i][:, :], in0=xb[:, :], scalar1=rs4[:, i:i + 1],
                                                                      scalar2=None, op0=ALU.mult),
                         reads=[("xt", i % 2), ("rs4", i)], writes=[("xn", i)])
                for c in range(8):
                    bk = c // 2
                    for i in range(4):
                        S.op("pe", lambda e, c=c, i=i, bk=bk: e.transpose(
                            out=bankbf(bk)[:, (c % 2) * 512 + i * 128:(c % 2) * 512 + (i + 1) * 128],
                            in_=xn[i][:, c * 128:(c + 1) * 128], identity=idt[:, :]),
                            reads=[("xn", i), "idt"], writes=[PB(bk)], sig=(i == 3))
                for c in range(8):
                    bk = c // 2
                    S.op("act", lambda e, c=c, bk=bk: e.activation(
                        out=dst[:, c, :], in_=bankbf(bk)[:, (c % 2) * 512:(c % 2 + 1) * 512], func=AF.Identity,
                        scale=gs[:, c:c + 1], bias=sh[:, c:c + 1]),
                        reads=[PB(bk), "gsa", "modsb"], writes=[(key, c)])

            def qk_post(src_ap, nheads, gb, gname, tt, dst_fn, dst_keys, sb_i, grp=None):
                H = nheads
                W = H * 64
                v3 = lambda ap: ap.rearrange("p (h d) -> p h d", d=64)
                src3 = v3(src_ap)
                tmp3 = v3(qtmp[:, 0:W])
                S.op("dve", lambda e: e.tensor_tensor(out=qtmp[:, 0:W], in0=src_ap, in1=src_ap, op=ALU.mult),
                     reads=[("qsb", sb_i)], writes=["qtmp"])
                S.op("dve", lambda e: e.tensor_reduce(out=ssq[:, 0:H], in_=tmp3, axis=AX.X, op=ALU.add),
                     reads=["qtmp"], writes=["ssq"])
                S.op("dve", lambda e: e.tensor_scalar(out=rq[:, 0:H], in0=ssq[:, 0:H], scalar1=1.0 / 64, scalar2=EPS,
                                                      op0=ALU.mult, op1=ALU.add), reads=["ssq"], writes=["rq"])
                S.op("act", lambda e: e.activation(out=rq[:, 0:H], in_=rq[:, 0:H], func=AF.Ln),
                     reads=["rq", "mh"], writes=["rq"])
                S.op("act", lambda e: e.activation(out=rq[:, 0:H], in_=rq[:, 0:H], func=AF.Exp, scale=-0.5),
                     reads=["rq", "mh"], writes=["rq"])
                S.op("dve", lambda e: e.tensor_tensor(out=tmp3, in0=src3, in1=rq[:, 0:H].unsqueeze(2).to_broadcast([128, H, 64]),
                                                      op=ALU.mult), reads=[("qsb", sb_i), "rq"], writes=["qtmp"])
                S.op("dve", lambda e: e.tensor_tensor(out=tmp3, in0=tmp3, in1=gb[:, :].unsqueeze(1).to_broadcast([128, H, 64]),
                                                      op=ALU.mult), reads=["qtmp", gname], writes=["qtmp"])
                x1 = tmp3[:, :, 0:32]
                x2 = tmp3[:, :, 32:64]
                t1 = rt1[:, 0:H, :]
                t2 = rt2[:, 0:H, :]
                if grp is None:
                    cosb = cs[:, tt, 0:32].unsqueeze(1).to_broadcast([128, H, 32])
                    sinb = cs[:, tt, 32:64].unsqueeze(1).to_broadcast([128, H, 32])
                else:
                    A, B = grp
                    f4 = lambda ap: ap.rearrange("p (a b) d -> p a b d", a=A)
                    x1, x2, t1, t2 = f4(x1), f4(x2), f4(t1), f4(t2)
                    cosb = cs[:, tt:tt + A, 0:32].unsqueeze(2).to_broadcast([128, A, B, 32])
                    sinb = cs[:, tt:tt + A, 32:64].unsqueeze(2).to_broadcast([128, A, B, 32])
                    _dst = dst_fn
                    dst_fn = lambda half: f4(_dst(half))
                S.op("dve", lambda e: e.tensor_tensor(out=t1, in0=x1, in1=cosb, op=ALU.mult), reads=["qtmp", "cs"], writes=["rt1"])
                S.op("dve", lambda e: e.tensor_tensor(out=t2, in0=x2, in1=sinb, op=ALU.mult), reads=["qtmp", "cs"], writes=["rt2"])
                S.op("dve", lambda e: e.tensor_tensor(out=dst_fn(0), in0=t1, in1=t2, op=ALU.subtract),
                     reads=["rt1", "rt2"], writes=dst_keys)
                S.op("dve", lambda e: e.tensor_tensor(out=t1, in0=x1, in1=sinb, op=ALU.mult), reads=["qtmp", "cs"], writes=["rt1"])
                S.op("dve", lambda e: e.tensor_tensor(out=t2, in0=x2, in1=cosb, op=ALU.mult), reads=["qtmp", "cs"], writes=["rt2"])
                S.op("dve", lambda e: e.tensor_tensor(out=dst_fn(1), in0=t1, in1=t2, op=ALU.add),
                     reads=["rt1", "rt2"], writes=dst_keys)

            def p1a_post(tb, hb, hk):
                for i in range(4):
                    bk = 4 + i // 2
                    for c in range(8):
                        S.op("pe", lambda e, c=c, i=i, bk=bk: e.matmul(bank(bk, 256, (i % 2) * 256), lhsT=hb[:, c, i * 128:(i + 1) * 128],
                                                                     rhs=win[:, c, 2048:2304], start=(c == 0), stop=(c == 7),
                                                                     skip_group_check=True),
                             reads=[(hk, c), "win"], writes=[PB(bk)], sig=(c == 7))
                for bb in range(2):
                    src = bank(4 + bb).rearrange("p (t x) -> p t x", t=2)
                    S.op("act", lambda e, bb=bb, src=src: e.copy(
                        out=qsb[0][:, bb * 256:(bb + 1) * 256].rearrange("p (t x) -> p t x", t=2), in_=src[:, :, 0:128]),
                        reads=[PB(4 + bb)], writes=[("qsb", 0)])
                    tt0 = tb * 4 + 2 * bb
                    S.op("dve", lambda e, tt0=tt0, src=src: e.tensor_copy(
                        out=Vg[:, tt0:tt0 + 2, :, 0:64], in_=src[:, :, 128:256].rearrange("p t (h d) -> p t h d", d=64)),
                        reads=[PB(4 + bb)], writes=[("Vg", tt0), ("Vg", tt0 + 1)])
                kh3 = khat[:, :, :].rearrange("p t (h d) -> p (t h) d", d=64)
                qk_post(qsb[0][:, :], 8, gkb, "gkb", tb * 4,
                        lambda half, kh3=kh3: kh3[:, :, half * 32:(half + 1) * 32], ["khat"], 0, grp=(4, 2))
                for i in range(4):
                    S.op("pe", lambda e, i=i: e.transpose(out=bankbf(6)[:, i * 128:(i + 1) * 128], in_=khat[:, i, :], identity=idt[:, :]),
                         reads=["khat", "idt"], writes=[PB(6)], sig=(i == 3))
                S.op("act", lambda e, tb=tb: e.copy(out=KTg[:, tb * 512:(tb + 1) * 512], in_=bankbf(6)[:, 0:512]),
                     reads=[PB(6)], writes=[("KTg", tb * 4 + i) for i in range(4)])

            def stage_a(tb, part="all"):
                slot = tb % 3
                if part in ("all", "kv"):
                    norm_block(tb, gsa, sha)
                n_mm = 0
                for m in range(4):
                    for which in range(2):
                        if which == 0 and part == "kv":
                            continue
                        if which == 1 and part == "q":
                            continue
                        bk = 4 + (n_mm % 2)
                        n_mm += 1
                        col0 = which * 512 + m * 128
                        for c in range(8):
                            S.op("pe", lambda e, c=c, bk=bk, col0=col0: e.matmul(
                                bank(bk), lhsT=win[:, c, col0:col0 + 128], rhs=hT[:, c, :], start=(c == 0), stop=(c == 7)),
                                reads=[("hT", c), "win"], writes=[PB(bk)], sig=(c == 7))
                        if which == 0:
                            S.op("act", lambda e, bk=bk, m=m: e.copy(out=qTlo[0:64, m, :], in_=bank(bk)[0:64, :]),
                                 reads=[PB(bk), "qTlo0"], writes=[("qTlo", m)])
                            S.op("dve", lambda e, bk=bk, m=m: e.tensor_copy(out=qThi[64:128, m, :], in_=bank(bk)[64:128, :]),
                                 reads=[PB(bk), "qThi0"], writes=[("qThi", m)])
                        else:
                            S.op("act", lambda e, bk=bk, m=m: e.copy(out=kTn[:, slot, m, :], in_=bank(bk)),
                                 reads=[PB(bk)], writes=[("kTn", slot, m)])
                for i in range(4 if part in ("all", "kv") else 0):
                    tt = tb * 4 + i
                    bk = 4 + (i % 2)
                    for c in range(8):
                        S.op("pe", lambda e, c=c, i=i, bk=bk: e.matmul(bank(bk), lhsT=hT[:, c, i * 128:(i + 1) * 128],
                                                                     rhs=win[:, c, 1024:1536], start=(c == 0), stop=(c == 7)),
                             reads=[("hT", c), "win"], writes=[PB(bk)], sig=(c == 7))
                    S.op("act", lambda e, bk=bk, i=i: e.copy(
                        out=Vn[:, slot * 4 + i, :, 0:64], in_=bank(bk).rearrange("p (h d) -> p h d", d=64)),
                        reads=[PB(bk), "Vn1"], writes=[("Vn", slot * 4 + i)])
                if part == "kv":
                    return
                for i in range(4):
                    tt = tb * 4 + i
                    bk = 6 + (i % 2)
                    for c in range(8):
                        S.op("pe", lambda e, c=c, i=i, bk=bk: e.matmul(bank(bk), lhsT=hT[:, c, i * 128:(i + 1) * 128],
                                                                     rhs=win[:, c, 1536:2048], start=(c == 0), stop=(c == 7)),
                             reads=[("hT", c), "win"], writes=[PB(bk)], sig=(c == 7))
                    sbi = i % 2
                    S.op("act", lambda e, bk=bk, sbi=sbi: e.copy(
                        out=qsb[sbi][:, :].rearrange("p (g kv d) -> p g kv d", g=4, kv=2),
                        in_=bank(bk).rearrange("p (kv g d) -> p g kv d", g=4, kv=2)),
                        reads=[PB(bk)], writes=[("qsb", sbi)])
                    qh3 = qhat[:, i, :].rearrange("p (h d) -> p h d", d=64)
                    qk_post(qsb[sbi][:, :], 8, gqb, "gqb", tt,
                            lambda half, qh3=qh3: qh3[:, :, half * 32:(half + 1) * 32], [("qhat", i)], sbi)
                for g in range(4):
                    bk = 4 + g // 2
                    for i in range(4):
                        S.op("pe", lambda e, g=g, i=i, bk=bk: e.transpose(
                            out=bankbf(bk)[:, (g % 2) * 512 + i * 128:(g % 2) * 512 + (i + 1) * 128],
                            in_=qhat[:, i, g * 128:(g + 1) * 128], identity=idt[:, :]),
                            reads=[("qhat", i), "idt"], writes=[PB(bk)], sig=(i == 3))
                for g in range(4):
                    bk = 4 + g // 2
                    src = bankbf(bk)[:, (g % 2) * 512:(g % 2 + 1) * 512]
                    S.op("act", lambda e, g=g, src=src: e.copy(out=QTlo[0:64, g, :], in_=src[0:64, :]),
                         reads=[PB(bk), "QTlo0"], writes=[("QTlo", g)])
                    S.op("dve", lambda e, g=g, src=src: e.tensor_copy(out=QThi[64:128, g, :], in_=src[64:128, :]),
                         reads=[PB(bk), "QThi0"], writes=[("QThi", g)])

            def normalize_head(acc_bk, h_chunk, half, bc_bk):
                S.op("dve", lambda e: e.reciprocal(out=rec[64:65, :], in_=bank(acc_bk)[64:65, :]),
                     reads=[PB(acc_bk)], writes=["rec"])
                S.op("pe", lambda e: e.matmul(bank(bc_bk)[0:128, :], lhsT=sel64[:, :], rhs=rec[:, :], start=True, stop=True),
                     reads=["sel64", "rec"], writes=[PB(bc_bk)])
                S.op("act", lambda e: e.copy(out=bcsb[:, :], in_=bank(bc_bk)[0:64, :]), reads=[PB(bc_bk)], writes=["bcsb"])
                S.op("dve", lambda e: e.tensor_tensor(out=OT[half * 64:(half + 1) * 64, h_chunk, :], in0=bank(acc_bk)[0:64, :],
                                                      in1=bcsb[:, :], op=ALU.mult),
                     reads=[PB(acc_bk), "bcsb"], writes=[("OT", h_chunk, half)])

            sp_loaded = {}
            sp_next = [0]

            def na_block(tb):
                steps = []
                for h in range(8):
                    items = []
                    for j in range(max(0, 4 * tb - 2), min(31, 4 * tb + 5) + 1):
                        r0, r1 = _chunk_rows(j)
                        lo, hi = max(8 * tb, r0), min(8 * tb + 7, r1)
                        if lo <= hi:
                            items.append((j, lo, hi))
                    for idx, (j, lo, hi) in enumerate(items):
                        steps.append(dict(h=h, j=j, lo=lo, hi=hi, first=(idx == 0), last=(idx == len(items) - 1)))
                n = len(steps)
                for i, stp in enumerate(steps):
                    stp["sbk"] = 3 + (i % 3)
                    stp["bi"] = i % 3
                    stp["nq"] = (stp["hi"] - stp["lo"] + 1) * 64
                    stp["qc0"] = (stp["lo"] - 8 * tb) * 64

                def qk(i):
                    p = steps[i]
                    h, j = p["h"], p["j"]
                    m, half = h // 2, h % 2
                    qT = qTlo if half == 0 else qThi
                    qkey = ("qTlo", m) if half == 0 else ("qThi", m)
                    kslot, kcol = (j // 4) % 3, (j % 4) * 128
                    nq, qc0, sbk = p["nq"], p["qc0"], p["sbk"]
                    S.op("pe", lambda e: e.matmul(bank(sbk, nq), lhsT=kTn[:, kslot, m, kcol:kcol + 128],
                                                  rhs=qT[:, m, qc0:qc0 + nq], start=True, stop=True),
                         reads=[("kTn", kslot, m), qkey], writes=[PB(sbk)])

                def bias_exp(i):
                    p = steps[i]
                    h, j, lo = p["h"], p["j"], p["lo"]
                    nq, sbk, bi = p["nq"], p["sbk"], p["bi"]
                    boff = (lo - _tile_row0(j)) * 64
                    if j in SPECIAL:
                        key = (h, j)
                        if key not in sp_loaded:
                            si = sp_next[0] % 3
                            sp_next[0] += 1
                            S.dma("sp", bsp[si][:, :], bsp_d.ap()[h, SPECIAL[j]], writes=[("bsp", si)])
                            for k2 in [k for k, v in sp_loaded.items() if v == si]:
                                del sp_loaded[k2]
                            sp_loaded[key] = si
                        si = sp_loaded[key]
                        b_ap = bsp[si][:, boff:boff + nq]
                        bkey = ("bsp", si)
                    else:
                        b_ap = bint[:, h, boff:boff + nq]
                        bkey = "bint"
                    S.op("dve", lambda e: e.scalar_tensor_tensor(
                        out=Ssb[bi][:, 0:nq], in0=bank(sbk, nq), scalar=0.125, in1=b_ap, op0=ALU.mult, op1=ALU.add),
                        reads=[PB(sbk), bkey], writes=[("Ssb", bi)])
                    S.op("act", lambda e: e.activation(out=Pna[bi][:, 0:nq], in_=Ssb[bi][:, 0:nq], func=AF.Exp),
                         reads=[("Ssb", bi)], writes=[("Pna", bi)])

                def pv(i):
                    p = steps[i]
                    h, j = p["h"], p["j"]
                    vt = ((j // 4) % 3) * 4 + (j % 4)
                    nq, qc0, bi = p["nq"], p["qc0"], p["bi"]
                    acc_bk = 6 + (h % 2)
                    S.op("pe", lambda e: e.matmul(bank(acc_bk)[0:65, qc0:qc0 + nq], lhsT=Vn[:, vt, h, 0:65],
                                                  rhs=Pna[bi][:, 0:nq], start=p["first"], stop=p["last"],
                                                  skip_group_check=True),
                         reads=[("Vn", vt), "Vn1", ("Pna", bi)], writes=[PB(acc_bk)])

                deferred = []

                def sched_normalize(i, h):
                    acc_bk, m, half = 6 + (h % 2), h // 2, h % 2
                    def recip_row():
                        S.op("act", lambda e: e.activation(out=rec[64:65, :], in_=bank(acc_bk)[64:65, :], func=AF.Ln),
                             reads=[PB(acc_bk)], writes=["rec"])
                        S.op("act", lambda e: e.activation(out=rec[64:65, :], in_=rec[64:65, :], func=AF.Exp, scale=-1.0),
                             reads=["rec"], writes=["rec"])
                    deferred.append((i + 1, recip_row))
                    deferred.append((i + 2, lambda: S.op(
                        "pe", lambda e: e.matmul(bank(2)[0:128, :], lhsT=sel64[:, :], rhs=rec[:, :], start=True, stop=True),
                        reads=["sel64", "rec"], writes=[PB(2)])))
                    deferred.append((i + 3, lambda: S.op(
                        "act", lambda e: e.copy(out=bcsb[:, :], in_=bank(2)[0:64, :]), reads=[PB(2)], writes=["bcsb"])))
                    deferred.append((i + 4, lambda: S.op(
                        "dve", lambda e: e.tensor_tensor(out=OT[half * 64:(half + 1) * 64, m, :], in0=bank(acc_bk)[0:64, :],
                                                         in1=bcsb[:, :], op=ALU.mult),
                        reads=[PB(acc_bk), "bcsb"], writes=[("OT", m, half)])))

                def run_deferred(i):
                    rest = []
                    for at, th in deferred:
                        if at <= i:
                            th()
                        else:
                            rest.append((at, th))
                    deferred[:] = rest

                qk(0)
                if n > 1:
                    qk(1)
                for i in range(n):
                    bias_exp(i)
                    if i + 2 < n:
                        qk(i + 2)
                    run_deferred(i)
                    pv(i)
                    if steps[i]["last"]:
                        sched_normalize(i, steps[i]["h"])
                run_deferred(10 ** 9)

            fb = [0]

            def fbank():
                fb[0] += 1
                return 6 + (fb[0] % 2)

            def th_norm(tb):
                L = []

                def ld(i):
                    xb = xt[i % 2]
                    S.dma("sp", xb[:, :], x_v[tb * 4 + i], writes=[("xt", i % 2)])
                    S.op("act", lambda e: e.activation(out=xn[i][:, :], in_=xb[:, :], func=AF.Square,
                                                       accum_out=ss4[:, i:i + 1]),
                         reads=[("xt", i % 2)], writes=[("xn", i), ("ss4", i)])

                def rsd(i):
                    S.op("dve", lambda e: e.tensor_scalar(out=rs4[:, i:i + 1], in0=ss4[:, i:i + 1], scalar1=1.0 / D,
                                                          scalar2=EPS, op0=ALU.mult, op1=ALU.add),
                         reads=[("ss4", i)], writes=[("rs4", i)])
                    S.op("act", lambda e: e.activation(out=rs4[:, i:i + 1], in_=rs4[:, i:i + 1], func=AF.Ln),
                         reads=[("rs4", i), "mh"], writes=[("rs4", i)])
                    S.op("act", lambda e: e.activation(out=rs4[:, i:i + 1], in_=rs4[:, i:i + 1], func=AF.Exp, scale=-0.5),
                         reads=[("rs4", i), "mh"], writes=[("rs4", i)])

                def scl(i):
                    xb = xt[i % 2]
                    S.op("dve", lambda e: e.tensor_scalar(out=xn[i][:, :], in0=xb[:, :], scalar1=rs4[:, i:i + 1],
                                                          scalar2=None, op0=ALU.mult),
                         reads=[("xt", i % 2), ("rs4", i)], writes=[("xn", i)])

                for i0 in (0, 2):
                    for i in (i0, i0 + 1):
                        L.append(lambda i=i: ld(i))
                    for i in (i0, i0 + 1):
                        L.append(lambda i=i: rsd(i))
                        L.append(lambda i=i: scl(i))

                def tr(r, bb):
                    bk = 6 + bb
                    for cc in range(2):
                        c = 4 * r + 2 * bb + cc
                        for i in range(4):
                            S.op("pe", lambda e, c=c, cc=cc, i=i: e.transpose(
                                out=bankbf(bk)[:, cc * 512 + i * 128:cc * 512 + (i + 1) * 128],
                                in_=xn[i][:, c * 128:(c + 1) * 128], identity=idt[:, :]),
                                reads=[("xn", i), "idt"], writes=[PB(bk)], sig=(i == 3))
                            if i % 2 == 1:
                                yield

                def ev(r, bb):
                    bk = 6 + bb
                    for cc in range(2):
                        c = 4 * r + 2 * bb + cc
                        S.op("dve", lambda e, c=c, cc=cc: e.tensor_scalar(
                            out=hT[:, c, :], in0=bankbf(bk)[:, cc * 512:(cc + 1) * 512],
                            scalar1=gsa[:, c:c + 1], scalar2=sha[:, c:c + 1], op0=ALU.mult, op1=ALU.add),
                            reads=[PB(bk), "gsa", "modsb"], writes=[("hT", c)])

                L2 = []
                for r in range(2):
                    for bb in range(2):
                        L2.append(lambda r=r, bb=bb: tr(r, bb))
                    for bb in range(2):
                        L2.append(lambda r=r, bb=bb: ev(r, bb))
                return L, L2

            def proj_group(lhs_fn, rhs_fn, bk, n=512, hk="hT"):
                for c in range(8):
                    S.op("pe", lambda e, c=c: e.matmul(bank(bk, n), lhsT=lhs_fn(c), rhs=rhs_fn(c), start=(c == 0), stop=(c == 7)),
                         reads=[(hk, c), "win"], writes=[PB(bk)], sig=(c == 7))
                    if c % 2 == 1:
                        yield

            def th_kv(tb, hT=hT, hk="hT"):
                slot = tb % 3
                L = []
                for m in range(4):
                    def pk(m=m):
                        bk = fbank()
                        col0 = 512 + m * 128
                        yield from proj_group(lambda c: win[:, c, col0:col0 + 128], lambda c: hT[:, c, :], bk, hk=hk)
                        S.op("dve", lambda e: e.tensor_copy(out=kTn[:, slot, m, :], in_=bank(bk)),
                             reads=[PB(bk)], writes=[("kTn", slot, m)])
                    L.append(pk)
                for i in range(4):
                    def pvv(i=i):
                        bk = fbank()
                        yield from proj_group(lambda c: hT[:, c, i * 128:(i + 1) * 128], lambda c: win[:, c, 1024:1536], bk, hk=hk)
                        S.op("dve", lambda e: e.tensor_copy(
                            out=Vn[:, slot * 4 + i, :, 0:64], in_=bank(bk).rearrange("p (h d) -> p h d", d=64)),
                            reads=[PB(bk), "Vn1"], writes=[("Vn", slot * 4 + i)])
                    L.append(pvv)
                return L

            def th_q(tb, hT=hT, hk="hT"):
                L = []
                for m in range(4):
                    def pq(m=m):
                        bk = fbank()
                        col0 = m * 128
                        yield from proj_group(lambda c: win[:, c, col0:col0 + 128], lambda c: hT[:, c, :], bk, hk=hk)
                        S.op("dve", lambda e: e.tensor_copy(out=qTlo[0:64, m, :], in_=bank(bk)[0:64, :]),
                             reads=[PB(bk), "qTlo0"], writes=[("qTlo", m)])
                        S.op("dve", lambda e: e.tensor_copy(out=qThi[64:128, m, :], in_=bank(bk)[64:128, :]),
                             reads=[PB(bk), "qThi0"], writes=[("qThi", m)])
                    L.append(pq)
                for i in range(4):
                    tt = tb * 4 + i
                    sbi = i % 2

                    def pg(i=i, sbi=sbi):
                        bk = fbank()
                        yield from proj_group(lambda c: hT[:, c, i * 128:(i + 1) * 128], lambda c: win[:, c, 1536:2048], bk, hk=hk)
                        S.op("dve", lambda e: e.tensor_copy(
                            out=qsb[sbi][:, :].rearrange("p (g kv d) -> p g kv d", g=4, kv=2),
                            in_=bank(bk).rearrange("p (kv g d) -> p g kv d", g=4, kv=2)),
                            reads=[PB(bk)], writes=[("qsb", sbi)])
                    L.append(pg)

                    def post(i=i, sbi=sbi, tt=tt):
                        qh3 = qhat[:, i, :].rearrange("p (h d) -> p h d", d=64)
                        qk_post(qsb[sbi][:, :], 8, gqb, "gqb", tt,
                                lambda half: qh3[:, :, half * 32:(half + 1) * 32], [("qhat", i)], sbi)
                    L.append(post)
                return L

            def finalize_q():
                for g in range(4):
                    bk = g // 2
                    for i in range(4):
                        S.op("pe", lambda e, g=g, i=i, bk=bk: e.transpose(
                            out=bankbf(bk)[:, (g % 2) * 512 + i * 128:(g % 2) * 512 + (i + 1) * 128],
                            in_=qhat[:, i, g * 128:(g + 1) * 128], identity=idt[:, :]),
                            reads=[("qhat", i), "idt"], writes=[PB(bk)], sig=(i == 3))
                for g in range(4):
                    bk = g // 2
                    src = bankbf(bk)[:, (g % 2) * 512:(g % 2 + 1) * 512]
                    S.op("dve", lambda e, g=g, src=src: e.tensor_copy(out=QTlo[0:64, g, :], in_=src[0:64, :]),
                         reads=[PB(bk), "QTlo0"], writes=[("QTlo", g)])
                    S.op("dve", lambda e, g=g, src=src: e.tensor_copy(out=QThi[64:128, g, :], in_=src[64:128, :]),
                         reads=[PB(bk), "QThi0"], writes=[("QThi", g)])

            def gqa_block(tb, filler):
                nf = int(len(filler) * 2.6) + 1
                fl = Filler(filler)
                state = {"emitted": 0, "step": 0}
                for kv in range(2):
                    QT = QTlo if kv == 0 else QThi
                    qname = "QTlo" if kv == 0 else "QThi"
                    for gp in range(2):
                        def qk(kc):
                            b0 = 2 + 2 * (kc % 2)
                            for u in range(2):
                                g = gp * 2 + u
                                S.op("pe", lambda e, u=u, g=g: e.matmul(
                                    bank(b0 + u), lhsT=KTg[:, kc * 128:(kc + 1) * 128], rhs=QT[:, g, :], start=True, stop=True),
                                    reads=[("KTg", kc), (qname, g)], writes=[PB(b0), PB(b0 + 1)], sig=(u == 1))

                        def ex(kc):
                            b0 = 2 + 2 * (kc % 2)
                            S.op("act", lambda e: e.activation(
                                out=Pg[kc % 2][:, :], in_=ps[:, b0 * 512:(b0 + 2) * 512], func=AF.Exp, scale=0.125),
                                reads=[PB(b0), PB(b0 + 1)], writes=[("Pg", kc % 2)])

                        def pv(kc):
                            for u in range(2):
                                S.op("pe", lambda e, u=u: e.matmul(
                                    bank(u), lhsT=Vg[:, kc, kv, :], rhs=Pg[kc % 2][:, u * 512:(u + 1) * 512],
                                    start=(kc == 0), stop=(kc == 31)),
                                    reads=[("Vg", kc), "Vg1", ("Pg", kc % 2)], writes=[PB(u)])

                        qk(0)
                        for kc in range(32):
                            if kc + 1 < 32:
                                qk(kc + 1)
                            ex(kc)
                            state["step"] += 1
                            target = (nf * state["step"]) // 112
                            while state["emitted"] < target:
                                fl.step()
                                state["emitted"] += 1
                            pv(kc)
                        for u in range(2):
                            h = kv * 4 + gp * 2 + u
                            half, chk = h % 2, 4 + h // 2
                            S.op("act", lambda e, u=u: e.activation(out=rg[u][:, :], in_=bank(u)[64:128, :], func=AF.Ln),
                                 reads=[PB(u)], writes=[("rg", u)])
                            S.op("act", lambda e, u=u: e.activation(out=rg[u][:, :], in_=rg[u][:, :], func=AF.Exp, scale=-1.0),
                                 reads=[("rg", u)], writes=[("rg", u)])
                            S.op("dve", lambda e, u=u, half=half, chk=chk: e.tensor_tensor(
                                out=OT[half * 64:(half + 1) * 64, chk, :], in0=bank(u)[0:64, :], in1=rg[u][:, :], op=ALU.mult),
                                reads=[PB(u), ("rg", u)], writes=[("OT", chk, half)])
                fl.drain()

            def store_ot(tb):
                for c in range(8):
                    S.dma("pool", ot_s.ap()[c][:, tb * 512:(tb + 1) * 512], OT[:, c, :],
                          reads=[("OT", c, 0), ("OT", c, 1)], writes=[("ot_s", tb)])

            def run(L):
                for t in L:
                    r = t()
                    if hasattr(r, "__next__"):
                        for _ in r:
                            pass

            class Filler:
                def __init__(self, items):
                    self.items = list(items)
                    self.cur = None

                def step(self):
                    while True:
                        if self.cur is None:
                            if not self.items:
                                return False
                            r = self.items.pop(0)()
                            if hasattr(r, "__next__"):
                                self.cur = r
                            else:
                                return True
                        try:
                            next(self.cur)
                            return True
                        except StopIteration:
                            self.cur = None

                def drain(self):
                    while self.step():
                        pass

            order = [2, 3, 4, 5, 6, 7, 0, 1]
            hbufs = [(OT, "hTa"), (hT, "hT")]
            norm_block(order[0], gsa, sha, dst=hbufs[0][0], key=hbufs[0][1])
            for k, tbk in enumerate(order):
                hb, hk = hbufs[k % 2]
                if k + 1 < len(order):
                    nb_, nk_ = hbufs[(k + 1) % 2]
                    norm_block(order[k + 1], gsa, sha, dst=nb_, key=nk_)
                p1a_post(tbk, hb, hk)
                if tbk in (0, 1):
                    run(th_kv(tbk, hT=hb, hk=hk))
                if tbk == 0:
                    run(th_q(0, hT=hb, hk=hk))
                    finalize_q()
            S.barrier()
            if stop_after == "p1a":
                S.finish("sp")
                return nc
            for tb in range(NB):
                for _ in range(6):
                    if prep:
                        prep.pop(0)()
                na_block(tb)
                filler = []
                npre, npost = ([], [])
                if tb + 2 < NB:
                    npre, npost = th_norm(tb + 2)
                filler += npre
                if tb + 1 < NB:
                    filler += th_q(tb + 1)
                if tb + 2 < NB:
                    filler += npost + th_kv(tb + 2)
                gqa_block(tb, filler)
                store_ot(tb)
                if tb + 1 < NB:
                    finalize_q()
            if debug:
                S.barrier()
                S.dma("sp", dbg["d_ot"].ap(), ot_s.ap(), writes=["d_ot"])
            S.barrier()

        if stop_after == "p1":
            S.finish("sp")
            return nc

        with ExitStack() as s2:
            wo = sbt(s2, "wo", [128, 8, D], BF16)
            OTb = [sbt(s2, f"OTb{i}", [128, 8, 512], BF16) for i in range(2)]
            x1s = [[sbt(s2, f"x1_{p}_{i}", [128, D], F32) for i in range(4)] for p in range(2)]
            ytmp = [sbt(s2, f"ytmp{i}", [128, 512], F32) for i in range(2)]
            ytx = [sbt(s2, f"ytx{i}", [128, 512], F32) for i in range(2)]
            ss4c = sbt(s2, "ss4c", [128, 4], F32)
            rs4c = sbt(s2, "rs4c", [128, 4], F32)
            xn2 = [sbt(s2, f"xn2_{i}", [128, D], BF16) for i in range(4)]
            junk2 = sbt(s2, "junk2", [128, D], BF16)
            ss4b = sbt(s2, "ss4b", [128, 4], F32)
            rs4b = sbt(s2, "rs4b", [128, 4], F32)
            hT2s = [sbt(s2, f"hT2_{p}", [128, 8, 512], BF16) for p in range(2)]
            wgb = [sbt(s2, f"wgb{i}", [128, D], BF16) for i in range(3)]
            wub = [sbt(s2, f"wub{i}", [128, D], BF16) for i in range(3)]
            wdb = [sbt(s2, f"wdb{i}", [128, 2, 512], BF16) for i in range(4)]
            sg = [sbt(s2, f"sg{i}", [128, 512], F32) for i in range(2)]
            hid = sbt(s2, "hid", [128, NF, 512], BF16)
            ot = [sbt(s2, f"ot{i}", [128, D], F32) for i in range(2)]

            S.dma("pool", wo[:, :, :], wo_d.ap().rearrange("(c p) n -> p c n", p=128), writes=["wo"])
            gate_a = sbt(s2, "gate_a", [128, D], F32)
            gate_f = sbt(s2, "gate_f", [128, D], F32)
            gfin = sbt(s2, "gfin", [128, D], F32)
            dg = [sbt(s2, f"dg{i}", [128, 128], F32) for i in range(2)]
            S.dma("sp", gfin[:, :], gfin_d.ap(), writes=["gfin"])
            for gi, (off, gt, gname) in enumerate(((16, gate_a, "gate_a"), (40, gate_f, "gate_f"))):
                for j in range(8):
                    d = dg[j % 2]
                    S.op("dve", lambda e, d=d, off=off, j=j: e.tensor_scalar(
                        out=d[:, :], in0=identf[:, :], scalar1=modsb[:, off + j:off + j + 1], scalar2=None, op0=ALU.mult),
                        reads=["identf", "modsb"], writes=[("dg", j % 2)])
                    bk = 1 + (j // 4)
                    S.op("pe", lambda e, d=d, bk=bk, j=j: e.matmul(bank(bk, 128, (j % 4) * 128), lhsT=onesf[:, :], rhs=d[:, :],
                                                                 start=True, stop=True),
                         reads=["onesf", ("dg", j % 2)], writes=[PB(bk)])
                for hb in range(2):
                    S.op("act", lambda e, gt=gt, hb=hb: e.copy(out=gt[:, hb * 512:(hb + 1) * 512], in_=bank(1 + hb)),
                         reads=[PB(1 + hb)], writes=[(gname, hb)])
            x_v = x_d.ap().rearrange("(tt p) d -> tt p d", p=128)
            y_v = y_d.ap().rearrange("(tt p) d -> tt p d", p=128)
            n_ld = {"g": 0, "d": 0}
            n_out = [0]

            def x_thunks(tb):
                par = tb % 2
                ob = OTb[par]
                x1 = x1s[par]
                hT2 = hT2s[par]
                L = []

                def ldot():
                    for c in range(8):
                        S.dma("sp", ob[:, c, :], ot_s.ap()[c][:, tb * 512:(tb + 1) * 512], writes=[("OTb", par, c)])
                L.append(ldot)
                for i in range(4):
                    def ldx(i=i):
                        S.dma("sp", x1[i][:, :], x_v[tb * 4 + i], writes=[("x1", par, i, 0), ("x1", par, i, 1)])
                    L.append(ldx)
                for i in range(4):
                    for hf in range(2):
                        def opj(i=i, hf=hf):
                            bk = 4 + (i * 2 + hf) % 2
                            for c in range(8):
                                S.op("pe", lambda e, c=c: e.matmul(
                                    bank(bk), lhsT=ob[:, c, i * 128:(i + 1) * 128], rhs=wo[:, c, hf * 512:(hf + 1) * 512],
                                    start=(c == 0), stop=(c == 7)),
                                    reads=[("OTb", par, c), "wo"], writes=[PB(bk)], sig=(c == 7))
                            yt = ytx[hf]
                            S.op("dve", lambda e: e.tensor_tensor(
                                out=yt[:, :], in0=bank(bk), in1=gate_a[:, hf * 512:(hf + 1) * 512], op=ALU.mult),
                                reads=[PB(bk), ("gate_a", hf)], writes=[("ytx", hf)])
                            S.op("dve", lambda e: e.tensor_tensor(
                                out=x1[i][:, hf * 512:(hf + 1) * 512], in0=x1[i][:, hf * 512:(hf + 1) * 512], in1=yt[:, :], op=ALU.add),
                                reads=[("x1", par, i, hf), ("ytx", hf)], writes=[("x1", par, i, hf)])
                        L.append(opj)
                for i in range(4):
                    def nrm1(i=i):
                        S.op("act", lambda e: e.activation(out=xn2[i][:, :], in_=x1[i][:, :], func=AF.Square,
                                                           accum_out=ss4b[:, i:i + 1]),
                             reads=[("x1", par, i, 0), ("x1", par, i, 1)], writes=[("xn2", i), ("ss4b", i)])
                    L.append(nrm1)

                def nrmr():
                    S.op("dve", lambda e: e.tensor_scalar(out=rs4b[:, :], in0=ss4b[:, :], scalar1=1.0 / D,
                                                          scalar2=EPS, op0=ALU.mult, op1=ALU.add),
                         reads=[("ss4b", i) for i in range(4)], writes=["rs4b"])
                    S.op("act", lambda e: e.activation(out=rs4b[:, :], in_=rs4b[:, :], func=AF.Ln),
                         reads=["rs4b"], writes=["rs4b"])
                    S.op("act", lambda e: e.activation(out=rs4b[:, :], in_=rs4b[:, :], func=AF.Exp, scale=-0.5),
                         reads=["rs4b"], writes=["rs4b"])
                L.append(nrmr)
                for i in range(4):
                    def nrm2(i=i):
                        S.op("dve", lambda e: e.tensor_scalar(out=xn2[i][:, :], in0=x1[i][:, :], scalar1=rs4b[:, i:i + 1],
                                                              scalar2=None, op0=ALU.mult),
                             reads=[("x1", par, i, 0), ("x1", par, i, 1), "rs4b"], writes=[("xn2", i)])
                    L.append(nrm2)
                for bb in range(4):
                    def tr(bb=bb):
                        bk = 4 + bb
                        for cc in range(2):
                            c = 2 * bb + cc
                            for i in range(4):
                                S.op("pe", lambda e, c=c, cc=cc, i=i: e.transpose(
                                    out=bankbf(bk)[:, cc * 512 + i * 128:cc * 512 + (i + 1) * 128],
                                    in_=xn2[i][:, c * 128:(c + 1) * 128], identity=idt[:, :]),
                                    reads=[("xn2", i), "idt"], writes=[PB(bk)], sig=(i == 3))
                    L.append(tr)
                for bb in range(4):
                    def ev(bb=bb):
                        bk = 4 + bb
                        for cc in range(2):
                            c = 2 * bb + cc
                            S.op("act", lambda e, c=c, cc=cc: e.activation(
                                out=hT2[:, c, :], in_=bankbf(bk)[:, cc * 512:(cc + 1) * 512], func=AF.Identity,
                                scale=gsf[:, c:c + 1], bias=shf[:, c:c + 1]),
                                reads=[PB(bk), "gsf", "modsb"], writes=[("hT2", par, c)])
                    L.append(ev)
                return L

            def wd_dma(hf, gi):
                wi = n_ld["d"] % 4
                n_ld["d"] += 1
                S.dma("sp", wdb[wi][:, :, :], wd_s.ap()[hf, 2 * gi:2 * gi + 2].rearrange("f p n -> p f n"),
                      reads=[("wd_s", hf)], writes=[("wdb", wi)])
                return wi

            for t in x_thunks(0):
                t()
            for tb in range(NB):
                par = tb % 2
                x1 = x1s[par]
                hT2 = hT2s[par]
                filler = x_thunks(tb + 1) if tb + 1 < NB else []
                nfl = len(filler)
                emitted = 0
                pre_wd = []
                for f in range(NF):
                    wi = n_ld["g"] % 3
                    n_ld["g"] += 1
                    S.dma("sp", wgb[wi][:, :], wg_s.ap()[f], reads=[("wg_s", f)], writes=[("wgb", wi)])
                    S.dma("sp", wub[wi][:, :], wu_s.ap()[f], reads=[("wu_s", f)], writes=[("wub", wi)])
                    if f >= NF - 4:
                        pre_wd.append(wd_dma(0, len(pre_wd)))
                    bg = 2 * (f % 2)
                    bu = bg + 1
                    for c in range(8):
                        S.op("pe", lambda e, c=c, wi=wi, bg=bg: e.matmul(bank(bg), lhsT=wgb[wi][:, c * 128:(c + 1) * 128],
                                                                       rhs=hT2[:, c, :], start=(c == 0), stop=(c == 7)),
                             reads=[("wgb", wi), ("hT2", par, c)], writes=[PB(bg)], sig=(c == 7))
                    for c in range(8):
                        S.op("pe", lambda e, c=c, wi=wi, bu=bu: e.matmul(bank(bu), lhsT=wub[wi][:, c * 128:(c + 1) * 128],
                                                                       rhs=hT2[:, c, :], start=(c == 0), stop=(c == 7)),
                             reads=[("wub", wi), ("hT2", par, c)], writes=[PB(bu)], sig=(c == 7))
                    S.op("act", lambda e, f=f, bg=bg: e.activation(out=sg[f % 2][:, :], in_=bank(bg), func=AF.Silu),
                         reads=[PB(bg)], writes=[("sg", f % 2)])
                    S.op("dve", lambda e, f=f, bu=bu: e.tensor_tensor(out=hid[:, f, :], in0=bank(bu), in1=sg[f % 2][:, :], op=ALU.mult),
                         reads=[PB(bu), ("sg", f % 2)], writes=[("hid", f)])
                    target = min(nfl, (nfl * (f + 1)) // (NF - 2))
                    while emitted < target:
                        filler[emitted]()
                        emitted += 1
                while emitted < nfl:
                    filler[emitted]()
                    emitted += 1
                for hf in range(2):
                    for gi in range(NF // 2):
                        if hf == 0 and gi < len(pre_wd):
                            wi = pre_wd[gi]
                        else:
                            wi = wd_dma(hf, gi)
                        for ff in range(2):
                            f = 2 * gi + ff
                            for i in range(4):
                                S.op("pe", lambda e, f=f, ff=ff, i=i, wi=wi: e.matmul(
                                    bank(4 + i), lhsT=hid[:, f, i * 128:(i + 1) * 128], rhs=wdb[wi][:, ff, :],
                                    start=(f == 0), stop=(f == NF - 1)),
                                    reads=[("hid", f), ("wdb", wi)], writes=[PB(4 + i)], sig=(f == NF - 1 or i == 3))
                    for i in range(4):
                        yt = ytmp[i % 2]
                        S.op("dve", lambda e, i=i, hf=hf, yt=yt: e.tensor_tensor(
                            out=yt[:, :], in0=bank(4 + i), in1=gate_f[:, hf * 512:(hf + 1) * 512], op=ALU.mult),
                            reads=[PB(4 + i), ("gate_f", hf)], writes=[("ytmp", i % 2)])
                        S.op("dve", lambda e, i=i, hf=hf, yt=yt: e.tensor_tensor(
                            out=x1[i][:, hf * 512:(hf + 1) * 512], in0=x1[i][:, hf * 512:(hf + 1) * 512], in1=yt[:, :], op=ALU.add),
                            reads=[("x1", par, i, hf), ("ytmp", i % 2)], writes=[("x1", par, i, hf)])
                for i in range(4):
                    S.op("act", lambda e, i=i: e.activation(out=junk2[:, :], in_=x1[i][:, :], func=AF.Square,
                                                            accum_out=ss4c[:, i:i + 1]),
                         reads=[("x1", par, i, 0), ("x1", par, i, 1)], writes=["junk2", ("ss4c", i)])
                S.op("dve", lambda e: e.tensor_scalar(out=rs4c[:, :], in0=ss4c[:, :], scalar1=1.0 / D, scalar2=EPS,
                                                      op0=ALU.mult, op1=ALU.add),
                     reads=[("ss4c", i) for i in range(4)], writes=["rs4c"])
                S.op("act", lambda e: e.activation(out=rs4c[:, :], in_=rs4c[:, :], func=AF.Ln),
                     reads=["rs4c", "mh"], writes=["rs4c"])
                S.op("act", lambda e: e.activation(out=rs4c[:, :], in_=rs4c[:, :], func=AF.Exp, scale=-0.5),
                     reads=["rs4c", "mh"], writes=["rs4c"])
                for i in range(4):
                    oi = n_out[0] % 2
                    n_out[0] += 1
                    S.op("dve", lambda e, i=i, oi=oi: e.scalar_tensor_tensor(
                        out=ot[oi][:, :], in0=x1[i][:, :], scalar=rs4c[:, i:i + 1], in1=gfin[:, :], op0=ALU.mult, op1=ALU.mult),
                        reads=[("x1", par, i, 0), ("x1", par, i, 1), "rs4c", "gfin"], writes=[("ot", oi)])
                    S.dma("pool", y_v[tb * 4 + i], ot[oi][:, :], reads=[("ot", oi)], writes=[("y", tb * 4 + i)])
            S.barrier()
        S.finish("sp")
    return nc


_CACHE = {}


def _get_program():
    if "nc" not in _CACHE:
        _CACHE["nc"] = build_program()
    return _CACHE["nc"]


def make_in_maps(x, c, w_ada, b_ada, g_attn, w_in, g_q, g_k, rpb, w_o, g_ffn, w_gate, w_up, w_down, g_final):
    f32 = lambda a: np.ascontiguousarray(np.asarray(a, dtype=np.float32))
    colmajor = lambda v, n: f32(np.asarray(v, np.float32).reshape(n, 128).T)
    b_int, b_sp = build_bias_tiles(np.asarray(rpb)[0])
    sel = np.zeros((128, 128), np.float32)
    sel[64, :] = 1.0
    shared = {
        "w_ada": f32(np.asarray(w_ada)[0]),
        "bada": colmajor(np.asarray(b_ada)[0], 48),
        "gattn": colmajor(np.asarray(g_attn)[0], 8),
        "gffn": colmajor(np.asarray(g_ffn)[0], 8),
        "gfin": f32(np.broadcast_to(np.asarray(g_final, np.float32)[None, :], (128, D))),
        "w_in": f32(np.asarray(w_in)[0]),
        "gq": f32(np.broadcast_to(np.asarray(g_q, np.float32)[0][None, :], (128, 64))),
        "gk": f32(np.broadcast_to(np.asarray(g_k, np.float32)[0][None, :], (128, 64))),
        "bint": b_int,
        "bsp": b_sp,
        "w_o": f32(np.asarray(w_o)[0]),
        "w_gate": f32(np.asarray(w_gate)[0]),
        "w_up": f32(np.asarray(w_up)[0]),
        "w_down": f32(np.asarray(w_down)[0]),
        "ident": np.eye(128, dtype=np.float32).astype(ml_dtypes.bfloat16),
        "identf": np.eye(128, dtype=np.float32),
        "sel64": sel,
        "cs": rope_table(),
    }
    x = np.asarray(x, np.float32)
    c = np.asarray(c, np.float32)
    maps = []
    for b in range(x.shape[0]):
        m = dict(shared)
        m["x"] = np.ascontiguousarray(x[b])
        m["ccol"] = colmajor(c[b], 8)
        maps.append(m)
    return maps


def kernel(x, c, w_ada, b_ada, g_attn, w_in, g_q, g_k, rpb, w_o, g_ffn, w_gate, w_up, w_down, g_final):
    nc = _get_program()
    in_maps = make_in_maps(x, c, w_ada, b_ada, g_attn, w_in, g_q, g_k, rpb, w_o, g_ffn, w_gate, w_up, w_down, g_final)
    res = run_bass_kernel_spmd(nc, in_maps, core_ids=list(range(N_CORES)))
    out = np.stack([np.asarray(r["y"], dtype=np.float32) for r in res.results], axis=0)
    return out
```

```python
import numpy as np
import ml_dtypes
from contextlib import ExitStack
import concourse.bass as bass
import concourse.mybir as mybir
from concourse.bass_utils import run_bass_kernel_spmd

F32 = mybir.dt.float32
BF16 = mybir.dt.bfloat16
AF = mybir.ActivationFunctionType
ALU = mybir.AluOpType
AX = mybir.AxisListType

T = 4096
D = 1024
DFF = 2816
NF = DFF // 128
NB = 8
EPS = 1e-6
NEG = -30000.0
N_CORES = 8


class Sched:
    N_DMA_SEMS = 12

    def __init__(self, nc, stack):
        self.nc = nc
        self.eng = {"pe": nc.tensor, "act": nc.scalar, "dve": nc.vector,
                    "pool": nc.gpsimd, "sp": nc.sync}
        self.sem = {e: stack.enter_context(nc.semaphore("s_" + e)) for e in self.eng}
        self.cnt = {e: 0 for e in self.eng}
        self.dsem = {q: [stack.enter_context(nc.semaphore(f"d_{q}{i}")) for i in range(self.N_DMA_SEMS)]
                     for q in ("sp", "pool", "act")}
        self.dcnt = {q: [0] * self.N_DMA_SEMS for q in self.dsem}
        self.dnext = {q: 0 for q in self.dsem}
        self.dlast = {q: [None] * self.N_DMA_SEMS for q in self.dsem}
        self.seen = {e: {} for e in self.eng}
        self.res = {}
        self.nwaits = 0
        self.nops = {e: 0 for e in self.eng}

    def _wait(self, e, tok):
        sem, val, peng = tok
        key = sem.name
        if self.seen[e].get(key, 0) >= val:
            return
        self.eng[e].wait_ge(sem, val)
        self.seen[e][key] = val
        self.nwaits += 1

    def _deps(self, e, reads, writes):
        toks = []
        for r in reads:
            st = self.res.get(r)
            if st and st[0] is not None:
                toks.append((st[0], "raw"))
        for w in writes:
            st = self.res.get(w)
            if st:
                if st[0] is not None:
                    toks.append((st[0], "waw"))
                for t in st[1]:
                    toks.append((t, "war"))
        for tok, kind in toks:
            if tok[2] == e and e == "pe":
                continue
            self._wait(e, tok)

    def _commit(self, tok, reads, writes):
        for r in reads:
            st = self.res.setdefault(r, [None, []])
            st[1].append(tok)
            if len(st[1]) > 64:
                best = {}
                for t in st[1]:
                    k = t[0].name
                    if k not in best or best[k][1] < t[1]:
                        best[k] = t
                st[1] = list(best.values())
        for w in writes:
            self.res[w] = [tok, []]

    def op(self, e, fn, reads=(), writes=(), sig=True):
        pbr = [r for r in reads if isinstance(r, tuple) and r[0] == "pb"]
        if pbr:
            writes = list(writes) + [r for r in pbr if r not in writes]
            reads = [r for r in reads if not (isinstance(r, tuple) and r[0] == "pb")]
        self._deps(e, reads, writes)
        ins = fn(self.eng[e])
        self.nops[e] += 1
        if sig:
            self.cnt[e] += 1
            ins.then_inc(self.sem[e], 1)
            tok = (self.sem[e], self.cnt[e], e)
        else:
            tok = (self.sem[e], self.cnt[e] + 1, e)
        self._commit(tok, reads, writes)
        return ins

    def dma(self, q, out, in_, reads=(), writes=(), **kw):
        self._deps(q, reads, writes)
        i = self.dnext[q]
        self.dnext[q] = (i + 1) % self.N_DMA_SEMS
        prev = self.dlast[q][i]
        if prev is not None:
            self._wait(q, prev)
        self.dcnt[q][i] += 16
        ins = self.eng[q].dma_start(out=out, in_=in_, **kw)
        ins.then_inc(self.dsem[q][i], 16)
        tok = (self.dsem[q][i], self.dcnt[q][i], None)
        self.dlast[q][i] = tok
        self._commit(tok, reads, writes)
        return ins

    def all_tokens(self):
        toks = []
        for e in self.eng:
            if self.cnt[e] > 0:
                toks.append((self.sem[e], self.cnt[e], e))
        for q in self.dsem:
            for t in self.dlast[q]:
                if t is not None:
                    toks.append(t)
        return toks

    def barrier(self):
        toks = self.all_tokens()
        for e in self.eng:
            for t in toks:
                if t[2] == e:
                    continue
                self._wait(e, t)
        self.res = {}

    def finish(self, e="sp"):
        for t in self.all_tokens():
            if t[2] != e:
                self._wait(e, t)


def _rs(r):
    return min(max(r - 4, 0), 56)


def _cs(w):
    return min(max(w - 8, 0), 48)


def _chunk_rows(j):
    rows = [r for r in range(64) if any(_rs(r) <= ka < _rs(r) + 8 for ka in (2 * j, 2 * j + 1))]
    return rows[0], rows[-1]


SPECIAL = {2: 0, 3: 1, 28: 2, 29: 3}


def _tile_row0(j):
    return _chunk_rows(j)[0] if j in SPECIAL else 2 * j - 4


def _bias_block(rpb_h, ka, r):
    blk = np.full((64, 64), NEG, np.float32)
    if not (_rs(r) <= ka < _rs(r) + 8):
        return blk
    a = ka - r + 7
    for w in range(64):
        c0 = _cs(w)
        kc = np.arange(c0, c0 + 16)
        blk[kc, w] = rpb_h[a, kc - w + 15]
    return blk


def build_bias_tiles(rpb):
    rpb = np.asarray(rpb, np.float32)
    b_int = np.full((8, 128, 640), NEG, np.float32)
    b_sp = np.full((8, 4, 128, 768), NEG, np.float32)
    j0 = 10
    for h in range(8):
        for kdr in range(2):
            for qr in range(10):
                b_int[h, kdr * 64:(kdr + 1) * 64, qr * 64:(qr + 1) * 64] = \
                    _bias_block(rpb[h], 2 * j0 + kdr, 2 * j0 - 4 + qr)
        for j, si in SPECIAL.items():
            r0, r1 = _chunk_rows(j)
            for kdr in range(2):
                for r in range(r0, r1 + 1):
                    b_sp[h, si, kdr * 64:(kdr + 1) * 64, (r - r0) * 64:(r - r0 + 1) * 64] = \
                        _bias_block(rpb[h], 2 * j + kdr, r)
    return b_int, b_sp


def rope_table():
    t = np.arange(T)
    row = (t // 64).astype(np.float64)
    col = (t % 64).astype(np.float64)
    inv = 10000.0 ** (-np.arange(0, 32, 2, dtype=np.float64) / 32)
    ang = np.concatenate([row[:, None] * inv[None, :], col[:, None] * inv[None, :]], axis=-1)
    return np.concatenate([np.cos(ang), np.sin(ang)], axis=-1).astype(np.float32)


def build_program(stop_after=None, debug=False):
    nc = bass.Bass("TRN2", target_bir_lowering=False)
    dt_in = lambda name, shape, dt=F32: nc.dram_tensor(name, shape, dt, kind="ExternalInput")
    x_d = dt_in("x", [T, D])
    ccol_d = dt_in("ccol", [128, 8])
    wada_d = dt_in("w_ada", [D, 6 * D])
    bada_d = dt_in("bada", [128, 48])
    gattn_d = dt_in("gattn", [128, 8])
    gffn_d = dt_in("gffn", [128, 8])
    gfin_d = dt_in("gfin", [128, D])
    win_d = dt_in("w_in", [D, 2304])
    gq_d = dt_in("gq", [128, 64])
    gk_d = dt_in("gk", [128, 64])
    bint_d = dt_in("bint", [8, 128, 640])
    bsp_d = dt_in("bsp", [8, 4, 128, 768])
    wo_d = dt_in("w_o", [D, D])
    wg_d = dt_in("w_gate", [D, DFF])
    wu_d = dt_in("w_up", [D, DFF])
    wd_d = dt_in("w_down", [DFF, D])
    ident_d = dt_in("ident", [128, 128], BF16)
    identf_d = dt_in("identf", [128, 128])
    sel_d = dt_in("sel64", [128, 128])
    cs_d = dt_in("cs", [T, 64])
    y_d = nc.dram_tensor("y", [T, D], F32, kind="ExternalOutput")
    wg_s = nc.dram_tensor("wg_s", [NF, 128, D], BF16)
    wu_s = nc.dram_tensor("wu_s", [NF, 128, D], BF16)
    wd_s = nc.dram_tensor("wd_s", [2, NF, 128, 512], BF16)
    ot_s = nc.dram_tensor("ot_s", [8, 128, T], BF16)
    dbg = {}
    if debug:
        dbg["d_mod"] = nc.dram_tensor("d_mod", [128, 48], F32, kind="ExternalOutput")
        dbg["d_ot"] = nc.dram_tensor("d_ot", [8, 128, T], BF16, kind="ExternalOutput")

    with ExitStack() as st:
        S = Sched(nc, st)
        ps = st.enter_context(nc.psum_tensor("ps", [128, 8 * 512], F32))

        def bank(b, n=512, off=0):
            return ps[:, b * 512 + off: b * 512 + off + n]

        def bankbf(b):
            return ps[:, b * 512:(b + 1) * 512].bitcast(BF16)

        PB = lambda b: ("pb", b)

        def sbt(stack, name, shape, dt):
            return stack.enter_context(nc.sbuf_tensor("sb_" + name, shape, dt))

        idt = sbt(st, "idt", [128, 128], BF16)
        identf = sbt(st, "identf", [128, 128], F32)
        sel64 = sbt(st, "sel64", [128, 128], F32)
        onesf = sbt(st, "onesf", [128, 128], F32)
        mh = sbt(st, "mh", [128, 8], F32)
        modsb = sbt(st, "modsb", [128, 48], F32)
        gsa = sbt(st, "gsa", [128, 8], F32)
        gsf = sbt(st, "gsf", [128, 8], F32)
        rec = sbt(st, "rec", [128, 512], F32)
        bcsb = sbt(st, "bcsb", [64, 512], F32)

        S.dma("sp", idt[:, :], ident_d.ap(), writes=["idt"])
        S.dma("sp", identf[:, :], identf_d.ap(), writes=["identf"])
        S.dma("sp", sel64[:, :], sel_d.ap(), writes=["sel64"])
        S.op("dve", lambda e: e.memset(onesf[:, :], 1.0), writes=["onesf"])
        S.op("dve", lambda e: e.memset(mh[:, :], -0.5), writes=["mh"])
        S.op("dve", lambda e: e.memset(rec[:, :], 0.0), writes=["rec"])

        prep = []
        for f in range(NF):
            prep.append(lambda f=f: S.dma("pool", wg_s.ap()[f].rearrange("p (c j) -> p c j", c=8),
                                          wg_d.ap()[:, f * 128:(f + 1) * 128].rearrange("(c p) j -> p c j", p=128),
                                          writes=[("wg_s", f)]))
            prep.append(lambda f=f: S.dma("pool", wu_s.ap()[f].rearrange("p (c j) -> p c j", c=8),
                                          wu_d.ap()[:, f * 128:(f + 1) * 128].rearrange("(c p) j -> p c j", p=128),
                                          writes=[("wu_s", f)]))
        for hf in range(2):
            prep.append(lambda hf=hf: S.dma("pool", wd_s.ap()[hf].rearrange("f p n -> p f n"),
                                            wd_d.ap()[:, hf * 512:(hf + 1) * 512].rearrange("(f p) n -> p f n", p=128),
                                            writes=[("wd_s", hf)]))

        ccol = sbt(st, "ccol", [128, 8], F32)
        ctmp = sbt(st, "ctmp", [128, 8], F32)
        cact = sbt(st, "cact", [128, 8], BF16)
        bada = sbt(st, "bada", [128, 48], F32)
        gat = sbt(st, "gat", [128, 8], F32)
        gff = sbt(st, "gff", [128, 8], F32)
        S.dma("sp", ccol[:, :], ccol_d.ap(), writes=["ccol"])
        S.dma("sp", bada[:, :], bada_d.ap(), writes=["bada"])
        S.dma("sp", gat[:, :], gattn_d.ap(), writes=["gat"])
        S.dma("sp", gff[:, :], gffn_d.ap(), writes=["gff"])
        S.op("act", lambda e: e.activation(out=ctmp[:, :], in_=ccol[:, :], func=AF.Exp, scale=-1.0),
             reads=["ccol"], writes=["ctmp"])
        S.op("dve", lambda e: e.tensor_scalar(out=ctmp[:, :], in0=ctmp[:, :], scalar1=1.0, scalar2=None, op0=ALU.add),
             reads=["ctmp"], writes=["ctmp"])
        S.op("dve", lambda e: e.reciprocal(out=ctmp[:, :], in_=ctmp[:, :]), reads=["ctmp"], writes=["ctmp"])
        S.op("dve", lambda e: e.tensor_tensor(out=cact[:, :], in0=ctmp[:, :], in1=ccol[:, :], op=ALU.mult),
             reads=["ctmp", "ccol"], writes=["cact"])
        wada_v = wada_d.ap().rearrange("(k p) n -> p k n", p=128)
        sha = modsb[:, 0:8]
        shf = modsb[:, 24:32]

        with ExitStack() as s1:
            win = sbt(s1, "win", [128, 8, 2304], BF16)
            cs = sbt(s1, "cs", [128, 32, 64], F32)
            gqb = sbt(s1, "gqb", [128, 64], F32)
            gkb = sbt(s1, "gkb", [128, 64], F32)
            KTg = sbt(s1, "KTg", [128, T], BF16)
            Vg = sbt(s1, "Vg", [128, 32, 2, 128], BF16)
            bint = sbt(s1, "bint", [128, 8, 640], F32)
            bsp = [sbt(s1, f"bsp{i}", [128, 768], F32) for i in range(3)]
            kTn = sbt(s1, "kTn", [128, 3, 4, 512], BF16)
            Vn = sbt(s1, "Vn", [128, 12, 8, 65], BF16)
            qTlo = sbt(s1, "qTlo", [128, 4, 512], BF16)
            qThi = sbt(s1, "qThi", [128, 4, 512], BF16)
            QTlo = sbt(s1, "QTlo", [128, 4, 512], BF16)
            QThi = sbt(s1, "QThi", [128, 4, 512], BF16)
            xt = [sbt(s1, f"xt{i}", [128, D], F32) for i in range(2)]
            xn = [sbt(s1, f"xn{i}", [128, D], BF16) for i in range(4)]
            ss4 = sbt(s1, "ss4", [128, 4], F32)
            rs4 = sbt(s1, "rs4", [128, 4], F32)
            hT = sbt(s1, "hT", [128, 8, 512], BF16)
            qsb = [sbt(s1, f"qsb{i}", [128, 512], F32) for i in range(2)]
            qtmp = sbt(s1, "qtmp", [128, 512], F32)
            ssq = sbt(s1, "ssq", [128, 8], F32)
            rq = sbt(s1, "rq", [128, 8], F32)
            rt1 = sbt(s1, "rt1", [128, 8, 32], F32)
            rt2 = sbt(s1, "rt2", [128, 8, 32], F32)
            qhat = sbt(s1, "qhat", [128, 4, 512], BF16)
            khat = sbt(s1, "khat", [128, 4, 128], BF16)
            Ssb = [sbt(s1, f"Ssb{i}", [128, 512], F32) for i in range(3)]
            Pna = [sbt(s1, f"Pna{i}", [128, 512], BF16) for i in range(3)]
            Pg = [sbt(s1, f"Pg{i}", [128, 1024], BF16) for i in range(2)]
            OT = sbt(s1, "OT", [128, 8, 512], BF16)
            rg = [sbt(s1, f"rg{i}", [64, 512], F32) for i in range(2)]

            wa = [kTn[:, 0:2, :, :].rearrange("p a m n -> p (a m) n"),
                  Vn.reshape([128, 12 * 8 * 65])[:, 0:4096].rearrange("p (k n) -> p k n", k=8)]

            def ada_dma(g):
                S.dma("pool", wa[g % 2], wada_v[:, :, g * 512:(g + 1) * 512], writes=[("wa", g % 2)])

            def ada_mm(g):
                wb = wa[g % 2]
                for jj in range(4):
                    j = g * 4 + jj
                    for k in range(8):
                        S.op("pe", lambda e, j=j, jj=jj, k=k: e.matmul(
                            bank(7, 1, j), lhsT=wb[:, k, jj * 128:(jj + 1) * 128], rhs=cact[:, k:k + 1],
                            start=(k == 0), stop=(k == 7), skip_group_check=True),
                            reads=[("wa", g % 2), "cact"], writes=[PB(7)], sig=(k == 7))

            ada_dma(0)
            ada_dma(1)
            S.dma("pool", win[:, :, :], win_d.ap().rearrange("(c p) n -> p c n", p=128), writes=["win"])
            S.dma("sp", cs[:, :, :], cs_d.ap().rearrange("(tt p) k -> p tt k", p=128), writes=["cs"])
            S.dma("sp", gqb[:, :], gq_d.ap(), writes=["gqb"])
            S.dma("sp", gkb[:, :], gk_d.ap(), writes=["gkb"])
            S.dma("sp", bint[:, :, :], bint_d.ap().rearrange("h p n -> p h n"), writes=["bint"])
            S.op("dve", lambda e: e.memset(Vg[:, :, :, 64:128], 1.0), writes=["Vg1"])
            for tns, nm in ((qTlo, "qTlo0"), (qThi, "qThi0"), (QTlo, "QTlo0"), (QThi, "QThi0")):
                S.op("dve", lambda e, tns=tns: e.memset(tns[:, :, :], 0.0), writes=[nm])
            zero_deps = {"qTlo": "qTlo0", "qThi": "qThi0", "QTlo": "QTlo0", "QThi": "QThi0"}

            x_v = x_d.ap().rearrange("(tt p) d -> tt p d", p=128)

            def norm_block(tb, gs, sh, dst=None, key="hT"):
                dst = hT if dst is None else dst
                for i in range(4):
                    xb = xt[i % 2]
                    S.dma("sp", xb[:, :], x_v[tb * 4 + i], writes=[("xt", i % 2)])
                    S.op("act", lambda e, i=i, xb=xb: e.activation(out=xn[i][:, :], in_=xb[:, :], func=AF.Square,
                                                                   accum_out=ss4[:, i:i + 1]),
                         reads=[("xt", i % 2)], writes=[("xn", i), ("ss4", i)])
                    S.op("dve", lambda e, i=i: e.tensor_scalar(out=rs4[:, i:i + 1], in0=ss4[:, i:i + 1], scalar1=1.0 / D,
                                                               scalar2=EPS, op0=ALU.mult, op1=ALU.add),
                         reads=[("ss4", i)], writes=[("rs4", i)])
                    S.op("act", lambda e, i=i: e.activation(out=rs4[:, i:i + 1], in_=rs4[:, i:i + 1], func=AF.Ln),
                         reads=[("rs4", i), "mh"], writes=[("rs4", i)])
                    S.op("act", lambda e, i=i: e.activation(out=rs4[:, i:i + 1], in_=rs4[:, i:i + 1], func=AF.Exp, scale=-0.5),
                         reads=[("rs4", i), "mh"], writes=[("rs4", i)])
                    S.op("dve", lambda e, i=i, xb=xb: e.tensor_scalar(out=xn[i][:, :], in0=xb[:, :], scalar1=rs4[:, i:i + 1],
                                                                      scalar2=None, op0=ALU.mult),
                         reads=[("xt", i % 2), ("rs4", i)], writes=[("xn", i)])
                for c in range(8):
                    bk = c // 2
                    for i in range(4):
                        S.op("pe", lambda e, c=c, i=i, bk=bk: e.transpose(
                            out=bankbf(bk)[:, (c % 2) * 512 + i * 128:(c % 2) * 512 + (i + 1) * 128],
                            in_=xn[i][:, c * 128:(c + 1) * 128], identity=idt[:, :]),
                            reads=[("xn", i), "idt"], writes=[PB(bk)], sig=(i == 3))
                for c in range(8):
                    bk = c // 2
                    S.op("act", lambda e, c=c, bk=bk: e.activation(
                        out=dst[:, c, :], in_=bankbf(bk)[:, (c % 2) * 512:(c % 2 + 1) * 512], func=AF.Identity,
                        scale=gs[:, c:c + 1], bias=sh[:, c:c + 1]),
                        reads=[PB(bk), "gsa", "modsb"], writes=[(key, c)])

            def qk_post(src_ap, nheads, gb, gname, tt, dst_fn, dst_keys, sb_i, grp=None):
                H = nheads
                W = H * 64
                v3 = lambda ap: ap.rearrange("p (h d) -> p h d", d=64)
                src3 = v3(src_ap)
                tmp3 = v3(qtmp[:, 0:W])
                S.op("dve", lambda e: e.tensor_tensor(out=qtmp[:, 0:W], in0=src_ap, in1=src_ap, op=ALU.mult),
                     reads=[("qsb", sb_i)], writes=["qtmp"])
                S.op("dve", lambda e: e.tensor_reduce(out=ssq[:, 0:H], in_=tmp3, axis=AX.X, op=ALU.add),
                     reads=["qtmp"], writes=["ssq"])
                S.op("dve", lambda e: e.tensor_scalar(out=rq[:, 0:H], in0=ssq[:, 0:H], scalar1=1.0 / 64, scalar2=EPS,
                                                      op0=ALU.mult, op1=ALU.add), reads=["ssq"], writes=["rq"])
                yield
                yield
                S.op("act", lambda e: e.activation(out=rq[:, 0:H], in_=rq[:, 0:H], func=AF.Ln),
                     reads=["rq", "mh"], writes=["rq"])
                S.op("act", lambda e: e.activation(out=rq[:, 0:H], in_=rq[:, 0:H], func=AF.Exp, scale=-0.5),
                     reads=["rq", "mh"], writes=["rq"])
                yield
                yield
                S.op("dve", lambda e: e.tensor_tensor(out=tmp3, in0=src3, in1=rq[:, 0:H].unsqueeze(2).to_broadcast([128, H, 64]),
                                                      op=ALU.mult), reads=[("qsb", sb_i), "rq"], writes=["qtmp"])
                S.op("dve", lambda e: e.tensor_tensor(out=tmp3, in0=tmp3, in1=gb[:, :].unsqueeze(1).to_broadcast([128, H, 64]),
                                                      op=ALU.mult), reads=["qtmp", gname], writes=["qtmp"])
                x1 = tmp3[:, :, 0:32]
                x2 = tmp3[:, :, 32:64]
                t1 = rt1[:, 0:H, :]
                t2 = rt2[:, 0:H, :]
                if grp is None:
                    cosb = cs[:, tt, 0:32].unsqueeze(1).to_broadcast([128, H, 32])
                    sinb = cs[:, tt, 32:64].unsqueeze(1).to_broadcast([128, H, 32])
                else:
                    A, B = grp
                    f4 = lambda ap: ap.rearrange("p (a b) d -> p a b d", a=A)
                    x1, x2, t1, t2 = f4(x1), f4(x2), f4(t1), f4(t2)
                    cosb = cs[:, tt:tt + A, 0:32].unsqueeze(2).to_broadcast([128, A, B, 32])
                    sinb = cs[:, tt:tt + A, 32:64].unsqueeze(2).to_broadcast([128, A, B, 32])
                    _dst = dst_fn
                    dst_fn = lambda half: f4(_dst(half))
                S.op("dve", lambda e: e.tensor_tensor(out=t1, in0=x1, in1=cosb, op=ALU.mult), reads=["qtmp", "cs"], writes=["rt1"])
                S.op("dve", lambda e: e.tensor_tensor(out=t2, in0=x2, in1=sinb, op=ALU.mult), reads=["qtmp", "cs"], writes=["rt2"])
                S.op("dve", lambda e: e.tensor_tensor(out=dst_fn(0), in0=t1, in1=t2, op=ALU.subtract),
                     reads=["rt1", "rt2"], writes=dst_keys)
                S.op("dve", lambda e: e.tensor_tensor(out=t1, in0=x1, in1=sinb, op=ALU.mult), reads=["qtmp", "cs"], writes=["rt1"])
                S.op("dve", lambda e: e.tensor_tensor(out=t2, in0=x2, in1=cosb, op=ALU.mult), reads=["qtmp", "cs"], writes=["rt2"])
                S.op("dve", lambda e: e.tensor_tensor(out=dst_fn(1), in0=t1, in1=t2, op=ALU.add),
                     reads=["rt1", "rt2"], writes=dst_keys)

            def p1a_post(tb, hb, hk):
                for i in range(4):
                    bk = 4 + i // 2
                    for c in range(8):
                        S.op("pe", lambda e, c=c, i=i, bk=bk: e.matmul(bank(bk, 256, (i % 2) * 256), lhsT=hb[:, c, i * 128:(i + 1) * 128],
                                                                     rhs=win[:, c, 2048:2304], start=(c == 0), stop=(c == 7),
                                                                     skip_group_check=True),
                             reads=[(hk, c), "win"], writes=[PB(bk)], sig=(c == 7))
                for bb in range(2):
                    src = bank(4 + bb).rearrange("p (t x) -> p t x", t=2)
                    S.op("act", lambda e, bb=bb, src=src: e.copy(
                        out=qsb[0][:, bb * 256:(bb + 1) * 256].rearrange("p (t x) -> p t x", t=2), in_=src[:, :, 0:128]),
                        reads=[PB(4 + bb)], writes=[("qsb", 0)])
                    tt0 = tb * 4 + 2 * bb
                    S.op("dve", lambda e, tt0=tt0, src=src: e.tensor_copy(
                        out=Vg[:, tt0:tt0 + 2, :, 0:64], in_=src[:, :, 128:256].rearrange("p t (h d) -> p t h d", d=64)),
                        reads=[PB(4 + bb)], writes=[("Vg", tt0), ("Vg", tt0 + 1)])
                kh3 = khat[:, :, :].rearrange("p t (h d) -> p (t h) d", d=64)
                for _ in qk_post(qsb[0][:, :], 8, gkb, "gkb", tb * 4,
                                 lambda half, kh3=kh3: kh3[:, :, half * 32:(half + 1) * 32], ["khat"], 0, grp=(4, 2)):
                    pass
                for i in range(4):
                    S.op("pe", lambda e, i=i: e.transpose(out=bankbf(6)[:, i * 128:(i + 1) * 128], in_=khat[:, i, :], identity=idt[:, :]),
                         reads=["khat", "idt"], writes=[PB(6)], sig=(i == 3))
                S.op("act", lambda e, tb=tb: e.copy(out=KTg[:, tb * 512:(tb + 1) * 512], in_=bankbf(6)[:, 0:512]),
                     reads=[PB(6)], writes=[("KTg", tb * 4 + i) for i in range(4)])

            def stage_a(tb, part="all"):
                slot = tb % 3
                if part in ("all", "kv"):
                    norm_block(tb, gsa, sha)
                n_mm = 0
                for m in range(4):
                    for which in range(2):
                        if which == 0 and part == "kv":
                            continue
                        if which == 1 and part == "q":
                            continue
                        bk = 4 + (n_mm % 2)
                        n_mm += 1
                        col0 = which * 512 + m * 128
                        for c in range(8):
                            S.op("pe", lambda e, c=c, bk=bk, col0=col0: e.matmul(
                                bank(bk), lhsT=win[:, c, col0:col0 + 128], rhs=hT[:, c, :], start=(c == 0), stop=(c == 7)),
                                reads=[("hT", c), "win"], writes=[PB(bk)], sig=(c == 7))
                        if which == 0:
                            S.op("act", lambda e, bk=bk, m=m: e.copy(out=qTlo[0:64, m, :], in_=bank(bk)[0:64, :]),
                                 reads=[PB(bk), "qTlo0"], writes=[("qTlo", m)])
                            S.op("dve", lambda e, bk=bk, m=m: e.tensor_copy(out=qThi[64:128, m, :], in_=bank(bk)[64:128, :]),
                                 reads=[PB(bk), "qThi0"], writes=[("qThi", m)])
                        else:
                            S.op("act", lambda e, bk=bk, m=m: e.copy(out=kTn[:, slot, m, :], in_=bank(bk)),
                                 reads=[PB(bk)], writes=[("kTn", slot, m)])
                for i in range(4 if part in ("all", "kv") else 0):
                    tt = tb * 4 + i
                    bk = 4 + (i % 2)
                    for c in range(8):
                        S.op("pe", lambda e, c=c, i=i, bk=bk: e.matmul(bank(bk), lhsT=hT[:, c, i * 128:(i + 1) * 128],
                                                                     rhs=win[:, c, 1024:1536], start=(c == 0), stop=(c == 7)),
                             reads=[("hT", c), "win"], writes=[PB(bk)], sig=(c == 7))
                    S.op("act", lambda e, bk=bk, i=i: e.copy(
                        out=Vn[:, slot * 4 + i, :, 0:64], in_=bank(bk).rearrange("p (h d) -> p h d", d=64)),
                        reads=[PB(bk), "Vn1"], writes=[("Vn", slot * 4 + i)])
                if part == "kv":
                    return
                for i in range(4):
                    tt = tb * 4 + i
                    bk = 6 + (i % 2)
                    for c in range(8):
                        S.op("pe", lambda e, c=c, i=i, bk=bk: e.matmul(bank(bk), lhsT=hT[:, c, i * 128:(i + 1) * 128],
                                                                     rhs=win[:, c, 1536:2048], start=(c == 0), stop=(c == 7)),
                             reads=[("hT", c), "win"], writes=[PB(bk)], sig=(c == 7))
                    sbi = i % 2
                    S.op("act", lambda e, bk=bk, sbi=sbi: e.copy(
                        out=qsb[sbi][:, :].rearrange("p (g kv d) -> p g kv d", g=4, kv=2),
                        in_=bank(bk).rearrange("p (kv g d) -> p g kv d", g=4, kv=2)),
                        reads=[PB(bk)], writes=[("qsb", sbi)])
                    qh3 = qhat[:, i, :].rearrange("p (h d) -> p h d", d=64)
                    for _ in qk_post(qsb[sbi][:, :], 8, gqb, "gqb", tt,
                                     lambda half, qh3=qh3: qh3[:, :, half * 32:(half + 1) * 32], [("qhat", i)], sbi):
                        pass
                for g in range(4):
                    bk = 4 + g // 2
                    for i in range(4):
                        S.op("pe", lambda e, g=g, i=i, bk=bk: e.transpose(
                            out=bankbf(bk)[:, (g % 2) * 512 + i * 128:(g % 2) * 512 + (i + 1) * 128],
                            in_=qhat[:, i, g * 128:(g + 1) * 128], identity=idt[:, :]),
                            reads=[("qhat", i), "idt"], writes=[PB(bk)], sig=(i == 3))
                for g in range(4):
                    bk = 4 + g // 2
                    src = bankbf(bk)[:, (g % 2) * 512:(g % 2 + 1) * 512]
                    S.op("act", lambda e, g=g, src=src: e.copy(out=QTlo[0:64, g, :], in_=src[0:64, :]),
                         reads=[PB(bk), "QTlo0"], writes=[("QTlo", g)])
                    S.op("dve", lambda e, g=g, src=src: e.tensor_copy(out=QThi[64:128, g, :], in_=src[64:128, :]),
                         reads=[PB(bk), "QThi0"], writes=[("QThi", g)])

            def normalize_head(acc_bk, h_chunk, half, bc_bk):
                S.op("dve", lambda e: e.reciprocal(out=rec[64:65, :], in_=bank(acc_bk)[64:65, :]),
                     reads=[PB(acc_bk)], writes=["rec"])
                S.op("pe", lambda e: e.matmul(bank(bc_bk)[0:128, :], lhsT=sel64[:, :], rhs=rec[:, :], start=True, stop=True),
                     reads=["sel64", "rec"], writes=[PB(bc_bk)])
                S.op("act", lambda e: e.copy(out=bcsb[:, :], in_=bank(bc_bk)[0:64, :]), reads=[PB(bc_bk)], writes=["bcsb"])
                S.op("dve", lambda e: e.tensor_tensor(out=OT[half * 64:(half + 1) * 64, h_chunk, :], in0=bank(acc_bk)[0:64, :],
                                                      in1=bcsb[:, :], op=ALU.mult),
                     reads=[PB(acc_bk), "bcsb"], writes=[("OT", h_chunk, half)])

            sp_loaded = {}
            sp_next = [0]

            def na_block(tb):
                steps = []
                for h in range(8):
                    items = []
                    for j in range(max(0, 4 * tb - 2), min(31, 4 * tb + 5) + 1):
                        r0, r1 = _chunk_rows(j)
                        lo, hi = max(8 * tb, r0), min(8 * tb + 7, r1)
                        if lo <= hi:
                            items.append((j, lo, hi))
                    for idx, (j, lo, hi) in enumerate(items):
                        steps.append(dict(h=h, j=j, lo=lo, hi=hi, first=(idx == 0), last=(idx == len(items) - 1)))
                n = len(steps)
                for i, stp in enumerate(steps):
                    stp["sbk"] = 3 + (i % 3)
                    stp["bi"] = i % 3
                    stp["nq"] = (stp["hi"] - stp["lo"] + 1) * 64
                    stp["qc0"] = (stp["lo"] - 8 * tb) * 64

                def qk(i):
                    p = steps[i]
                    h, j = p["h"], p["j"]
                    m, half = h // 2, h % 2
                    qT = qTlo if half == 0 else qThi
                    qkey = ("qTlo", m) if half == 0 else ("qThi", m)
                    kslot, kcol = (j // 4) % 3, (j % 4) * 128
                    nq, qc0, sbk = p["nq"], p["qc0"], p["sbk"]
                    S.op("pe", lambda e: e.matmul(bank(sbk, nq), lhsT=kTn[:, kslot, m, kcol:kcol + 128],
                                                  rhs=qT[:, m, qc0:qc0 + nq], start=True, stop=True),
                         reads=[("kTn", kslot, m), qkey], writes=[PB(sbk)])

                def bias_exp(i):
                    p = steps[i]
                    h, j, lo = p["h"], p["j"], p["lo"]
                    nq, sbk, bi = p["nq"], p["sbk"], p["bi"]
                    boff = (lo - _tile_row0(j)) * 64
                    if j in SPECIAL:
                        key = (h, j)
                        if key not in sp_loaded:
                            si = sp_next[0] % 3
                            sp_next[0] += 1
                            S.dma("sp", bsp[si][:, :], bsp_d.ap()[h, SPECIAL[j]], writes=[("bsp", si)])
                            for k2 in [k for k, v in sp_loaded.items() if v == si]:
                                del sp_loaded[k2]
                            sp_loaded[key] = si
                        si = sp_loaded[key]
                        b_ap = bsp[si][:, boff:boff + nq]
                        bkey = ("bsp", si)
                    else:
                        b_ap = bint[:, h, boff:boff + nq]
                        bkey = "bint"
                    S.op("dve", lambda e: e.scalar_tensor_tensor(
                        out=Ssb[bi][:, 0:nq], in0=bank(sbk, nq), scalar=0.125, in1=b_ap, op0=ALU.mult, op1=ALU.add),
                        reads=[PB(sbk), bkey], writes=[("Ssb", bi)])
                    S.op("act", lambda e: e.activation(out=Pna[bi][:, 0:nq], in_=Ssb[bi][:, 0:nq], func=AF.Exp),
                         reads=[("Ssb", bi)], writes=[("Pna", bi)])

                def pv(i):
                    p = steps[i]
                    h, j = p["h"], p["j"]
                    vt = ((j // 4) % 3) * 4 + (j % 4)
                    nq, qc0, bi = p["nq"], p["qc0"], p["bi"]
                    acc_bk = 6 + (h % 2)
                    S.op("pe", lambda e: e.matmul(bank(acc_bk)[0:65, qc0:qc0 + nq], lhsT=Vn[:, vt, h, 0:65],
                                                  rhs=Pna[bi][:, 0:nq], start=p["first"], stop=p["last"],
                                                  skip_group_check=True),
                         reads=[("Vn", vt), "Vn1", ("Pna", bi)], writes=[PB(acc_bk)])

                deferred = []

                def sched_normalize(i, h):
                    acc_bk, m, half = 6 + (h % 2), h // 2, h % 2
                    def recip_row():
                        S.op("act", lambda e: e.activation(out=rec[64:65, :], in_=bank(acc_bk)[64:65, :], func=AF.Ln),
                             reads=[PB(acc_bk)], writes=["rec"])
                        S.op("act", lambda e: e.activation(out=rec[64:65, :], in_=rec[64:65, :], func=AF.Exp, scale=-1.0),
                             reads=["rec"], writes=["rec"])
                    deferred.append((i + 1, recip_row))
                    deferred.append((i + 2, lambda: S.op(
                        "pe", lambda e: e.matmul(bank(2)[0:128, :], lhsT=sel64[:, :], rhs=rec[:, :], start=True, stop=True),
                        reads=["sel64", "rec"], writes=[PB(2)])))
                    deferred.append((i + 3, lambda: S.op(
                        "act", lambda e: e.copy(out=bcsb[:, :], in_=bank(2)[0:64, :]), reads=[PB(2)], writes=["bcsb"])))
                    deferred.append((i + 4, lambda: S.op(
                        "dve", lambda e: e.tensor_tensor(out=OT[half * 64:(half + 1) * 64, m, :], in0=bank(acc_bk)[0:64, :],
                                                         in1=bcsb[:, :], op=ALU.mult),
                        reads=[PB(acc_bk), "bcsb"], writes=[("OT", m, half)])))

                def run_deferred(i):
                    rest = []
                    for at, th in deferred:
                        if at <= i:
                            th()
                        else:
                            rest.append((at, th))
                    deferred[:] = rest

                qk(0)
                if n > 1:
                    qk(1)
                for i in range(n):
                    bias_exp(i)
                    if i + 2 < n:
                        qk(i + 2)
                    run_deferred(i)
                    pv(i)
                    if steps[i]["last"]:
                        sched_normalize(i, steps[i]["h"])
                run_deferred(10 ** 9)

            fb = [0]

            def fbank():
                fb[0] += 1
                return 6 + (fb[0] % 2)

            def th_norm(tb):
                L = []

                def dm(i):
                    S.dma("sp", xt[i % 2][:, :], x_v[tb * 4 + i], writes=[("xt", i % 2)])

                def sq(i):
                    xb = xt[i % 2]
                    S.op("act", lambda e: e.activation(out=xn[i][:, :], in_=xb[:, :], func=AF.Square,
                                                       accum_out=ss4[:, i:i + 1]),
                         reads=[("xt", i % 2)], writes=[("xn", i), ("ss4", i)])

                def ms(i):
                    S.op("dve", lambda e: e.tensor_scalar(out=rs4[:, i:i + 1], in0=ss4[:, i:i + 1], scalar1=1.0 / D,
                                                          scalar2=EPS, op0=ALU.mult, op1=ALU.add),
                         reads=[("ss4", i)], writes=[("rs4", i)])

                def le(i):
                    S.op("act", lambda e: e.activation(out=rs4[:, i:i + 1], in_=rs4[:, i:i + 1], func=AF.Ln),
                         reads=[("rs4", i)], writes=[("rs4", i)])
                    S.op("act", lambda e: e.activation(out=rs4[:, i:i + 1], in_=rs4[:, i:i + 1], func=AF.Exp, scale=-0.5),
                         reads=[("rs4", i)], writes=[("rs4", i)])

                def scl(i):
                    xb = xt[i % 2]
                    S.op("dve", lambda e: e.tensor_scalar(out=xn[i][:, :], in0=xb[:, :], scalar1=rs4[:, i:i + 1],
                                                          scalar2=None, op0=ALU.mult),
                         reads=[("xt", i % 2), ("rs4", i)], writes=[("xn", i)])

                seq = [(dm, 0), (dm, 1), None, None, (sq, 0), (ms, 0), (sq, 1), (le, 0), (ms, 1), (scl, 0), (le, 1), (dm, 2),
                       (scl, 1), (dm, 3), None, None, (sq, 2), (ms, 2), (sq, 3), (le, 2), (ms, 3), (scl, 2), (le, 3), (scl, 3)]
                for it in seq:
                    if it is None:
                        L.append(lambda: None)
                    else:
                        L.append(lambda fn=it[0], i=it[1]: fn(i))

                def tr(r, bb):
                    bk = 6 + bb
                    for cc in range(2):
                        c = 4 * r + 2 * bb + cc
                        for i in range(4):
                            S.op("pe", lambda e, c=c, cc=cc, i=i: e.transpose(
                                out=bankbf(bk)[:, cc * 512 + i * 128:cc * 512 + (i + 1) * 128],
                                in_=xn[i][:, c * 128:(c + 1) * 128], identity=idt[:, :]),
                                reads=[("xn", i), "idt"], writes=[PB(bk)], sig=(i == 3))
                            if i % 2 == 1:
                                yield

                def ev(r, bb):
                    bk = 6 + bb
                    for cc in range(2):
                        c = 4 * r + 2 * bb + cc
                        S.op("dve", lambda e, c=c, cc=cc: e.tensor_scalar(
                            out=hT[:, c, :], in0=bankbf(bk)[:, cc * 512:(cc + 1) * 512],
                            scalar1=gsa[:, c:c + 1], scalar2=sha[:, c:c + 1], op0=ALU.mult, op1=ALU.add),
                            reads=[PB(bk), "gsa", "modsb"], writes=[("hT", c)])

                L2 = []
                for r in range(2):
                    for bb in range(2):
                        trf = (lambda r=r, bb=bb: tr(r, bb))
                        trf.units = 5
                        L2.append(trf)
                    for bb in range(2):
                        L2.append(lambda r=r, bb=bb: ev(r, bb))
                return L, L2

            def proj_group(lhs_fn, rhs_fn, bk, n=512, hk="hT"):
                for c in range(8):
                    S.op("pe", lambda e, c=c: e.matmul(bank(bk, n), lhsT=lhs_fn(c), rhs=rhs_fn(c), start=(c == 0), stop=(c == 7)),
                         reads=[(hk, c), "win"], writes=[PB(bk)], sig=(c == 7))
                    if c % 2 == 1:
                        yield

            def th_kv(tb, hT=hT, hk="hT"):
                slot = tb % 3
                L = []
                for m in range(4):
                    def pk(m=m):
                        bk = fbank()
                        col0 = 512 + m * 128
                        yield from proj_group(lambda c: win[:, c, col0:col0 + 128], lambda c: hT[:, c, :], bk, hk=hk)
                        S.op("dve", lambda e: e.tensor_copy(out=kTn[:, slot, m, :], in_=bank(bk)),
                             reads=[PB(bk)], writes=[("kTn", slot, m)])
                    pk.units = 5
                    L.append(pk)
                for i in range(4):
                    def pvv(i=i):
                        bk = fbank()
                        yield from proj_group(lambda c: hT[:, c, i * 128:(i + 1) * 128], lambda c: win[:, c, 1024:1536], bk, hk=hk)
                        S.op("dve", lambda e: e.tensor_copy(
                            out=Vn[:, slot * 4 + i, :, 0:64], in_=bank(bk).rearrange("p (h d) -> p h d", d=64)),
                            reads=[PB(bk), "Vn1"], writes=[("Vn", slot * 4 + i)])
                    pvv.units = 5
                    L.append(pvv)
                return L

            def th_q(tb, hT=hT, hk="hT"):
                L = []
                for m in range(4):
                    def pq(m=m):
                        bk = fbank()
                        col0 = m * 128
                        yield from proj_group(lambda c: win[:, c, col0:col0 + 128], lambda c: hT[:, c, :], bk, hk=hk)
                        S.op("dve", lambda e: e.tensor_copy(out=qTlo[0:64, m, :], in_=bank(bk)[0:64, :]),
                             reads=[PB(bk), "qTlo0"], writes=[("qTlo", m)])
                        S.op("dve", lambda e: e.tensor_copy(out=qThi[64:128, m, :], in_=bank(bk)[64:128, :]),
                             reads=[PB(bk), "qThi0"], writes=[("qThi", m)])
                    pq.units = 5
                    L.append(pq)
                for i in range(4):
                    tt = tb * 4 + i
                    sbi = i % 2

                    def pg(i=i, sbi=sbi):
                        bk = fbank()
                        yield from proj_group(lambda c: hT[:, c, i * 128:(i + 1) * 128], lambda c: win[:, c, 1536:2048], bk, hk=hk)
                        S.op("dve", lambda e: e.tensor_copy(
                            out=qsb[sbi][:, :].rearrange("p (g kv d) -> p g kv d", g=4, kv=2),
                            in_=bank(bk).rearrange("p (kv g d) -> p g kv d", g=4, kv=2)),
                            reads=[PB(bk)], writes=[("qsb", sbi)])
                    pg.units = 5
                    L.append(pg)

                    def post(i=i, sbi=sbi, tt=tt):
                        qh3 = qhat[:, i, :].rearrange("p (h d) -> p h d", d=64)
                        yield from qk_post(qsb[sbi][:, :], 8, gqb, "gqb", tt,
                                           lambda half: qh3[:, :, half * 32:(half + 1) * 32], [("qhat", i)], sbi)
                    post.units = 5
                    L.append(post)
                return L

            def finalize_q():
                for g in range(4):
                    bk = g // 2
                    for i in range(4):
                        S.op("pe", lambda e, g=g, i=i, bk=bk: e.transpose(
                            out=bankbf(bk)[:, (g % 2) * 512 + i * 128:(g % 2) * 512 + (i + 1) * 128],
                            in_=qhat[:, i, g * 128:(g + 1) * 128], identity=idt[:, :]),
                            reads=[("qhat", i), "idt"], writes=[PB(bk)], sig=(i == 3))
                for g in range(4):
                    bk = g // 2
                    src = bankbf(bk)[:, (g % 2) * 512:(g % 2 + 1) * 512]
                    S.op("dve", lambda e, g=g, src=src: e.tensor_copy(out=QTlo[0:64, g, :], in_=src[0:64, :]),
                         reads=[PB(bk), "QTlo0"], writes=[("QTlo", g)])
                    S.op("dve", lambda e, g=g, src=src: e.tensor_copy(out=QThi[64:128, g, :], in_=src[64:128, :]),
                         reads=[PB(bk), "QThi0"], writes=[("QThi", g)])

            def gqa_block(tb, filler):
                nf = sum(getattr(t, "units", 1) for t in filler)
                fl = Filler(filler)
                state = {"emitted": 0, "step": 0}
                for kv in range(2):
                    QT = QTlo if kv == 0 else QThi
                    qname = "QTlo" if kv == 0 else "QThi"
                    for gp in range(2):
                        def qk(kc):
                            b0 = 2 + 2 * (kc % 2)
                            for u in range(2):
                                g = gp * 2 + u
                                S.op("pe", lambda e, u=u, g=g: e.matmul(
                                    bank(b0 + u), lhsT=KTg[:, kc * 128:(kc + 1) * 128], rhs=QT[:, g, :], start=True, stop=True),
                                    reads=[("KTg", kc), (qname, g)], writes=[PB(b0), PB(b0 + 1)], sig=(u == 1))

                        def ex(kc):
                            b0 = 2 + 2 * (kc % 2)
                            S.op("act", lambda e: e.activation(
                                out=Pg[kc % 2][:, :], in_=ps[:, b0 * 512:(b0 + 2) * 512], func=AF.Exp, scale=0.125),
                                reads=[PB(b0), PB(b0 + 1)], writes=[("Pg", kc % 2)])

                        def pv(kc):
                            for u in range(2):
                                S.op("pe", lambda e, u=u: e.matmul(
                                    bank(u), lhsT=Vg[:, kc, kv, :], rhs=Pg[kc % 2][:, u * 512:(u + 1) * 512],
                                    start=(kc == 0), stop=(kc == 31)),
                                    reads=[("Vg", kc), "Vg1", ("Pg", kc % 2)], writes=[PB(u)])

                        qk(0)
                        for kc in range(32):
                            if kc + 1 < 32:
                                qk(kc + 1)
                            ex(kc)
                            state["step"] += 1
                            target = (nf * state["step"]) // 112
                            while state["emitted"] < target:
                                fl.step()
                                state["emitted"] += 1
                            pv(kc)
                        for u in range(2):
                            h = kv * 4 + gp * 2 + u
                            half, chk = h % 2, 4 + h // 2
                            S.op("act", lambda e, u=u: e.activation(out=rg[u][:, :], in_=bank(u)[64:128, :], func=AF.Ln),
                                 reads=[PB(u)], writes=[("rg", u)])
                            S.op("act", lambda e, u=u: e.activation(out=rg[u][:, :], in_=rg[u][:, :], func=AF.Exp, scale=-1.0),
                                 reads=[("rg", u)], writes=[("rg", u)])
                            S.op("dve", lambda e, u=u, half=half, chk=chk: e.tensor_tensor(
                                out=OT[half * 64:(half + 1) * 64, chk, :], in0=bank(u)[0:64, :], in1=rg[u][:, :], op=ALU.mult),
                                reads=[PB(u), ("rg", u)], writes=[("OT", chk, half)])
                fl.drain()

            def store_ot(tb):
                for c in range(8):
                    S.dma("pool", ot_s.ap()[c][:, tb * 512:(tb + 1) * 512], OT[:, c, :],
                          reads=[("OT", c, 0), ("OT", c, 1)], writes=[("ot_s", tb)])

            def run(L):
                for t in L:
                    r = t()
                    if hasattr(r, "__next__"):
                        for _ in r:
                            pass

            class Filler:
                def __init__(self, items):
                    self.items = list(items)
                    self.cur = None

                def step(self):
                    while True:
                        if self.cur is None:
                            if not self.items:
                                return False
                            r = self.items.pop(0)()
                            if hasattr(r, "__next__"):
                                self.cur = r
                            else:
                                return True
                        try:
                            next(self.cur)
                            return True
                        except StopIteration:
                            self.cur = None

                def drain(self):
                    while self.step():
                        pass

            order = [2, 3, 4, 5, 6, 7, 0, 1]
            hbufs = [(OT, "hTa"), (hT, "hT")]
            for g in range(4):
                ada_mm(g)
                ada_dma(g + 2)
            S.op("dve", lambda e: e.tensor_tensor(out=modsb[:, 0:16], in0=bank(7, 16), in1=bada[:, 0:16], op=ALU.add),
                 reads=[PB(7), "bada"], writes=["modsb"])
            S.op("dve", lambda e: e.scalar_tensor_tensor(out=gsa[:, :], in0=modsb[:, 8:16], scalar=1.0, in1=gat[:, :],
                                                         op0=ALU.add, op1=ALU.mult),
                 reads=["modsb", "gat"], writes=["gsa"])
            ada_plan = {0: [4, 5], 1: [6, 7], 2: [8], 3: [9], 4: [10], 5: [11]}
            norm_block(order[0], gsa, sha, dst=hbufs[0][0], key=hbufs[0][1])
            for k, tbk in enumerate(order):
                hb, hk = hbufs[k % 2]
                if k + 1 < len(order):
                    nb_, nk_ = hbufs[(k + 1) % 2]
                    norm_block(order[k + 1], gsa, sha, dst=nb_, key=nk_)
                p1a_post(tbk, hb, hk)
                for g in ada_plan.get(k, []):
                    ada_mm(g)
                    if g + 2 < 12:
                        ada_dma(g + 2)
                if k == 5:
                    S.op("dve", lambda e: e.tensor_tensor(out=modsb[:, 16:48], in0=bank(7, 32, 16), in1=bada[:, 16:48], op=ALU.add),
                         reads=[PB(7), "bada"], writes=["modsb2"])
                    S.op("dve", lambda e: e.scalar_tensor_tensor(out=gsf[:, :], in0=modsb[:, 32:40], scalar=1.0, in1=gff[:, :],
                                                                 op0=ALU.add, op1=ALU.mult),
                         reads=["modsb2", "gff"], writes=["gsf"])
                    if debug:
                        S.dma("sp", dbg["d_mod"].ap(), modsb[:, :], reads=["modsb", "modsb2"], writes=["d_mod"])
                    S.op("dve", lambda e: e.memset(Vn[:, :, :, 64:65], 1.0), writes=["Vn1", ("wa", 0), ("wa", 1)])
                if tbk in (0, 1):
                    run(th_kv(tbk, hT=hb, hk=hk))
                if tbk == 0:
                    run(th_q(0, hT=hb, hk=hk))
                    finalize_q()
            S.barrier()
            if stop_after == "p1a":
                S.finish("sp")
                return nc
            for tb in range(NB):
                for _ in range(6):
                    if prep:
                        prep.pop(0)()
                na_block(tb)
                filler = []
                npre, npost = ([], [])
                if tb + 2 < NB:
                    npre, npost = th_norm(tb + 2)
                filler += npre
                if tb + 1 < NB:
                    filler += th_q(tb + 1)
                if tb + 2 < NB:
                    filler += npost + th_kv(tb + 2)
                gqa_block(tb, filler)
                store_ot(tb)
                if tb + 1 < NB:
                    finalize_q()
            if debug:
                S.barrier()
                S.dma("sp", dbg["d_ot"].ap(), ot_s.ap(), writes=["d_ot"])
            S.barrier()

        if stop_after == "p1":
            S.finish("sp")
            return nc

        with ExitStack() as s2:
            wo = sbt(s2, "wo", [128, 8, D], BF16)
            OTb = [sbt(s2, f"OTb{i}", [128, 8, 512], BF16) for i in range(2)]
            x1s = [[sbt(s2, f"x1_{p}_{i}", [128, D], F32) for i in range(4)] for p in range(2)]
            ytmp = [sbt(s2, f"ytmp{i}", [128, 512], F32) for i in range(2)]
            ytx = [sbt(s2, f"ytx{i}", [128, 512], F32) for i in range(2)]
            ss4c = sbt(s2, "ss4c", [128, 4], F32)
            rs4c = sbt(s2, "rs4c", [128, 4], F32)
            xn2 = [sbt(s2, f"xn2_{i}", [128, D], BF16) for i in range(4)]
            junk2 = sbt(s2, "junk2", [128, D], BF16)
            ss4b = sbt(s2, "ss4b", [128, 4], F32)
            rs4b = sbt(s2, "rs4b", [128, 4], F32)
            hT2s = [sbt(s2, f"hT2_{p}", [128, 8, 512], BF16) for p in range(2)]
            wgb = [sbt(s2, f"wgb{i}", [128, D], BF16) for i in range(3)]
            wub = [sbt(s2, f"wub{i}", [128, D], BF16) for i in range(3)]
            wdb = [sbt(s2, f"wdb{i}", [128, 2, 512], BF16) for i in range(4)]
            sg = [sbt(s2, f"sg{i}", [128, 512], F32) for i in range(2)]
            hid = sbt(s2, "hid", [128, NF, 512], BF16)
            ot = [sbt(s2, f"ot{i}", [128, D], F32) for i in range(2)]

            S.dma("pool", wo[:, :, :], wo_d.ap().rearrange("(c p) n -> p c n", p=128), writes=["wo"])
            gate_a = sbt(s2, "gate_a", [128, D], F32)
            gate_f = sbt(s2, "gate_f", [128, D], F32)
            gfin = sbt(s2, "gfin", [128, D], F32)
            dg = [sbt(s2, f"dg{i}", [128, 128], F32) for i in range(2)]
            S.dma("sp", gfin[:, :], gfin_d.ap(), writes=["gfin"])
            for gi, (off, gt, gname) in enumerate(((16, gate_a, "gate_a"), (40, gate_f, "gate_f"))):
                for j in range(8):
                    d = dg[j % 2]
                    S.op("dve", lambda e, d=d, off=off, j=j: e.tensor_scalar(
                        out=d[:, :], in0=identf[:, :], scalar1=modsb[:, off + j:off + j + 1], scalar2=None, op0=ALU.mult),
                        reads=["identf", "modsb"], writes=[("dg", j % 2)])
                    bk = 1 + (j // 4)
                    S.op("pe", lambda e, d=d, bk=bk, j=j: e.matmul(bank(bk, 128, (j % 4) * 128), lhsT=onesf[:, :], rhs=d[:, :],
                                                                 start=True, stop=True),
                         reads=["onesf", ("dg", j % 2)], writes=[PB(bk)])
                for hb in range(2):
                    S.op("act", lambda e, gt=gt, hb=hb: e.copy(out=gt[:, hb * 512:(hb + 1) * 512], in_=bank(1 + hb)),
                         reads=[PB(1 + hb)], writes=[(gname, hb)])
            x_v = x_d.ap().rearrange("(tt p) d -> tt p d", p=128)
            y_v = y_d.ap().rearrange("(tt p) d -> tt p d", p=128)
            n_ld = {"g": 0, "d": 0}
            n_out = [0]

            def x_thunks(tb):
                par = tb % 2
                ob = OTb[par]
                x1 = x1s[par]
                hT2 = hT2s[par]
                L = []

                def ldot():
                    for c in range(8):
                        S.dma("sp", ob[:, c, :], ot_s.ap()[c][:, tb * 512:(tb + 1) * 512], writes=[("OTb", par, c)])
                L.append(ldot)
                for i in range(4):
                    def ldx(i=i):
                        S.dma("sp", x1[i][:, :], x_v[tb * 4 + i], writes=[("x1", par, i, 0), ("x1", par, i, 1)])
                    L.append(ldx)
                for i in range(4):
                    for hf in range(2):
                        def opj(i=i, hf=hf):
                            bk = 4 + (i * 2 + hf) % 2
                            for c in range(8):
                                S.op("pe", lambda e, c=c: e.matmul(
                                    bank(bk), lhsT=ob[:, c, i * 128:(i + 1) * 128], rhs=wo[:, c, hf * 512:(hf + 1) * 512],
                                    start=(c == 0), stop=(c == 7)),
                                    reads=[("OTb", par, c), "wo"], writes=[PB(bk)], sig=(c == 7))
                            yt = ytx[hf]
                            S.op("dve", lambda e: e.tensor_tensor(
                                out=yt[:, :], in0=bank(bk), in1=gate_a[:, hf * 512:(hf + 1) * 512], op=ALU.mult),
                                reads=[PB(bk), ("gate_a", hf)], writes=[("ytx", hf)])
                            S.op("dve", lambda e: e.tensor_tensor(
                                out=x1[i][:, hf * 512:(hf + 1) * 512], in0=x1[i][:, hf * 512:(hf + 1) * 512], in1=yt[:, :], op=ALU.add),
                                reads=[("x1", par, i, hf), ("ytx", hf)], writes=[("x1", par, i, hf)])
                        L.append(opj)
                for i in range(4):
                    def nrm1(i=i):
                        S.op("act", lambda e: e.activation(out=xn2[i][:, :], in_=x1[i][:, :], func=AF.Square,
                                                           accum_out=ss4b[:, i:i + 1]),
                             reads=[("x1", par, i, 0), ("x1", par, i, 1)], writes=[("xn2", i), ("ss4b", i)])
                    L.append(nrm1)

                def nrmr():
                    S.op("dve", lambda e: e.tensor_scalar(out=rs4b[:, :], in0=ss4b[:, :], scalar1=1.0 / D,
                                                          scalar2=EPS, op0=ALU.mult, op1=ALU.add),
                         reads=[("ss4b", i) for i in range(4)], writes=["rs4b"])
                    S.op("act", lambda e: e.activation(out=rs4b[:, :], in_=rs4b[:, :], func=AF.Ln),
                         reads=["rs4b"], writes=["rs4b"])
                    S.op("act", lambda e: e.activation(out=rs4b[:, :], in_=rs4b[:, :], func=AF.Exp, scale=-0.5),
                         reads=["rs4b"], writes=["rs4b"])
                L.append(nrmr)
                for i in range(4):
                    def nrm2(i=i):
                        S.op("dve", lambda e: e.tensor_scalar(out=xn2[i][:, :], in0=x1[i][:, :], scalar1=rs4b[:, i:i + 1],
                                                              scalar2=None, op0=ALU.mult),
                             reads=[("x1", par, i, 0), ("x1", par, i, 1), "rs4b"], writes=[("xn2", i)])
                    L.append(nrm2)
                for bb in range(4):
                    def tr(bb=bb):
                        bk = 4 + bb
                        for cc in range(2):
                            c = 2 * bb + cc
                            for i in range(4):
                                S.op("pe", lambda e, c=c, cc=cc, i=i: e.transpose(
                                    out=bankbf(bk)[:, cc * 512 + i * 128:cc * 512 + (i + 1) * 128],
                                    in_=xn2[i][:, c * 128:(c + 1) * 128], identity=idt[:, :]),
                                    reads=[("xn2", i), "idt"], writes=[PB(bk)], sig=(i == 3))
                    L.append(tr)
                for bb in range(4):
                    def ev(bb=bb):
                        bk = 4 + bb
                        for cc in range(2):
                            c = 2 * bb + cc
                            S.op("act", lambda e, c=c, cc=cc: e.activation(
                                out=hT2[:, c, :], in_=bankbf(bk)[:, cc * 512:(cc + 1) * 512], func=AF.Identity,
                                scale=gsf[:, c:c + 1], bias=shf[:, c:c + 1]),
                                reads=[PB(bk), "gsf", "modsb"], writes=[("hT2", par, c)])
                    L.append(ev)
                return L

            pre_g = []

            def gu_dma(f):
                wi = n_ld["g"] % 3
                n_ld["g"] += 1
                S.dma("sp", wgb[wi][:, :], wg_s.ap()[f], reads=[("wg_s", f)], writes=[("wgb", wi)])
                S.dma("sp", wub[wi][:, :], wu_s.ap()[f], reads=[("wu_s", f)], writes=[("wub", wi)])
                return wi

            def wd_dma(hf, gi):
                wi = n_ld["d"] % 4
                n_ld["d"] += 1
                S.dma("sp", wdb[wi][:, :, :], wd_s.ap()[hf, 2 * gi:2 * gi + 2].rearrange("f p n -> p f n"),
                      reads=[("wd_s", hf)], writes=[("wdb", wi)])
                return wi

            for t in x_thunks(0):
                t()
            for tb in range(NB):
                par = tb % 2
                x1 = x1s[par]
                hT2 = hT2s[par]
                filler = x_thunks(tb + 1) if tb + 1 < NB else []
                nfl = len(filler)
                emitted = 0
                pre_wd = []
                for f in range(NF):
                    if pre_g:
                        wi = pre_g.pop(0)
                    else:
                        wi = gu_dma(f)
                    if f >= NF - 4:
                        pre_wd.append(wd_dma(0, len(pre_wd)))
                    bg = 2 * (f % 2)
                    bu = bg + 1
                    for c in range(8):
                        S.op("pe", lambda e, c=c, wi=wi, bg=bg: e.matmul(bank(bg), lhsT=wgb[wi][:, c * 128:(c + 1) * 128],
                                                                       rhs=hT2[:, c, :], start=(c == 0), stop=(c == 7)),
                             reads=[("wgb", wi), ("hT2", par, c)], writes=[PB(bg)], sig=(c == 7))
                    for c in range(8):
                        S.op("pe", lambda e, c=c, wi=wi, bu=bu: e.matmul(bank(bu), lhsT=wub[wi][:, c * 128:(c + 1) * 128],
                                                                       rhs=hT2[:, c, :], start=(c == 0), stop=(c == 7)),
                             reads=[("wub", wi), ("hT2", par, c)], writes=[PB(bu)], sig=(c == 7))
                    S.op("act", lambda e, f=f, bg=bg: e.activation(out=sg[f % 2][:, :], in_=bank(bg), func=AF.Silu),
                         reads=[PB(bg)], writes=[("sg", f % 2)])
                    S.op("dve", lambda e, f=f, bu=bu: e.tensor_tensor(out=hid[:, f, :], in0=bank(bu), in1=sg[f % 2][:, :], op=ALU.mult),
                         reads=[PB(bu), ("sg", f % 2)], writes=[("hid", f)])
                    target = min(nfl, (nfl * (f + 1)) // (NF - 2))
                    while emitted < target:
                        filler[emitted]()
                        emitted += 1
                while emitted < nfl:
                    filler[emitted]()
                    emitted += 1
                if tb + 1 < NB:
                    for f2 in range(3):
                        pre_g.append(gu_dma(f2))
                for hf in range(2):
                    dbk = 4 if hf == 0 else 0
                    for gi in range(NF // 2):
                        if hf == 0 and gi < len(pre_wd):
                            wi = pre_wd[gi]
                        else:
                            wi = wd_dma(hf, gi)
                        for ff in range(2):
                            f = 2 * gi + ff
                            for i in range(4):
                                S.op("pe", lambda e, f=f, ff=ff, i=i, wi=wi: e.matmul(
                                    bank(dbk + i), lhsT=hid[:, f, i * 128:(i + 1) * 128], rhs=wdb[wi][:, ff, :],
                                    start=(f == 0), stop=(f == NF - 1)),
                                    reads=[("hid", f), ("wdb", wi)], writes=[PB(dbk + i)], sig=(f == NF - 1 or i == 3))
                    for i in range(4):
                        yt = ytmp[i % 2]
                        S.op("dve", lambda e, i=i, hf=hf, yt=yt: e.tensor_tensor(
                            out=yt[:, :], in0=bank(dbk + i), in1=gate_f[:, hf * 512:(hf + 1) * 512], op=ALU.mult),
                            reads=[PB(dbk + i), ("gate_f", hf)], writes=[("ytmp", i % 2)])
                        S.op("dve", lambda e, i=i, hf=hf, yt=yt: e.tensor_tensor(
                            out=x1[i][:, hf * 512:(hf + 1) * 512], in0=x1[i][:, hf * 512:(hf + 1) * 512], in1=yt[:, :], op=ALU.add),
                            reads=[("x1", par, i, hf), ("ytmp", i % 2)], writes=[("x1", par, i, hf)])
                for i in range(4):
                    S.op("act", lambda e, i=i: e.activation(out=junk2[:, :], in_=x1[i][:, :], func=AF.Square,
                                                            accum_out=ss4c[:, i:i + 1]),
                         reads=[("x1", par, i, 0), ("x1", par, i, 1)], writes=["junk2", ("ss4c", i)])
                S.op("dve", lambda e: e.tensor_scalar(out=rs4c[:, :], in0=ss4c[:, :], scalar1=1.0 / D, scalar2=EPS,
                                                      op0=ALU.mult, op1=ALU.add),
                     reads=[("ss4c", i) for i in range(4)], writes=["rs4c"])
                S.op("act", lambda e: e.activation(out=rs4c[:, :], in_=rs4c[:, :], func=AF.Ln),
                     reads=["rs4c", "mh"], writes=["rs4c"])
                S.op("act", lambda e: e.activation(out=rs4c[:, :], in_=rs4c[:, :], func=AF.Exp, scale=-0.5),
                     reads=["rs4c", "mh"], writes=["rs4c"])
                for i in range(4):
                    oi = n_out[0] % 2
                    n_out[0] += 1
                    S.op("dve", lambda e, i=i, oi=oi: e.scalar_tensor_tensor(
                        out=ot[oi][:, :], in0=x1[i][:, :], scalar=rs4c[:, i:i + 1], in1=gfin[:, :], op0=ALU.mult, op1=ALU.mult),
                        reads=[("x1", par, i, 0), ("x1", par, i, 1), "rs4c", "gfin"], writes=[("ot", oi)])
                    S.dma("pool", y_v[tb * 4 + i], ot[oi][:, :], reads=[("ot", oi)], writes=[("y", tb * 4 + i)])
            S.barrier()
        S.finish("sp")
    return nc


_CACHE = {}


def _get_program():
    if "nc" not in _CACHE:
        _CACHE["nc"] = build_program()
    return _CACHE["nc"]


def make_in_maps(x, c, w_ada, b_ada, g_attn, w_in, g_q, g_k, rpb, w_o, g_ffn, w_gate, w_up, w_down, g_final):
    f32 = lambda a: np.ascontiguousarray(np.asarray(a, dtype=np.float32))
    colmajor = lambda v, n: f32(np.asarray(v, np.float32).reshape(n, 128).T)
    b_int, b_sp = build_bias_tiles(np.asarray(rpb)[0])
    sel = np.zeros((128, 128), np.float32)
    sel[64, :] = 1.0
    shared = {
        "w_ada": f32(np.asarray(w_ada)[0]),
        "bada": colmajor(np.asarray(b_ada)[0], 48),
        "gattn": colmajor(np.asarray(g_attn)[0], 8),
        "gffn": colmajor(np.asarray(g_ffn)[0], 8),
        "gfin": f32(np.broadcast_to(np.asarray(g_final, np.float32)[None, :], (128, D))),
        "w_in": f32(np.asarray(w_in)[0]),
        "gq": f32(np.broadcast_to(np.asarray(g_q, np.float32)[0][None, :], (128, 64))),
        "gk": f32(np.broadcast_to(np.asarray(g_k, np.float32)[0][None, :], (128, 64))),
        "bint": b_int,
        "bsp": b_sp,
        "w_o": f32(np.asarray(w_o)[0]),
        "w_gate": f32(np.asarray(w_gate)[0]),
        "w_up": f32(np.asarray(w_up)[0]),
        "w_down": f32(np.asarray(w_down)[0]),
        "ident": np.eye(128, dtype=np.float32).astype(ml_dtypes.bfloat16),
        "identf": np.eye(128, dtype=np.float32),
        "sel64": sel,
        "cs": rope_table(),
    }
    x = np.asarray(x, np.float32)
    c = np.asarray(c, np.float32)
    maps = []
    for b in range(x.shape[0]):
        m = dict(shared)
        m["x"] = np.ascontiguousarray(x[b])
        m["ccol"] = colmajor(c[b], 8)
        maps.append(m)
    return maps


def kernel(x, c, w_ada, b_ada, g_attn, w_in, g_q, g_k, rpb, w_o, g_ffn, w_gate, w_up, w_down, g_final):
    nc = _get_program()
    in_maps = make_in_maps(x, c, w_ada, b_ada, g_attn, w_in, g_q, g_k, rpb, w_o, g_ffn, w_gate, w_up, w_down, g_final)
    res = run_bass_kernel_spmd(nc, in_maps, core_ids=list(range(N_CORES)))
    out = np.stack([np.asarray(r["y"], dtype=np.float32) for r in res.results], axis=0)
    return out
```

```python
import numpy as np
import ml_dtypes
from contextlib import ExitStack
import concourse.bass as bass
import concourse.mybir as mybir
from concourse.bass_utils import run_bass_kernel_spmd

F32 = mybir.dt.float32
BF16 = mybir.dt.bfloat16
AF = mybir.ActivationFunctionType
ALU = mybir.AluOpType
AX = mybir.AxisListType

T = 4096
D = 1024
DFF = 2816
NF = DFF // 128
NB = 8
EPS = 1e-6
NEG = -30000.0
N_CORES = 8


class Sched:
    N_DMA_SEMS = 12

    def __init__(self, nc, stack):
        self.nc = nc
        self.eng = {"pe": nc.tensor, "act": nc.scalar, "dve": nc.vector,
                    "pool": nc.gpsimd, "sp": nc.sync}
        self.sem = {e: stack.enter_context(nc.semaphore("s_" + e)) for e in self.eng}
        self.cnt = {e: 0 for e in self.eng}
        self.dsem = {q: [stack.enter_context(nc.semaphore(f"d_{q}{i}")) for i in range(self.N_DMA_SEMS)]
                     for q in ("sp", "pool", "act")}
        self.dcnt = {q: [0] * self.N_DMA_SEMS for q in self.dsem}
        self.dnext = {q: 0 for q in self.dsem}
        self.dlast = {q: [None] * self.N_DMA_SEMS for q in self.dsem}
        self.seen = {e: {} for e in self.eng}
        self.res = {}
        self.nwaits = 0
        self.nops = {e: 0 for e in self.eng}

    def _wait(self, e, tok):
        sem, val, peng = tok
        key = sem.name
        if self.seen[e].get(key, 0) >= val:
            return
        self.eng[e].wait_ge(sem, val)
        self.seen[e][key] = val
        self.nwaits += 1

    def _deps(self, e, reads, writes):
        toks = []
        for r in reads:
            st = self.res.get(r)
            if st and st[0] is not None:
                toks.append((st[0], "raw"))
        for w in writes:
            st = self.res.get(w)
            if st:
                if st[0] is not None:
                    toks.append((st[0], "waw"))
                for t in st[1]:
                    toks.append((t, "war"))
        for tok, kind in toks:
            if tok[2] == e and e == "pe":
                continue
            self._wait(e, tok)

    def _commit(self, tok, reads, writes):
        for r in reads:
            st = self.res.setdefault(r, [None, []])
            st[1].append(tok)
            if len(st[1]) > 64:
                best = {}
                for t in st[1]:
                    k = t[0].name
                    if k not in best or best[k][1] < t[1]:
                        best[k] = t
                st[1] = list(best.values())
        for w in writes:
            self.res[w] = [tok, []]

    def op(self, e, fn, reads=(), writes=(), sig=True):
        pbr = [r for r in reads if isinstance(r, tuple) and r[0] == "pb"]
        if pbr:
            writes = list(writes) + [r for r in pbr if r not in writes]
            reads = [r for r in reads if not (isinstance(r, tuple) and r[0] == "pb")]
        self._deps(e, reads, writes)
        ins = fn(self.eng[e])
        self.nops[e] += 1
        if sig:
            self.cnt[e] += 1
            ins.then_inc(self.sem[e], 1)
            tok = (self.sem[e], self.cnt[e], e)
        else:
            tok = (self.sem[e], self.cnt[e] + 1, e)
        self._commit(tok, reads, writes)
        return ins

    def dma(self, q, out, in_, reads=(), writes=(), **kw):
        self._deps(q, reads, writes)
        i = self.dnext[q]
        self.dnext[q] = (i + 1) % self.N_DMA_SEMS
        prev = self.dlast[q][i]
        if prev is not None:
            self._wait(q, prev)
        self.dcnt[q][i] += 16
        ins = self.eng[q].dma_start(out=out, in_=in_, **kw)
        ins.then_inc(self.dsem[q][i], 16)
        tok = (self.dsem[q][i], self.dcnt[q][i], None)
        self.dlast[q][i] = tok
        self._commit(tok, reads, writes)
        return ins

    def all_tokens(self):
        toks = []
        for e in self.eng:
            if self.cnt[e] > 0:
                toks.append((self.sem[e], self.cnt[e], e))
        for q in self.dsem:
            for t in self.dlast[q]:
                if t is not None:
                    toks.append(t)
        return toks

    def barrier(self):
        toks = self.all_tokens()
        for e in self.eng:
            for t in toks:
                if t[2] == e:
                    continue
                self._wait(e, t)
        self.res = {}

    def finish(self, e="sp"):
        for t in self.all_tokens():
            if t[2] != e:
                self._wait(e, t)


def _rs(r):
    return min(max(r - 4, 0), 56)


def _cs(w):
    return min(max(w - 8, 0), 48)


def _chunk_rows(j):
    rows = [r for r in range(64) if any(_rs(r) <= ka < _rs(r) + 8 for ka in (2 * j, 2 * j + 1))]
    return rows[0], rows[-1]


SPECIAL = {2: 0, 3: 1, 28: 2, 29: 3}


def _tile_row0(j):
    return _chunk_rows(j)[0] if j in SPECIAL else 2 * j - 4


def _bias_block(rpb_h, ka, r):
    blk = np.full((64, 64), NEG, np.float32)
    if not (_rs(r) <= ka < _rs(r) + 8):
        return blk
    a = ka - r + 7
    for w in range(64):
        c0 = _cs(w)
        kc = np.arange(c0, c0 + 16)
        blk[kc, w] = rpb_h[a, kc - w + 15]
    return blk


def build_bias_tiles(rpb):
    rpb = np.asarray(rpb, np.float32)
    b_int = np.full((8, 128, 640), NEG, np.float32)
    b_sp = np.full((8, 4, 128, 768), NEG, np.float32)
    j0 = 10
    for h in range(8):
        for kdr in range(2):
            for qr in range(10):
                b_int[h, kdr * 64:(kdr + 1) * 64, qr * 64:(qr + 1) * 64] = \
                    _bias_block(rpb[h], 2 * j0 + kdr, 2 * j0 - 4 + qr)
        for j, si in SPECIAL.items():
            r0, r1 = _chunk_rows(j)
            for kdr in range(2):
                for r in range(r0, r1 + 1):
                    b_sp[h, si, kdr * 64:(kdr + 1) * 64, (r - r0) * 64:(r - r0 + 1) * 64] = \
                        _bias_block(rpb[h], 2 * j + kdr, r)
    return b_int, b_sp


def rope_table():
    t = np.arange(T)
    row = (t // 64).astype(np.float64)
    col = (t % 64).astype(np.float64)
    inv = 10000.0 ** (-np.arange(0, 32, 2, dtype=np.float64) / 32)
    ang = np.concatenate([row[:, None] * inv[None, :], col[:, None] * inv[None, :]], axis=-1)
    return np.concatenate([np.cos(ang), np.sin(ang)], axis=-1).astype(np.float32)


def build_program(stop_after=None, debug=False):
    nc = bass.Bass("TRN2", target_bir_lowering=False)
    dt_in = lambda name, shape, dt=F32: nc.dram_tensor(name, shape, dt, kind="ExternalInput")
    x_d = dt_in("x", [T, D])
    ccol_d = dt_in("ccol", [128, 8])
    wada_d = dt_in("w_ada", [D, 6 * D])
    bada_d = dt_in("bada", [128, 48])
    gattn_d = dt_in("gattn", [128, 8])
    gffn_d = dt_in("gffn", [128, 8])
    gfin_d = dt_in("gfin", [128, D])
    win_d = dt_in("w_in", [D, 2304])
    gq_d = dt_in("gq", [128, 64])
    gk_d = dt_in("gk", [128, 64])
    bint_d = dt_in("bint", [8, 128, 640])
    bsp_d = dt_in("bsp", [8, 4, 128, 768])
    wo_d = dt_in("w_o", [D, D])
    wg_d = dt_in("w_gate", [D, DFF])
    wu_d = dt_in("w_up", [D, DFF])
    wd_d = dt_in("w_down", [DFF, D])
    ident_d = dt_in("ident", [128, 128], BF16)
    identf_d = dt_in("identf", [128, 128])
    sel_d = dt_in("sel64", [128, 128])
    cs_d = dt_in("cs", [T, 64])
    y_d = nc.dram_tensor("y", [T, D], F32, kind="ExternalOutput")
    wg_s = nc.dram_tensor("wg_s", [NF, 128, D], BF16)
    wu_s = nc.dram_tensor("wu_s", [NF, 128, D], BF16)
    wd_s = nc.dram_tensor("wd_s", [2, NF, 128, 512], BF16)
    ot_s = nc.dram_tensor("ot_s", [8, 128, T], BF16)
    dbg = {}
    if debug:
        dbg["d_mod"] = nc.dram_tensor("d_mod", [128, 48], F32, kind="ExternalOutput")
        dbg["d_ot"] = nc.dram_tensor("d_ot", [8, 128, T], BF16, kind="ExternalOutput")

    with ExitStack() as st:
        S = Sched(nc, st)
        ps = st.enter_context(nc.psum_tensor("ps", [128, 8 * 512], F32))

        def bank(b, n=512, off=0):
            return ps[:, b * 512 + off: b * 512 + off + n]

        def bankbf(b):
            return ps[:, b * 512:(b + 1) * 512].bitcast(BF16)

        PB = lambda b: ("pb", b)

        def sbt(stack, name, shape, dt):
            return stack.enter_context(nc.sbuf_tensor("sb_" + name, shape, dt))

        idt = sbt(st, "idt", [128, 128], BF16)
        identf = sbt(st, "identf", [128, 128], F32)
        sel64 = sbt(st, "sel64", [128, 128], F32)
        onesf = sbt(st, "onesf", [128, 128], F32)
        mh = sbt(st, "mh", [128, 8], F32)
        modsb = sbt(st, "modsb", [128, 48], F32)
        gsa = sbt(st, "gsa", [128, 8], F32)
        gsf = sbt(st, "gsf", [128, 8], F32)
        rec = sbt(st, "rec", [128, 512], F32)
        bcsb = sbt(st, "bcsb", [64, 512], F32)

        S.dma("sp", idt[:, :], ident_d.ap(), writes=["idt"])
        S.dma("sp", identf[:, :], identf_d.ap(), writes=["identf"])
        S.dma("sp", sel64[:, :], sel_d.ap(), writes=["sel64"])
        S.op("dve", lambda e: e.memset(onesf[:, :], 1.0), writes=["onesf"])
        S.op("dve", lambda e: e.memset(mh[:, :], -0.5), writes=["mh"])
        S.op("dve", lambda e: e.memset(rec[:, :], 0.0), writes=["rec"])

        prep = []
        for f in range(NF):
            prep.append(lambda f=f: S.dma("pool", wg_s.ap()[f].rearrange("p (c j) -> p c j", c=8),
                                          wg_d.ap()[:, f * 128:(f + 1) * 128].rearrange("(c p) j -> p c j", p=128),
                                          writes=[("wg_s", f)]))
            prep.append(lambda f=f: S.dma("pool", wu_s.ap()[f].rearrange("p (c j) -> p c j", c=8),
                                          wu_d.ap()[:, f * 128:(f + 1) * 128].rearrange("(c p) j -> p c j", p=128),
                                          writes=[("wu_s", f)]))
        for hf in range(2):
            prep.append(lambda hf=hf: S.dma("pool", wd_s.ap()[hf].rearrange("f p n -> p f n"),
                                            wd_d.ap()[:, hf * 512:(hf + 1) * 512].rearrange("(f p) n -> p f n", p=128),
                                            writes=[("wd_s", hf)]))

        ccol = sbt(st, "ccol", [128, 8], F32)
        ctmp = sbt(st, "ctmp", [128, 8], F32)
        cact = sbt(st, "cact", [128, 8], BF16)
        bada = sbt(st, "bada", [128, 48], F32)
        gat = sbt(st, "gat", [128, 8], F32)
        gff = sbt(st, "gff", [128, 8], F32)
        S.dma("sp", ccol[:, :], ccol_d.ap(), writes=["ccol"])
        S.dma("sp", bada[:, :], bada_d.ap(), writes=["bada"])
        S.dma("sp", gat[:, :], gattn_d.ap(), writes=["gat"])
        S.dma("sp", gff[:, :], gffn_d.ap(), writes=["gff"])
        S.op("act", lambda e: e.activation(out=ctmp[:, :], in_=ccol[:, :], func=AF.Exp, scale=-1.0),
             reads=["ccol"], writes=["ctmp"])
        S.op("dve", lambda e: e.tensor_scalar(out=ctmp[:, :], in0=ctmp[:, :], scalar1=1.0, scalar2=None, op0=ALU.add),
             reads=["ctmp"], writes=["ctmp"])
        S.op("dve", lambda e: e.reciprocal(out=ctmp[:, :], in_=ctmp[:, :]), reads=["ctmp"], writes=["ctmp"])
        S.op("dve", lambda e: e.tensor_tensor(out=cact[:, :], in0=ctmp[:, :], in1=ccol[:, :], op=ALU.mult),
             reads=["ctmp", "ccol"], writes=["cact"])
        wada_v = wada_d.ap().rearrange("(k p) n -> p k n", p=128)
        sha = modsb[:, 0:8]
        shf = modsb[:, 24:32]

        with ExitStack() as s1:
            win = sbt(s1, "win", [128, 8, 2304], BF16)
            cs = sbt(s1, "cs", [128, 32, 64], F32)
            gqb = sbt(s1, "gqb", [128, 64], F32)
            gkb = sbt(s1, "gkb", [128, 64], F32)
            KTg = sbt(s1, "KTg", [128, T], BF16)
            Vg = sbt(s1, "Vg", [128, 32, 2, 128], BF16)
            bint = sbt(s1, "bint", [128, 8, 640], F32)
            bsp = [sbt(s1, f"bsp{i}", [128, 768], F32) for i in range(3)]
            kTn = sbt(s1, "kTn", [128, 3, 4, 512], BF16)
            Vn = sbt(s1, "Vn", [128, 12, 8, 65], BF16)
            qTlo = sbt(s1, "qTlo", [128, 4, 512], BF16)
            qThi = sbt(s1, "qThi", [128, 4, 512], BF16)
            QTlo = sbt(s1, "QTlo", [128, 4, 512], BF16)
            QThi = sbt(s1, "QThi", [128, 4, 512], BF16)
            xt = [sbt(s1, f"xt{i}", [128, D], F32) for i in range(2)]
            xn = [sbt(s1, f"xn{i}", [128, D], BF16) for i in range(4)]
            ss4 = sbt(s1, "ss4", [128, 4], F32)
            rs4 = sbt(s1, "rs4", [128, 4], F32)
            hT = sbt(s1, "hT", [128, 8, 512], BF16)
            qsb = [sbt(s1, f"qsb{i}", [128, 512], F32) for i in range(2)]
            qtmp = sbt(s1, "qtmp", [128, 512], F32)
            ssq = sbt(s1, "ssq", [128, 8], F32)
            rq = sbt(s1, "rq", [128, 8], F32)
            rt1 = sbt(s1, "rt1", [128, 8, 32], F32)
            rt2 = sbt(s1, "rt2", [128, 8, 32], F32)
            qhat = sbt(s1, "qhat", [128, 4, 512], BF16)
            khat = sbt(s1, "khat", [128, 4, 128], BF16)
            Ssb = [sbt(s1, f"Ssb{i}", [128, 512], F32) for i in range(3)]
            Pna = [sbt(s1, f"Pna{i}", [128, 512], BF16) for i in range(3)]
            Pg = [sbt(s1, f"Pg{i}", [128, 1024], BF16) for i in range(2)]
            OT = sbt(s1, "OT", [128, 8, 512], BF16)
            rg = [sbt(s1, f"rg{i}", [64, 512], F32) for i in range(2)]

            wa = [kTn[:, 0:2, :, :].rearrange("p a m n -> p (a m) n"),
                  Vn.reshape([128, 12 * 8 * 65])[:, 0:4096].rearrange("p (k n) -> p k n", k=8)]

            def ada_dma(g):
                S.dma("pool", wa[g % 2], wada_v[:, :, g * 512:(g + 1) * 512], writes=[("wa", g % 2)])

            def ada_mm(g):
                wb = wa[g % 2]
                for jj in range(4):
                    j = g * 4 + jj
                    for k in range(8):
                        S.op("pe", lambda e, j=j, jj=jj, k=k: e.matmul(
                            bank(7, 1, j), lhsT=wb[:, k, jj * 128:(jj + 1) * 128], rhs=cact[:, k:k + 1],
                            start=(k == 0), stop=(k == 7), skip_group_check=True),
                            reads=[("wa", g % 2), "cact"], writes=[PB(7)], sig=(k == 7))

            ada_dma(0)
            ada_dma(1)
            S.dma("sp", cs[:, :, :], cs_d.ap().rearrange("(tt p) k -> p tt k", p=128), writes=["cs"])
            S.dma("sp", gqb[:, :], gq_d.ap(), writes=["gqb"])
            S.dma("sp", gkb[:, :], gk_d.ap(), writes=["gkb"])
            S.dma("sp", bint[:, :, :], bint_d.ap().rearrange("h p n -> p h n"), writes=["bint"])
            S.op("dve", lambda e: e.memset(Vg[:, :, :, 64:128], 1.0), writes=["Vg1"])
            for tns, nm in ((qTlo, "qTlo0"), (qThi, "qThi0"), (QTlo, "QTlo0"), (QThi, "QThi0")):
                S.op("dve", lambda e, tns=tns: e.memset(tns[:, :, :], 0.0), writes=[nm])
            zero_deps = {"qTlo": "qTlo0", "qThi": "qThi0", "QTlo": "QTlo0", "QThi": "QThi0"}

            x_v = x_d.ap().rearrange("(tt p) d -> tt p d", p=128)

            def norm_block(tb, gs, sh, dst=None, key="hT"):
                dst = hT if dst is None else dst
                for i in range(4):
                    xb = xt[i % 2]
                    S.dma("sp", xb[:, :], x_v[tb * 4 + i], writes=[("xt", i % 2)])
                    S.op("act", lambda e, i=i, xb=xb: e.activation(out=xn[i][:, :], in_=xb[:, :], func=AF.Square,
                                                                   accum_out=ss4[:, i:i + 1]),
                         reads=[("xt", i % 2)], writes=[("xn", i), ("ss4", i)])
                    S.op("dve", lambda e, i=i: e.tensor_scalar(out=rs4[:, i:i + 1], in0=ss4[:, i:i + 1], scalar1=1.0 / D,
                                                               scalar2=EPS, op0=ALU.mult, op1=ALU.add),
                         reads=[("ss4", i)], writes=[("rs4", i)])
                    S.op("act", lambda e, i=i: e.activation(out=rs4[:, i:i + 1], in_=rs4[:, i:i + 1], func=AF.Ln),
                         reads=[("rs4", i), "mh"], writes=[("rs4", i)])
                    S.op("act", lambda e, i=i: e.activation(out=rs4[:, i:i + 1], in_=rs4[:, i:i + 1], func=AF.Exp, scale=-0.5),
                         reads=[("rs4", i), "mh"], writes=[("rs4", i)])
                    S.op("dve", lambda e, i=i, xb=xb: e.tensor_scalar(out=xn[i][:, :], in0=xb[:, :], scalar1=rs4[:, i:i + 1],
                                                                      scalar2=None, op0=ALU.mult),
                         reads=[("xt", i % 2), ("rs4", i)], writes=[("xn", i)])
                for c in range(8):
                    bk = c // 2
                    for i in range(4):
                        S.op("pe", lambda e, c=c, i=i, bk=bk: e.transpose(
                            out=bankbf(bk)[:, (c % 2) * 512 + i * 128:(c % 2) * 512 + (i + 1) * 128],
                            in_=xn[i][:, c * 128:(c + 1) * 128], identity=idt[:, :]),
                            reads=[("xn", i), "idt"], writes=[PB(bk)], sig=(i == 3))
                for c in range(8):
                    bk = c // 2
                    S.op("act", lambda e, c=c, bk=bk: e.activation(
                        out=dst[:, c, :], in_=bankbf(bk)[:, (c % 2) * 512:(c % 2 + 1) * 512], func=AF.Identity,
                        scale=gs[:, c:c + 1], bias=sh[:, c:c + 1]),
                        reads=[PB(bk), "gsa", "modsb"], writes=[(key, c)])

            def qk_post(src_ap, nheads, gb, gname, tt, dst_fn, dst_keys, sb_i, grp=None):
                H = nheads
                W = H * 64
                v3 = lambda ap: ap.rearrange("p (h d) -> p h d", d=64)
                src3 = v3(src_ap)
                tmp3 = v3(qtmp[:, 0:W])
                S.op("dve", lambda e: e.tensor_tensor(out=qtmp[:, 0:W], in0=src_ap, in1=src_ap, op=ALU.mult),
                     reads=[("qsb", sb_i)], writes=["qtmp"])
                S.op("dve", lambda e: e.tensor_reduce(out=ssq[:, 0:H], in_=tmp3, axis=AX.X, op=ALU.add),
                     reads=["qtmp"], writes=["ssq"])
                S.op("dve", lambda e: e.tensor_scalar(out=rq[:, 0:H], in0=ssq[:, 0:H], scalar1=1.0 / 64, scalar2=EPS,
                                                      op0=ALU.mult, op1=ALU.add), reads=["ssq"], writes=["rq"])
                yield
                yield
                S.op("act", lambda e: e.activation(out=rq[:, 0:H], in_=rq[:, 0:H], func=AF.Ln),
                     reads=["rq", "mh"], writes=["rq"])
                S.op("act", lambda e: e.activation(out=rq[:, 0:H], in_=rq[:, 0:H], func=AF.Exp, scale=-0.5),
                     reads=["rq", "mh"], writes=["rq"])
                yield
                yield
                S.op("dve", lambda e: e.tensor_tensor(out=tmp3, in0=src3, in1=rq[:, 0:H].unsqueeze(2).to_broadcast([128, H, 64]),
                                                      op=ALU.mult), reads=[("qsb", sb_i), "rq"], writes=["qtmp"])
                S.op("dve", lambda e: e.tensor_tensor(out=tmp3, in0=tmp3, in1=gb[:, :].unsqueeze(1).to_broadcast([128, H, 64]),
                                                      op=ALU.mult), reads=["qtmp", gname], writes=["qtmp"])
                x1 = tmp3[:, :, 0:32]
                x2 = tmp3[:, :, 32:64]
                t1 = rt1[:, 0:H, :]
                t2 = rt2[:, 0:H, :]
                if grp is None:
                    cosb = cs[:, tt, 0:32].unsqueeze(1).to_broadcast([128, H, 32])
                    sinb = cs[:, tt, 32:64].unsqueeze(1).to_broadcast([128, H, 32])
                else:
                    A, B = grp
                    f4 = lambda ap: ap.rearrange("p (a b) d -> p a b d", a=A)
                    x1, x2, t1, t2 = f4(x1), f4(x2), f4(t1), f4(t2)
                    cosb = cs[:, tt:tt + A, 0:32].unsqueeze(2).to_broadcast([128, A, B, 32])
                    sinb = cs[:, tt:tt + A, 32:64].unsqueeze(2).to_broadcast([128, A, B, 32])
                    _dst = dst_fn
                    dst_fn = lambda half: f4(_dst(half))
                S.op("dve", lambda e: e.tensor_tensor(out=t1, in0=x1, in1=cosb, op=ALU.mult), reads=["qtmp", "cs"], writes=["rt1"])
                S.op("dve", lambda e: e.tensor_tensor(out=t2, in0=x2, in1=sinb, op=ALU.mult), reads=["qtmp", "cs"], writes=["rt2"])
                S.op("dve", lambda e: e.tensor_tensor(out=dst_fn(0), in0=t1, in1=t2, op=ALU.subtract),
                     reads=["rt1", "rt2"], writes=dst_keys)
                S.op("dve", lambda e: e.tensor_tensor(out=t1, in0=x1, in1=sinb, op=ALU.mult), reads=["qtmp", "cs"], writes=["rt1"])
                S.op("dve", lambda e: e.tensor_tensor(out=t2, in0=x2, in1=cosb, op=ALU.mult), reads=["qtmp", "cs"], writes=["rt2"])
                S.op("dve", lambda e: e.tensor_tensor(out=dst_fn(1), in0=t1, in1=t2, op=ALU.add),
                     reads=["rt1", "rt2"], writes=dst_keys)

            def p1a_post(tb, hb, hk):
                for i in range(4):
                    bk = 4 + i // 2
                    for c in range(8):
                        S.op("pe", lambda e, c=c, i=i, bk=bk: e.matmul(bank(bk, 256, (i % 2) * 256), lhsT=hb[:, c, i * 128:(i + 1) * 128],
                                                                     rhs=win[:, c, 2048:2304], start=(c == 0), stop=(c == 7),
                                                                     skip_group_check=True),
                             reads=[(hk, c), "win"], writes=[PB(bk)], sig=(c == 7))
                for bb in range(2):
                    src = bank(4 + bb).rearrange("p (t x) -> p t x", t=2)
                    S.op("act", lambda e, bb=bb, src=src: e.copy(
                        out=qsb[0][:, bb * 256:(bb + 1) * 256].rearrange("p (t x) -> p t x", t=2), in_=src[:, :, 0:128]),
                        reads=[PB(4 + bb)], writes=[("qsb", 0)])
                    tt0 = tb * 4 + 2 * bb
                    S.op("dve", lambda e, tt0=tt0, src=src: e.tensor_copy(
                        out=Vg[:, tt0:tt0 + 2, :, 0:64], in_=src[:, :, 128:256].rearrange("p t (h d) -> p t h d", d=64)),
                        reads=[PB(4 + bb)], writes=[("Vg", tt0), ("Vg", tt0 + 1)])
                kh3 = khat[:, :, :].rearrange("p t (h d) -> p (t h) d", d=64)
                for _ in qk_post(qsb[0][:, :], 8, gkb, "gkb", tb * 4,
                                 lambda half, kh3=kh3: kh3[:, :, half * 32:(half + 1) * 32], ["khat"], 0, grp=(4, 2)):
                    pass
                for i in range(4):
                    S.op("pe", lambda e, i=i: e.transpose(out=bankbf(6)[:, i * 128:(i + 1) * 128], in_=khat[:, i, :], identity=idt[:, :]),
                         reads=["khat", "idt"], writes=[PB(6)], sig=(i == 3))
                S.op("act", lambda e, tb=tb: e.copy(out=KTg[:, tb * 512:(tb + 1) * 512], in_=bankbf(6)[:, 0:512]),
                     reads=[PB(6)], writes=[("KTg", tb * 4 + i) for i in range(4)])

            def stage_a(tb, part="all"):
                slot = tb % 3
                if part in ("all", "kv"):
                    norm_block(tb, gsa, sha)
                n_mm = 0
                for m in range(4):
                    for which in range(2):
                        if which == 0 and part == "kv":
                            continue
                        if which == 1 and part == "q":
                            continue
                        bk = 4 + (n_mm % 2)
                        n_mm += 1
                        col0 = which * 512 + m * 128
                        for c in range(8):
                            S.op("pe", lambda e, c=c, bk=bk, col0=col0: e.matmul(
                                bank(bk), lhsT=win[:, c, col0:col0 + 128], rhs=hT[:, c, :], start=(c == 0), stop=(c == 7)),
                                reads=[("hT", c), "win"], writes=[PB(bk)], sig=(c == 7))
                        if which == 0:
                            S.op("act", lambda e, bk=bk, m=m: e.copy(out=qTlo[0:64, m, :], in_=bank(bk)[0:64, :]),
                                 reads=[PB(bk), "qTlo0"], writes=[("qTlo", m)])
                            S.op("dve", lambda e, bk=bk, m=m: e.tensor_copy(out=qThi[64:128, m, :], in_=bank(bk)[64:128, :]),
                                 reads=[PB(bk), "qThi0"], writes=[("qThi", m)])
                        else:
                            S.op("act", lambda e, bk=bk, m=m: e.copy(out=kTn[:, slot, m, :], in_=bank(bk)),
                                 reads=[PB(bk)], writes=[("kTn", slot, m)])
                for i in range(4 if part in ("all", "kv") else 0):
                    tt = tb * 4 + i
                    bk = 4 + (i % 2)
                    for c in range(8):
                        S.op("pe", lambda e, c=c, i=i, bk=bk: e.matmul(bank(bk), lhsT=hT[:, c, i * 128:(i + 1) * 128],
                                                                     rhs=win[:, c, 1024:1536], start=(c == 0), stop=(c == 7)),
                             reads=[("hT", c), "win"], writes=[PB(bk)], sig=(c == 7))
                    S.op("act", lambda e, bk=bk, i=i: e.copy(
                        out=Vn[:, slot * 4 + i, :, 0:64], in_=bank(bk).rearrange("p (h d) -> p h d", d=64)),
                        reads=[PB(bk), "Vn1"], writes=[("Vn", slot * 4 + i)])
                if part == "kv":
                    return
                for i in range(4):
                    tt = tb * 4 + i
                    bk = 6 + (i % 2)
                    for c in range(8):
                        S.op("pe", lambda e, c=c, i=i, bk=bk: e.matmul(bank(bk), lhsT=hT[:, c, i * 128:(i + 1) * 128],
                                                                     rhs=win[:, c, 1536:2048], start=(c == 0), stop=(c == 7)),
                             reads=[("hT", c), "win"], writes=[PB(bk)], sig=(c == 7))
                    sbi = i % 2
                    S.op("act", lambda e, bk=bk, sbi=sbi: e.copy(
                        out=qsb[sbi][:, :].rearrange("p (g kv d) -> p g kv d", g=4, kv=2),
                        in_=bank(bk).rearrange("p (kv g d) -> p g kv d", g=4, kv=2)),
                        reads=[PB(bk)], writes=[("qsb", sbi)])
                    qh3 = qhat[:, i, :].rearrange("p (h d) -> p h d", d=64)
                    for _ in qk_post(qsb[sbi][:, :], 8, gqb, "gqb", tt,
                                     lambda half, qh3=qh3: qh3[:, :, half * 32:(half + 1) * 32], [("qhat", i)], sbi):
                        pass
                for g in range(4):
                    bk = 4 + g // 2
                    for i in range(4):
                        S.op("pe", lambda e, g=g, i=i, bk=bk: e.transpose(
                            out=bankbf(bk)[:, (g % 2) * 512 + i * 128:(g % 2) * 512 + (i + 1) * 128],
                            in_=qhat[:, i, g * 128:(g + 1) * 128], identity=idt[:, :]),
                            reads=[("qhat", i), "idt"], writes=[PB(bk)], sig=(i == 3))
                for g in range(4):
                    bk = 4 + g // 2
                    src = bankbf(bk)[:, (g % 2) * 512:(g % 2 + 1) * 512]
                    S.op("act", lambda e, g=g, src=src: e.copy(out=QTlo[0:64, g, :], in_=src[0:64, :]),
                         reads=[PB(bk), "QTlo0"], writes=[("QTlo", g)])
                    S.op("dve", lambda e, g=g, src=src: e.tensor_copy(out=QThi[64:128, g, :], in_=src[64:128, :]),
                         reads=[PB(bk), "QThi0"], writes=[("QThi", g)])

            def normalize_head(acc_bk, h_chunk, half, bc_bk):
                S.op("dve", lambda e: e.reciprocal(out=rec[64:65, :], in_=bank(acc_bk)[64:65, :]),
                     reads=[PB(acc_bk)], writes=["rec"])
                S.op("pe", lambda e: e.matmul(bank(bc_bk)[0:128, :], lhsT=sel64[:, :], rhs=rec[:, :], start=True, stop=True),
                     reads=["sel64", "rec"], writes=[PB(bc_bk)])
                S.op("act", lambda e: e.copy(out=bcsb[:, :], in_=bank(bc_bk)[0:64, :]), reads=[PB(bc_bk)], writes=["bcsb"])
                S.op("dve", lambda e: e.tensor_tensor(out=OT[half * 64:(half + 1) * 64, h_chunk, :], in0=bank(acc_bk)[0:64, :],
                                                      in1=bcsb[:, :], op=ALU.mult),
                     reads=[PB(acc_bk), "bcsb"], writes=[("OT", h_chunk, half)])

            sp_loaded = {}
            sp_next = [0]

            def na_block(tb):
                steps = []
                for h in range(8):
                    items = []
                    for j in range(max(0, 4 * tb - 2), min(31, 4 * tb + 5) + 1):
                        r0, r1 = _chunk_rows(j)
                        lo, hi = max(8 * tb, r0), min(8 * tb + 7, r1)
                        if lo <= hi:
                            items.append((j, lo, hi))
                    for idx, (j, lo, hi) in enumerate(items):
                        steps.append(dict(h=h, j=j, lo=lo, hi=hi, first=(idx == 0), last=(idx == len(items) - 1)))
                n = len(steps)
                for i, stp in enumerate(steps):
                    stp["sbk"] = 3 + (i % 3)
                    stp["bi"] = i % 3
                    stp["nq"] = (stp["hi"] - stp["lo"] + 1) * 64
                    stp["qc0"] = (stp["lo"] - 8 * tb) * 64

                def qk(i):
                    p = steps[i]
                    h, j = p["h"], p["j"]
                    m, half = h // 2, h % 2
                    qT = qTlo if half == 0 else qThi
                    qkey = ("qTlo", m) if half == 0 else ("qThi", m)
                    kslot, kcol = (j // 4) % 3, (j % 4) * 128
                    nq, qc0, sbk = p["nq"], p["qc0"], p["sbk"]
                    S.op("pe", lambda e: e.matmul(bank(sbk, nq), lhsT=kTn[:, kslot, m, kcol:kcol + 128],
                                                  rhs=qT[:, m, qc0:qc0 + nq], start=True, stop=True),
                         reads=[("kTn", kslot, m), qkey], writes=[PB(sbk)])

                def bias_exp(i):
                    p = steps[i]
                    h, j, lo = p["h"], p["j"], p["lo"]
                    nq, sbk, bi = p["nq"], p["sbk"], p["bi"]
                    boff = (lo - _tile_row0(j)) * 64
                    if j in SPECIAL:
                        key = (h, j)
                        if key not in sp_loaded:
                            si = sp_next[0] % 3
                            sp_next[0] += 1
                            S.dma("sp", bsp[si][:, :], bsp_d.ap()[h, SPECIAL[j]], writes=[("bsp", si)])
                            for k2 in [k for k, v in sp_loaded.items() if v == si]:
                                del sp_loaded[k2]
                            sp_loaded[key] = si
                        si = sp_loaded[key]
                        b_ap = bsp[si][:, boff:boff + nq]
                        bkey = ("bsp", si)
                    else:
                        b_ap = bint[:, h, boff:boff + nq]
                        bkey = "bint"
                    S.op("dve", lambda e: e.scalar_tensor_tensor(
                        out=Ssb[bi][:, 0:nq], in0=bank(sbk, nq), scalar=0.125, in1=b_ap, op0=ALU.mult, op1=ALU.add),
                        reads=[PB(sbk), bkey], writes=[("Ssb", bi)])
                    S.op("act", lambda e: e.activation(out=Pna[bi][:, 0:nq], in_=Ssb[bi][:, 0:nq], func=AF.Exp),
                         reads=[("Ssb", bi)], writes=[("Pna", bi)])

                def pv(i):
                    p = steps[i]
                    h, j = p["h"], p["j"]
                    vt = ((j // 4) % 3) * 4 + (j % 4)
                    nq, qc0, bi = p["nq"], p["qc0"], p["bi"]
                    acc_bk = 6 + (h % 2)
                    S.op("pe", lambda e: e.matmul(bank(acc_bk)[0:65, qc0:qc0 + nq], lhsT=Vn[:, vt, h, 0:65],
                                                  rhs=Pna[bi][:, 0:nq], start=p["first"], stop=p["last"],
                                                  skip_group_check=True),
                         reads=[("Vn", vt), "Vn1", ("Pna", bi)], writes=[PB(acc_bk)])

                deferred = []

                def sched_normalize(i, h):
                    acc_bk, m, half = 6 + (h % 2), h // 2, h % 2
                    def recip_row():
                        S.op("act", lambda e: e.activation(out=rec[64:65, :], in_=bank(acc_bk)[64:65, :], func=AF.Ln),
                             reads=[PB(acc_bk)], writes=["rec"])
                        S.op("act", lambda e: e.activation(out=rec[64:65, :], in_=rec[64:65, :], func=AF.Exp, scale=-1.0),
                             reads=["rec"], writes=["rec"])
                    deferred.append((i + 1, recip_row))
                    deferred.append((i + 2, lambda: S.op(
                        "pe", lambda e: e.matmul(bank(2)[0:128, :], lhsT=sel64[:, :], rhs=rec[:, :], start=True, stop=True),
                        reads=["sel64", "rec"], writes=[PB(2)])))
                    deferred.append((i + 3, lambda: S.op(
                        "act", lambda e: e.copy(out=bcsb[:, :], in_=bank(2)[0:64, :]), reads=[PB(2)], writes=["bcsb"])))
                    deferred.append((i + 4, lambda: S.op(
                        "dve", lambda e: e.tensor_tensor(out=OT[half * 64:(half + 1) * 64, m, :], in0=bank(acc_bk)[0:64, :],
                                                         in1=bcsb[:, :], op=ALU.mult),
                        reads=[PB(acc_bk), "bcsb"], writes=[("OT", m, half)])))

                def run_deferred(i):
                    rest = []
                    for at, th in deferred:
                        if at <= i:
                            th()
                        else:
                            rest.append((at, th))
                    deferred[:] = rest

                qk(0)
                if n > 1:
                    qk(1)
                for i in range(n):
                    bias_exp(i)
                    if i + 2 < n:
                        qk(i + 2)
                    run_deferred(i)
                    pv(i)
                    if steps[i]["last"]:
                        sched_normalize(i, steps[i]["h"])
                run_deferred(10 ** 9)

            fb = [0]

            def fbank():
                fb[0] += 1
                return 6 + (fb[0] % 2)

            def th_norm(tb):
                L = []

                def dm(i):
                    S.dma("sp", xt[i % 2][:, :], x_v[tb * 4 + i], writes=[("xt", i % 2)])

                def sq(i):
                    xb = xt[i % 2]
                    S.op("act", lambda e: e.activation(out=xn[i][:, :], in_=xb[:, :], func=AF.Square,
                                                       accum_out=ss4[:, i:i + 1]),
                         reads=[("xt", i % 2)], writes=[("xn", i), ("ss4", i)])

                def ms(i):
                    S.op("dve", lambda e: e.tensor_scalar(out=rs4[:, i:i + 1], in0=ss4[:, i:i + 1], scalar1=1.0 / D,
                                                          scalar2=EPS, op0=ALU.mult, op1=ALU.add),
                         reads=[("ss4", i)], writes=[("rs4", i)])

                def le(i):
                    S.op("act", lambda e: e.activation(out=rs4[:, i:i + 1], in_=rs4[:, i:i + 1], func=AF.Ln),
                         reads=[("rs4", i)], writes=[("rs4", i)])
                    S.op("act", lambda e: e.activation(out=rs4[:, i:i + 1], in_=rs4[:, i:i + 1], func=AF.Exp, scale=-0.5),
                         reads=[("rs4", i)], writes=[("rs4", i)])

                def scl(i):
                    xb = xt[i % 2]
                    S.op("dve", lambda e: e.tensor_scalar(out=xn[i][:, :], in0=xb[:, :], scalar1=rs4[:, i:i + 1],
                                                          scalar2=None, op0=ALU.mult),
                         reads=[("xt", i % 2), ("rs4", i)], writes=[("xn", i)])

                seq = [(dm, 0), (dm, 1), None, None, (sq, 0), (ms, 0), (sq, 1), (le, 0), (ms, 1), (scl, 0), (le, 1), (dm, 2),
                       (scl, 1), (dm, 3), None, None, (sq, 2), (ms, 2), (sq, 3), (le, 2), (ms, 3), (scl, 2), (le, 3), (scl, 3)]
                for it in seq:
                    if it is None:
                        L.append(lambda: None)
                    else:
                        L.append(lambda fn=it[0], i=it[1]: fn(i))

                def tr(r, bb):
                    bk = 6 + bb
                    for cc in range(2):
                        c = 4 * r + 2 * bb + cc
                        for i in range(4):
                            S.op("pe", lambda e, c=c, cc=cc, i=i: e.transpose(
                                out=bankbf(bk)[:, cc * 512 + i * 128:cc * 512 + (i + 1) * 128],
                                in_=xn[i][:, c * 128:(c + 1) * 128], identity=idt[:, :]),
                                reads=[("xn", i), "idt"], writes=[PB(bk)], sig=(i == 3))
                            if i % 2 == 1:
                                yield

                def ev(r, bb):
                    bk = 6 + bb
                    for cc in range(2):
                        c = 4 * r + 2 * bb + cc
                        S.op("dve", lambda e, c=c, cc=cc: e.tensor_scalar(
                            out=hT[:, c, :], in0=bankbf(bk)[:, cc * 512:(cc + 1) * 512],
                            scalar1=gsa[:, c:c + 1], scalar2=sha[:, c:c + 1], op0=ALU.mult, op1=ALU.add),
                            reads=[PB(bk), "gsa", "modsb"], writes=[("hT", c)])

                L2 = []
                for r in range(2):
                    for bb in range(2):
                        trf = (lambda r=r, bb=bb: tr(r, bb))
                        trf.units = 5
                        L2.append(trf)
                    for bb in range(2):
                        L2.append(lambda r=r, bb=bb: ev(r, bb))
                return L, L2

            def proj_group(lhs_fn, rhs_fn, bk, n=512, hk="hT"):
                for c in range(8):
                    S.op("pe", lambda e, c=c: e.matmul(bank(bk, n), lhsT=lhs_fn(c), rhs=rhs_fn(c), start=(c == 0), stop=(c == 7)),
                         reads=[(hk, c), "win"], writes=[PB(bk)], sig=(c == 7))
                    if c % 2 == 1:
                        yield

            def th_kv(tb, hT=hT, hk="hT"):
                slot = tb % 3
                L = []
                for m in range(4):
                    def pk(m=m):
                        bk = fbank()
                        col0 = 512 + m * 128
                        yield from proj_group(lambda c: win[:, c, col0:col0 + 128], lambda c: hT[:, c, :], bk, hk=hk)
                        S.op("dve", lambda e: e.tensor_copy(out=kTn[:, slot, m, :], in_=bank(bk)),
                             reads=[PB(bk)], writes=[("kTn", slot, m)])
                    pk.units = 5
                    L.append(pk)
                for i in range(4):
                    def pvv(i=i):
                        bk = fbank()
                        yield from proj_group(lambda c: hT[:, c, i * 128:(i + 1) * 128], lambda c: win[:, c, 1024:1536], bk, hk=hk)
                        S.op("dve", lambda e: e.tensor_copy(
                            out=Vn[:, slot * 4 + i, :, 0:64], in_=bank(bk).rearrange("p (h d) -> p h d", d=64)),
                            reads=[PB(bk), "Vn1"], writes=[("Vn", slot * 4 + i)])
                    pvv.units = 5
                    L.append(pvv)
                return L

            def th_q(tb, hT=hT, hk="hT"):
                L = []
                for m in range(4):
                    def pq(m=m):
                        bk = fbank()
                        col0 = m * 128
                        yield from proj_group(lambda c: win[:, c, col0:col0 + 128], lambda c: hT[:, c, :], bk, hk=hk)
                        S.op("dve", lambda e: e.tensor_copy(out=qTlo[0:64, m, :], in_=bank(bk)[0:64, :]),
                             reads=[PB(bk), "qTlo0"], writes=[("qTlo", m)])
                        S.op("dve", lambda e: e.tensor_copy(out=qThi[64:128, m, :], in_=bank(bk)[64:128, :]),
                             reads=[PB(bk), "qThi0"], writes=[("qThi", m)])
                    pq.units = 5
                    L.append(pq)
                for i in range(4):
                    tt = tb * 4 + i
                    sbi = i % 2

                    def pg(i=i, sbi=sbi):
                        bk = fbank()
                        yield from proj_group(lambda c: hT[:, c, i * 128:(i + 1) * 128], lambda c: win[:, c, 1536:2048], bk, hk=hk)
                        S.op("dve", lambda e: e.tensor_copy(
                            out=qsb[sbi][:, :].rearrange("p (g kv d) -> p g kv d", g=4, kv=2),
                            in_=bank(bk).rearrange("p (kv g d) -> p g kv d", g=4, kv=2)),
                            reads=[PB(bk)], writes=[("qsb", sbi)])
                    pg.units = 5
                    L.append(pg)

                    def post(i=i, sbi=sbi, tt=tt):
                        qh3 = qhat[:, i, :].rearrange("p (h d) -> p h d", d=64)
                        yield from qk_post(qsb[sbi][:, :], 8, gqb, "gqb", tt,
                                           lambda half: qh3[:, :, half * 32:(half + 1) * 32], [("qhat", i)], sbi)
                    post.units = 5
                    L.append(post)
                return L

            def finalize_q():
                for g in range(4):
                    bk = g // 2
                    for i in range(4):
                        S.op("pe", lambda e, g=g, i=i, bk=bk: e.transpose(
                            out=bankbf(bk)[:, (g % 2) * 512 + i * 128:(g % 2) * 512 + (i + 1) * 128],
                            in_=qhat[:, i, g * 128:(g + 1) * 128], identity=idt[:, :]),
                            reads=[("qhat", i), "idt"], writes=[PB(bk)], sig=(i == 3))
                for g in range(4):
                    bk = g // 2
                    src = bankbf(bk)[:, (g % 2) * 512:(g % 2 + 1) * 512]
                    S.op("dve", lambda e, g=g, src=src: e.tensor_copy(out=QTlo[0:64, g, :], in_=src[0:64, :]),
                         reads=[PB(bk), "QTlo0"], writes=[("QTlo", g)])
                    S.op("dve", lambda e, g=g, src=src: e.tensor_copy(out=QThi[64:128, g, :], in_=src[64:128, :]),
                         reads=[PB(bk), "QThi0"], writes=[("QThi", g)])

            def gqa_block(tb, filler):
                nf = sum(getattr(t, "units", 1) for t in filler)
                fl = Filler(filler)
                state = {"emitted": 0, "step": 0}
                for kv in range(2):
                    QT = QTlo if kv == 0 else QThi
                    qname = "QTlo" if kv == 0 else "QThi"
                    for gp in range(2):
                        def qk(kc):
                            b0 = 2 + 2 * (kc % 2)
                            for u in range(2):
                                g = gp * 2 + u
                                S.op("pe", lambda e, u=u, g=g: e.matmul(
                                    bank(b0 + u), lhsT=KTg[:, kc * 128:(kc + 1) * 128], rhs=QT[:, g, :], start=True, stop=True),
                                    reads=[("KTg", kc), (qname, g)], writes=[PB(b0), PB(b0 + 1)], sig=(u == 1))

                        def ex(kc):
                            b0 = 2 + 2 * (kc % 2)
                            S.op("act", lambda e: e.activation(
                                out=Pg[kc % 2][:, :], in_=ps[:, b0 * 512:(b0 + 2) * 512], func=AF.Exp, scale=0.125),
                                reads=[PB(b0), PB(b0 + 1)], writes=[("Pg", kc % 2)])

                        def pv(kc):
                            for u in range(2):
                                S.op("pe", lambda e, u=u: e.matmul(
                                    bank(u), lhsT=Vg[:, kc, kv, :], rhs=Pg[kc % 2][:, u * 512:(u + 1) * 512],
                                    start=(kc == 0), stop=(kc == 31)),
                                    reads=[("Vg", kc), "Vg1", ("Pg", kc % 2)], writes=[PB(u)])

                        qk(0)
                        for kc in range(32):
                            if kc + 1 < 32:
                                qk(kc + 1)
                            ex(kc)
                            state["step"] += 1
                            target = (nf * state["step"]) // 112
                            while state["emitted"] < target:
                                fl.step()
                                state["emitted"] += 1
                            pv(kc)
                        for u in range(2):
                            h = kv * 4 + gp * 2 + u
                            half, chk = h % 2, 4 + h // 2
                            S.op("act", lambda e, u=u: e.activation(out=rg[u][:, :], in_=bank(u)[64:128, :], func=AF.Ln),
                                 reads=[PB(u)], writes=[("rg", u)])
                            S.op("act", lambda e, u=u: e.activation(out=rg[u][:, :], in_=rg[u][:, :], func=AF.Exp, scale=-1.0),
                                 reads=[("rg", u)], writes=[("rg", u)])
                            S.op("dve", lambda e, u=u, half=half, chk=chk: e.tensor_tensor(
                                out=OT[half * 64:(half + 1) * 64, chk, :], in0=bank(u)[0:64, :], in1=rg[u][:, :], op=ALU.mult),
                                reads=[PB(u), ("rg", u)], writes=[("OT", chk, half)])
                fl.drain()

            def store_ot(tb):
                for c in range(8):
                    S.dma("pool", ot_s.ap()[c][:, tb * 512:(tb + 1) * 512], OT[:, c, :],
                          reads=[("OT", c, 0), ("OT", c, 1)], writes=[("ot_s", tb)])

            def run(L):
                for t in L:
                    r = t()
                    if hasattr(r, "__next__"):
                        for _ in r:
                            pass

            class Filler:
                def __init__(self, items):
                    self.items = list(items)
                    self.cur = None

                def step(self):
                    while True:
                        if self.cur is None:
                            if not self.items:
                                return False
                            r = self.items.pop(0)()
                            if hasattr(r, "__next__"):
                                self.cur = r
                            else:
                                return True
                        try:
                            next(self.cur)
                            return True
                        except StopIteration:
                            self.cur = None

                def drain(self):
                    while self.step():
                        pass

            order = [2, 3, 4, 5, 6, 7, 0, 1]
            hbufs = [(OT, "hTa"), (hT, "hT")]
            for g in range(4):
                ada_mm(g)
                ada_dma(g + 2)
                if g == 1:
                    S.dma("pool", win[:, :, :], win_d.ap().rearrange("(c p) n -> p c n", p=128), writes=["win"])
            S.op("dve", lambda e: e.tensor_tensor(out=modsb[:, 0:16], in0=bank(7, 16), in1=bada[:, 0:16], op=ALU.add),
                 reads=[PB(7), "bada"], writes=["modsb"])
            S.op("dve", lambda e: e.scalar_tensor_tensor(out=gsa[:, :], in0=modsb[:, 8:16], scalar=1.0, in1=gat[:, :],
                                                         op0=ALU.add, op1=ALU.mult),
                 reads=["modsb", "gat"], writes=["gsa"])
            ada_plan = {0: [4, 5], 1: [6, 7], 2: [8], 3: [9], 4: [10], 5: [11]}
            norm_block(order[0], gsa, sha, dst=hbufs[0][0], key=hbufs[0][1])
            for k, tbk in enumerate(order):
                hb, hk = hbufs[k % 2]
                if k + 1 < len(order):
                    nb_, nk_ = hbufs[(k + 1) % 2]
                    norm_block(order[k + 1], gsa, sha, dst=nb_, key=nk_)
                p1a_post(tbk, hb, hk)
                for g in ada_plan.get(k, []):
                    ada_mm(g)
                    if g + 2 < 12:
                        ada_dma(g + 2)
                if k == 5:
                    S.op("dve", lambda e: e.tensor_tensor(out=modsb[:, 16:48], in0=bank(7, 32, 16), in1=bada[:, 16:48], op=ALU.add),
                         reads=[PB(7), "bada"], writes=["modsb2"])
                    S.op("dve", lambda e: e.scalar_tensor_tensor(out=gsf[:, :], in0=modsb[:, 32:40], scalar=1.0, in1=gff[:, :],
                                                                 op0=ALU.add, op1=ALU.mult),
                         reads=["modsb2", "gff"], writes=["gsf"])
                    if debug:
                        S.dma("sp", dbg["d_mod"].ap(), modsb[:, :], reads=["modsb", "modsb2"], writes=["d_mod"])
                    S.op("dve", lambda e: e.memset(Vn[:, :, :, 64:65], 1.0), writes=["Vn1", ("wa", 0), ("wa", 1)])
                if tbk in (0, 1):
                    run(th_kv(tbk, hT=hb, hk=hk))
                if tbk == 0:
                    run(th_q(0, hT=hb, hk=hk))
                    finalize_q()
            S.barrier()
            if stop_after == "p1a":
                S.finish("sp")
                return nc
            for tb in range(NB):
                for _ in range(6):
                    if prep:
                        prep.pop(0)()
                na_block(tb)
                filler = []
                npre, npost = ([], [])
                if tb + 2 < NB:
                    npre, npost = th_norm(tb + 2)
                filler += npre
                if tb + 1 < NB:
                    filler += th_q(tb + 1)
                if tb + 2 < NB:
                    filler += npost + th_kv(tb + 2)
                gqa_block(tb, filler)
                store_ot(tb)
                if tb + 1 < NB:
                    finalize_q()
            if debug:
                S.barrier()
                S.dma("sp", dbg["d_ot"].ap(), ot_s.ap(), writes=["d_ot"])
            S.barrier()

        if stop_after == "p1":
            S.finish("sp")
            return nc

        with ExitStack() as s2:
            wo = sbt(s2, "wo", [128, 8, D], BF16)
            OTb = [sbt(s2, f"OTb{i}", [128, 8, 512], BF16) for i in range(2)]
            x1s = [[sbt(s2, f"x1_{p}_{i}", [128, D], F32) for i in range(4)] for p in range(2)]
            ytmp = [sbt(s2, f"ytmp{i}", [128, 512], F32) for i in range(2)]
            ytx = [sbt(s2, f"ytx{i}", [128, 512], F32) for i in range(2)]
            ss4c = sbt(s2, "ss4c", [128, 4], F32)
            rs4c = sbt(s2, "rs4c", [128, 4], F32)
            xn2 = [sbt(s2, f"xn2_{i}", [128, D], BF16) for i in range(4)]
            junk2 = sbt(s2, "junk2", [128, D], BF16)
            ss4b = sbt(s2, "ss4b", [128, 4], F32)
            rs4b = sbt(s2, "rs4b", [128, 4], F32)
            hT2s = [sbt(s2, f"hT2_{p}", [128, 8, 512], BF16) for p in range(2)]
            wgb = [sbt(s2, f"wgb{i}", [128, D], BF16) for i in range(3)]
            wub = [sbt(s2, f"wub{i}", [128, D], BF16) for i in range(3)]
            wdb = [sbt(s2, f"wdb{i}", [128, 2, 512], BF16) for i in range(4)]
            sg = [sbt(s2, f"sg{i}", [128, 512], F32) for i in range(2)]
            hid = sbt(s2, "hid", [128, NF, 512], BF16)
            ot = [sbt(s2, f"ot{i}", [128, D], F32) for i in range(2)]

            S.dma("pool", wo[:, :, :], wo_d.ap().rearrange("(c p) n -> p c n", p=128), writes=["wo"])
            gate_a = sbt(s2, "gate_a", [128, D], F32)
            gate_f = sbt(s2, "gate_f", [128, D], F32)
            gfin = sbt(s2, "gfin", [128, D], F32)
            dg = [sbt(s2, f"dg{i}", [128, 128], F32) for i in range(2)]
            S.dma("sp", gfin[:, :], gfin_d.ap(), writes=["gfin"])
            for gi, (off, gt, gname) in enumerate(((16, gate_a, "gate_a"), (40, gate_f, "gate_f"))):
                for j in range(8):
                    d = dg[j % 2]
                    S.op("dve", lambda e, d=d, off=off, j=j: e.tensor_scalar(
                        out=d[:, :], in0=identf[:, :], scalar1=modsb[:, off + j:off + j + 1], scalar2=None, op0=ALU.mult),
                        reads=["identf", "modsb"], writes=[("dg", j % 2)])
                    bk = 1 + (j // 4)
                    S.op("pe", lambda e, d=d, bk=bk, j=j: e.matmul(bank(bk, 128, (j % 4) * 128), lhsT=onesf[:, :], rhs=d[:, :],
                                                                 start=True, stop=True),
                         reads=["onesf", ("dg", j % 2)], writes=[PB(bk)])
                for hb in range(2):
                    S.op("act", lambda e, gt=gt, hb=hb: e.copy(out=gt[:, hb * 512:(hb + 1) * 512], in_=bank(1 + hb)),
                         reads=[PB(1 + hb)], writes=[(gname, hb)])
            x_v = x_d.ap().rearrange("(tt p) d -> tt p d", p=128)
            y_v = y_d.ap().rearrange("(tt p) d -> tt p d", p=128)
            n_ld = {"g": 0, "d": 0}
            n_out = [0]

            def x_thunks(tb):
                par = tb % 2
                ob = OTb[par]
                x1 = x1s[par]
                hT2 = hT2s[par]
                L = []

                def ldot():
                    for c in range(8):
                        S.dma("sp", ob[:, c, :], ot_s.ap()[c][:, tb * 512:(tb + 1) * 512], writes=[("OTb", par, c)])
                L.append(ldot)
                for i in range(4):
                    def ldx(i=i):
                        S.dma("sp", x1[i][:, :], x_v[tb * 4 + i], writes=[("x1", par, i, 0), ("x1", par, i, 1)])
                    L.append(ldx)
                for i in range(4):
                    for hf in range(2):
                        def opj(i=i, hf=hf):
                            bk = 4 + (i * 2 + hf) % 2
                            for c in range(8):
                                S.op("pe", lambda e, c=c: e.matmul(
                                    bank(bk), lhsT=ob[:, c, i * 128:(i + 1) * 128], rhs=wo[:, c, hf * 512:(hf + 1) * 512],
                                    start=(c == 0), stop=(c == 7)),
                                    reads=[("OTb", par, c), "wo"], writes=[PB(bk)], sig=(c == 7))
                            yt = ytx[hf]
                            S.op("dve", lambda e: e.tensor_tensor(
                                out=yt[:, :], in0=bank(bk), in1=gate_a[:, hf * 512:(hf + 1) * 512], op=ALU.mult),
                                reads=[PB(bk), ("gate_a", hf)], writes=[("ytx", hf)])
                            S.op("dve", lambda e: e.tensor_tensor(
                                out=x1[i][:, hf * 512:(hf + 1) * 512], in0=x1[i][:, hf * 512:(hf + 1) * 512], in1=yt[:, :], op=ALU.add),
                                reads=[("x1", par, i, hf), ("ytx", hf)], writes=[("x1", par, i, hf)])
                        L.append(opj)
                for i in range(4):
                    def nrm1(i=i):
                        S.op("act", lambda e: e.activation(out=xn2[i][:, :], in_=x1[i][:, :], func=AF.Square,
                                                           accum_out=ss4b[:, i:i + 1]),
                             reads=[("x1", par, i, 0), ("x1", par, i, 1)], writes=[("xn2", i), ("ss4b", i)])
                    L.append(nrm1)

                def nrmr():
                    S.op("dve", lambda e: e.tensor_scalar(out=rs4b[:, :], in0=ss4b[:, :], scalar1=1.0 / D,
                                                          scalar2=EPS, op0=ALU.mult, op1=ALU.add),
                         reads=[("ss4b", i) for i in range(4)], writes=["rs4b"])
                    S.op("act", lambda e: e.activation(out=rs4b[:, :], in_=rs4b[:, :], func=AF.Ln),
                         reads=["rs4b"], writes=["rs4b"])
                    S.op("act", lambda e: e.activation(out=rs4b[:, :], in_=rs4b[:, :], func=AF.Exp, scale=-0.5),
                         reads=["rs4b"], writes=["rs4b"])
                L.append(nrmr)
                for i in range(4):
                    def nrm2(i=i):
                        S.op("dve", lambda e: e.tensor_scalar(out=xn2[i][:, :], in0=x1[i][:, :], scalar1=rs4b[:, i:i + 1],
                                                              scalar2=None, op0=ALU.mult),
                             reads=[("x1", par, i, 0), ("x1", par, i, 1), "rs4b"], writes=[("xn2", i)])
                    L.append(nrm2)
                for bb in range(4):
                    def tr(bb=bb):
                        bk = 4 + bb
                        for cc in range(2):
                            c = 2 * bb + cc
                            for i in range(4):
                                S.op("pe", lambda e, c=c, cc=cc, i=i: e.transpose(
                                    out=bankbf(bk)[:, cc * 512 + i * 128:cc * 512 + (i + 1) * 128],
                                    in_=xn2[i][:, c * 128:(c + 1) * 128], identity=idt[:, :]),
                                    reads=[("xn2", i), "idt"], writes=[PB(bk)], sig=(i == 3))
                    L.append(tr)
                for bb in range(4):
                    def ev(bb=bb):
                        bk = 4 + bb
                        for cc in range(2):
                            c = 2 * bb + cc
                            S.op("act", lambda e, c=c, cc=cc: e.activation(
                                out=hT2[:, c, :], in_=bankbf(bk)[:, cc * 512:(cc + 1) * 512], func=AF.Identity,
                                scale=gsf[:, c:c + 1], bias=shf[:, c:c + 1]),
                                reads=[PB(bk), "gsf", "modsb"], writes=[("hT2", par, c)])
                    L.append(ev)
                return L

            pre_g = []

            def gu_dma(f):
                wi = n_ld["g"] % 3
                n_ld["g"] += 1
                S.dma("sp", wgb[wi][:, :], wg_s.ap()[f], reads=[("wg_s", f)], writes=[("wgb", wi)])
                S.dma("sp", wub[wi][:, :], wu_s.ap()[f], reads=[("wu_s", f)], writes=[("wub", wi)])
                return wi

            def wd_dma(hf, gi):
                wi = n_ld["d"] % 4
                n_ld["d"] += 1
                S.dma("sp", wdb[wi][:, :, :], wd_s.ap()[hf, 2 * gi:2 * gi + 2].rearrange("f p n -> p f n"),
                      reads=[("wd_s", hf)], writes=[("wdb", wi)])
                return wi

            def fn_thunks(tb):
                par = tb % 2
                x1 = x1s[par]
                L = []
                for i in range(4):
                    def sqf(i=i):
                        S.op("act", lambda e: e.activation(out=junk2[:, :], in_=x1[i][:, :], func=AF.Square,
                                                           accum_out=ss4c[:, i:i + 1]),
                             reads=[("x1", par, i, 0), ("x1", par, i, 1)], writes=["junk2", ("ss4c", i)])
                    L.append(sqf)

                def rsf():
                    S.op("dve", lambda e: e.tensor_scalar(out=rs4c[:, :], in0=ss4c[:, :], scalar1=1.0 / D, scalar2=EPS,
                                                          op0=ALU.mult, op1=ALU.add),
                         reads=[("ss4c", i) for i in range(4)], writes=["rs4c"])
                L.append(rsf)

                def lef():
                    S.op("act", lambda e: e.activation(out=rs4c[:, :], in_=rs4c[:, :], func=AF.Ln),
                         reads=["rs4c"], writes=["rs4c"])
                    S.op("act", lambda e: e.activation(out=rs4c[:, :], in_=rs4c[:, :], func=AF.Exp, scale=-0.5),
                         reads=["rs4c"], writes=["rs4c"])
                L.append(lef)
                for i in range(4):
                    def outf(i=i):
                        oi = n_out[0] % 2
                        n_out[0] += 1
                        S.op("dve", lambda e: e.scalar_tensor_tensor(
                            out=ot[oi][:, :], in0=x1[i][:, :], scalar=rs4c[:, i:i + 1], in1=gfin[:, :], op0=ALU.mult, op1=ALU.mult),
                            reads=[("x1", par, i, 0), ("x1", par, i, 1), "rs4c", "gfin"], writes=[("ot", oi)])
                        S.dma("pool", y_v[tb * 4 + i], ot[oi][:, :], reads=[("ot", oi)], writes=[("y", tb * 4 + i)])
                    L.append(outf)
                return L

            pending_fn = []
            for t in x_thunks(0):
                t()
            for tb in range(NB):
                par = tb % 2
                x1 = x1s[par]
                hT2 = hT2s[par]
                filler = list(pending_fn) + (x_thunks(tb + 1) if tb + 1 < NB else [])
                nfl = len(filler)
                emitted = 0
                pre_wd = []
                for f in range(NF):
                    if pre_g:
                        wi = pre_g.pop(0)
                    else:
                        wi = gu_dma(f)
                    if f >= NF - 4:
                        pre_wd.append(wd_dma(0, len(pre_wd)))
                    bg = 2 * (f % 2)
                    bu = bg + 1
                    for c in range(8):
                        S.op("pe", lambda e, c=c, wi=wi, bg=bg: e.matmul(bank(bg), lhsT=wgb[wi][:, c * 128:(c + 1) * 128],
                                                                       rhs=hT2[:, c, :], start=(c == 0), stop=(c == 7)),
                             reads=[("wgb", wi), ("hT2", par, c)], writes=[PB(bg)], sig=(c == 7))
                    for c in range(8):
                        S.op("pe", lambda e, c=c, wi=wi, bu=bu: e.matmul(bank(bu), lhsT=wub[wi][:, c * 128:(c + 1) * 128],
                                                                       rhs=hT2[:, c, :], start=(c == 0), stop=(c == 7)),
                             reads=[("wub", wi), ("hT2", par, c)], writes=[PB(bu)], sig=(c == 7))
                    S.op("act", lambda e, f=f, bg=bg: e.activation(out=sg[f % 2][:, :], in_=bank(bg), func=AF.Silu),
                         reads=[PB(bg)], writes=[("sg", f % 2)])
                    S.op("dve", lambda e, f=f, bu=bu: e.tensor_tensor(out=hid[:, f, :], in0=bank(bu), in1=sg[f % 2][:, :], op=ALU.mult),
                         reads=[PB(bu), ("sg", f % 2)], writes=[("hid", f)])
                    target = min(nfl, (nfl * (f + 1)) // (NF - 2))
                    while emitted < target:
                        filler[emitted]()
                        emitted += 1
                while emitted < nfl:
                    filler[emitted]()
                    emitted += 1
                if tb + 1 < NB:
                    for f2 in range(3):
                        pre_g.append(gu_dma(f2))
                for hf in range(2):
                    dbk = 4 if hf == 0 else 0
                    for gi in range(NF // 2):
                        if hf == 0 and gi < len(pre_wd):
                            wi = pre_wd[gi]
                        else:
                            wi = wd_dma(hf, gi)
                        for ff in range(2):
                            f = 2 * gi + ff
                            for i in range(4):
                                S.op("pe", lambda e, f=f, ff=ff, i=i, wi=wi: e.matmul(
                                    bank(dbk + i), lhsT=hid[:, f, i * 128:(i + 1) * 128], rhs=wdb[wi][:, ff, :],
                                    start=(f == 0), stop=(f == NF - 1)),
                                    reads=[("hid", f), ("wdb", wi)], writes=[PB(dbk + i)], sig=(f == NF - 1 or i == 3))
                    for i in range(4):
                        yt = ytmp[i % 2]
                        S.op("dve", lambda e, i=i, hf=hf, yt=yt: e.tensor_tensor(
                            out=yt[:, :], in0=bank(dbk + i), in1=gate_f[:, hf * 512:(hf + 1) * 512], op=ALU.mult),
                            reads=[PB(dbk + i), ("gate_f", hf)], writes=[("ytmp", i % 2)])
                        S.op("dve", lambda e, i=i, hf=hf, yt=yt: e.tensor_tensor(
                            out=x1[i][:, hf * 512:(hf + 1) * 512], in0=x1[i][:, hf * 512:(hf + 1) * 512], in1=yt[:, :], op=ALU.add),
                            reads=[("x1", par, i, hf), ("ytmp", i % 2)], writes=[("x1", par, i, hf)])
                pending_fn = fn_thunks(tb)
                if tb + 1 == NB:
                    for t in pending_fn:
                        t()
            S.barrier()
        S.finish("sp")
    return nc


_CACHE = {}


def _get_program():
    if "nc" not in _CACHE:
        _CACHE["nc"] = build_program()
    return _CACHE["nc"]


def make_in_maps(x, c, w_ada, b_ada, g_attn, w_in, g_q, g_k, rpb, w_o, g_ffn, w_gate, w_up, w_down, g_final):
    f32 = lambda a: np.ascontiguousarray(np.asarray(a, dtype=np.float32))
    colmajor = lambda v, n: f32(np.asarray(v, np.float32).reshape(n, 128).T)
    b_int, b_sp = build_bias_tiles(np.asarray(rpb)[0])
    sel = np.zeros((128, 128), np.float32)
    sel[64, :] = 1.0
    shared = {
        "w_ada": f32(np.asarray(w_ada)[0]),
        "bada": colmajor(np.asarray(b_ada)[0], 48),
        "gattn": colmajor(np.asarray(g_attn)[0], 8),
        "gffn": colmajor(np.asarray(g_ffn)[0], 8),
        "gfin": f32(np.broadcast_to(np.asarray(g_final, np.float32)[None, :], (128, D))),
        "w_in": f32(np.asarray(w_in)[0]),
        "gq": f32(np.broadcast_to(np.asarray(g_q, np.float32)[0][None, :], (128, 64))),
        "gk": f32(np.broadcast_to(np.asarray(g_k, np.float32)[0][None, :], (128, 64))),
        "bint": b_int,
        "bsp": b_sp,
        "w_o": f32(np.asarray(w_o)[0]),
        "w_gate": f32(np.asarray(w_gate)[0]),
        "w_up": f32(np.asarray(w_up)[0]),
        "w_down": f32(np.asarray(w_down)[0]),
        "ident": np.eye(128, dtype=np.float32).astype(ml_dtypes.bfloat16),
        "identf": np.eye(128, dtype=np.float32),
        "sel64": sel,
        "cs": rope_table(),
    }
    x = np.asarray(x, np.float32)
    c = np.asarray(c, np.float32)
    maps = []
    for b in range(x.shape[0]):
        m = dict(shared)
        m["x"] = np.ascontiguousarray(x[b])
        m["ccol"] = colmajor(c[b], 8)
        maps.append(m)
    return maps


def kernel(x, c, w_ada, b_ada, g_attn, w_in, g_q, g_k, rpb, w_o, g_ffn, w_gate, w_up, w_down, g_final):
    nc = _get_program()
    in_maps = make_in_maps(x, c, w_ada, b_ada, g_attn, w_in, g_q, g_k, rpb, w_o, g_ffn, w_gate, w_up, w_down, g_final)
    res = run_bass_kernel_spmd(nc, in_maps, core_ids=list(range(N_CORES)))
    out = np.stack([np.asarray(r["y"], dtype=np.float32) for r in res.results], axis=0)
    return out
```

```python
import numpy as np
import ml_dtypes
from contextlib import ExitStack
import concourse.bass as bass
import concourse.mybir as mybir
from concourse.bass_utils import run_bass_kernel_spmd

F32 = mybir.dt.float32
BF16 = mybir.dt.bfloat16
AF = mybir.ActivationFunctionType
ALU = mybir.AluOpType
AX = mybir.AxisListType

T = 4096
D = 1024
DFF = 2816
NF = DFF // 128
NB = 8
EPS = 1e-6
NEG = -30000.0
N_CORES = 8


class Sched:
    N_DMA_SEMS = 12

    def __init__(self, nc, stack):
        self.nc = nc
        self.eng = {"pe": nc.tensor, "act": nc.scalar, "dve": nc.vector,
                    "pool": nc.gpsimd, "sp": nc.sync}
        self.sem = {e: stack.enter_context(nc.semaphore("s_" + e)) for e in self.eng}
        self.cnt = {e: 0 for e in self.eng}
        self.dsem = {q: [stack.enter_context(nc.semaphore(f"d_{q}{i}")) for i in range(self.N_DMA_SEMS)]
                     for q in ("sp", "pool", "act")}
        self.dcnt = {q: [0] * self.N_DMA_SEMS for q in self.dsem}
        self.dnext = {q: 0 for q in self.dsem}
        self.dlast = {q: [None] * self.N_DMA_SEMS for q in self.dsem}
        self.seen = {e: {} for e in self.eng}
        self.res = {}
        self.nwaits = 0
        self.nops = {e: 0 for e in self.eng}

    def _wait(self, e, tok):
        sem, val, peng = tok
        key = sem.name
        if self.seen[e].get(key, 0) >= val:
            return
        self.eng[e].wait_ge(sem, val)
        self.seen[e][key] = val
        self.nwaits += 1

    def _deps(self, e, reads, writes):
        toks = []
        for r in reads:
            st = self.res.get(r)
            if st and st[0] is not None:
                toks.append((st[0], "raw"))
        for w in writes:
            st = self.res.get(w)
            if st:
                if st[0] is not None:
                    toks.append((st[0], "waw"))
                for t in st[1]:
                    toks.append((t, "war"))
        for tok, kind in toks:
            if tok[2] == e and e == "pe":
                continue
            self._wait(e, tok)

    def _commit(self, tok, reads, writes):
        for r in reads:
            st = self.res.setdefault(r, [None, []])
            st[1].append(tok)
            if len(st[1]) > 64:
                best = {}
                for t in st[1]:
                    k = t[0].name
                    if k not in best or best[k][1] < t[1]:
                        best[k] = t
                st[1] = list(best.values())
        for w in writes:
            self.res[w] = [tok, []]

    def op(self, e, fn, reads=(), writes=(), sig=True):
        pbr = [r for r in reads if isinstance(r, tuple) and r[0] == "pb"]
        if pbr:
            writes = list(writes) + [r for r in pbr if r not in writes]
            reads = [r for r in reads if not (isinstance(r, tuple) and r[0] == "pb")]
        self._deps(e, reads, writes)
        ins = fn(self.eng[e])
        self.nops[e] += 1
        if sig:
            self.cnt[e] += 1
            ins.then_inc(self.sem[e], 1)
            tok = (self.sem[e], self.cnt[e], e)
        else:
            tok = (self.sem[e], self.cnt[e] + 1, e)
        self._commit(tok, reads, writes)
        return ins

    def dma(self, q, out, in_, reads=(), writes=(), **kw):
        self._deps(q, reads, writes)
        i = self.dnext[q]
        self.dnext[q] = (i + 1) % self.N_DMA_SEMS
        prev = self.dlast[q][i]
        if prev is not None:
            self._wait(q, prev)
        self.dcnt[q][i] += 16
        ins = self.eng[q].dma_start(out=out, in_=in_, **kw)
        ins.then_inc(self.dsem[q][i], 16)
        tok = (self.dsem[q][i], self.dcnt[q][i], None)
        self.dlast[q][i] = tok
        self._commit(tok, reads, writes)
        return ins

    def all_tokens(self):
        toks = []
        for e in self.eng:
            if self.cnt[e] > 0:
                toks.append((self.sem[e], self.cnt[e], e))
        for q in self.dsem:
            for t in self.dlast[q]:
                if t is not None:
                    toks.append(t)
        return toks

    def barrier(self):
        toks = self.all_tokens()
        for e in self.eng:
            for t in toks:
                if t[2] == e:
                    continue
                self._wait(e, t)
        self.res = {}

    def finish(self, e="sp"):
        for t in self.all_tokens():
            if t[2] != e:
                self._wait(e, t)


def _rs(r):
    return min(max(r - 4, 0), 56)


def _cs(w):
    return min(max(w - 8, 0), 48)


def _chunk_rows(j):
    rows = [r for r in range(64) if any(_rs(r) <= ka < _rs(r) + 8 for ka in (2 * j, 2 * j + 1))]
    return rows[0], rows[-1]


SPECIAL = {2: 0, 3: 1, 28: 2, 29: 3}


def _tile_row0(j):
    return _chunk_rows(j)[0] if j in SPECIAL else 2 * j - 4


def _bias_block(rpb_h, ka, r):
    blk = np.full((64, 64), NEG, np.float32)
    if not (_rs(r) <= ka < _rs(r) + 8):
        return blk
    a = ka - r + 7
    for w in range(64):
        c0 = _cs(w)
        kc = np.arange(c0, c0 + 16)
        blk[kc, w] = rpb_h[a, kc - w + 15]
    return blk


def build_bias_tiles(rpb):
    rpb = np.asarray(rpb, np.float32)
    b_int = np.full((8, 128, 640), NEG, np.float32)
    b_sp = np.full((8, 4, 128, 768), NEG, np.float32)
    j0 = 10
    for h in range(8):
        for kdr in range(2):
            for qr in range(10):
                b_int[h, kdr * 64:(kdr + 1) * 64, qr * 64:(qr + 1) * 64] = \
                    _bias_block(rpb[h], 2 * j0 + kdr, 2 * j0 - 4 + qr)
        for j, si in SPECIAL.items():
            r0, r1 = _chunk_rows(j)
            for kdr in range(2):
                for r in range(r0, r1 + 1):
                    b_sp[h, si, kdr * 64:(kdr + 1) * 64, (r - r0) * 64:(r - r0 + 1) * 64] = \
                        _bias_block(rpb[h], 2 * j + kdr, r)
    return b_int, b_sp


def rope_table():
    t = np.arange(T)
    row = (t // 64).astype(np.float64)
    col = (t % 64).astype(np.float64)
    inv = 10000.0 ** (-np.arange(0, 32, 2, dtype=np.float64) / 32)
    ang = np.concatenate([row[:, None] * inv[None, :], col[:, None] * inv[None, :]], axis=-1)
    return np.concatenate([np.cos(ang), np.sin(ang)], axis=-1).astype(np.float32)


def build_program(stop_after=None, debug=False):
    nc = bass.Bass("TRN2", target_bir_lowering=False)
    dt_in = lambda name, shape, dt=F32: nc.dram_tensor(name, shape, dt, kind="ExternalInput")
    x_d = dt_in("x", [T, D])
    ccol_d = dt_in("ccol", [128, 8])
    wada_d = dt_in("w_ada", [D, 6 * D])
    bada_d = dt_in("bada", [128, 48])
    gattn_d = dt_in("gattn", [128, 8])
    gffn_d = dt_in("gffn", [128, 8])
    gfin_d = dt_in("gfin", [128, D])
    win_d = dt_in("w_in", [D, 2304])
    gq_d = dt_in("gq", [128, 64])
    gk_d = dt_in("gk", [128, 64])
    bint_d = dt_in("bint", [8, 128, 640])
    bsp_d = dt_in("bsp", [8, 4, 128, 768])
    wo_d = dt_in("w_o", [D, D])
    wg_d = dt_in("w_gate", [D, DFF])
    wu_d = dt_in("w_up", [D, DFF])
    wd_d = dt_in("w_down", [DFF, D])
    ident_d = dt_in("ident", [128, 128], BF16)
    identf_d = dt_in("identf", [128, 128])
    sel_d = dt_in("sel64", [128, 128])
    cs_d = dt_in("cs", [T, 64])
    y_d = nc.dram_tensor("y", [T, D], F32, kind="ExternalOutput")
    wg_s = nc.dram_tensor("wg_s", [NF, 128, D], BF16)
    wu_s = nc.dram_tensor("wu_s", [NF, 128, D], BF16)
    wd_s = nc.dram_tensor("wd_s", [2, NF, 128, 512], BF16)
    ot_s = nc.dram_tensor("ot_s", [8, 128, T], BF16)
    dbg = {}
    if debug:
        dbg["d_mod"] = nc.dram_tensor("d_mod", [128, 48], F32, kind="ExternalOutput")
        dbg["d_ot"] = nc.dram_tensor("d_ot", [8, 128, T], BF16, kind="ExternalOutput")

    with ExitStack() as st:
        S = Sched(nc, st)
        ps = st.enter_context(nc.psum_tensor("ps", [128, 8 * 512], F32))

        def bank(b, n=512, off=0):
            return ps[:, b * 512 + off: b * 512 + off + n]

        def bankbf(b):
            return ps[:, b * 512:(b + 1) * 512].bitcast(BF16)

        PB = lambda b: ("pb", b)

        def sbt(stack, name, shape, dt):
            return stack.enter_context(nc.sbuf_tensor("sb_" + name, shape, dt))

        idt = sbt(st, "idt", [128, 128], BF16)
        identf = sbt(st, "identf", [128, 128], F32)
        sel64 = sbt(st, "sel64", [128, 128], F32)
        onesf = sbt(st, "onesf", [128, 128], F32)
        mh = sbt(st, "mh", [128, 8], F32)
        modsb = sbt(st, "modsb", [128, 48], F32)
        gsa = sbt(st, "gsa", [128, 8], F32)
        gsf = sbt(st, "gsf", [128, 8], F32)
        rec = sbt(st, "rec", [128, 512], F32)
        bcsb = sbt(st, "bcsb", [64, 512], F32)

        S.dma("sp", idt[:, :], ident_d.ap(), writes=["idt"])
        S.dma("sp", identf[:, :], identf_d.ap(), writes=["identf"])
        S.dma("sp", sel64[:, :], sel_d.ap(), writes=["sel64"])
        S.op("dve", lambda e: e.memset(onesf[:, :], 1.0), writes=["onesf"])
        S.op("dve", lambda e: e.memset(mh[:, :], -0.5), writes=["mh"])
        S.op("dve", lambda e: e.memset(rec[:, :], 0.0), writes=["rec"])

        prep = []
        for f in range(NF):
            prep.append(lambda f=f: S.dma("pool", wg_s.ap()[f].rearrange("p (c j) -> p c j", c=8),
                                          wg_d.ap()[:, f * 128:(f + 1) * 128].rearrange("(c p) j -> p c j", p=128),
                                          writes=[("wg_s", f)]))
            prep.append(lambda f=f: S.dma("pool", wu_s.ap()[f].rearrange("p (c j) -> p c j", c=8),
                                          wu_d.ap()[:, f * 128:(f + 1) * 128].rearrange("(c p) j -> p c j", p=128),
                                          writes=[("wu_s", f)]))
        for hf in range(2):
            prep.append(lambda hf=hf: S.dma("pool", wd_s.ap()[hf].rearrange("f p n -> p f n"),
                                            wd_d.ap()[:, hf * 512:(hf + 1) * 512].rearrange("(f p) n -> p f n", p=128),
                                            writes=[("wd_s", hf)]))

        ccol = sbt(st, "ccol", [128, 8], F32)
        ctmp = sbt(st, "ctmp", [128, 8], F32)
        cact = sbt(st, "cact", [128, 8], BF16)
        bada = sbt(st, "bada", [128, 48], F32)
        gat = sbt(st, "gat", [128, 8], F32)
        gff = sbt(st, "gff", [128, 8], F32)
        S.dma("sp", ccol[:, :], ccol_d.ap(), writes=["ccol"])
        S.dma("sp", bada[:, :], bada_d.ap(), writes=["bada"])
        S.dma("sp", gat[:, :], gattn_d.ap(), writes=["gat"])
        S.dma("sp", gff[:, :], gffn_d.ap(), writes=["gff"])
        S.op("act", lambda e: e.activation(out=ctmp[:, :], in_=ccol[:, :], func=AF.Exp, scale=-1.0),
             reads=["ccol"], writes=["ctmp"])
        S.op("dve", lambda e: e.tensor_scalar(out=ctmp[:, :], in0=ctmp[:, :], scalar1=1.0, scalar2=None, op0=ALU.add),
             reads=["ctmp"], writes=["ctmp"])
        S.op("dve", lambda e: e.reciprocal(out=ctmp[:, :], in_=ctmp[:, :]), reads=["ctmp"], writes=["ctmp"])
        S.op("dve", lambda e: e.tensor_tensor(out=cact[:, :], in0=ctmp[:, :], in1=ccol[:, :], op=ALU.mult),
             reads=["ctmp", "ccol"], writes=["cact"])
        wada_v = wada_d.ap().rearrange("(k p) n -> p k n", p=128)
        sha = modsb[:, 0:8]
        shf = modsb[:, 24:32]

        with ExitStack() as s1:
            win = sbt(s1, "win", [128, 8, 2304], BF16)
            cs = sbt(s1, "cs", [128, 32, 64], F32)
            gqb = sbt(s1, "gqb", [128, 64], F32)
            gkb = sbt(s1, "gkb", [128, 64], F32)
            KTg = sbt(s1, "KTg", [128, T], BF16)
            Vg = sbt(s1, "Vg", [128, 32, 2, 128], BF16)
            bint = sbt(s1, "bint", [128, 8, 640], F32)
            bsp = [sbt(s1, f"bsp{i}", [128, 768], F32) for i in range(3)]
            kTn = sbt(s1, "kTn", [128, 3, 4, 512], BF16)
            Vn = sbt(s1, "Vn", [128, 12, 8, 65], BF16)
            qTlo = sbt(s1, "qTlo", [128, 4, 512], BF16)
            qThi = sbt(s1, "qThi", [128, 4, 512], BF16)
            QTlo = sbt(s1, "QTlo", [128, 4, 512], BF16)
            QThi = sbt(s1, "QThi", [128, 4, 512], BF16)
            xt = [sbt(s1, f"xt{i}", [128, D], F32) for i in range(2)]
            xn = [sbt(s1, f"xn{i}", [128, D], BF16) for i in range(4)]
            ss4 = sbt(s1, "ss4", [128, 4], F32)
            rs4 = sbt(s1, "rs4", [128, 4], F32)
            hT = sbt(s1, "hT", [128, 8, 512], BF16)
            qsb = [sbt(s1, f"qsb{i}", [128, 512], F32) for i in range(2)]
            qtmp = sbt(s1, "qtmp", [128, 512], F32)
            ssq = sbt(s1, "ssq", [128, 8], F32)
            rq = sbt(s1, "rq", [128, 8], F32)
            rt1 = sbt(s1, "rt1", [128, 8, 32], F32)
            rt2 = sbt(s1, "rt2", [128, 8, 32], F32)
            qhat = sbt(s1, "qhat", [128, 4, 512], BF16)
            khat = sbt(s1, "khat", [128, 4, 128], BF16)
            Ssb = [sbt(s1, f"Ssb{i}", [128, 512], F32) for i in range(3)]
            Pna = [sbt(s1, f"Pna{i}", [128, 512], BF16) for i in range(3)]
            Pg = [sbt(s1, f"Pg{i}", [128, 1024], BF16) for i in range(2)]
            OT = sbt(s1, "OT", [128, 8, 512], BF16)
            rg = [sbt(s1, f"rg{i}", [64, 512], F32) for i in range(2)]

            wa = [kTn[:, 0:2, :, :].rearrange("p a m n -> p (a m) n"),
                  Vn.reshape([128, 12 * 8 * 65])[:, 0:4096].rearrange("p (k n) -> p k n", k=8)]

            def ada_dma(g):
                S.dma("pool", wa[g % 2], wada_v[:, :, g * 512:(g + 1) * 512], writes=[("wa", g % 2)])

            def ada_mm(g):
                wb = wa[g % 2]
                for jj in range(4):
                    j = g * 4 + jj
                    for k in range(8):
                        S.op("pe", lambda e, j=j, jj=jj, k=k: e.matmul(
                            bank(7, 1, j), lhsT=wb[:, k, jj * 128:(jj + 1) * 128], rhs=cact[:, k:k + 1],
                            start=(k == 0), stop=(k == 7), skip_group_check=True),
                            reads=[("wa", g % 2), "cact"], writes=[PB(7)], sig=(k == 7))

            ada_dma(0)
            ada_dma(1)
            S.dma("sp", cs[:, :, :], cs_d.ap().rearrange("(tt p) k -> p tt k", p=128), writes=["cs"])
            S.dma("sp", gqb[:, :], gq_d.ap(), writes=["gqb"])
            S.dma("sp", gkb[:, :], gk_d.ap(), writes=["gkb"])
            bflat = bint[:, :, :].rearrange("p h n -> p (h n)")
            xt4 = [xt[0][:, :], xt[1][:, :], bflat[:, 0:D], bflat[:, D:2 * D]]
            S.op("dve", lambda e: e.memset(Vg[:, :, :, 64:128], 1.0), writes=["Vg1"])
            for tns, nm in ((qTlo, "qTlo0"), (qThi, "qThi0"), (QTlo, "QTlo0"), (QThi, "QThi0")):
                S.op("dve", lambda e, tns=tns: e.memset(tns[:, :, :], 0.0), writes=[nm])
            zero_deps = {"qTlo": "qTlo0", "qThi": "qThi0", "QTlo": "QTlo0", "QThi": "QThi0"}

            x_v = x_d.ap().rearrange("(tt p) d -> tt p d", p=128)

            def norm_block(tb, gs, sh, dst=None, key="hT"):
                dst = hT if dst is None else dst
                for i in range(4):
                    xb = xt4[i]
                    S.dma("sp", xb, x_v[tb * 4 + i], writes=[("xt", i)])
                    S.op("act", lambda e, i=i, xb=xb: e.activation(out=xn[i][:, :], in_=xb, func=AF.Square,
                                                                   accum_out=ss4[:, i:i + 1]),
                         reads=[("xt", i)], writes=[("xn", i), ("ss4", i)])
                    S.op("dve", lambda e, i=i: e.tensor_scalar(out=rs4[:, i:i + 1], in0=ss4[:, i:i + 1], scalar1=1.0 / D,
                                                               scalar2=EPS, op0=ALU.mult, op1=ALU.add),
                         reads=[("ss4", i)], writes=[("rs4", i)])
                    S.op("act", lambda e, i=i: e.activation(out=rs4[:, i:i + 1], in_=rs4[:, i:i + 1], func=AF.Ln),
                         reads=[("rs4", i), "mh"], writes=[("rs4", i)])
                    S.op("act", lambda e, i=i: e.activation(out=rs4[:, i:i + 1], in_=rs4[:, i:i + 1], func=AF.Exp, scale=-0.5),
                         reads=[("rs4", i), "mh"], writes=[("rs4", i)])
                    S.op("dve", lambda e, i=i, xb=xb: e.tensor_scalar(out=xn[i][:, :], in0=xb, scalar1=rs4[:, i:i + 1],
                                                                      scalar2=None, op0=ALU.mult),
                         reads=[("xt", i), ("rs4", i)], writes=[("xn", i)])
                for c in range(8):
                    bk = c // 2
                    for i in range(4):
                        S.op("pe", lambda e, c=c, i=i, bk=bk: e.transpose(
                            out=bankbf(bk)[:, (c % 2) * 512 + i * 128:(c % 2) * 512 + (i + 1) * 128],
                            in_=xn[i][:, c * 128:(c + 1) * 128], identity=idt[:, :]),
                            reads=[("xn", i), "idt"], writes=[PB(bk)], sig=(i == 3))
                for c in range(8):
                    bk = c // 2
                    if c % 2 == 0:
                        S.op("act", lambda e, c=c, bk=bk: e.activation(
                            out=dst[:, c, :], in_=bankbf(bk)[:, (c % 2) * 512:(c % 2 + 1) * 512], func=AF.Identity,
                            scale=gs[:, c:c + 1], bias=sh[:, c:c + 1]),
                            reads=[PB(bk), "gsa", "modsb"], writes=[(key, c)])
                    else:
                        S.op("dve", lambda e, c=c, bk=bk: e.tensor_scalar(
                            out=dst[:, c, :], in0=bankbf(bk)[:, (c % 2) * 512:(c % 2 + 1) * 512],
                            scalar1=gs[:, c:c + 1], scalar2=sh[:, c:c + 1], op0=ALU.mult, op1=ALU.add),
                            reads=[PB(bk), "gsa", "modsb"], writes=[(key, c)])

            def qk_post(src_ap, nheads, gb, gname, tt, dst_fn, dst_keys, sb_i, grp=None):
                H = nheads
                W = H * 64
                v3 = lambda ap: ap.rearrange("p (h d) -> p h d", d=64)
                src3 = v3(src_ap)
                tmp3 = v3(qtmp[:, 0:W])
                S.op("dve", lambda e: e.tensor_tensor(out=qtmp[:, 0:W], in0=src_ap, in1=src_ap, op=ALU.mult),
                     reads=[("qsb", sb_i)], writes=["qtmp"])
                S.op("dve", lambda e: e.tensor_reduce(out=ssq[:, 0:H], in_=tmp3, axis=AX.X, op=ALU.add),
                     reads=["qtmp"], writes=["ssq"])
                S.op("dve", lambda e: e.tensor_scalar(out=rq[:, 0:H], in0=ssq[:, 0:H], scalar1=1.0 / 64, scalar2=EPS,
                                                      op0=ALU.mult, op1=ALU.add), reads=["ssq"], writes=["rq"])
                yield
                yield
                S.op("act", lambda e: e.activation(out=rq[:, 0:H], in_=rq[:, 0:H], func=AF.Ln),
                     reads=["rq", "mh"], writes=["rq"])
                S.op("act", lambda e: e.activation(out=rq[:, 0:H], in_=rq[:, 0:H], func=AF.Exp, scale=-0.5),
                     reads=["rq", "mh"], writes=["rq"])
                yield
                yield
                S.op("dve", lambda e: e.tensor_tensor(out=tmp3, in0=src3, in1=rq[:, 0:H].unsqueeze(2).to_broadcast([128, H, 64]),
                                                      op=ALU.mult), reads=[("qsb", sb_i), "rq"], writes=["qtmp"])
                S.op("dve", lambda e: e.tensor_tensor(out=tmp3, in0=tmp3, in1=gb[:, :].unsqueeze(1).to_broadcast([128, H, 64]),
                                                      op=ALU.mult), reads=["qtmp", gname], writes=["qtmp"])
                x1 = tmp3[:, :, 0:32]
                x2 = tmp3[:, :, 32:64]
                t1 = rt1[:, 0:H, :]
                t2 = rt2[:, 0:H, :]
                if grp is None:
                    cosb = cs[:, tt, 0:32].unsqueeze(1).to_broadcast([128, H, 32])
                    sinb = cs[:, tt, 32:64].unsqueeze(1).to_broadcast([128, H, 32])
                else:
                    A, B = grp
                    f4 = lambda ap: ap.rearrange("p (a b) d -> p a b d", a=A)
                    x1, x2, t1, t2 = f4(x1), f4(x2), f4(t1), f4(t2)
                    cosb = cs[:, tt:tt + A, 0:32].unsqueeze(2).to_broadcast([128, A, B, 32])
                    sinb = cs[:, tt:tt + A, 32:64].unsqueeze(2).to_broadcast([128, A, B, 32])
                    _dst = dst_fn
                    dst_fn = lambda half: f4(_dst(half))
                S.op("dve", lambda e: e.tensor_tensor(out=t1, in0=x1, in1=cosb, op=ALU.mult), reads=["qtmp", "cs"], writes=["rt1"])
                S.op("dve", lambda e: e.tensor_tensor(out=t2, in0=x2, in1=sinb, op=ALU.mult), reads=["qtmp", "cs"], writes=["rt2"])
                S.op("dve", lambda e: e.tensor_tensor(out=dst_fn(0), in0=t1, in1=t2, op=ALU.subtract),
                     reads=["rt1", "rt2"], writes=dst_keys)
                S.op("dve", lambda e: e.tensor_tensor(out=t1, in0=x1, in1=sinb, op=ALU.mult), reads=["qtmp", "cs"], writes=["rt1"])
                S.op("dve", lambda e: e.tensor_tensor(out=t2, in0=x2, in1=cosb, op=ALU.mult), reads=["qtmp", "cs"], writes=["rt2"])
                S.op("dve", lambda e: e.tensor_tensor(out=dst_fn(1), in0=t1, in1=t2, op=ALU.add),
                     reads=["rt1", "rt2"], writes=dst_keys)

            def p1a_post(tb, hb, hk):
                for i in range(4):
                    bk = 4 + i // 2
                    for c in range(8):
                        S.op("pe", lambda e, c=c, i=i, bk=bk: e.matmul(bank(bk, 256, (i % 2) * 256), lhsT=hb[:, c, i * 128:(i + 1) * 128],
                                                                     rhs=win[:, c, 2048:2304], start=(c == 0), stop=(c == 7),
                                                                     skip_group_check=True),
                             reads=[(hk, c), "win"], writes=[PB(bk)], sig=(c == 7))
                for bb in range(2):
                    src = bank(4 + bb).rearrange("p (t x) -> p t x", t=2)
                    S.op("act", lambda e, bb=bb, src=src: e.copy(
                        out=qsb[0][:, bb * 256:(bb + 1) * 256].rearrange("p (t x) -> p t x", t=2), in_=src[:, :, 0:128]),
                        reads=[PB(4 + bb)], writes=[("qsb", 0)])
                    tt0 = tb * 4 + 2 * bb
                    S.op("dve", lambda e, tt0=tt0, src=src: e.tensor_copy(
                        out=Vg[:, tt0:tt0 + 2, :, 0:64], in_=src[:, :, 128:256].rearrange("p t (h d) -> p t h d", d=64)),
                        reads=[PB(4 + bb)], writes=[("Vg", tt0), ("Vg", tt0 + 1)])
                kh3 = khat[:, :, :].rearrange("p t (h d) -> p (t h) d", d=64)
                for _ in qk_post(qsb[0][:, :], 8, gkb, "gkb", tb * 4,
                                 lambda half, kh3=kh3: kh3[:, :, half * 32:(half + 1) * 32], ["khat"], 0, grp=(4, 2)):
                    pass
                for i in range(4):
                    S.op("pe", lambda e, i=i: e.transpose(out=bankbf(6)[:, i * 128:(i + 1) * 128], in_=khat[:, i, :], identity=idt[:, :]),
                         reads=["khat", "idt"], writes=[PB(6)], sig=(i == 3))
                S.op("act", lambda e, tb=tb: e.copy(out=KTg[:, tb * 512:(tb + 1) * 512], in_=bankbf(6)[:, 0:512]),
                     reads=[PB(6)], writes=[("KTg", tb * 4 + i) for i in range(4)])

            def stage_a(tb, part="all"):
                slot = tb % 3
                if part in ("all", "kv"):
                    norm_block(tb, gsa, sha)
                n_mm = 0
                for m in range(4):
                    for which in range(2):
                        if which == 0 and part == "kv":
                            continue
                        if which == 1 and part == "q":
                            continue
                        bk = 4 + (n_mm % 2)
                        n_mm += 1
                        col0 = which * 512 + m * 128
                        for c in range(8):
                            S.op("pe", lambda e, c=c, bk=bk, col0=col0: e.matmul(
                                bank(bk), lhsT=win[:, c, col0:col0 + 128], rhs=hT[:, c, :], start=(c == 0), stop=(c == 7)),
                                reads=[("hT", c), "win"], writes=[PB(bk)], sig=(c == 7))
                        if which == 0:
                            S.op("act", lambda e, bk=bk, m=m: e.copy(out=qTlo[0:64, m, :], in_=bank(bk)[0:64, :]),
                                 reads=[PB(bk), "qTlo0"], writes=[("qTlo", m)])
                            S.op("dve", lambda e, bk=bk, m=m: e.tensor_copy(out=qThi[64:128, m, :], in_=bank(bk)[64:128, :]),
                                 reads=[PB(bk), "qThi0"], writes=[("qThi", m)])
                        else:
                            S.op("act", lambda e, bk=bk, m=m: e.copy(out=kTn[:, slot, m, :], in_=bank(bk)),
                                 reads=[PB(bk)], writes=[("kTn", slot, m)])
                for i in range(4 if part in ("all", "kv") else 0):
                    tt = tb * 4 + i
                    bk = 4 + (i % 2)
                    for c in range(8):
                        S.op("pe", lambda e, c=c, i=i, bk=bk: e.matmul(bank(bk), lhsT=hT[:, c, i * 128:(i + 1) * 128],
                                                                     rhs=win[:, c, 1024:1536], start=(c == 0), stop=(c == 7)),
                             reads=[("hT", c), "win"], writes=[PB(bk)], sig=(c == 7))
                    S.op("act", lambda e, bk=bk, i=i: e.copy(
                        out=Vn[:, slot * 4 + i, :, 0:64], in_=bank(bk).rearrange("p (h d) -> p h d", d=64)),
                        reads=[PB(bk), "Vn1"], writes=[("Vn", slot * 4 + i)])
                if part == "kv":
                    return
                for i in range(4):
                    tt = tb * 4 + i
                    bk = 6 + (i % 2)
                    for c in range(8):
                        S.op("pe", lambda e, c=c, i=i, bk=bk: e.matmul(bank(bk), lhsT=hT[:, c, i * 128:(i + 1) * 128],
                                                                     rhs=win[:, c, 1536:2048], start=(c == 0), stop=(c == 7)),
                             reads=[("hT", c), "win"], writes=[PB(bk)], sig=(c == 7))
                    sbi = i % 2
                    S.op("act", lambda e, bk=bk, sbi=sbi: e.copy(
                        out=qsb[sbi][:, :].rearrange("p (g kv d) -> p g kv d", g=4, kv=2),
                        in_=bank(bk).rearrange("p (kv g d) -> p g kv d", g=4, kv=2)),
                        reads=[PB(bk)], writes=[("qsb", sbi)])
                    qh3 = qhat[:, i, :].rearrange("p (h d) -> p h d", d=64)
                    for _ in qk_post(qsb[sbi][:, :], 8, gqb, "gqb", tt,
                                     lambda half, qh3=qh3: qh3[:, :, half * 32:(half + 1) * 32], [("qhat", i)], sbi):
                        pass
                for g in range(4):
                    bk = 4 + g // 2
                    for i in range(4):
                        S.op("pe", lambda e, g=g, i=i, bk=bk: e.transpose(
                            out=bankbf(bk)[:, (g % 2) * 512 + i * 128:(g % 2) * 512 + (i + 1) * 128],
                            in_=qhat[:, i, g * 128:(g + 1) * 128], identity=idt[:, :]),
                            reads=[("qhat", i), "idt"], writes=[PB(bk)], sig=(i == 3))
                for g in range(4):
                    bk = 4 + g // 2
                    src = bankbf(bk)[:, (g % 2) * 512:(g % 2 + 1) * 512]
                    S.op("act", lambda e, g=g, src=src: e.copy(out=QTlo[0:64, g, :], in_=src[0:64, :]),
                         reads=[PB(bk), "QTlo0"], writes=[("QTlo", g)])
                    S.op("dve", lambda e, g=g, src=src: e.tensor_copy(out=QThi[64:128, g, :], in_=src[64:128, :]),
                         reads=[PB(bk), "QThi0"], writes=[("QThi", g)])

            def normalize_head(acc_bk, h_chunk, half, bc_bk):
                S.op("dve", lambda e: e.reciprocal(out=rec[64:65, :], in_=bank(acc_bk)[64:65, :]),
                     reads=[PB(acc_bk)], writes=["rec"])
                S.op("pe", lambda e: e.matmul(bank(bc_bk)[0:128, :], lhsT=sel64[:, :], rhs=rec[:, :], start=True, stop=True),
                     reads=["sel64", "rec"], writes=[PB(bc_bk)])
                S.op("act", lambda e: e.copy(out=bcsb[:, :], in_=bank(bc_bk)[0:64, :]), reads=[PB(bc_bk)], writes=["bcsb"])
                S.op("dve", lambda e: e.tensor_tensor(out=OT[half * 64:(half + 1) * 64, h_chunk, :], in0=bank(acc_bk)[0:64, :],
                                                      in1=bcsb[:, :], op=ALU.mult),
                     reads=[PB(acc_bk), "bcsb"], writes=[("OT", h_chunk, half)])

            sp_loaded = {}
            sp_next = [0]

            def na_block(tb):
                steps = []
                for h in range(8):
                    items = []
                    for j in range(max(0, 4 * tb - 2), min(31, 4 * tb + 5) + 1):
                        r0, r1 = _chunk_rows(j)
                        lo, hi = max(8 * tb, r0), min(8 * tb + 7, r1)
                        if lo <= hi:
                            items.append((j, lo, hi))
                    for idx, (j, lo, hi) in enumerate(items):
                        steps.append(dict(h=h, j=j, lo=lo, hi=hi, first=(idx == 0), last=(idx == len(items) - 1)))
                n = len(steps)
                for i, stp in enumerate(steps):
                    stp["sbk"] = 3 + (i % 3)
                    stp["bi"] = i % 3
                    stp["nq"] = (stp["hi"] - stp["lo"] + 1) * 64
                    stp["qc0"] = (stp["lo"] - 8 * tb) * 64

                def qk(i):
                    p = steps[i]
                    h, j = p["h"], p["j"]
                    m, half = h // 2, h % 2
                    qT = qTlo if half == 0 else qThi
                    qkey = ("qTlo", m) if half == 0 else ("qThi", m)
                    kslot, kcol = (j // 4) % 3, (j % 4) * 128
                    nq, qc0, sbk = p["nq"], p["qc0"], p["sbk"]
                    S.op("pe", lambda e: e.matmul(bank(sbk, nq), lhsT=kTn[:, kslot, m, kcol:kcol + 128],
                                                  rhs=qT[:, m, qc0:qc0 + nq], start=True, stop=True),
                         reads=[("kTn", kslot, m), qkey], writes=[PB(sbk)])

                def bias_exp(i):
                    p = steps[i]
                    h, j, lo = p["h"], p["j"], p["lo"]
                    nq, sbk, bi = p["nq"], p["sbk"], p["bi"]
                    boff = (lo - _tile_row0(j)) * 64
                    if j in SPECIAL:
                        key = (h, j)
                        if key not in sp_loaded:
                            si = sp_next[0] % 3
                            sp_next[0] += 1
                            S.dma("sp", bsp[si][:, :], bsp_d.ap()[h, SPECIAL[j]], writes=[("bsp", si)])
                            for k2 in [k for k, v in sp_loaded.items() if v == si]:
                                del sp_loaded[k2]
                            sp_loaded[key] = si
                        si = sp_loaded[key]
                        b_ap = bsp[si][:, boff:boff + nq]
                        bkey = ("bsp", si)
                    else:
                        b_ap = bint[:, h, boff:boff + nq]
                        bkey = "bint"
                    S.op("dve", lambda e: e.scalar_tensor_tensor(
                        out=Ssb[bi][:, 0:nq], in0=bank(sbk, nq), scalar=0.125, in1=b_ap, op0=ALU.mult, op1=ALU.add),
                        reads=[PB(sbk), bkey], writes=[("Ssb", bi)])
                    S.op("act", lambda e: e.activation(out=Pna[bi][:, 0:nq], in_=Ssb[bi][:, 0:nq], func=AF.Exp),
                         reads=[("Ssb", bi)], writes=[("Pna", bi)])

                def pv(i):
                    p = steps[i]
                    h, j = p["h"], p["j"]
                    vt = ((j // 4) % 3) * 4 + (j % 4)
                    nq, qc0, bi = p["nq"], p["qc0"], p["bi"]
                    acc_bk = 6 + (h % 2)
                    S.op("pe", lambda e: e.matmul(bank(acc_bk)[0:65, qc0:qc0 + nq], lhsT=Vn[:, vt, h, 0:65],
                                                  rhs=Pna[bi][:, 0:nq], start=p["first"], stop=p["last"],
                                                  skip_group_check=True),
                         reads=[("Vn", vt), "Vn1", ("Pna", bi)], writes=[PB(acc_bk)])

                deferred = []

                def sched_normalize(i, h):
                    acc_bk, m, half = 6 + (h % 2), h // 2, h % 2
                    def recip_row():
                        S.op("act", lambda e: e.activation(out=rec[64:65, :], in_=bank(acc_bk)[64:65, :], func=AF.Ln),
                             reads=[PB(acc_bk)], writes=["rec"])
                        S.op("act", lambda e: e.activation(out=rec[64:65, :], in_=rec[64:65, :], func=AF.Exp, scale=-1.0),
                             reads=["rec"], writes=["rec"])
                    deferred.append((i + 1, recip_row))
                    deferred.append((i + 2, lambda: S.op(
                        "pe", lambda e: e.matmul(bank(2)[0:128, :], lhsT=sel64[:, :], rhs=rec[:, :], start=True, stop=True),
                        reads=["sel64", "rec"], writes=[PB(2)])))
                    deferred.append((i + 3, lambda: S.op(
                        "act", lambda e: e.copy(out=bcsb[:, :], in_=bank(2)[0:64, :]), reads=[PB(2)], writes=["bcsb"])))
                    deferred.append((i + 4, lambda: S.op(
                        "dve", lambda e: e.tensor_tensor(out=OT[half * 64:(half + 1) * 64, m, :], in0=bank(acc_bk)[0:64, :],
                                                         in1=bcsb[:, :], op=ALU.mult),
                        reads=[PB(acc_bk), "bcsb"], writes=[("OT", m, half)])))

                def run_deferred(i):
                    rest = []
                    for at, th in deferred:
                        if at <= i:
                            th()
                        else:
                            rest.append((at, th))
                    deferred[:] = rest

                qk(0)
                if n > 1:
                    qk(1)
                for i in range(n):
                    bias_exp(i)
                    if i + 2 < n:
                        qk(i + 2)
                    run_deferred(i)
                    pv(i)
                    if steps[i]["last"]:
                        sched_normalize(i, steps[i]["h"])
                run_deferred(10 ** 9)

            fb = [0]

            def fbank():
                fb[0] += 1
                return 6 + (fb[0] % 2)

            def th_norm(tb):
                L = []

                def dm(i):
                    S.dma("sp", xt[i % 2][:, :], x_v[tb * 4 + i], writes=[("xt", i % 2)])

                def sq(i):
                    xb = xt[i % 2]
                    S.op("act", lambda e: e.activation(out=xn[i][:, :], in_=xb[:, :], func=AF.Square,
                                                       accum_out=ss4[:, i:i + 1]),
                         reads=[("xt", i % 2)], writes=[("xn", i), ("ss4", i)])

                def ms(i):
                    S.op("dve", lambda e: e.tensor_scalar(out=rs4[:, i:i + 1], in0=ss4[:, i:i + 1], scalar1=1.0 / D,
                                                          scalar2=EPS, op0=ALU.mult, op1=ALU.add),
                         reads=[("ss4", i)], writes=[("rs4", i)])

                def le(i):
                    S.op("act", lambda e: e.activation(out=rs4[:, i:i + 1], in_=rs4[:, i:i + 1], func=AF.Ln),
                         reads=[("rs4", i)], writes=[("rs4", i)])
                    S.op("act", lambda e: e.activation(out=rs4[:, i:i + 1], in_=rs4[:, i:i + 1], func=AF.Exp, scale=-0.5),
                         reads=[("rs4", i)], writes=[("rs4", i)])

                def scl(i):
                    xb = xt[i % 2]
                    S.op("dve", lambda e: e.tensor_scalar(out=xn[i][:, :], in0=xb[:, :], scalar1=rs4[:, i:i + 1],
                                                          scalar2=None, op0=ALU.mult),
                         reads=[("xt", i % 2), ("rs4", i)], writes=[("xn", i)])

                seq = [(dm, 0), (dm, 1), None, None, (sq, 0), (ms, 0), (sq, 1), (le, 0), (ms, 1), (scl, 0), (le, 1), (dm, 2),
                       (scl, 1), (dm, 3), None, None, (sq, 2), (ms, 2), (sq, 3), (le, 2), (ms, 3), (scl, 2), (le, 3), (scl, 3)]
                for it in seq:
                    if it is None:
                        L.append(lambda: None)
                    else:
                        L.append(lambda fn=it[0], i=it[1]: fn(i))

                def tr(r, bb):
                    bk = 6 + bb
                    for cc in range(2):
                        c = 4 * r + 2 * bb + cc
                        for i in range(4):
                            S.op("pe", lambda e, c=c, cc=cc, i=i: e.transpose(
                                out=bankbf(bk)[:, cc * 512 + i * 128:cc * 512 + (i + 1) * 128],
                                in_=xn[i][:, c * 128:(c + 1) * 128], identity=idt[:, :]),
                                reads=[("xn", i), "idt"], writes=[PB(bk)], sig=(i == 3))
                            if i % 2 == 1:
                                yield

                def ev(r, bb):
                    bk = 6 + bb
                    for cc in range(2):
                        c = 4 * r + 2 * bb + cc
                        S.op("dve", lambda e, c=c, cc=cc: e.tensor_scalar(
                            out=hT[:, c, :], in0=bankbf(bk)[:, cc * 512:(cc + 1) * 512],
                            scalar1=gsa[:, c:c + 1], scalar2=sha[:, c:c + 1], op0=ALU.mult, op1=ALU.add),
                            reads=[PB(bk), "gsa", "modsb"], writes=[("hT", c)])

                L2 = []
                for r in range(2):
                    for bb in range(2):
                        trf = (lambda r=r, bb=bb: tr(r, bb))
                        trf.units = 5
                        L2.append(trf)
                    for bb in range(2):
                        L2.append(lambda r=r, bb=bb: ev(r, bb))
                return L, L2

            def proj_group(lhs_fn, rhs_fn, bk, n=512, hk="hT"):
                for c in range(8):
                    S.op("pe", lambda e, c=c: e.matmul(bank(bk, n), lhsT=lhs_fn(c), rhs=rhs_fn(c), start=(c == 0), stop=(c == 7)),
                         reads=[(hk, c), "win"], writes=[PB(bk)], sig=(c == 7))
                    if c % 2 == 1:
                        yield

            def th_kv(tb, hT=hT, hk="hT"):
                slot = tb % 3
                L = []
                for m in range(4):
                    def pk(m=m):
                        bk = fbank()
                        col0 = 512 + m * 128
                        yield from proj_group(lambda c: win[:, c, col0:col0 + 128], lambda c: hT[:, c, :], bk, hk=hk)
                        S.op("dve", lambda e: e.tensor_copy(out=kTn[:, slot, m, :], in_=bank(bk)),
                             reads=[PB(bk)], writes=[("kTn", slot, m)])
                    pk.units = 5
                    L.append(pk)
                for i in range(4):
                    def pvv(i=i):
                        bk = fbank()
                        yield from proj_group(lambda c: hT[:, c, i * 128:(i + 1) * 128], lambda c: win[:, c, 1024:1536], bk, hk=hk)
                        S.op("dve", lambda e: e.tensor_copy(
                            out=Vn[:, slot * 4 + i, :, 0:64], in_=bank(bk).rearrange("p (h d) -> p h d", d=64)),
                            reads=[PB(bk), "Vn1"], writes=[("Vn", slot * 4 + i)])
                    pvv.units = 5
                    L.append(pvv)
                return L

            def th_q(tb, hT=hT, hk="hT"):
                L = []
                for m in range(4):
                    def pq(m=m):
                        bk = fbank()
                        col0 = m * 128
                        yield from proj_group(lambda c: win[:, c, col0:col0 + 128], lambda c: hT[:, c, :], bk, hk=hk)
                        S.op("dve", lambda e: e.tensor_copy(out=qTlo[0:64, m, :], in_=bank(bk)[0:64, :]),
                             reads=[PB(bk), "qTlo0"], writes=[("qTlo", m)])
                        S.op("dve", lambda e: e.tensor_copy(out=qThi[64:128, m, :], in_=bank(bk)[64:128, :]),
                             reads=[PB(bk), "qThi0"], writes=[("qThi", m)])
                    pq.units = 5
                    L.append(pq)
                for i in range(4):
                    tt = tb * 4 + i
                    sbi = i % 2

                    def pg(i=i, sbi=sbi):
                        bk = fbank()
                        yield from proj_group(lambda c: hT[:, c, i * 128:(i + 1) * 128], lambda c: win[:, c, 1536:2048], bk, hk=hk)
                        S.op("dve", lambda e: e.tensor_copy(
                            out=qsb[sbi][:, :].rearrange("p (g kv d) -> p g kv d", g=4, kv=2),
                            in_=bank(bk).rearrange("p (kv g d) -> p g kv d", g=4, kv=2)),
                            reads=[PB(bk)], writes=[("qsb", sbi)])
                    pg.units = 5
                    L.append(pg)

                    def post(i=i, sbi=sbi, tt=tt):
                        qh3 = qhat[:, i, :].rearrange("p (h d) -> p h d", d=64)
                        yield from qk_post(qsb[sbi][:, :], 8, gqb, "gqb", tt,
                                           lambda half: qh3[:, :, half * 32:(half + 1) * 32], [("qhat", i)], sbi)
                    post.units = 5
                    L.append(post)
                return L

            def finalize_q():
                for g in range(4):
                    bk = g // 2
                    for i in range(4):
                        S.op("pe", lambda e, g=g, i=i, bk=bk: e.transpose(
                            out=bankbf(bk)[:, (g % 2) * 512 + i * 128:(g % 2) * 512 + (i + 1) * 128],
                            in_=qhat[:, i, g * 128:(g + 1) * 128], identity=idt[:, :]),
                            reads=[("qhat", i), "idt"], writes=[PB(bk)], sig=(i == 3))
                for g in range(4):
                    bk = g // 2
                    src = bankbf(bk)[:, (g % 2) * 512:(g % 2 + 1) * 512]
                    S.op("dve", lambda e, g=g, src=src: e.tensor_copy(out=QTlo[0:64, g, :], in_=src[0:64, :]),
                         reads=[PB(bk), "QTlo0"], writes=[("QTlo", g)])
                    S.op("dve", lambda e, g=g, src=src: e.tensor_copy(out=QThi[64:128, g, :], in_=src[64:128, :]),
                         reads=[PB(bk), "QThi0"], writes=[("QThi", g)])

            def gqa_block(tb, filler):
                nf = sum(getattr(t, "units", 1) for t in filler)
                fl = Filler(filler)
                state = {"emitted": 0, "step": 0}
                for kv in range(2):
                    QT = QTlo if kv == 0 else QThi
                    qname = "QTlo" if kv == 0 else "QThi"
                    for gp in range(2):
                        def qk(kc):
                            b0 = 2 + 2 * (kc % 2)
                            for u in range(2):
                                g = gp * 2 + u
                                S.op("pe", lambda e, u=u, g=g: e.matmul(
                                    bank(b0 + u), lhsT=KTg[:, kc * 128:(kc + 1) * 128], rhs=QT[:, g, :], start=True, stop=True),
                                    reads=[("KTg", kc), (qname, g)], writes=[PB(b0), PB(b0 + 1)], sig=(u == 1))

                        def ex(kc):
                            b0 = 2 + 2 * (kc % 2)
                            S.op("act", lambda e: e.activation(
                                out=Pg[kc % 2][:, :], in_=ps[:, b0 * 512:(b0 + 2) * 512], func=AF.Exp, scale=0.125),
                                reads=[PB(b0), PB(b0 + 1)], writes=[("Pg", kc % 2)])

                        def pv(kc):
                            for u in range(2):
                                S.op("pe", lambda e, u=u: e.matmul(
                                    bank(u), lhsT=Vg[:, kc, kv, :], rhs=Pg[kc % 2][:, u * 512:(u + 1) * 512],
                                    start=(kc == 0), stop=(kc == 31)),
                                    reads=[("Vg", kc), "Vg1", ("Pg", kc % 2)], writes=[PB(u)])

                        qk(0)
                        for kc in range(32):
                            if kc + 1 < 32:
                                qk(kc + 1)
                            ex(kc)
                            state["step"] += 1
                            target = (nf * state["step"]) // 112
                            while state["emitted"] < target:
                                fl.step()
                                state["emitted"] += 1
                            pv(kc)
                        for u in range(2):
                            h = kv * 4 + gp * 2 + u
                            half, chk = h % 2, 4 + h // 2
                            S.op("act", lambda e, u=u: e.activation(out=rg[u][:, :], in_=bank(u)[64:128, :], func=AF.Ln),
                                 reads=[PB(u)], writes=[("rg", u)])
                            S.op("act", lambda e, u=u: e.activation(out=rg[u][:, :], in_=rg[u][:, :], func=AF.Exp, scale=-1.0),
                                 reads=[("rg", u)], writes=[("rg", u)])
                            S.op("dve", lambda e, u=u, half=half, chk=chk: e.tensor_tensor(
                                out=OT[half * 64:(half + 1) * 64, chk, :], in0=bank(u)[0:64, :], in1=rg[u][:, :], op=ALU.mult),
                                reads=[PB(u), ("rg", u)], writes=[("OT", chk, half)])
                fl.drain()

            def store_ot(tb):
                for c in range(8):
                    S.dma("pool", ot_s.ap()[c][:, tb * 512:(tb + 1) * 512], OT[:, c, :],
                          reads=[("OT", c, 0), ("OT", c, 1)], writes=[("ot_s", tb)])

            def run(L):
                for t in L:
                    r = t()
                    if hasattr(r, "__next__"):
                        for _ in r:
                            pass

            class Filler:
                def __init__(self, items):
                    self.items = list(items)
                    self.cur = None

                def step(self):
                    while True:
                        if self.cur is None:
                            if not self.items:
                                return False
                            r = self.items.pop(0)()
                            if hasattr(r, "__next__"):
                                self.cur = r
                            else:
                                return True
                        try:
                            next(self.cur)
                            return True
                        except StopIteration:
                            self.cur = None

                def drain(self):
                    while self.step():
                        pass

            order = [2, 3, 4, 5, 6, 7, 0, 1]
            hbufs = [(OT, "hTa"), (hT, "hT")]
            for g in range(4):
                ada_mm(g)
                ada_dma(g + 2)
                if g == 1:
                    S.dma("pool", win[:, :, :], win_d.ap().rearrange("(c p) n -> p c n", p=128), writes=["win"])
            S.op("dve", lambda e: e.tensor_tensor(out=modsb[:, 0:16], in0=bank(7, 16), in1=bada[:, 0:16], op=ALU.add),
                 reads=[PB(7), "bada"], writes=["modsb"])
            S.op("dve", lambda e: e.scalar_tensor_tensor(out=gsa[:, :], in0=modsb[:, 8:16], scalar=1.0, in1=gat[:, :],
                                                         op0=ALU.add, op1=ALU.mult),
                 reads=["modsb", "gat"], writes=["gsa"])
            ada_plan = {0: [4, 5], 1: [6, 7], 2: [8], 3: [9], 4: [10], 5: [11]}
            norm_block(order[0], gsa, sha, dst=hbufs[0][0], key=hbufs[0][1])
            for k, tbk in enumerate(order):
                hb, hk = hbufs[k % 2]
                if k + 1 < len(order):
                    nb_, nk_ = hbufs[(k + 1) % 2]
                    norm_block(order[k + 1], gsa, sha, dst=nb_, key=nk_)
                p1a_post(tbk, hb, hk)
                for g in ada_plan.get(k, []):
                    ada_mm(g)
                    if g + 2 < 12:
                        ada_dma(g + 2)
                if k == 5:
                    S.op("dve", lambda e: e.tensor_tensor(out=modsb[:, 16:48], in0=bank(7, 32, 16), in1=bada[:, 16:48], op=ALU.add),
                         reads=[PB(7), "bada"], writes=["modsb2"])
                    S.op("dve", lambda e: e.scalar_tensor_tensor(out=gsf[:, :], in0=modsb[:, 32:40], scalar=1.0, in1=gff[:, :],
                                                                 op0=ALU.add, op1=ALU.mult),
                         reads=["modsb2", "gff"], writes=["gsf"])
                    if debug:
                        S.dma("sp", dbg["d_mod"].ap(), modsb[:, :], reads=["modsb", "modsb2"], writes=["d_mod"])
                    S.op("dve", lambda e: e.memset(Vn[:, :, :, 64:65], 1.0), writes=["Vn1", ("wa", 0), ("wa", 1)])
                if tbk in (0, 1):
                    run(th_kv(tbk, hT=hb, hk=hk))
                if tbk == 0:
                    run(th_q(0, hT=hb, hk=hk))
                    finalize_q()
            S.dma("sp", bint[:, :, :], bint_d.ap().rearrange("h p n -> p h n"),
                  writes=["bint", ("xt", 2), ("xt", 3)])
            S.barrier()
            if stop_after == "p1a":
                S.finish("sp")
                return nc
            for tb in range(NB):
                for _ in range(6):
                    if prep:
                        prep.pop(0)()
                na_block(tb)
                filler = []
                npre, npost = ([], [])
                if tb + 2 < NB:
                    npre, npost = th_norm(tb + 2)
                filler += npre
                if tb + 1 < NB:
                    filler += th_q(tb + 1)
                if tb + 2 < NB:
                    filler += npost + th_kv(tb + 2)
                gqa_block(tb, filler)
                store_ot(tb)
                if tb + 1 < NB:
                    finalize_q()
            if debug:
                S.barrier()
                S.dma("sp", dbg["d_ot"].ap(), ot_s.ap(), writes=["d_ot"])
            S.barrier()

        if stop_after == "p1":
            S.finish("sp")
            return nc

        with ExitStack() as s2:
            wo = sbt(s2, "wo", [128, 8, D], BF16)
            OTb = [sbt(s2, f"OTb{i}", [128, 8, 512], BF16) for i in range(2)]
            x1s = [[sbt(s2, f"x1_{p}_{i}", [128, D], F32) for i in range(4)] for p in range(2)]
            ytmp = [sbt(s2, f"ytmp{i}", [128, 512], F32) for i in range(2)]
            ytx = [sbt(s2, f"ytx{i}", [128, 512], F32) for i in range(2)]
            ss4c = sbt(s2, "ss4c", [128, 4], F32)
            rs4c = sbt(s2, "rs4c", [128, 4], F32)
            xn2 = [sbt(s2, f"xn2_{i}", [128, D], BF16) for i in range(4)]
            junk2 = sbt(s2, "junk2", [128, D], BF16)
            ss4b = sbt(s2, "ss4b", [128, 4], F32)
            rs4b = sbt(s2, "rs4b", [128, 4], F32)
            hT2s = [sbt(s2, f"hT2_{p}", [128, 8, 512], BF16) for p in range(2)]
            wgb = [sbt(s2, f"wgb{i}", [128, D], BF16) for i in range(3)]
            wub = [sbt(s2, f"wub{i}", [128, D], BF16) for i in range(3)]
            wdb = [sbt(s2, f"wdb{i}", [128, 2, 512], BF16) for i in range(4)]
            sg = [sbt(s2, f"sg{i}", [128, 512], F32) for i in range(2)]
            hid = sbt(s2, "hid", [128, NF, 512], BF16)
            ot = [sbt(s2, f"ot{i}", [128, D], F32) for i in range(2)]

            S.dma("pool", wo[:, :, :], wo_d.ap().rearrange("(c p) n -> p c n", p=128), writes=["wo"])
            gate_a = sbt(s2, "gate_a", [128, D], F32)
            gate_f = sbt(s2, "gate_f", [128, D], F32)
            gfin = sbt(s2, "gfin", [128, D], F32)
            dg = [sbt(s2, f"dg{i}", [128, 128], F32) for i in range(2)]
            S.dma("sp", gfin[:, :], gfin_d.ap(), writes=["gfin"])
            for gi, (off, gt, gname) in enumerate(((16, gate_a, "gate_a"), (40, gate_f, "gate_f"))):
                for j in range(8):
                    d = dg[j % 2]
                    S.op("dve", lambda e, d=d, off=off, j=j: e.tensor_scalar(
                        out=d[:, :], in0=identf[:, :], scalar1=modsb[:, off + j:off + j + 1], scalar2=None, op0=ALU.mult),
                        reads=["identf", "modsb"], writes=[("dg", j % 2)])
                    bk = 1 + (j // 4)
                    S.op("pe", lambda e, d=d, bk=bk, j=j: e.matmul(bank(bk, 128, (j % 4) * 128), lhsT=onesf[:, :], rhs=d[:, :],
                                                                 start=True, stop=True),
                         reads=["onesf", ("dg", j % 2)], writes=[PB(bk)])
                for hb in range(2):
                    S.op("act", lambda e, gt=gt, hb=hb: e.copy(out=gt[:, hb * 512:(hb + 1) * 512], in_=bank(1 + hb)),
                         reads=[PB(1 + hb)], writes=[(gname, hb)])
            x_v = x_d.ap().rearrange("(tt p) d -> tt p d", p=128)
            y_v = y_d.ap().rearrange("(tt p) d -> tt p d", p=128)
            n_ld = {"g": 0, "d": 0}
            n_out = [0]

            def x_thunks(tb):
                par = tb % 2
                ob = OTb[par]
                x1 = x1s[par]
                hT2 = hT2s[par]
                L = []

                def ldot():
                    for c in range(8):
                        S.dma("sp", ob[:, c, :], ot_s.ap()[c][:, tb * 512:(tb + 1) * 512], writes=[("OTb", par, c)])
                L.append(ldot)
                for i in range(4):
                    def ldx(i=i):
                        S.dma("sp", x1[i][:, :], x_v[tb * 4 + i], writes=[("x1", par, i, 0), ("x1", par, i, 1)])
                    L.append(ldx)
                for i in range(4):
                    for hf in range(2):
                        def opj(i=i, hf=hf):
                            bk = 4 + (i * 2 + hf) % 2
                            for c in range(8):
                                S.op("pe", lambda e, c=c: e.matmul(
                                    bank(bk), lhsT=ob[:, c, i * 128:(i + 1) * 128], rhs=wo[:, c, hf * 512:(hf + 1) * 512],
                                    start=(c == 0), stop=(c == 7)),
                                    reads=[("OTb", par, c), "wo"], writes=[PB(bk)], sig=(c == 7))
                            yt = ytx[hf]
                            S.op("dve", lambda e: e.tensor_tensor(
                                out=yt[:, :], in0=bank(bk), in1=gate_a[:, hf * 512:(hf + 1) * 512], op=ALU.mult),
                                reads=[PB(bk), ("gate_a", hf)], writes=[("ytx", hf)])
                            S.op("dve", lambda e: e.tensor_tensor(
                                out=x1[i][:, hf * 512:(hf + 1) * 512], in0=x1[i][:, hf * 512:(hf + 1) * 512], in1=yt[:, :], op=ALU.add),
                                reads=[("x1", par, i, hf), ("ytx", hf)], writes=[("x1", par, i, hf)])
                        L.append(opj)
                for i in range(4):
                    def nrm1(i=i):
                        S.op("act", lambda e: e.activation(out=xn2[i][:, :], in_=x1[i][:, :], func=AF.Square,
                                                           accum_out=ss4b[:, i:i + 1]),
                             reads=[("x1", par, i, 0), ("x1", par, i, 1)], writes=[("xn2", i), ("ss4b", i)])
                    L.append(nrm1)

                def nrmr():
                    S.op("dve", lambda e: e.tensor_scalar(out=rs4b[:, :], in0=ss4b[:, :], scalar1=1.0 / D,
                                                          scalar2=EPS, op0=ALU.mult, op1=ALU.add),
                         reads=[("ss4b", i) for i in range(4)], writes=["rs4b"])
                    S.op("act", lambda e: e.activation(out=rs4b[:, :], in_=rs4b[:, :], func=AF.Ln),
                         reads=["rs4b"], writes=["rs4b"])
                    S.op("act", lambda e: e.activation(out=rs4b[:, :], in_=rs4b[:, :], func=AF.Exp, scale=-0.5),
                         reads=["rs4b"], writes=["rs4b"])
                L.append(nrmr)
                for i in range(4):
                    def nrm2(i=i):
                        S.op("dve", lambda e: e.tensor_scalar(out=xn2[i][:, :], in0=x1[i][:, :], scalar1=rs4b[:, i:i + 1],
                                                              scalar2=None, op0=ALU.mult),
                             reads=[("x1", par, i, 0), ("x1", par, i, 1), "rs4b"], writes=[("xn2", i)])
                    L.append(nrm2)
                for bb in range(4):
                    def tr(bb=bb):
                        bk = 4 + bb
                        for cc in range(2):
                            c = 2 * bb + cc
                            for i in range(4):
                                S.op("pe", lambda e, c=c, cc=cc, i=i: e.transpose(
                                    out=bankbf(bk)[:, cc * 512 + i * 128:cc * 512 + (i + 1) * 128],
                                    in_=xn2[i][:, c * 128:(c + 1) * 128], identity=idt[:, :]),
                                    reads=[("xn2", i), "idt"], writes=[PB(bk)], sig=(i == 3))
                    L.append(tr)
                for bb in range(4):
                    def ev(bb=bb):
                        bk = 4 + bb
                        for cc in range(2):
                            c = 2 * bb + cc
                            S.op("act", lambda e, c=c, cc=cc: e.activation(
                                out=hT2[:, c, :], in_=bankbf(bk)[:, cc * 512:(cc + 1) * 512], func=AF.Identity,
                                scale=gsf[:, c:c + 1], bias=shf[:, c:c + 1]),
                                reads=[PB(bk), "gsf", "modsb"], writes=[("hT2", par, c)])
                    L.append(ev)
                return L

            pre_g = []

            def gu_dma(f):
                wi = n_ld["g"] % 3
                n_ld["g"] += 1
                S.dma("sp", wgb[wi][:, :], wg_s.ap()[f], reads=[("wg_s", f)], writes=[("wgb", wi)])
                S.dma("sp", wub[wi][:, :], wu_s.ap()[f], reads=[("wu_s", f)], writes=[("wub", wi)])
                return wi

            def wd_dma(hf, gi):
                wi = n_ld["d"] % 4
                n_ld["d"] += 1
                S.dma("sp", wdb[wi][:, :, :], wd_s.ap()[hf, 2 * gi:2 * gi + 2].rearrange("f p n -> p f n"),
                      reads=[("wd_s", hf)], writes=[("wdb", wi)])
                return wi

            def fn_thunks(tb):
                par = tb % 2
                x1 = x1s[par]
                L = []
                for i in range(4):
                    def sqf(i=i):
                        S.op("act", lambda e: e.activation(out=junk2[:, :], in_=x1[i][:, :], func=AF.Square,
                                                           accum_out=ss4c[:, i:i + 1]),
                             reads=[("x1", par, i, 0), ("x1", par, i, 1)], writes=["junk2", ("ss4c", i)])
                    L.append(sqf)

                def rsf():
                    S.op("dve", lambda e: e.tensor_scalar(out=rs4c[:, :], in0=ss4c[:, :], scalar1=1.0 / D, scalar2=EPS,
                                                          op0=ALU.mult, op1=ALU.add),
                         reads=[("ss4c", i) for i in range(4)], writes=["rs4c"])
                L.append(rsf)

                def lef():
                    S.op("act", lambda e: e.activation(out=rs4c[:, :], in_=rs4c[:, :], func=AF.Ln),
                         reads=["rs4c"], writes=["rs4c"])
                    S.op("act", lambda e: e.activation(out=rs4c[:, :], in_=rs4c[:, :], func=AF.Exp, scale=-0.5),
                         reads=["rs4c"], writes=["rs4c"])
                L.append(lef)
                for i in range(4):
                    def outf(i=i):
                        oi = n_out[0] % 2
                        n_out[0] += 1
                        S.op("dve", lambda e: e.scalar_tensor_tensor(
                            out=ot[oi][:, :], in0=x1[i][:, :], scalar=rs4c[:, i:i + 1], in1=gfin[:, :], op0=ALU.mult, op1=ALU.mult),
                            reads=[("x1", par, i, 0), ("x1", par, i, 1), "rs4c", "gfin"], writes=[("ot", oi)])
                        S.dma("pool", y_v[tb * 4 + i], ot[oi][:, :], reads=[("ot", oi)], writes=[("y", tb * 4 + i)])
                    L.append(outf)
                return L

            pending_fn = []
            for t in x_thunks(0):
                t()
            for tb in range(NB):
                par = tb % 2
                x1 = x1s[par]
                hT2 = hT2s[par]
                filler = list(pending_fn) + (x_thunks(tb + 1) if tb + 1 < NB else [])
                nfl = len(filler)
                emitted = 0
                pre_wd = []
                for f in range(NF):
                    if pre_g:
                        wi = pre_g.pop(0)
                    else:
                        wi = gu_dma(f)
                    if f >= NF - 4:
                        pre_wd.append(wd_dma(0, len(pre_wd)))
                    bg = 2 * (f % 2)
                    bu = bg + 1
                    for c in range(8):
                        S.op("pe", lambda e, c=c, wi=wi, bg=bg: e.matmul(bank(bg), lhsT=wgb[wi][:, c * 128:(c + 1) * 128],
                                                                       rhs=hT2[:, c, :], start=(c == 0), stop=(c == 7)),
                             reads=[("wgb", wi), ("hT2", par, c)], writes=[PB(bg)], sig=(c == 7))
                    for c in range(8):
                        S.op("pe", lambda e, c=c, wi=wi, bu=bu: e.matmul(bank(bu), lhsT=wub[wi][:, c * 128:(c + 1) * 128],
                                                                       rhs=hT2[:, c, :], start=(c == 0), stop=(c == 7)),
                             reads=[("wub", wi), ("hT2", par, c)], writes=[PB(bu)], sig=(c == 7))
                    S.op("act", lambda e, f=f, bg=bg: e.activation(out=sg[f % 2][:, :], in_=bank(bg), func=AF.Silu),
                         reads=[PB(bg)], writes=[("sg", f % 2)])
                    S.op("dve", lambda e, f=f, bu=bu: e.tensor_tensor(out=hid[:, f, :], in0=bank(bu), in1=sg[f % 2][:, :], op=ALU.mult),
                         reads=[PB(bu), ("sg", f % 2)], writes=[("hid", f)])
                    target = min(nfl, (nfl * (f + 1)) // (NF - 2))
                    while emitted < target:
                        filler[emitted]()
                        emitted += 1
                while emitted < nfl:
                    filler[emitted]()
                    emitted += 1
                if tb + 1 < NB:
                    for f2 in range(3):
                        pre_g.append(gu_dma(f2))
                for hf in range(2):
                    dbk = 4 if hf == 0 else 0
                    for gi in range(NF // 2):
                        if hf == 0 and gi < len(pre_wd):
                            wi = pre_wd[gi]
                        else:
                            wi = wd_dma(hf, gi)
                        for ff in range(2):
                            f = 2 * gi + ff
                            for i in range(4):
                                S.op("pe", lambda e, f=f, ff=ff, i=i, wi=wi: e.matmul(
                                    bank(dbk + i), lhsT=hid[:, f, i * 128:(i + 1) * 128], rhs=wdb[wi][:, ff, :],
                                    start=(f == 0), stop=(f == NF - 1)),
                                    reads=[("hid", f), ("wdb", wi)], writes=[PB(dbk + i)], sig=(f == NF - 1 or i == 3))
                    for i in range(4):
                        yt = ytmp[i % 2]
                        S.op("dve", lambda e, i=i, hf=hf, yt=yt: e.tensor_tensor(
                            out=yt[:, :], in0=bank(dbk + i), in1=gate_f[:, hf * 512:(hf + 1) * 512], op=ALU.mult),
                            reads=[PB(dbk + i), ("gate_f", hf)], writes=[("ytmp", i % 2)])
                        S.op("dve", lambda e, i=i, hf=hf, yt=yt: e.tensor_tensor(
                            out=x1[i][:, hf * 512:(hf + 1) * 512], in0=x1[i][:, hf * 512:(hf + 1) * 512], in1=yt[:, :], op=ALU.add),
                            reads=[("x1", par, i, hf), ("ytmp", i % 2)], writes=[("x1", par, i, hf)])
                pending_fn = fn_thunks(tb)
                if tb + 1 == NB:
                    for t in pending_fn:
                        t()
            S.barrier()
        S.finish("sp")
    return nc


_CACHE = {}


def _get_program():
    if "nc" not in _CACHE:
        _CACHE["nc"] = build_program()
    return _CACHE["nc"]


def make_in_maps(x, c, w_ada, b_ada, g_attn, w_in, g_q, g_k, rpb, w_o, g_ffn, w_gate, w_up, w_down, g_final):
    f32 = lambda a: np.ascontiguousarray(np.asarray(a, dtype=np.float32))
    colmajor = lambda v, n: f32(np.asarray(v, np.float32).reshape(n, 128).T)
    b_int, b_sp = build_bias_tiles(np.asarray(rpb)[0])
    sel = np.zeros((128, 128), np.float32)
    sel[64, :] = 1.0
    shared = {
        "w_ada": f32(np.asarray(w_ada)[0]),
        "bada": colmajor(np.asarray(b_ada)[0], 48),
        "gattn": colmajor(np.asarray(g_attn)[0], 8),
        "gffn": colmajor(np.asarray(g_ffn)[0], 8),
        "gfin": f32(np.broadcast_to(np.asarray(g_final, np.float32)[None, :], (128, D))),
        "w_in": f32(np.asarray(w_in)[0]),
        "gq": f32(np.broadcast_to(np.asarray(g_q, np.float32)[0][None, :], (128, 64))),
        "gk": f32(np.broadcast_to(np.asarray(g_k, np.float32)[0][None, :], (128, 64))),
        "bint": b_int,
        "bsp": b_sp,
        "w_o": f32(np.asarray(w_o)[0]),
        "w_gate": f32(np.asarray(w_gate)[0]),
        "w_up": f32(np.asarray(w_up)[0]),
        "w_down": f32(np.asarray(w_down)[0]),
        "ident": np.eye(128, dtype=np.float32).astype(ml_dtypes.bfloat16),
        "identf": np.eye(128, dtype=np.float32),
        "sel64": sel,
        "cs": rope_table(),
    }
    x = np.asarray(x, np.float32)
    c = np.asarray(c, np.float32)
    maps = []
    for b in range(x.shape[0]):
        m = dict(shared)
        m["x"] = np.ascontiguousarray(x[b])
        m["ccol"] = colmajor(c[b], 8)
        maps.append(m)
    return maps


def kernel(x, c, w_ada, b_ada, g_attn, w_in, g_q, g_k, rpb, w_o, g_ffn, w_gate, w_up, w_down, g_final):
    nc = _get_program()
    in_maps = make_in_maps(x, c, w_ada, b_ada, g_attn, w_in, g_q, g_k, rpb, w_o, g_ffn, w_gate, w_up, w_down, g_final)
    res = run_bass_kernel_spmd(nc, in_maps, core_ids=list(range(N_CORES)))
    out = np.stack([np.asarray(r["y"], dtype=np.float32) for r in res.results], axis=0)
    return out
```

```python
import numpy as np
import ml_dtypes
from contextlib import ExitStack
import concourse.bass as bass
import concourse.mybir as mybir
from concourse.bass_utils import run_bass_kernel_spmd

F32 = mybir.dt.float32
BF16 = mybir.dt.bfloat16
AF = mybir.ActivationFunctionType
ALU = mybir.AluOpType
AX = mybir.AxisListType

T = 4096
D = 1024
DFF = 2816
NF = DFF // 128
NB = 8
EPS = 1e-6
NEG = -30000.0
N_CORES = 8


class Sched:
    N_DMA_SEMS = 12

    def __init__(self, nc, stack):
        self.nc = nc
        self.eng = {"pe": nc.tensor, "act": nc.scalar, "dve": nc.vector,
                    "pool": nc.gpsimd, "sp": nc.sync}
        self.sem = {e: stack.enter_context(nc.semaphore("s_" + e)) for e in self.eng}
        self.cnt = {e: 0 for e in self.eng}
        self.dsem = {q: [stack.enter_context(nc.semaphore(f"d_{q}{i}")) for i in range(self.N_DMA_SEMS)]
                     for q in ("sp", "pool", "act")}
        self.dcnt = {q: [0] * self.N_DMA_SEMS for q in self.dsem}
        self.dnext = {q: 0 for q in self.dsem}
        self.dlast = {q: [None] * self.N_DMA_SEMS for q in self.dsem}
        self.seen = {e: {} for e in self.eng}
        self.res = {}
        self.nwaits = 0
        self.nops = {e: 0 for e in self.eng}

    def _wait(self, e, tok):
        sem, val, peng = tok
        key = sem.name
        if self.seen[e].get(key, 0) >= val:
            return
        self.eng[e].wait_ge(sem, val)
        self.seen[e][key] = val
        self.nwaits += 1

    def _deps(self, e, reads, writes):
        toks = []
        for r in reads:
            st = self.res.get(r)
            if st and st[0] is not None:
                toks.append((st[0], "raw"))
        for w in writes:
            st = self.res.get(w)
            if st:
                if st[0] is not None:
                    toks.append((st[0], "waw"))
                for t in st[1]:
                    toks.append((t, "war"))
        for tok, kind in toks:
            if tok[2] == e and e == "pe":
                continue
            self._wait(e, tok)

    def _commit(self, tok, reads, writes):
        for r in reads:
            st = self.res.setdefault(r, [None, []])
            st[1].append(tok)
            if len(st[1]) > 64:
                best = {}
                for t in st[1]:
                    k = t[0].name
                    if k not in best or best[k][1] < t[1]:
                        best[k] = t
                st[1] = list(best.values())
        for w in writes:
            self.res[w] = [tok, []]

    def op(self, e, fn, reads=(), writes=(), sig=True):
        pbr = [r for r in reads if isinstance(r, tuple) and r[0] == "pb"]
        if pbr:
            writes = list(writes) + [r for r in pbr if r not in writes]
            reads = [r for r in reads if not (isinstance(r, tuple) and r[0] == "pb")]
        self._deps(e, reads, writes)
        ins = fn(self.eng[e])
        self.nops[e] += 1
        if sig:
            self.cnt[e] += 1
            ins.then_inc(self.sem[e], 1)
            tok = (self.sem[e], self.cnt[e], e)
        else:
            tok = (self.sem[e], self.cnt[e] + 1, e)
        self._commit(tok, reads, writes)
        return ins

    def dma(self, q, out, in_, reads=(), writes=(), **kw):
        self._deps(q, reads, writes)
        i = self.dnext[q]
        self.dnext[q] = (i + 1) % self.N_DMA_SEMS
        prev = self.dlast[q][i]
        if prev is not None:
            self._wait(q, prev)
        self.dcnt[q][i] += 16
        ins = self.eng[q].dma_start(out=out, in_=in_, **kw)
        ins.then_inc(self.dsem[q][i], 16)
        tok = (self.dsem[q][i], self.dcnt[q][i], None)
        self.dlast[q][i] = tok
        self._commit(tok, reads, writes)
        return ins

    def all_tokens(self):
        toks = []
        for e in self.eng:
            if self.cnt[e] > 0:
                toks.append((self.sem[e], self.cnt[e], e))
        for q in self.dsem:
            for t in self.dlast[q]:
                if t is not None:
                    toks.append(t)
        return toks

    def barrier(self):
        toks = self.all_tokens()
        for e in self.eng:
            for t in toks:
                if t[2] == e:
                    continue
                self._wait(e, t)
        self.res = {}

    def finish(self, e="sp"):
        for t in self.all_tokens():
            if t[2] != e:
                self._wait(e, t)


def _rs(r):
    return min(max(r - 4, 0), 56)


def _cs(w):
    return min(max(w - 8, 0), 48)


def _chunk_rows(j):
    rows = [r for r in range(64) if any(_rs(r) <= ka < _rs(r) + 8 for ka in (2 * j, 2 * j + 1))]
    return rows[0], rows[-1]


SPECIAL = {2: 0, 3: 1, 28: 2, 29: 3}


def _tile_row0(j):
    return _chunk_rows(j)[0] if j in SPECIAL else 2 * j - 4


def _bias_block(rpb_h, ka, r):
    blk = np.full((64, 64), NEG, np.float32)
    if not (_rs(r) <= ka < _rs(r) + 8):
        return blk
    a = ka - r + 7
    for w in range(64):
        c0 = _cs(w)
        kc = np.arange(c0, c0 + 16)
        blk[kc, w] = rpb_h[a, kc - w + 15]
    return blk


def build_bias_tiles(rpb):
    rpb = np.asarray(rpb, np.float32)
    b_int = np.full((8, 128, 640), NEG, np.float32)
    b_sp = np.full((8, 4, 128, 768), NEG, np.float32)
    j0 = 10
    for h in range(8):
        for kdr in range(2):
            for qr in range(10):
                b_int[h, kdr * 64:(kdr + 1) * 64, qr * 64:(qr + 1) * 64] = \
                    _bias_block(rpb[h], 2 * j0 + kdr, 2 * j0 - 4 + qr)
        for j, si in SPECIAL.items():
            r0, r1 = _chunk_rows(j)
            for kdr in range(2):
                for r in range(r0, r1 + 1):
                    b_sp[h, si, kdr * 64:(kdr + 1) * 64, (r - r0) * 64:(r - r0 + 1) * 64] = \
                        _bias_block(rpb[h], 2 * j + kdr, r)
    return b_int, b_sp


def rope_table():
    t = np.arange(T)
    row = (t // 64).astype(np.float64)
    col = (t % 64).astype(np.float64)
    inv = 10000.0 ** (-np.arange(0, 32, 2, dtype=np.float64) / 32)
    ang = np.concatenate([row[:, None] * inv[None, :], col[:, None] * inv[None, :]], axis=-1)
    return np.concatenate([np.cos(ang), np.sin(ang)], axis=-1).astype(np.float32)


def build_program(stop_after=None, debug=False):
    nc = bass.Bass("TRN2", target_bir_lowering=False)
    dt_in = lambda name, shape, dt=F32: nc.dram_tensor(name, shape, dt, kind="ExternalInput")
    x_d = dt_in("x", [T, D])
    ccol_d = dt_in("ccol", [128, 8])
    wada_d = dt_in("w_ada", [D, 6 * D])
    bada_d = dt_in("bada", [128, 48])
    gattn_d = dt_in("gattn", [128, 8])
    gffn_d = dt_in("gffn", [128, 8])
    gfin_d = dt_in("gfin", [128, D])
    win_d = dt_in("w_in", [D, 2304])
    gq_d = dt_in("gq", [128, 64])
    gk_d = dt_in("gk", [128, 64])
    bint_d = dt_in("bint", [8, 128, 640])
    bsp_d = dt_in("bsp", [8, 4, 128, 768])
    wo_d = dt_in("w_o", [D, D])
    wg_d = dt_in("w_gate", [D, DFF])
    wu_d = dt_in("w_up", [D, DFF])
    wd_d = dt_in("w_down", [DFF, D])
    ident_d = dt_in("ident", [128, 128], BF16)
    identf_d = dt_in("identf", [128, 128])
    sel_d = dt_in("sel64", [128, 128])
    cs_d = dt_in("cs", [T, 64])
    y_d = nc.dram_tensor("y", [T, D], F32, kind="ExternalOutput")
    wg_s = nc.dram_tensor("wg_s", [NF, 128, D], BF16)
    wu_s = nc.dram_tensor("wu_s", [NF, 128, D], BF16)
    wd_s = nc.dram_tensor("wd_s", [2, NF, 128, 512], BF16)
    ot_s = nc.dram_tensor("ot_s", [8, 128, T], BF16)
    dbg = {}
    if debug:
        dbg["d_mod"] = nc.dram_tensor("d_mod", [128, 48], F32, kind="ExternalOutput")
        dbg["d_ot"] = nc.dram_tensor("d_ot", [8, 128, T], BF16, kind="ExternalOutput")

    with ExitStack() as st:
        S = Sched(nc, st)
        ps = st.enter_context(nc.psum_tensor("ps", [128, 8 * 512], F32))

        def bank(b, n=512, off=0):
            return ps[:, b * 512 + off: b * 512 + off + n]

        def bankbf(b):
            return ps[:, b * 512:(b + 1) * 512].bitcast(BF16)

        PB = lambda b: ("pb", b)

        def sbt(stack, name, shape, dt):
            return stack.enter_context(nc.sbuf_tensor("sb_" + name, shape, dt))

        idt = sbt(st, "idt", [128, 128], BF16)
        identf = sbt(st, "identf", [128, 128], F32)
        sel64 = sbt(st, "sel64", [128, 128], F32)
        onesf = sbt(st, "onesf", [128, 128], F32)
        mh = sbt(st, "mh", [128, 8], F32)
        modsb = sbt(st, "modsb", [128, 48], F32)
        gsa = sbt(st, "gsa", [128, 8], F32)
        gsf = sbt(st, "gsf", [128, 8], F32)
        rec = sbt(st, "rec", [128, 512], F32)
        bcsb = sbt(st, "bcsb", [64, 512], F32)

        S.dma("sp", idt[:, :], ident_d.ap(), writes=["idt"])
        S.dma("sp", identf[:, :], identf_d.ap(), writes=["identf"])
        S.dma("sp", sel64[:, :], sel_d.ap(), writes=["sel64"])
        S.op("dve", lambda e: e.memset(onesf[:, :], 1.0), writes=["onesf"])
        S.op("dve", lambda e: e.memset(mh[:, :], -0.5), writes=["mh"])
        S.op("dve", lambda e: e.memset(rec[:, :], 0.0), writes=["rec"])

        prep = []
        for f in range(NF):
            prep.append(lambda f=f: S.dma("pool", wg_s.ap()[f].rearrange("p (c j) -> p c j", c=8),
                                          wg_d.ap()[:, f * 128:(f + 1) * 128].rearrange("(c p) j -> p c j", p=128),
                                          writes=[("wg_s", f)]))
            prep.append(lambda f=f: S.dma("pool", wu_s.ap()[f].rearrange("p (c j) -> p c j", c=8),
                                          wu_d.ap()[:, f * 128:(f + 1) * 128].rearrange("(c p) j -> p c j", p=128),
                                          writes=[("wu_s", f)]))
        for hf in range(2):
            prep.append(lambda hf=hf: S.dma("pool", wd_s.ap()[hf].rearrange("f p n -> p f n"),
                                            wd_d.ap()[:, hf * 512:(hf + 1) * 512].rearrange("(f p) n -> p f n", p=128),
                                            writes=[("wd_s", hf)]))

        ccol = sbt(st, "ccol", [128, 8], F32)
        ctmp = sbt(st, "ctmp", [128, 8], F32)
        cact = sbt(st, "cact", [128, 8], BF16)
        bada = sbt(st, "bada", [128, 48], F32)
        gat = sbt(st, "gat", [128, 8], F32)
        gff = sbt(st, "gff", [128, 8], F32)
        S.dma("sp", ccol[:, :], ccol_d.ap(), writes=["ccol"])
        S.dma("sp", bada[:, :], bada_d.ap(), writes=["bada"])
        S.dma("sp", gat[:, :], gattn_d.ap(), writes=["gat"])
        S.dma("sp", gff[:, :], gffn_d.ap(), writes=["gff"])
        S.op("act", lambda e: e.activation(out=ctmp[:, :], in_=ccol[:, :], func=AF.Exp, scale=-1.0),
             reads=["ccol"], writes=["ctmp"])
        S.op("dve", lambda e: e.tensor_scalar(out=ctmp[:, :], in0=ctmp[:, :], scalar1=1.0, scalar2=None, op0=ALU.add),
             reads=["ctmp"], writes=["ctmp"])
        S.op("dve", lambda e: e.reciprocal(out=ctmp[:, :], in_=ctmp[:, :]), reads=["ctmp"], writes=["ctmp"])
        S.op("dve", lambda e: e.tensor_tensor(out=cact[:, :], in0=ctmp[:, :], in1=ccol[:, :], op=ALU.mult),
             reads=["ctmp", "ccol"], writes=["cact"])
        wada_v = wada_d.ap().rearrange("(k p) n -> p k n", p=128)
        sha = modsb[:, 0:8]
        shf = modsb[:, 24:32]

        with ExitStack() as s1:
            win = sbt(s1, "win", [128, 8, 2304], BF16)
            cs = sbt(s1, "cs", [128, 32, 64], F32)
            gqb = sbt(s1, "gqb", [128, 64], F32)
            gkb = sbt(s1, "gkb", [128, 64], F32)
            KTg = sbt(s1, "KTg", [128, T], BF16)
            Vg = sbt(s1, "Vg", [128, 32, 2, 128], BF16)
            bint = sbt(s1, "bint", [128, 8, 640], F32)
            bsp = [sbt(s1, f"bsp{i}", [128, 768], F32) for i in range(3)]
            kTn = sbt(s1, "kTn", [128, 3, 4, 512], BF16)
            Vn = sbt(s1, "Vn", [128, 12, 8, 65], BF16)
            qTlo = sbt(s1, "qTlo", [128, 4, 512], BF16)
            qThi = sbt(s1, "qThi", [128, 4, 512], BF16)
            QTlo = sbt(s1, "QTlo", [128, 4, 512], BF16)
            QThi = sbt(s1, "QThi", [128, 4, 512], BF16)
            xt = [sbt(s1, f"xt{i}", [128, D], F32) for i in range(2)]
            xn = [sbt(s1, f"xn{i}", [128, D], BF16) for i in range(4)]
            ss4 = sbt(s1, "ss4", [128, 4], F32)
            rs4 = sbt(s1, "rs4", [128, 4], F32)
            hT = sbt(s1, "hT", [128, 8, 512], BF16)
            qsb = [sbt(s1, f"qsb{i}", [128, 512], F32) for i in range(2)]
            qtmp = sbt(s1, "qtmp", [128, 512], F32)
            ssq = sbt(s1, "ssq", [128, 8], F32)
            rq = sbt(s1, "rq", [128, 8], F32)
            rt1 = sbt(s1, "rt1", [128, 8, 32], F32)
            rt2 = sbt(s1, "rt2", [128, 8, 32], F32)
            qhat = sbt(s1, "qhat", [128, 4, 512], BF16)
            khat = sbt(s1, "khat", [128, 4, 128], BF16)
            Ssb = [sbt(s1, f"Ssb{i}", [128, 512], F32) for i in range(3)]
            Pna = [sbt(s1, f"Pna{i}", [128, 512], BF16) for i in range(3)]
            Pg = [sbt(s1, f"Pg{i}", [128, 1024], BF16) for i in range(2)]
            OT = sbt(s1, "OT", [128, 8, 512], BF16)
            rg = [sbt(s1, f"rg{i}", [64, 512], F32) for i in range(2)]

            wa = [kTn[:, 0:2, :, :].rearrange("p a m n -> p (a m) n"),
                  Vn.reshape([128, 12 * 8 * 65])[:, 0:4096].rearrange("p (k n) -> p k n", k=8)]

            def ada_dma(g):
                S.dma("pool", wa[g % 2], wada_v[:, :, g * 512:(g + 1) * 512], writes=[("wa", g % 2)])

            def ada_mm(g):
                wb = wa[g % 2]
                for jj in range(4):
                    j = g * 4 + jj
                    for k in range(8):
                        S.op("pe", lambda e, j=j, jj=jj, k=k: e.matmul(
                            bank(7, 1, j), lhsT=wb[:, k, jj * 128:(jj + 1) * 128], rhs=cact[:, k:k + 1],
                            start=(k == 0), stop=(k == 7), skip_group_check=True),
                            reads=[("wa", g % 2), "cact"], writes=[PB(7)], sig=(k == 7))

            ada_dma(0)
            ada_dma(1)
            S.dma("sp", cs[:, :, :], cs_d.ap().rearrange("(tt p) k -> p tt k", p=128), writes=["cs"])
            S.dma("sp", gqb[:, :], gq_d.ap(), writes=["gqb"])
            S.dma("sp", gkb[:, :], gk_d.ap(), writes=["gkb"])
            bflat = bint[:, :, :].rearrange("p h n -> p (h n)")
            xt4 = [xt[0][:, :], xt[1][:, :], bflat[:, 0:D], bflat[:, D:2 * D]]
            S.op("dve", lambda e: e.memset(Vg[:, :, :, 64:128], 1.0), writes=["Vg1"])
            for tns, nm in ((qTlo, "qTlo0"), (qThi, "qThi0"), (QTlo, "QTlo0"), (QThi, "QThi0")):
                S.op("dve", lambda e, tns=tns: e.memset(tns[:, :, :], 0.0), writes=[nm])
            zero_deps = {"qTlo": "qTlo0", "qThi": "qThi0", "QTlo": "QTlo0", "QThi": "QThi0"}

            x_v = x_d.ap().rearrange("(tt p) d -> tt p d", p=128)

            def norm_block(tb, gs, sh, dst=None, key="hT"):
                dst = hT if dst is None else dst
                for i in range(4):
                    xb = xt4[i]
                    S.dma("sp", xb, x_v[tb * 4 + i], writes=[("xt", i)])
                    S.op("act", lambda e, i=i, xb=xb: e.activation(out=xn[i][:, :], in_=xb, func=AF.Square,
                                                                   accum_out=ss4[:, i:i + 1]),
                         reads=[("xt", i)], writes=[("xn", i), ("ss4", i)])
                    S.op("dve", lambda e, i=i: e.tensor_scalar(out=rs4[:, i:i + 1], in0=ss4[:, i:i + 1], scalar1=1.0 / D,
                                                               scalar2=EPS, op0=ALU.mult, op1=ALU.add),
                         reads=[("ss4", i)], writes=[("rs4", i)])
                    S.op("act", lambda e, i=i: e.activation(out=rs4[:, i:i + 1], in_=rs4[:, i:i + 1], func=AF.Ln),
                         reads=[("rs4", i), "mh"], writes=[("rs4", i)])
                    S.op("act", lambda e, i=i: e.activation(out=rs4[:, i:i + 1], in_=rs4[:, i:i + 1], func=AF.Exp, scale=-0.5),
                         reads=[("rs4", i), "mh"], writes=[("rs4", i)])
                    S.op("dve", lambda e, i=i, xb=xb: e.tensor_scalar(out=xn[i][:, :], in0=xb, scalar1=rs4[:, i:i + 1],
                                                                      scalar2=None, op0=ALU.mult),
                         reads=[("xt", i), ("rs4", i)], writes=[("xn", i)])
                for c in range(8):
                    bk = c // 2
                    for i in range(4):
                        S.op("pe", lambda e, c=c, i=i, bk=bk: e.transpose(
                            out=bankbf(bk)[:, (c % 2) * 512 + i * 128:(c % 2) * 512 + (i + 1) * 128],
                            in_=xn[i][:, c * 128:(c + 1) * 128], identity=idt[:, :]),
                            reads=[("xn", i), "idt"], writes=[PB(bk)], sig=(i == 3))
                for c in range(8):
                    bk = c // 2
                    if c % 2 == 0:
                        S.op("act", lambda e, c=c, bk=bk: e.activation(
                            out=dst[:, c, :], in_=bankbf(bk)[:, (c % 2) * 512:(c % 2 + 1) * 512], func=AF.Identity,
                            scale=gs[:, c:c + 1], bias=sh[:, c:c + 1]),
                            reads=[PB(bk), "gsa", "modsb"], writes=[(key, c)])
                    else:
                        S.op("dve", lambda e, c=c, bk=bk: e.tensor_scalar(
                            out=dst[:, c, :], in0=bankbf(bk)[:, (c % 2) * 512:(c % 2 + 1) * 512],
                            scalar1=gs[:, c:c + 1], scalar2=sh[:, c:c + 1], op0=ALU.mult, op1=ALU.add),
                            reads=[PB(bk), "gsa", "modsb"], writes=[(key, c)])

            def qk_post(src_ap, nheads, gb, gname, tt, dst_fn, dst_keys, sb_i, grp=None):
                H = nheads
                W = H * 64
                v3 = lambda ap: ap.rearrange("p (h d) -> p h d", d=64)
                src3 = v3(src_ap)
                tmp3 = v3(qtmp[:, 0:W])
                S.op("dve", lambda e: e.tensor_tensor(out=qtmp[:, 0:W], in0=src_ap, in1=src_ap, op=ALU.mult),
                     reads=[("qsb", sb_i)], writes=["qtmp"])
                S.op("dve", lambda e: e.tensor_reduce(out=ssq[:, 0:H], in_=tmp3, axis=AX.X, op=ALU.add),
                     reads=["qtmp"], writes=["ssq"])
                S.op("dve", lambda e: e.tensor_scalar(out=rq[:, 0:H], in0=ssq[:, 0:H], scalar1=1.0 / 64, scalar2=EPS,
                                                      op0=ALU.mult, op1=ALU.add), reads=["ssq"], writes=["rq"])
                yield
                yield
                S.op("act", lambda e: e.activation(out=rq[:, 0:H], in_=rq[:, 0:H], func=AF.Ln),
                     reads=["rq", "mh"], writes=["rq"])
                S.op("act", lambda e: e.activation(out=rq[:, 0:H], in_=rq[:, 0:H], func=AF.Exp, scale=-0.5),
                     reads=["rq", "mh"], writes=["rq"])
                yield
                yield
                S.op("dve", lambda e: e.tensor_tensor(out=tmp3, in0=src3, in1=rq[:, 0:H].unsqueeze(2).to_broadcast([128, H, 64]),
                                                      op=ALU.mult), reads=[("qsb", sb_i), "rq"], writes=["qtmp"])
                S.op("dve", lambda e: e.tensor_tensor(out=tmp3, in0=tmp3, in1=gb[:, :].unsqueeze(1).to_broadcast([128, H, 64]),
                                                      op=ALU.mult), reads=["qtmp", gname], writes=["qtmp"])
                x1 = tmp3[:, :, 0:32]
                x2 = tmp3[:, :, 32:64]
                t1 = rt1[:, 0:H, :]
                t2 = rt2[:, 0:H, :]
                if grp is None:
                    cosb = cs[:, tt, 0:32].unsqueeze(1).to_broadcast([128, H, 32])
                    sinb = cs[:, tt, 32:64].unsqueeze(1).to_broadcast([128, H, 32])
                else:
                    A, B = grp
                    f4 = lambda ap: ap.rearrange("p (a b) d -> p a b d", a=A)
                    x1, x2, t1, t2 = f4(x1), f4(x2), f4(t1), f4(t2)
                    cosb = cs[:, tt:tt + A, 0:32].unsqueeze(2).to_broadcast([128, A, B, 32])
                    sinb = cs[:, tt:tt + A, 32:64].unsqueeze(2).to_broadcast([128, A, B, 32])
                    _dst = dst_fn
                    dst_fn = lambda half: f4(_dst(half))
                S.op("dve", lambda e: e.tensor_tensor(out=t1, in0=x1, in1=cosb, op=ALU.mult), reads=["qtmp", "cs"], writes=["rt1"])
                S.op("dve", lambda e: e.tensor_tensor(out=t2, in0=x2, in1=sinb, op=ALU.mult), reads=["qtmp", "cs"], writes=["rt2"])
                S.op("dve", lambda e: e.tensor_tensor(out=dst_fn(0), in0=t1, in1=t2, op=ALU.subtract),
                     reads=["rt1", "rt2"], writes=dst_keys)
                S.op("dve", lambda e: e.tensor_tensor(out=t1, in0=x1, in1=sinb, op=ALU.mult), reads=["qtmp", "cs"], writes=["rt1"])
                S.op("dve", lambda e: e.tensor_tensor(out=t2, in0=x2, in1=cosb, op=ALU.mult), reads=["qtmp", "cs"], writes=["rt2"])
                S.op("dve", lambda e: e.tensor_tensor(out=dst_fn(1), in0=t1, in1=t2, op=ALU.add),
                     reads=["rt1", "rt2"], writes=dst_keys)

            def p1a_post(tb, hb, hk):
                for i in range(4):
                    bk = 4 + i // 2
                    for c in range(8):
                        S.op("pe", lambda e, c=c, i=i, bk=bk: e.matmul(bank(bk, 256, (i % 2) * 256), lhsT=hb[:, c, i * 128:(i + 1) * 128],
                                                                     rhs=win[:, c, 2048:2304], start=(c == 0), stop=(c == 7),
                                                                     skip_group_check=True),
                             reads=[(hk, c), "win"], writes=[PB(bk)], sig=(c == 7))
                for bb in range(2):
                    src = bank(4 + bb).rearrange("p (t x) -> p t x", t=2)
                    S.op("act", lambda e, bb=bb, src=src: e.copy(
                        out=qsb[0][:, bb * 256:(bb + 1) * 256].rearrange("p (t x) -> p t x", t=2), in_=src[:, :, 0:128]),
                        reads=[PB(4 + bb)], writes=[("qsb", 0)])
                    tt0 = tb * 4 + 2 * bb
                    S.op("dve", lambda e, tt0=tt0, src=src: e.tensor_copy(
                        out=Vg[:, tt0:tt0 + 2, :, 0:64], in_=src[:, :, 128:256].rearrange("p t (h d) -> p t h d", d=64)),
                        reads=[PB(4 + bb)], writes=[("Vg", tt0), ("Vg", tt0 + 1)])
                kh3 = khat[:, :, :].rearrange("p t (h d) -> p (t h) d", d=64)
                for _ in qk_post(qsb[0][:, :], 8, gkb, "gkb", tb * 4,
                                 lambda half, kh3=kh3: kh3[:, :, half * 32:(half + 1) * 32], ["khat"], 0, grp=(4, 2)):
                    pass
                for i in range(4):
                    S.op("pe", lambda e, i=i: e.transpose(out=bankbf(6)[:, i * 128:(i + 1) * 128], in_=khat[:, i, :], identity=idt[:, :]),
                         reads=["khat", "idt"], writes=[PB(6)], sig=(i == 3))
                S.op("act", lambda e, tb=tb: e.copy(out=KTg[:, tb * 512:(tb + 1) * 512], in_=bankbf(6)[:, 0:512]),
                     reads=[PB(6)], writes=[("KTg", tb * 4 + i) for i in range(4)])

            def stage_a(tb, part="all"):
                slot = tb % 3
                if part in ("all", "kv"):
                    norm_block(tb, gsa, sha)
                n_mm = 0
                for m in range(4):
                    for which in range(2):
                        if which == 0 and part == "kv":
                            continue
                        if which == 1 and part == "q":
                            continue
                        bk = 4 + (n_mm % 2)
                        n_mm += 1
                        col0 = which * 512 + m * 128
                        for c in range(8):
                            S.op("pe", lambda e, c=c, bk=bk, col0=col0: e.matmul(
                                bank(bk), lhsT=win[:, c, col0:col0 + 128], rhs=hT[:, c, :], start=(c == 0), stop=(c == 7)),
                                reads=[("hT", c), "win"], writes=[PB(bk)], sig=(c == 7))
                        if which == 0:
                            S.op("act", lambda e, bk=bk, m=m: e.copy(out=qTlo[0:64, m, :], in_=bank(bk)[0:64, :]),
                                 reads=[PB(bk), "qTlo0"], writes=[("qTlo", m)])
                            S.op("dve", lambda e, bk=bk, m=m: e.tensor_copy(out=qThi[64:128, m, :], in_=bank(bk)[64:128, :]),
                                 reads=[PB(bk), "qThi0"], writes=[("qThi", m)])
                        else:
                            S.op("act", lambda e, bk=bk, m=m: e.copy(out=kTn[:, slot, m, :], in_=bank(bk)),
                                 reads=[PB(bk)], writes=[("kTn", slot, m)])
                for i in range(4 if part in ("all", "kv") else 0):
                    tt = tb * 4 + i
                    bk = 4 + (i % 2)
                    for c in range(8):
                        S.op("pe", lambda e, c=c, i=i, bk=bk: e.matmul(bank(bk), lhsT=hT[:, c, i * 128:(i + 1) * 128],
                                                                     rhs=win[:, c, 1024:1536], start=(c == 0), stop=(c == 7)),
                             reads=[("hT", c), "win"], writes=[PB(bk)], sig=(c == 7))
                    S.op("act", lambda e, bk=bk, i=i: e.copy(
                        out=Vn[:, slot * 4 + i, :, 0:64], in_=bank(bk).rearrange("p (h d) -> p h d", d=64)),
                        reads=[PB(bk), "Vn1"], writes=[("Vn", slot * 4 + i)])
                if part == "kv":
                    return
                for i in range(4):
                    tt = tb * 4 + i
                    bk = 6 + (i % 2)
                    for c in range(8):
                        S.op("pe", lambda e, c=c, i=i, bk=bk: e.matmul(bank(bk), lhsT=hT[:, c, i * 128:(i + 1) * 128],
                                                                     rhs=win[:, c, 1536:2048], start=(c == 0), stop=(c == 7)),
                             reads=[("hT", c), "win"], writes=[PB(bk)], sig=(c == 7))
                    sbi = i % 2
                    S.op("act", lambda e, bk=bk, sbi=sbi: e.copy(
                        out=qsb[sbi][:, :].rearrange("p (g kv d) -> p g kv d", g=4, kv=2),
                        in_=bank(bk).rearrange("p (kv g d) -> p g kv d", g=4, kv=2)),
                        reads=[PB(bk)], writes=[("qsb", sbi)])
                    qh3 = qhat[:, i, :].rearrange("p (h d) -> p h d", d=64)
                    for _ in qk_post(qsb[sbi][:, :], 8, gqb, "gqb", tt,
                                     lambda half, qh3=qh3: qh3[:, :, half * 32:(half + 1) * 32], [("qhat", i)], sbi):
                        pass
                for g in range(4):
                    bk = 4 + g // 2
                    for i in range(4):
                        S.op("pe", lambda e, g=g, i=i, bk=bk: e.transpose(
                            out=bankbf(bk)[:, (g % 2) * 512 + i * 128:(g % 2) * 512 + (i + 1) * 128],
                            in_=qhat[:, i, g * 128:(g + 1) * 128], identity=idt[:, :]),
                            reads=[("qhat", i), "idt"], writes=[PB(bk)], sig=(i == 3))
                for g in range(4):
                    bk = 4 + g // 2
                    src = bankbf(bk)[:, (g % 2) * 512:(g % 2 + 1) * 512]
                    S.op("act", lambda e, g=g, src=src: e.copy(out=QTlo[0:64, g, :], in_=src[0:64, :]),
                         reads=[PB(bk), "QTlo0"], writes=[("QTlo", g)])
                    S.op("dve", lambda e, g=g, src=src: e.tensor_copy(out=QThi[64:128, g, :], in_=src[64:128, :]),
                         reads=[PB(bk), "QThi0"], writes=[("QThi", g)])

            def normalize_head(acc_bk, h_chunk, half, bc_bk):
                S.op("dve", lambda e: e.reciprocal(out=rec[64:65, :], in_=bank(acc_bk)[64:65, :]),
                     reads=[PB(acc_bk)], writes=["rec"])
                S.op("pe", lambda e: e.matmul(bank(bc_bk)[0:128, :], lhsT=sel64[:, :], rhs=rec[:, :], start=True, stop=True),
                     reads=["sel64", "rec"], writes=[PB(bc_bk)])
                S.op("act", lambda e: e.copy(out=bcsb[:, :], in_=bank(bc_bk)[0:64, :]), reads=[PB(bc_bk)], writes=["bcsb"])
                S.op("dve", lambda e: e.tensor_tensor(out=OT[half * 64:(half + 1) * 64, h_chunk, :], in0=bank(acc_bk)[0:64, :],
                                                      in1=bcsb[:, :], op=ALU.mult),
                     reads=[PB(acc_bk), "bcsb"], writes=[("OT", h_chunk, half)])

            sp_loaded = {}
            sp_next = [0]

            def na_block(tb):
                steps = []
                for h in range(8):
                    items = []
                    for j in range(max(0, 4 * tb - 2), min(31, 4 * tb + 5) + 1):
                        r0, r1 = _chunk_rows(j)
                        lo, hi = max(8 * tb, r0), min(8 * tb + 7, r1)
                        if lo <= hi:
                            items.append((j, lo, hi))
                    for idx, (j, lo, hi) in enumerate(items):
                        steps.append(dict(h=h, j=j, lo=lo, hi=hi, first=(idx == 0), last=(idx == len(items) - 1)))
                n = len(steps)
                for i, stp in enumerate(steps):
                    stp["sbk"] = [3, 4, 5, 0, 1][i % 5]
                    stp["bi"] = i % 3
                    stp["nq"] = (stp["hi"] - stp["lo"] + 1) * 64
                    stp["qc0"] = (stp["lo"] - 8 * tb) * 64

                def qk(i):
                    p = steps[i]
                    h, j = p["h"], p["j"]
                    m, half = h // 2, h % 2
                    qT = qTlo if half == 0 else qThi
                    qkey = ("qTlo", m) if half == 0 else ("qThi", m)
                    kslot, kcol = (j // 4) % 3, (j % 4) * 128
                    nq, qc0, sbk = p["nq"], p["qc0"], p["sbk"]
                    S.op("pe", lambda e: e.matmul(bank(sbk, nq), lhsT=kTn[:, kslot, m, kcol:kcol + 128],
                                                  rhs=qT[:, m, qc0:qc0 + nq], start=True, stop=True),
                         reads=[("kTn", kslot, m), qkey], writes=[PB(sbk)])

                def bias_exp(i):
                    p = steps[i]
                    h, j, lo = p["h"], p["j"], p["lo"]
                    nq, sbk, bi = p["nq"], p["sbk"], p["bi"]
                    boff = (lo - _tile_row0(j)) * 64
                    if j in SPECIAL:
                        key = (h, j)
                        if key not in sp_loaded:
                            si = sp_next[0] % 3
                            sp_next[0] += 1
                            S.dma("sp", bsp[si][:, :], bsp_d.ap()[h, SPECIAL[j]], writes=[("bsp", si)])
                            for k2 in [k for k, v in sp_loaded.items() if v == si]:
                                del sp_loaded[k2]
                            sp_loaded[key] = si
                        si = sp_loaded[key]
                        b_ap = bsp[si][:, boff:boff + nq]
                        bkey = ("bsp", si)
                    else:
                        b_ap = bint[:, h, boff:boff + nq]
                        bkey = "bint"
                    S.op("dve", lambda e: e.scalar_tensor_tensor(
                        out=Ssb[bi][:, 0:nq], in0=bank(sbk, nq), scalar=0.125, in1=b_ap, op0=ALU.mult, op1=ALU.add),
                        reads=[PB(sbk), bkey], writes=[("Ssb", bi)])
                    S.op("act", lambda e: e.activation(out=Pna[bi][:, 0:nq], in_=Ssb[bi][:, 0:nq], func=AF.Exp),
                         reads=[("Ssb", bi)], writes=[("Pna", bi)])

                def pv(i):
                    p = steps[i]
                    h, j = p["h"], p["j"]
                    vt = ((j // 4) % 3) * 4 + (j % 4)
                    nq, qc0, bi = p["nq"], p["qc0"], p["bi"]
                    acc_bk = 6 + (h % 2)
                    S.op("pe", lambda e: e.matmul(bank(acc_bk)[0:65, qc0:qc0 + nq], lhsT=Vn[:, vt, h, 0:65],
                                                  rhs=Pna[bi][:, 0:nq], start=p["first"], stop=p["last"],
                                                  skip_group_check=True),
                         reads=[("Vn", vt), "Vn1", ("Pna", bi)], writes=[PB(acc_bk)])

                deferred = []

                def sched_normalize(i, h):
                    acc_bk, m, half = 6 + (h % 2), h // 2, h % 2
                    def recip_row():
                        S.op("act", lambda e: e.activation(out=rec[64:65, :], in_=bank(acc_bk)[64:65, :], func=AF.Ln),
                             reads=[PB(acc_bk)], writes=["rec"])
                        S.op("act", lambda e: e.activation(out=rec[64:65, :], in_=rec[64:65, :], func=AF.Exp, scale=-1.0),
                             reads=["rec"], writes=["rec"])
                    deferred.append((i + 1, recip_row))
                    deferred.append((i + 2, lambda: S.op(
                        "pe", lambda e: e.matmul(bank(2)[0:128, :], lhsT=sel64[:, :], rhs=rec[:, :], start=True, stop=True),
                        reads=["sel64", "rec"], writes=[PB(2)])))
                    deferred.append((i + 3, lambda: S.op(
                        "act", lambda e: e.copy(out=bcsb[:, :], in_=bank(2)[0:64, :]), reads=[PB(2)], writes=["bcsb"])))
                    deferred.append((i + 4, lambda: S.op(
                        "dve", lambda e: e.tensor_tensor(out=OT[half * 64:(half + 1) * 64, m, :], in0=bank(acc_bk)[0:64, :],
                                                         in1=bcsb[:, :], op=ALU.mult),
                        reads=[PB(acc_bk), "bcsb"], writes=[("OT", m, half)])))

                def run_deferred(i):
                    rest = []
                    for at, th in deferred:
                        if at <= i:
                            th()
                        else:
                            rest.append((at, th))
                    deferred[:] = rest

                for i0 in range(min(4, n)):
                    qk(i0)
                for i in range(n):
                    bias_exp(i)
                    if i + 4 < n:
                        qk(i + 4)
                    run_deferred(i)
                    pv(i)
                    if steps[i]["last"]:
                        sched_normalize(i, steps[i]["h"])
                run_deferred(10 ** 9)

            fb = [0]

            def fbank():
                fb[0] += 1
                return 6 + (fb[0] % 2)

            def th_norm(tb):
                L = []

                def dm(i):
                    S.dma("sp", xt[i % 2][:, :], x_v[tb * 4 + i], writes=[("xt", i % 2)])

                def sq(i):
                    xb = xt[i % 2]
                    S.op("act", lambda e: e.activation(out=xn[i][:, :], in_=xb[:, :], func=AF.Square,
                                                       accum_out=ss4[:, i:i + 1]),
                         reads=[("xt", i % 2)], writes=[("xn", i), ("ss4", i)])

                def ms(i):
                    S.op("dve", lambda e: e.tensor_scalar(out=rs4[:, i:i + 1], in0=ss4[:, i:i + 1], scalar1=1.0 / D,
                                                          scalar2=EPS, op0=ALU.mult, op1=ALU.add),
                         reads=[("ss4", i)], writes=[("rs4", i)])

                def le(i):
                    S.op("act", lambda e: e.activation(out=rs4[:, i:i + 1], in_=rs4[:, i:i + 1], func=AF.Ln),
                         reads=[("rs4", i)], writes=[("rs4", i)])
                    S.op("act", lambda e: e.activation(out=rs4[:, i:i + 1], in_=rs4[:, i:i + 1], func=AF.Exp, scale=-0.5),
                         reads=[("rs4", i)], writes=[("rs4", i)])

                def scl(i):
                    xb = xt[i % 2]
                    S.op("dve", lambda e: e.tensor_scalar(out=xn[i][:, :], in0=xb[:, :], scalar1=rs4[:, i:i + 1],
                                                          scalar2=None, op0=ALU.mult),
                         reads=[("xt", i % 2), ("rs4", i)], writes=[("xn", i)])

                seq = [(dm, 0), (dm, 1), None, None, (sq, 0), (ms, 0), (sq, 1), (le, 0), (ms, 1), (scl, 0), (le, 1), (dm, 2),
                       (scl, 1), (dm, 3), None, None, (sq, 2), (ms, 2), (sq, 3), (le, 2), (ms, 3), (scl, 2), (le, 3), (scl, 3)]
                for it in seq:
                    if it is None:
                        L.append(lambda: None)
                    else:
                        L.append(lambda fn=it[0], i=it[1]: fn(i))

                def tr(r, bb):
                    bk = 6 + bb
                    for cc in range(2):
                        c = 4 * r + 2 * bb + cc
                        for i in range(4):
                            S.op("pe", lambda e, c=c, cc=cc, i=i: e.transpose(
                                out=bankbf(bk)[:, cc * 512 + i * 128:cc * 512 + (i + 1) * 128],
                                in_=xn[i][:, c * 128:(c + 1) * 128], identity=idt[:, :]),
                                reads=[("xn", i), "idt"], writes=[PB(bk)], sig=(i == 3))
                            if i % 2 == 1:
                                yield

                def ev(r, bb):
                    bk = 6 + bb
                    for cc in range(2):
                        c = 4 * r + 2 * bb + cc
                        S.op("dve", lambda e, c=c, cc=cc: e.tensor_scalar(
                            out=hT[:, c, :], in0=bankbf(bk)[:, cc * 512:(cc + 1) * 512],
                            scalar1=gsa[:, c:c + 1], scalar2=sha[:, c:c + 1], op0=ALU.mult, op1=ALU.add),
                            reads=[PB(bk), "gsa", "modsb"], writes=[("hT", c)])

                L2 = []
                for r in range(2):
                    for bb in range(2):
                        trf = (lambda r=r, bb=bb: tr(r, bb))
                        trf.units = 5
                        L2.append(trf)
                    for bb in range(2):
                        L2.append(lambda r=r, bb=bb: ev(r, bb))
                return L, L2

            def proj_group(lhs_fn, rhs_fn, bk, n=512, hk="hT"):
                for c in range(8):
                    S.op("pe", lambda e, c=c: e.matmul(bank(bk, n), lhsT=lhs_fn(c), rhs=rhs_fn(c), start=(c == 0), stop=(c == 7)),
                         reads=[(hk, c), "win"], writes=[PB(bk)], sig=(c == 7))
                    if c % 2 == 1:
                        yield

            def th_kv(tb, hT=hT, hk="hT"):
                slot = tb % 3
                L = []
                for m in range(4):
                    def pk(m=m):
                        bk = fbank()
                        col0 = 512 + m * 128
                        yield from proj_group(lambda c: win[:, c, col0:col0 + 128], lambda c: hT[:, c, :], bk, hk=hk)
                        S.op("dve", lambda e: e.tensor_copy(out=kTn[:, slot, m, :], in_=bank(bk)),
                             reads=[PB(bk)], writes=[("kTn", slot, m)])
                    pk.units = 5
                    L.append(pk)
                for i in range(4):
                    def pvv(i=i):
                        bk = fbank()
                        yield from proj_group(lambda c: hT[:, c, i * 128:(i + 1) * 128], lambda c: win[:, c, 1024:1536], bk, hk=hk)
                        S.op("dve", lambda e: e.tensor_copy(
                            out=Vn[:, slot * 4 + i, :, 0:64], in_=bank(bk).rearrange("p (h d) -> p h d", d=64)),
                            reads=[PB(bk), "Vn1"], writes=[("Vn", slot * 4 + i)])
                    pvv.units = 5
                    L.append(pvv)
                return L

            def th_q(tb, hT=hT, hk="hT"):
                L = []
                for m in range(4):
                    def pq(m=m):
                        bk = fbank()
                        col0 = m * 128
                        yield from proj_group(lambda c: win[:, c, col0:col0 + 128], lambda c: hT[:, c, :], bk, hk=hk)
                        S.op("dve", lambda e: e.tensor_copy(out=qTlo[0:64, m, :], in_=bank(bk)[0:64, :]),
                             reads=[PB(bk), "qTlo0"], writes=[("qTlo", m)])
                        S.op("dve", lambda e: e.tensor_copy(out=qThi[64:128, m, :], in_=bank(bk)[64:128, :]),
                             reads=[PB(bk), "qThi0"], writes=[("qThi", m)])
                    pq.units = 5
                    L.append(pq)
                for i in range(4):
                    tt = tb * 4 + i
                    sbi = i % 2

                    def pg(i=i, sbi=sbi):
                        bk = fbank()
                        yield from proj_group(lambda c: hT[:, c, i * 128:(i + 1) * 128], lambda c: win[:, c, 1536:2048], bk, hk=hk)
                        S.op("dve", lambda e: e.tensor_copy(
                            out=qsb[sbi][:, :].rearrange("p (g kv d) -> p g kv d", g=4, kv=2),
                            in_=bank(bk).rearrange("p (kv g d) -> p g kv d", g=4, kv=2)),
                            reads=[PB(bk)], writes=[("qsb", sbi)])
                    pg.units = 5
                    L.append(pg)

                    def post(i=i, sbi=sbi, tt=tt):
                        qh3 = qhat[:, i, :].rearrange("p (h d) -> p h d", d=64)
                        yield from qk_post(qsb[sbi][:, :], 8, gqb, "gqb", tt,
                                           lambda half: qh3[:, :, half * 32:(half + 1) * 32], [("qhat", i)], sbi)
                    post.units = 5
                    L.append(post)
                return L

            def finalize_q():
                for g in range(4):
                    bk = g // 2
                    for i in range(4):
                        S.op("pe", lambda e, g=g, i=i, bk=bk: e.transpose(
                            out=bankbf(bk)[:, (g % 2) * 512 + i * 128:(g % 2) * 512 + (i + 1) * 128],
                            in_=qhat[:, i, g * 128:(g + 1) * 128], identity=idt[:, :]),
                            reads=[("qhat", i), "idt"], writes=[PB(bk)], sig=(i == 3))
                for g in range(4):
                    bk = g // 2
                    src = bankbf(bk)[:, (g % 2) * 512:(g % 2 + 1) * 512]
                    S.op("dve", lambda e, g=g, src=src: e.tensor_copy(out=QTlo[0:64, g, :], in_=src[0:64, :]),
                         reads=[PB(bk), "QTlo0"], writes=[("QTlo", g)])
                    S.op("dve", lambda e, g=g, src=src: e.tensor_copy(out=QThi[64:128, g, :], in_=src[64:128, :]),
                         reads=[PB(bk), "QThi0"], writes=[("QThi", g)])

            def gqa_block(tb, filler):
                nf = sum(getattr(t, "units", 1) for t in filler)
                fl = Filler(filler)
                state = {"emitted": 0, "step": 0}
                for kv in range(2):
                    QT = QTlo if kv == 0 else QThi
                    qname = "QTlo" if kv == 0 else "QThi"
                    for gp in range(2):
                        def qk(kc):
                            b0 = 2 + 2 * (kc % 2)
                            for u in range(2):
                                g = gp * 2 + u
                                S.op("pe", lambda e, u=u, g=g: e.matmul(
                                    bank(b0 + u), lhsT=KTg[:, kc * 128:(kc + 1) * 128], rhs=QT[:, g, :], start=True, stop=True),
                                    reads=[("KTg", kc), (qname, g)], writes=[PB(b0), PB(b0 + 1)], sig=(u == 1))

                        def ex(kc):
                            b0 = 2 + 2 * (kc % 2)
                            S.op("act", lambda e: e.activation(
                                out=Pg[kc % 2][:, :], in_=ps[:, b0 * 512:(b0 + 2) * 512], func=AF.Exp, scale=0.125),
                                reads=[PB(b0), PB(b0 + 1)], writes=[("Pg", kc % 2)])

                        def pv(kc):
                            for u in range(2):
                                S.op("pe", lambda e, u=u: e.matmul(
                                    bank(u), lhsT=Vg[:, kc, kv, :], rhs=Pg[kc % 2][:, u * 512:(u + 1) * 512],
                                    start=(kc == 0), stop=(kc == 31)),
                                    reads=[("Vg", kc), "Vg1", ("Pg", kc % 2)], writes=[PB(u)])

                        qk(0)
                        for kc in range(32):
                            if kc + 1 < 32:
                                qk(kc + 1)
                            ex(kc)
                            state["step"] += 1
                            target = (nf * state["step"]) // 124
                            while state["emitted"] < target:
                                fl.step()
                                state["emitted"] += 1
                            pv(kc)
                        for u in range(2):
                            h = kv * 4 + gp * 2 + u
                            half, chk = h % 2, 4 + h // 2
                            S.op("act", lambda e, u=u: e.activation(out=rg[u][:, :], in_=bank(u)[64:128, :], func=AF.Ln),
                                 reads=[PB(u)], writes=[("rg", u)])
                            S.op("act", lambda e, u=u: e.activation(out=rg[u][:, :], in_=rg[u][:, :], func=AF.Exp, scale=-1.0),
                                 reads=[("rg", u)], writes=[("rg", u)])
                            S.op("dve", lambda e, u=u, half=half, chk=chk: e.tensor_tensor(
                                out=OT[half * 64:(half + 1) * 64, chk, :], in0=bank(u)[0:64, :], in1=rg[u][:, :], op=ALU.mult),
                                reads=[PB(u), ("rg", u)], writes=[("OT", chk, half)])
                fl.drain()

            def store_ot(tb):
                for c in range(8):
                    S.dma("pool", ot_s.ap()[c][:, tb * 512:(tb + 1) * 512], OT[:, c, :],
                          reads=[("OT", c, 0), ("OT", c, 1)], writes=[("ot_s", tb)])

            def run(L):
                for t in L:
                    r = t()
                    if hasattr(r, "__next__"):
                        for _ in r:
                            pass

            class Filler:
                def __init__(self, items):
                    self.items = list(items)
                    self.cur = None

                def step(self):
                    while True:
                        if self.cur is None:
                            if not self.items:
                                return False
                            r = self.items.pop(0)()
                            if hasattr(r, "__next__"):
                                self.cur = r
                            else:
                                return True
                        try:
                            next(self.cur)
                            return True
                        except StopIteration:
                            self.cur = None

                def drain(self):
                    while self.step():
                        pass

            order = [2, 3, 4, 5, 6, 7, 0, 1]
            hbufs = [(OT, "hTa"), (hT, "hT")]
            for g in range(4):
                ada_mm(g)
                ada_dma(g + 2)
                if g == 1:
                    S.dma("pool", win[:, :, :], win_d.ap().rearrange("(c p) n -> p c n", p=128), writes=["win"])
            S.op("dve", lambda e: e.tensor_tensor(out=modsb[:, 0:16], in0=bank(7, 16), in1=bada[:, 0:16], op=ALU.add),
                 reads=[PB(7), "bada"], writes=["modsb"])
            S.op("dve", lambda e: e.scalar_tensor_tensor(out=gsa[:, :], in0=modsb[:, 8:16], scalar=1.0, in1=gat[:, :],
                                                         op0=ALU.add, op1=ALU.mult),
                 reads=["modsb", "gat"], writes=["gsa"])
            ada_plan = {0: [4, 5], 1: [6, 7], 2: [8], 3: [9], 4: [10], 5: [11]}
            norm_block(order[0], gsa, sha, dst=hbufs[0][0], key=hbufs[0][1])
            for k, tbk in enumerate(order):
                hb, hk = hbufs[k % 2]
                if k + 1 < len(order):
                    nb_, nk_ = hbufs[(k + 1) % 2]
                    norm_block(order[k + 1], gsa, sha, dst=nb_, key=nk_)
                p1a_post(tbk, hb, hk)
                for g in ada_plan.get(k, []):
                    ada_mm(g)
                    if g + 2 < 12:
                        ada_dma(g + 2)
                if k == 5:
                    S.op("dve", lambda e: e.tensor_tensor(out=modsb[:, 16:48], in0=bank(7, 32, 16), in1=bada[:, 16:48], op=ALU.add),
                         reads=[PB(7), "bada"], writes=["modsb2"])
                    S.op("dve", lambda e: e.scalar_tensor_tensor(out=gsf[:, :], in0=modsb[:, 32:40], scalar=1.0, in1=gff[:, :],
                                                                 op0=ALU.add, op1=ALU.mult),
                         reads=["modsb2", "gff"], writes=["gsf"])
                    if debug:
                        S.dma("sp", dbg["d_mod"].ap(), modsb[:, :], reads=["modsb", "modsb2"], writes=["d_mod"])
                    S.op("dve", lambda e: e.memset(Vn[:, :, :, 64:65], 1.0), writes=["Vn1", ("wa", 0), ("wa", 1)])
                if tbk in (0, 1):
                    run(th_kv(tbk, hT=hb, hk=hk))
                if tbk == 0:
                    run(th_q(0, hT=hb, hk=hk))
                    finalize_q()
            S.dma("sp", bint[:, :, :], bint_d.ap().rearrange("h p n -> p h n"),
                  writes=["bint", ("xt", 2), ("xt", 3)])
            S.barrier()
            if stop_after == "p1a":
                S.finish("sp")
                return nc
            for tb in range(NB):
                for _ in range(6):
                    if prep:
                        prep.pop(0)()
                na_block(tb)
                filler = []
                npre, npost = ([], [])
                if tb + 2 < NB:
                    npre, npost = th_norm(tb + 2)
                filler += npre
                if tb + 1 < NB:
                    filler += th_q(tb + 1)
                if tb + 2 < NB:
                    filler += npost + th_kv(tb + 2)
                gqa_block(tb, filler)
                store_ot(tb)
                if tb + 1 < NB:
                    finalize_q()
            if debug:
                S.barrier()
                S.dma("sp", dbg["d_ot"].ap(), ot_s.ap(), writes=["d_ot"])
            S.barrier()

        if stop_after == "p1":
            S.finish("sp")
            return nc

        with ExitStack() as s2:
            wo = sbt(s2, "wo", [128, 8, D], BF16)
            OTb = [sbt(s2, f"OTb{i}", [128, 8, 512], BF16) for i in range(2)]
            x1s = [[sbt(s2, f"x1_{p}_{i}", [128, D], F32) for i in range(4)] for p in range(2)]
            ytmp = [sbt(s2, f"ytmp{i}", [128, 512], F32) for i in range(2)]
            ytx = [sbt(s2, f"ytx{i}", [128, 512], F32) for i in range(2)]
            ss4c = sbt(s2, "ss4c", [128, 4], F32)
            rs4c = sbt(s2, "rs4c", [128, 4], F32)
            xn2 = [sbt(s2, f"xn2_{i}", [128, D], BF16) for i in range(4)]
            junk2 = sbt(s2, "junk2", [128, D], BF16)
            ss4b = sbt(s2, "ss4b", [128, 4], F32)
            rs4b = sbt(s2, "rs4b", [128, 4], F32)
            hT2s = [sbt(s2, f"hT2_{p}", [128, 8, 512], BF16) for p in range(2)]
            wgb = [sbt(s2, f"wgb{i}", [128, D], BF16) for i in range(5)]
            wub = [sbt(s2, f"wub{i}", [128, D], BF16) for i in range(5)]
            wdb = [sbt(s2, f"wdb{i}", [128, 2, 512], BF16) for i in range(6)]
            sg = [sbt(s2, f"sg{i}", [128, 512], F32) for i in range(2)]
            hid = sbt(s2, "hid", [128, NF, 512], BF16)
            ot = [sbt(s2, f"ot{i}", [128, D], F32) for i in range(2)]

            S.dma("pool", wo[:, :, :], wo_d.ap().rearrange("(c p) n -> p c n", p=128), writes=["wo"])
            gate_a = sbt(s2, "gate_a", [128, D], F32)
            gate_f = sbt(s2, "gate_f", [128, D], F32)
            gfin = sbt(s2, "gfin", [128, D], F32)
            dg = [sbt(s2, f"dg{i}", [128, 128], F32) for i in range(2)]
            S.dma("sp", gfin[:, :], gfin_d.ap(), writes=["gfin"])
            for gi, (off, gt, gname) in enumerate(((16, gate_a, "gate_a"), (40, gate_f, "gate_f"))):
                for j in range(8):
                    d = dg[j % 2]
                    S.op("dve", lambda e, d=d, off=off, j=j: e.tensor_scalar(
                        out=d[:, :], in0=identf[:, :], scalar1=modsb[:, off + j:off + j + 1], scalar2=None, op0=ALU.mult),
                        reads=["identf", "modsb"], writes=[("dg", j % 2)])
                    bk = 1 + (j // 4)
                    S.op("pe", lambda e, d=d, bk=bk, j=j: e.matmul(bank(bk, 128, (j % 4) * 128), lhsT=onesf[:, :], rhs=d[:, :],
                                                                 start=True, stop=True),
                         reads=["onesf", ("dg", j % 2)], writes=[PB(bk)])
                for hb in range(2):
                    S.op("act", lambda e, gt=gt, hb=hb: e.copy(out=gt[:, hb * 512:(hb + 1) * 512], in_=bank(1 + hb)),
                         reads=[PB(1 + hb)], writes=[(gname, hb)])
            x_v = x_d.ap().rearrange("(tt p) d -> tt p d", p=128)
            y_v = y_d.ap().rearrange("(tt p) d -> tt p d", p=128)
            n_ld = {"g": 0, "d": 0}
            n_out = [0]

            def x_thunks(tb):
                par = tb % 2
                ob = OTb[par]
                x1 = x1s[par]
                hT2 = hT2s[par]
                L = []

                def ldot():
                    for c in range(8):
                        S.dma("sp", ob[:, c, :], ot_s.ap()[c][:, tb * 512:(tb + 1) * 512], writes=[("OTb", par, c)])
                L.append(ldot)
                for i in range(4):
                    def ldx(i=i):
                        S.dma("sp", x1[i][:, :], x_v[tb * 4 + i], writes=[("x1", par, i, 0), ("x1", par, i, 1)])
                    L.append(ldx)
                for i in range(4):
                    for hf in range(2):
                        def opj(i=i, hf=hf):
                            bk = 4 + (i * 2 + hf) % 2
                            for c in range(8):
                                S.op("pe", lambda e, c=c: e.matmul(
                                    bank(bk), lhsT=ob[:, c, i * 128:(i + 1) * 128], rhs=wo[:, c, hf * 512:(hf + 1) * 512],
                                    start=(c == 0), stop=(c == 7)),
                                    reads=[("OTb", par, c), "wo"], writes=[PB(bk)], sig=(c == 7))
                            yt = ytx[hf]
                            S.op("dve", lambda e: e.tensor_tensor(
                                out=yt[:, :], in0=bank(bk), in1=gate_a[:, hf * 512:(hf + 1) * 512], op=ALU.mult),
                                reads=[PB(bk), ("gate_a", hf)], writes=[("ytx", hf)])
                            S.op("dve", lambda e: e.tensor_tensor(
                                out=x1[i][:, hf * 512:(hf + 1) * 512], in0=x1[i][:, hf * 512:(hf + 1) * 512], in1=yt[:, :], op=ALU.add),
                                reads=[("x1", par, i, hf), ("ytx", hf)], writes=[("x1", par, i, hf)])
                        L.append(opj)
                for i in range(4):
                    def nrm1(i=i):
                        S.op("act", lambda e: e.activation(out=xn2[i][:, :], in_=x1[i][:, :], func=AF.Square,
                                                           accum_out=ss4b[:, i:i + 1]),
                             reads=[("x1", par, i, 0), ("x1", par, i, 1)], writes=[("xn2", i), ("ss4b", i)])
                    L.append(nrm1)

                def nrmr():
                    S.op("dve", lambda e: e.tensor_scalar(out=rs4b[:, :], in0=ss4b[:, :], scalar1=1.0 / D,
                                                          scalar2=EPS, op0=ALU.mult, op1=ALU.add),
                         reads=[("ss4b", i) for i in range(4)], writes=["rs4b"])
                    S.op("act", lambda e: e.activation(out=rs4b[:, :], in_=rs4b[:, :], func=AF.Ln),
                         reads=["rs4b"], writes=["rs4b"])
                    S.op("act", lambda e: e.activation(out=rs4b[:, :], in_=rs4b[:, :], func=AF.Exp, scale=-0.5),
                         reads=["rs4b"], writes=["rs4b"])
                L.append(nrmr)
                for i in range(4):
                    def nrm2(i=i):
                        S.op("dve", lambda e: e.tensor_scalar(out=xn2[i][:, :], in0=x1[i][:, :], scalar1=rs4b[:, i:i + 1],
                                                              scalar2=None, op0=ALU.mult),
                             reads=[("x1", par, i, 0), ("x1", par, i, 1), "rs4b"], writes=[("xn2", i)])
                    L.append(nrm2)
                for bb in range(4):
                    def tr(bb=bb):
                        bk = 4 + bb
                        for cc in range(2):
                            c = 2 * bb + cc
                            for i in range(4):
                                S.op("pe", lambda e, c=c, cc=cc, i=i: e.transpose(
                                    out=bankbf(bk)[:, cc * 512 + i * 128:cc * 512 + (i + 1) * 128],
                                    in_=xn2[i][:, c * 128:(c + 1) * 128], identity=idt[:, :]),
                                    reads=[("xn2", i), "idt"], writes=[PB(bk)], sig=(i == 3))
                    L.append(tr)
                for bb in range(4):
                    def ev(bb=bb):
                        bk = 4 + bb
                        for cc in range(2):
                            c = 2 * bb + cc
                            S.op("act", lambda e, c=c, cc=cc: e.activation(
                                out=hT2[:, c, :], in_=bankbf(bk)[:, cc * 512:(cc + 1) * 512], func=AF.Identity,
                                scale=gsf[:, c:c + 1], bias=shf[:, c:c + 1]),
                                reads=[PB(bk), "gsf", "modsb"], writes=[("hT2", par, c)])
                    L.append(ev)
                return L

            pre_g = []

            def gu_dma(f):
                wi = n_ld["g"] % 5
                n_ld["g"] += 1
                S.dma("sp", wgb[wi][:, :], wg_s.ap()[f], reads=[("wg_s", f)], writes=[("wgb", wi)])
                S.dma("sp", wub[wi][:, :], wu_s.ap()[f], reads=[("wu_s", f)], writes=[("wub", wi)])
                return wi

            def wd_dma(hf, gi):
                wi = n_ld["d"] % 6
                n_ld["d"] += 1
                S.dma("sp", wdb[wi][:, :, :], wd_s.ap()[hf, 2 * gi:2 * gi + 2].rearrange("f p n -> p f n"),
                      reads=[("wd_s", hf)], writes=[("wdb", wi)])
                return wi

            def fn_thunks(tb):
                par = tb % 2
                x1 = x1s[par]
                L = []
                for i in range(4):
                    def sqf(i=i):
                        S.op("act", lambda e: e.activation(out=junk2[:, :], in_=x1[i][:, :], func=AF.Square,
                                                           accum_out=ss4c[:, i:i + 1]),
                             reads=[("x1", par, i, 0), ("x1", par, i, 1)], writes=["junk2", ("ss4c", i)])
                    L.append(sqf)

                def rsf():
                    S.op("dve", lambda e: e.tensor_scalar(out=rs4c[:, :], in0=ss4c[:, :], scalar1=1.0 / D, scalar2=EPS,
                                                          op0=ALU.mult, op1=ALU.add),
                         reads=[("ss4c", i) for i in range(4)], writes=["rs4c"])
                L.append(rsf)

                def lef():
                    S.op("act", lambda e: e.activation(out=rs4c[:, :], in_=rs4c[:, :], func=AF.Ln),
                         reads=["rs4c"], writes=["rs4c"])
                    S.op("act", lambda e: e.activation(out=rs4c[:, :], in_=rs4c[:, :], func=AF.Exp, scale=-0.5),
                         reads=["rs4c"], writes=["rs4c"])
                L.append(lef)
                for i in range(4):
                    def outf(i=i):
                        oi = n_out[0] % 2
                        n_out[0] += 1
                        S.op("dve", lambda e: e.scalar_tensor_tensor(
                            out=ot[oi][:, :], in0=x1[i][:, :], scalar=rs4c[:, i:i + 1], in1=gfin[:, :], op0=ALU.mult, op1=ALU.mult),
                            reads=[("x1", par, i, 0), ("x1", par, i, 1), "rs4c", "gfin"], writes=[("ot", oi)])
                        S.dma("pool", y_v[tb * 4 + i], ot[oi][:, :], reads=[("ot", oi)], writes=[("y", tb * 4 + i)])
                    L.append(outf)
                return L

            pending_fn = []
            for t in x_thunks(0):
                t()
            for tb in range(NB):
                par = tb % 2
                x1 = x1s[par]
                hT2 = hT2s[par]
                filler = list(pending_fn) + (x_thunks(tb + 1) if tb + 1 < NB else [])
                nfl = len(filler)
                emitted = 0
                pre_wd = []
                for f in range(NF):
                    if pre_g:
                        wi = pre_g.pop(0)
                    else:
                        wi = gu_dma(f)
                    if f >= NF - 5:
                        pre_wd.append(wd_dma(0, len(pre_wd)))
                    bg = 2 * (f % 2)
                    bu = bg + 1
                    for c in range(8):
                        S.op("pe", lambda e, c=c, wi=wi, bg=bg: e.matmul(bank(bg), lhsT=wgb[wi][:, c * 128:(c + 1) * 128],
                                                                       rhs=hT2[:, c, :], start=(c == 0), stop=(c == 7)),
                             reads=[("wgb", wi), ("hT2", par, c)], writes=[PB(bg)], sig=(c == 7))
                    for c in range(8):
                        S.op("pe", lambda e, c=c, wi=wi, bu=bu: e.matmul(bank(bu), lhsT=wub[wi][:, c * 128:(c + 1) * 128],
                                                                       rhs=hT2[:, c, :], start=(c == 0), stop=(c == 7)),
                             reads=[("wub", wi), ("hT2", par, c)], writes=[PB(bu)], sig=(c == 7))
                    S.op("act", lambda e, f=f, bg=bg: e.activation(out=sg[f % 2][:, :], in_=bank(bg), func=AF.Silu),
                         reads=[PB(bg)], writes=[("sg", f % 2)])
                    S.op("dve", lambda e, f=f, bu=bu: e.tensor_tensor(out=hid[:, f, :], in0=bank(bu), in1=sg[f % 2][:, :], op=ALU.mult),
                         reads=[PB(bu), ("sg", f % 2)], writes=[("hid", f)])
                    target = min(nfl, (nfl * (f + 1)) // (NF - 2))
                    while emitted < target:
                        filler[emitted]()
                        emitted += 1
                while emitted < nfl:
                    filler[emitted]()
                    emitted += 1
                if tb + 1 < NB:
                    for f2 in range(4):
                        pre_g.append(gu_dma(f2))
                for hf in range(2):
                    dbk = 4 if hf == 0 else 0
                    for gi in range(NF // 2):
                        if hf == 0 and gi < len(pre_wd):
                            wi = pre_wd[gi]
                        else:
                            wi = wd_dma(hf, gi)
                        for ff in range(2):
                            f = 2 * gi + ff
                            for i in range(4):
                                S.op("pe", lambda e, f=f, ff=ff, i=i, wi=wi: e.matmul(
                                    bank(dbk + i), lhsT=hid[:, f, i * 128:(i + 1) * 128], rhs=wdb[wi][:, ff, :],
                                    start=(f == 0), stop=(f == NF - 1)),
                                    reads=[("hid", f), ("wdb", wi)], writes=[PB(dbk + i)], sig=(f == NF - 1 or i == 3))
                    for i in range(4):
                        yt = ytmp[i % 2]
                        S.op("dve", lambda e, i=i, hf=hf, yt=yt: e.tensor_tensor(
                            out=yt[:, :], in0=bank(dbk + i), in1=gate_f[:, hf * 512:(hf + 1) * 512], op=ALU.mult),
                            reads=[PB(dbk + i), ("gate_f", hf)], writes=[("ytmp", i % 2)])
                        S.op("dve", lambda e, i=i, hf=hf, yt=yt: e.tensor_tensor(
                            out=x1[i][:, hf * 512:(hf + 1) * 512], in0=x1[i][:, hf * 512:(hf + 1) * 512], in1=yt[:, :], op=ALU.add),
                            reads=[("x1", par, i, hf), ("ytmp", i % 2)], writes=[("x1", par, i, hf)])
                pending_fn = fn_thunks(tb)
                if tb + 1 == NB:
                    for t in pending_fn:
                        t()
            S.barrier()
        S.finish("sp")
    return nc


_CACHE = {}


def _get_program():
    if "nc" not in _CACHE:
        _CACHE["nc"] = build_program()
    return _CACHE["nc"]


def make_in_maps(x, c, w_ada, b_ada, g_attn, w_in, g_q, g_k, rpb, w_o, g_ffn, w_gate, w_up, w_down, g_final):
    f32 = lambda a: np.ascontiguousarray(np.asarray(a, dtype=np.float32))
    colmajor = lambda v, n: f32(np.asarray(v, np.float32).reshape(n, 128).T)
    b_int, b_sp = build_bias_tiles(np.asarray(rpb)[0])
    sel = np.zeros((128, 128), np.float32)
    sel[64, :] = 1.0
    shared = {
        "w_ada": f32(np.asarray(w_ada)[0]),
        "bada": colmajor(np.asarray(b_ada)[0], 48),
        "gattn": colmajor(np.asarray(g_attn)[0], 8),
        "gffn": colmajor(np.asarray(g_ffn)[0], 8),
        "gfin": f32(np.broadcast_to(np.asarray(g_final, np.float32)[None, :], (128, D))),
        "w_in": f32(np.asarray(w_in)[0]),
        "gq": f32(np.broadcast_to(np.asarray(g_q, np.float32)[0][None, :], (128, 64))),
        "gk": f32(np.broadcast_to(np.asarray(g_k, np.float32)[0][None, :], (128, 64))),
        "bint": b_int,
        "bsp": b_sp,
        "w_o": f32(np.asarray(w_o)[0]),
        "w_gate": f32(np.asarray(w_gate)[0]),
        "w_up": f32(np.asarray(w_up)[0]),
        "w_down": f32(np.asarray(w_down)[0]),
        "ident": np.eye(128, dtype=np.float32).astype(ml_dtypes.bfloat16),
        "identf": np.eye(128, dtype=np.float32),
        "sel64": sel,
        "cs": rope_table(),
    }
    x = np.asarray(x, np.float32)
    c = np.asarray(c, np.float32)
    maps = []
    for b in range(x.shape[0]):
        m = dict(shared)
        m["x"] = np.ascontiguousarray(x[b])
        m["ccol"] = colmajor(c[b], 8)
        maps.append(m)
    return maps


def kernel(x, c, w_ada, b_ada, g_attn, w_in, g_q, g_k, rpb, w_o, g_ffn, w_gate, w_up, w_down, g_final):
    nc = _get_program()
    in_maps = make_in_maps(x, c, w_ada, b_ada, g_attn, w_in, g_q, g_k, rpb, w_o, g_ffn, w_gate, w_up, w_down, g_final)
    res = run_bass_kernel_spmd(nc, in_maps, core_ids=list(range(N_CORES)))
    out = np.stack([np.asarray(r["y"], dtype=np.float32) for r in res.results], axis=0)
    return out
```
